# Optimizing a Trainium2 kernel written in Bass

```python
import math
import jax
import jax.numpy as jnp
from jax import lax
import numpy as np


D_MODEL = 1024
BATCH = 32
SEQ = 2048
DEPTH = 2

GRID_W = 64
CTX_LEN = 256
EPS = 1e-6
N_MOD = 6
SSD_HEADS = 4
SSD_HEAD_DIM = 64
SSD_WIDTH = SSD_HEADS * SSD_HEAD_DIM
SSD_GROUPS = 2
SSD_STATE = 64
SSD_CONV = 5
SSD_CHUNK = 128
SSD_XBC = SSD_WIDTH + 2 * SSD_GROUPS * SSD_STATE
SSD_IN = SSD_WIDTH + SSD_XBC + 2 * SSD_HEADS
DIFF_HEADS = 4
DIFF_HEAD_DIM = 64
DIFF_WIDTH = DIFF_HEADS * 2 * DIFF_HEAD_DIM
DIFF_IN = 3 * DIFF_WIDTH
Q_BLOCK = 128
ROPE_THETA = 10000.0
S5_GROUP = 16
S5_WIDTH = 256
S5_GROUPS = S5_WIDTH // S5_GROUP
S5_STATE = 64
MIX_WIDTH = SSD_WIDTH + DIFF_WIDTH + S5_WIDTH
IN_WIDTH = SSD_IN + DIFF_IN + S5_WIDTH
FFN_DIM = 2816
FFN_CONV = 3
F32 = jnp.float32

kernel_name = 'hybrid_ssd_diffattn_s5_prefix_dit_block'


def flip(t):
    return jnp.flip(t, axis=1)


def ident(t):
    return t


def rmsnorm(x, w):
    xf = x.astype(F32)
    y = xf * lax.rsqrt(jnp.mean(xf * xf, axis=-1, keepdims=True) + EPS)
    return (y * w.astype(F32)).astype(x.dtype)


def modulate(h, shift, scale):
    return h * (1.0 + scale) + shift


def dwconv(x, w, b):
    k = w.shape[0]
    y = lax.conv_general_dilated(x, w[:, None, :].astype(x.dtype), window_strides=(1,),
                                 padding=[((k - 1) // 2, k // 2)],
                                 dimension_numbers=('NWC', 'WIO', 'NWC'),
                                 feature_group_count=x.shape[-1])
    return y + b.astype(x.dtype)


def axial_rope_tables(n_tokens):
    rows = n_tokens // GRID_W
    r, col = jnp.meshgrid(jnp.arange(rows, dtype=F32), jnp.arange(GRID_W, dtype=F32), indexing='ij')
    quarter = DIFF_HEAD_DIM // 4
    inv_freq = ROPE_THETA ** (-jnp.arange(quarter, dtype=F32) / quarter)
    ang_r = r.reshape(-1, 1) * inv_freq
    ang_c = col.reshape(-1, 1) * inv_freq
    return jnp.cos(ang_r), jnp.sin(ang_r), jnp.cos(ang_c), jnp.sin(ang_c)


def rotate(x, cos, sin):
    x1, x2 = jnp.split(x, 2, axis=-1)
    return jnp.concatenate([x1 * cos - x2 * sin, x2 * cos + x1 * sin], axis=-1)


def apply_axial_rope(x, cos_r, sin_r, cos_c, sin_c):
    def bc(t):
        return t[None, :, None, None, :]
    xf = x.astype(F32)
    x_row, x_col = jnp.split(xf, 2, axis=-1)
    out = jnp.concatenate([rotate(x_row, bc(cos_r), bc(sin_r)),
                           rotate(x_col, bc(cos_c), bc(sin_c))], axis=-1)
    return out.astype(x.dtype)


def ssd_scan(xs, dt, a, bm, cm, h0):
    bsz, n, nh, hp = xs.shape
    q = SSD_CHUNK
    nc = n // q
    xdt = (xs * dt[..., None]).reshape(bsz, nc, q, nh, hp)
    bm = bm.reshape(bsz, nc, q, nh, SSD_STATE)
    cm = cm.reshape(bsz, nc, q, nh, SSD_STATE)
    a_cs = jnp.cumsum((dt * a).reshape(bsz, nc, q, nh), axis=2).transpose(0, 3, 1, 2)
    tri = jnp.tril(jnp.ones((q, q), bool))
    decay_in = jnp.exp(jnp.where(tri, a_cs[..., :, None] - a_cs[..., None, :], -jnp.inf))
    scores = jnp.einsum('bclhn,bcshn->bhcls', cm, bm) * decay_in
    y_diag = jnp.einsum('bhcls,bcshp->bclhp', scores, xdt)
    decay_out = jnp.exp(a_cs[..., -1:] - a_cs).transpose(0, 2, 3, 1)
    states = jnp.einsum('bclhn,bclhp->bchpn', bm, xdt * decay_out[..., None])
    states = jnp.concatenate([h0[:, None], states], axis=1)
    chunk_cs = jnp.cumsum(jnp.pad(a_cs[..., -1], ((0, 0), (0, 0), (1, 0))), axis=-1)
    tri_c = jnp.tril(jnp.ones((nc + 1, nc + 1), bool))
    decay_chunk = jnp.exp(jnp.where(tri_c, chunk_cs[..., :, None] - chunk_cs[..., None, :], -jnp.inf))
    states = jnp.einsum('bhzc,bchpn->bzhpn', decay_chunk, states)
    y_off = jnp.einsum('bclhn,bchpn->bclhp', cm, states[:, :-1]) * jnp.exp(a_cs).transpose(0, 2, 3, 1)[..., None]
    return (y_diag + y_off).reshape(bsz, n, nh, hp), states[:, -1]


def ssd_prep(p, conv_w, conv_b, dt_bias):
    bsz, n, _ = p.shape
    z = p[..., :SSD_WIDTH]
    xbc = jax.nn.silu(dwconv(p[..., SSD_WIDTH:SSD_WIDTH + SSD_XBC], conv_w, conv_b)).astype(F32)
    dt_raw = p[..., SSD_WIDTH + SSD_XBC:].astype(F32).reshape(bsz, n, 2, SSD_HEADS)
    xs = xbc[..., :SSD_WIDTH].reshape(bsz, n, SSD_HEADS, SSD_HEAD_DIM)
    gn = SSD_GROUPS * SSD_STATE
    rep = SSD_HEADS // SSD_GROUPS
    bm = jnp.repeat(xbc[..., SSD_WIDTH:SSD_WIDTH + gn].reshape(bsz, n, SSD_GROUPS, SSD_STATE), rep, axis=2)
    cm = jnp.repeat(xbc[..., SSD_WIDTH + gn:].reshape(bsz, n, SSD_GROUPS, SSD_STATE), rep, axis=2)
    dt = jax.nn.softplus(dt_raw + dt_bias.astype(F32))
    return z, xs, bm, cm, dt


def ssd_gate(y, z, norm_w):
    bsz, n = y.shape[:2]
    return rmsnorm(y.reshape(bsz, n, SSD_WIDTH) * jax.nn.silu(z.astype(F32)), norm_w)


def ssd_mixer(p_c, p_l, conv_w, conv_b, dt_bias, a_log, d_skip, norm_w, need_ctx):
    out_dtype = p_l.dtype
    a = -jnp.exp(a_log.astype(F32))
    z_c, x_c, b_c, c_c, dt_c = ssd_prep(p_c, conv_w, conv_b, dt_bias)
    z_l, x_l, b_l, c_l, dt_l = ssd_prep(p_l, conv_w, conv_b, dt_bias)
    d = d_skip.astype(F32)[:, None]
    y_l = d * x_l
    y_c = d * x_c if need_ctx else None
    h0 = jnp.zeros((p_l.shape[0], SSD_HEADS, SSD_HEAD_DIM, SSD_STATE), F32)
    for direction in range(2):
        o = flip if direction else ident
        yc_d, hc_d = ssd_scan(o(x_c), o(dt_c[:, :, direction]), a[direction], o(b_c), o(c_c), h0)
        yl_d, _ = ssd_scan(o(x_l), o(dt_l[:, :, direction]), a[direction], o(b_l), o(c_l), hc_d)
        y_l = y_l + o(yl_d)
        if need_ctx:
            y_c = y_c + o(yc_d)
    out_l = ssd_gate(y_l, z_l, norm_w).astype(out_dtype)
    out_c = ssd_gate(y_c, z_c, norm_w).astype(out_dtype) if need_ctx else None
    return out_c, out_l


def diff_attend(q, k, v, lam):
    s = jnp.einsum('bqhcd,bshcd->bhcqs', q, k).astype(F32) * (DIFF_HEAD_DIM ** -0.5)
    p = jax.nn.softmax(s, axis=-1)
    w = p[:, :, 0] - lam * p[:, :, 1]
    return jnp.einsum('bhqs,bshe->bqhe', w.astype(v.dtype), v)


def diff_attn_mixer(p_c, p_l, lam_q1, lam_k1, lam_q2, lam_k2, subln_w, lam_init, need_ctx):
    bsz, n, _ = p_l.shape

    def split(p):
        m = p.shape[1]
        q = p[..., :DIFF_WIDTH].reshape(bsz, m, DIFF_HEADS, 2, DIFF_HEAD_DIM)
        k = p[..., DIFF_WIDTH:2 * DIFF_WIDTH].reshape(bsz, m, DIFF_HEADS, 2, DIFF_HEAD_DIM)
        v = p[..., 2 * DIFF_WIDTH:].reshape(bsz, m, DIFF_HEADS, 2 * DIFF_HEAD_DIM)
        return q, k, v

    def head_out(o):
        return (rmsnorm(o, subln_w) * (1.0 - lam_init)).reshape(bsz, o.shape[1], DIFF_WIDTH)

    q_c, k_c, v_c = split(p_c)
    q_l, k_l, v_l = split(p_l)
    tabs = axial_rope_tables(n)
    q_l = apply_axial_rope(q_l, *tabs)
    k_l = apply_axial_rope(k_l, *tabs)
    lam = (jnp.exp(jnp.sum(lam_q1.astype(F32) * lam_k1.astype(F32)))
           - jnp.exp(jnp.sum(lam_q2.astype(F32) * lam_k2.astype(F32))) + lam_init)
    k_all = jnp.concatenate([k_c, k_l], axis=1)
    v_all = jnp.concatenate([v_c, v_l], axis=1)
    nb = n // Q_BLOCK
    q_blocks = q_l.reshape(bsz, nb, Q_BLOCK, DIFF_HEADS, 2, DIFF_HEAD_DIM).swapaxes(0, 1)
    o_l = lax.map(lambda qb: diff_attend(qb, k_all, v_all, lam), q_blocks)
    o_l = o_l.swapaxes(0, 1).reshape(bsz, n, DIFF_HEADS, 2 * DIFF_HEAD_DIM)
    out_l = head_out(o_l)
    out_c = head_out(diff_attend(q_c, k_c, v_c, lam)) if need_ctx else None
    return out_c, out_l


def s5_discretise(lam_re, lam_im, log_step, b_re, b_im):
    step = jnp.exp(log_step.astype(F32))[:, None]
    lr, li = lam_re.astype(F32), lam_im.astype(F32)
    mag = jnp.exp(lr * step)
    ab_re = mag * jnp.cos(li * step)
    ab_im = mag * jnp.sin(li * step)
    den = lr * lr + li * li
    nr, ni = ab_re - 1.0, ab_im
    coef_re = ((nr * lr + ni * li) / den)[..., None]
    coef_im = ((ni * lr - nr * li) / den)[..., None]
    br, bi = b_re.astype(F32), b_im.astype(F32)
    return ab_re, ab_im, coef_re * br - coef_im * bi, coef_re * bi + coef_im * br


def s5_combine(e1, e2):
    a1r, a1i, b1r, b1i = e1
    a2r, a2i, b2r, b2i = e2
    return (a2r * a1r - a2i * a1i, a2r * a1i + a2i * a1r,
            a2r * b1r - a2i * b1i + b2r, a2r * b1i + a2i * b1r + b2i)


def s5_scan(u, ab_re, ab_im, bb_re, bb_im, h0_re, h0_im):
    n = u.shape[1]
    bu_re = jnp.einsum('blgc,gnc->blgn', u, bb_re)
    bu_im = jnp.einsum('blgc,gnc->blgn', u, bb_im)
    bu_re = bu_re.at[:, 0].add(ab_re * h0_re - ab_im * h0_im)
    bu_im = bu_im.at[:, 0].add(ab_re * h0_im + ab_im * h0_re)
    a_re = jnp.broadcast_to(ab_re, (1, n) + ab_re.shape)
    a_im = jnp.broadcast_to(ab_im, (1, n) + ab_im.shape)
    _, _, h_re, h_im = lax.associative_scan(s5_combine, (a_re, a_im, bu_re, bu_im), axis=1)
    return h_re, h_im


def s5_readout(h_re, h_im, c_re, c_im):
    return (jnp.einsum('blgn,gcn->blgc', h_re, c_re.astype(F32))
            - jnp.einsum('blgn,gcn->blgc', h_im, c_im.astype(F32)))


def s5_mixer(u_c, u_l, lam_re, lam_im, log_step, b_re, b_im, c_re, c_im, d_skip, glu_w, glu_b, need_ctx):
    out_dtype = u_l.dtype
    bsz = u_l.shape[0]

    def groups(u):
        return u.astype(F32).reshape(bsz, u.shape[1], S5_GROUPS, S5_GROUP)

    def finish(y):
        y = jax.nn.gelu(y.reshape(bsz, y.shape[1], S5_WIDTH))
        return (y * jax.nn.sigmoid(y @ glu_w.astype(F32) + glu_b.astype(F32))).astype(out_dtype)

    uc, ul = groups(u_c), groups(u_l)
    d = d_skip.astype(F32)
    y_l = d * ul
    y_c = d * uc if need_ctx else None
    zero = jnp.zeros((bsz, S5_GROUPS, S5_STATE), F32)
    for direction in range(2):
        o = flip if direction else ident
        ab_re, ab_im, bb_re, bb_im = s5_discretise(lam_re[direction], lam_im[direction], log_step[direction],
                                                   b_re[direction], b_im[direction])
        hc_re, hc_im = s5_scan(o(uc), ab_re, ab_im, bb_re, bb_im, zero, zero)
        hl_re, hl_im = s5_scan(o(ul), ab_re, ab_im, bb_re, bb_im, hc_re[:, -1], hc_im[:, -1])
        y_l = y_l + o(s5_readout(hl_re, hl_im, c_re[direction], c_im[direction]))
        if need_ctx:
            y_c = y_c + o(s5_readout(hc_re, hc_im, c_re[direction], c_im[direction]))
    out_l = finish(y_l)
    out_c = finish(y_c) if need_ctx else None
    return out_c, out_l


def token_mixer(h_c, h_l, w_in, w_out, ssd_p, diff_p, s5_p, lam_init, need_ctx):
    p_c = h_c @ w_in
    p_l = h_l @ w_in
    o1, o2 = SSD_IN, SSD_IN + DIFF_IN
    a_c, a_l = ssd_mixer(p_c[..., :o1], p_l[..., :o1], *ssd_p, need_ctx)
    b_c, b_l = diff_attn_mixer(p_c[..., o1:o2], p_l[..., o1:o2], *diff_p, lam_init, need_ctx)
    s_c, s_l = s5_mixer(p_c[..., o2:], p_l[..., o2:], *s5_p, need_ctx)
    y_l = jnp.concatenate([a_l, b_l, s_l], axis=-1) @ w_out
    y_c = jnp.concatenate([a_c, b_c, s_c], axis=-1) @ w_out if need_ctx else None
    return y_c, y_l


def conv_ffn(h, w_gate, w_up, conv_w, conv_b, w_down):
    g = dwconv(h @ w_gate, conv_w, conv_b)
    return (jax.nn.silu(g) * (h @ w_up)) @ w_down


def setup_inputs(seed: int = 0) -> dict:
    key = jax.random.key(seed)
    ks = jax.random.split(key, 40)
    L = DEPTH

    def nrm(i, shape, scale):
        return scale * jax.random.normal(ks[i], shape, F32)

    def unif(i, shape, lo, hi):
        return jax.random.uniform(ks[i], shape, F32, lo, hi)

    dt0 = jnp.exp(unif(14, (L, 2, SSD_HEADS), math.log(1e-3), math.log(1e-1)))
    return {
        'x': nrm(0, (BATCH, SEQ, D_MODEL), 1.0),
        'c': nrm(1, (BATCH, D_MODEL), 1.0),
        'ctx': nrm(2, (BATCH, CTX_LEN, D_MODEL), 1.0),
        'c_ctx': nrm(3, (D_MODEL,), 1.0),
        'mod_w': nrm(4, (L, D_MODEL, N_MOD * D_MODEL), D_MODEL ** -0.5),
        'mod_b': nrm(5, (L, N_MOD * D_MODEL), 0.02),
        'mix_norm_pre': 1.0 + nrm(6, (L, D_MODEL), 0.02),
        'mix_norm_post': 1.0 + nrm(7, (L, D_MODEL), 0.02),
        'ffn_norm_pre': 1.0 + nrm(8, (L, D_MODEL), 0.02),
        'ffn_norm_post': 1.0 + nrm(9, (L, D_MODEL), 0.02),
        'w_in': nrm(10, (L, D_MODEL, IN_WIDTH), D_MODEL ** -0.5),
        'w_out': nrm(11, (L, MIX_WIDTH, D_MODEL), MIX_WIDTH ** -0.5),
        'ssd_conv_w': nrm(12, (L, SSD_CONV, SSD_XBC), SSD_CONV ** -0.5),
        'ssd_conv_b': nrm(13, (L, SSD_XBC), 0.02),
        'ssd_dt_bias': dt0 + jnp.log(-jnp.expm1(-dt0)),
        'ssd_a_log': jnp.log(unif(15, (L, 2, SSD_HEADS), 1.0, 16.0)),
        'ssd_d': 1.0 + nrm(16, (L, SSD_HEADS), 0.1),
        'ssd_norm_w': 1.0 + nrm(17, (L, SSD_WIDTH), 0.02),
        'diff_lam_q1': nrm(18, (L, DIFF_HEAD_DIM), 0.1),
        'diff_lam_k1': nrm(19, (L, DIFF_HEAD_DIM), 0.1),
        'diff_lam_q2': nrm(20, (L, DIFF_HEAD_DIM), 0.1),
        'diff_lam_k2': nrm(21, (L, DIFF_HEAD_DIM), 0.1),
        'diff_subln_w': 1.0 + nrm(22, (L, 2 * DIFF_HEAD_DIM), 0.02),
        's5_lam_re': -0.5 + nrm(23, (L, 2, S5_GROUPS, S5_STATE), 0.01),
        's5_lam_im': jnp.pi * jnp.arange(S5_STATE, dtype=F32) + nrm(24, (L, 2, S5_GROUPS, S5_STATE), 0.01),
        's5_log_step': unif(25, (L, 2, S5_GROUPS), math.log(1e-3), math.log(1e-1)),
        's5_b_re': nrm(26, (L, 2, S5_GROUPS, S5_STATE, S5_GROUP), (2 * S5_GROUP) ** -0.5),
        's5_b_im': nrm(27, (L, 2, S5_GROUPS, S5_STATE, S5_GROUP), (2 * S5_GROUP) ** -0.5),
        's5_c_re': nrm(28, (L, 2, S5_GROUPS, S5_GROUP, S5_STATE), (2 * S5_STATE) ** -0.5),
        's5_c_im': nrm(29, (L, 2, S5_GROUPS, S5_GROUP, S5_STATE), (2 * S5_STATE) ** -0.5),
        's5_d': nrm(30, (L, S5_GROUPS, S5_GROUP), 1.0),
        's5_glu_w': nrm(31, (L, S5_WIDTH, S5_WIDTH), S5_WIDTH ** -0.5),
        's5_glu_b': nrm(32, (L, S5_WIDTH), 0.02),
        'ffn_w_gate': nrm(33, (L, D_MODEL, FFN_DIM), D_MODEL ** -0.5),
        'ffn_w_up': nrm(34, (L, D_MODEL, FFN_DIM), D_MODEL ** -0.5),
        'ffn_conv_w': nrm(35, (L, FFN_CONV, FFN_DIM), FFN_CONV ** -0.5),
        'ffn_conv_b': nrm(36, (L, FFN_DIM), 0.02),
        'ffn_w_down': nrm(37, (L, FFN_DIM, D_MODEL), FFN_DIM ** -0.5),
    }


def reference(x, c, ctx, c_ctx, mod_w, mod_b, mix_norm_pre, mix_norm_post, ffn_norm_pre, ffn_norm_post,
              w_in, w_out, ssd_conv_w, ssd_conv_b, ssd_dt_bias, ssd_a_log, ssd_d, ssd_norm_w,
              diff_lam_q1, diff_lam_k1, diff_lam_q2, diff_lam_k2, diff_subln_w,
              s5_lam_re, s5_lam_im, s5_log_step, s5_b_re, s5_b_im, s5_c_re, s5_c_im, s5_d, s5_glu_w, s5_glu_b,
              ffn_w_gate, ffn_w_up, ffn_conv_w, ffn_conv_b, ffn_w_down):
    for layer in range(DEPTH):
        need_ctx = layer < DEPTH - 1
        lam_init = 0.8 - 0.6 * math.exp(-0.3 * layer)
        mod_l = (jax.nn.silu(c) @ mod_w[layer] + mod_b[layer])[:, None, :]
        mod_c = jax.nn.silu(c_ctx) @ mod_w[layer] + mod_b[layer]
        sh_a, sc_a, g_a, sh_f, sc_f, g_f = jnp.split(mod_l, N_MOD, axis=-1)
        csh_a, csc_a, cg_a, csh_f, csc_f, cg_f = jnp.split(mod_c, N_MOD, axis=-1)

        h_l = modulate(rmsnorm(x, mix_norm_pre[layer]), sh_a, sc_a)
        h_c = modulate(rmsnorm(ctx, mix_norm_pre[layer]), csh_a, csc_a)
        ssd_p = (ssd_conv_w[layer], ssd_conv_b[layer], ssd_dt_bias[layer], ssd_a_log[layer],
                 ssd_d[layer], ssd_norm_w[layer])
        diff_p = (diff_lam_q1[layer], diff_lam_k1[layer], diff_lam_q2[layer], diff_lam_k2[layer],
                  diff_subln_w[layer])
        s5_p = (s5_lam_re[layer], s5_lam_im[layer], s5_log_step[layer], s5_b_re[layer], s5_b_im[layer],
                s5_c_re[layer], s5_c_im[layer], s5_d[layer], s5_glu_w[layer], s5_glu_b[layer])
        y_c, y_l = token_mixer(h_c, h_l, w_in[layer], w_out[layer], ssd_p, diff_p, s5_p, lam_init, need_ctx)
        x = x + g_a * rmsnorm(y_l, mix_norm_post[layer])

        ffn_p = (ffn_w_gate[layer], ffn_w_up[layer], ffn_conv_w[layer], ffn_conv_b[layer], ffn_w_down[layer])
        f_l = conv_ffn(modulate(rmsnorm(x, ffn_norm_pre[layer]), sh_f, sc_f), *ffn_p)
        x = x + g_f * rmsnorm(f_l, ffn_norm_post[layer])

        if need_ctx:
            ctx = ctx + cg_a * rmsnorm(y_c, mix_norm_post[layer])
            f_c = conv_ffn(modulate(rmsnorm(ctx, ffn_norm_pre[layer]), csh_f, csc_f), *ffn_p)
            ctx = ctx + cg_f * rmsnorm(f_c, ffn_norm_post[layer])
    return x
```

```python
import contextlib
import math
import numpy as np
import concourse.bass as bass
import concourse.mybir as mybir
from concourse.bass_utils import run_bass_kernel_spmd

F32 = mybir.dt.float32
BF16 = mybir.dt.bfloat16
AF = mybir.ActivationFunctionType
ALU = mybir.AluOpType

ENGS = ('pe', 'act', 'dve', 'pool', 'sp')
SAME_ENGINE_SYNC = ('act', 'dve', 'pool')
NDSEM = 6
EPS = 1e-6
NT = 18
SEQ = 2304
TWO_PI = 2.0 * math.pi


class Buf:
    def __init__(self, t, name, psum=False):
        self.t = t
        self.name = name
        self.psum = psum
        self.w = {}
        self.r = {}

    def _keys(self, key):
        if key is None:
            return list(set(self.w) | set(self.r) | {None})
        return [key, None]

    def wdeps(self, key):
        d = set()
        for k in self._keys(key):
            d |= self.w.get(k, set())
        return d

    def rdeps(self, key):
        d = set()
        for k in self._keys(key):
            d |= set(self.r.get(k, {}).items())
        return d

    def add_reader(self, key, ev):
        rr = self.r.setdefault(key, {})
        if rr.get(ev[0], 0) < ev[1]:
            rr[ev[0]] = ev[1]

    def set_writer(self, key, ev):
        if key is None:
            self.w = {None: {ev}}
            self.r = {}
        else:
            self.w[key] = {ev}
            self.r[key] = {}

    def add_dma_writer(self, key, ev):
        cur = self.w.get(key, set())
        cur = {e for e in cur if e[0][1] == ev[0][1] and e[0][0].startswith('q') and not (e[0] == ev[0] and e[1] <= ev[1])}
        cur.add(ev)
        if key is None:
            self.w = {None: cur}
            self.r = {}
        else:
            self.w[key] = cur
            self.r[key] = {}

    def __getitem__(self, idx):
        return self.t[idx]


def _norm(lst):
    return [(x, None) if isinstance(x, Buf) else x for x in lst]


class Sched:
    NQ = {'sp': 14, 'pool': 3}

    def __init__(self, nc, stack):
        self.nc = nc
        self.stack = stack
        self.eng = {'pe': nc.tensor, 'act': nc.scalar, 'dve': nc.vector, 'pool': nc.gpsimd, 'sp': nc.sync}
        self.phase = 0
        self.ninst = 0
        self.sets = []
        for i in range(2):
            d = {}
            for e in ENGS:
                d[e] = stack.enter_context(nc.semaphore(f"s_{e}_{i}"))
            for q, n in self.NQ.items():
                for j in range(n):
                    d[f"q{q}{j}"] = stack.enter_context(nc.semaphore(f"s_q{q}{j}_{i}"))
            self.sets.append(d)
        self.bsems = [stack.enter_context(nc.semaphore(f"s_bar{i}")) for i in range(2)]
        self.nbar = 0
        self._new_sems()

    def _new_sems(self):
        self.sem = self.sets[self.phase % 2]
        self.cnt = {k: 0 for k in self.sem}
        self.seen = {e: {} for e in ENGS}
        self.rr = {q: 0 for q in self.NQ}

    def _wait(self, e, deps):
        need = {}
        for (k, v) in deps:
            if k[1] != self.phase:
                continue
            if need.get(k, 0) < v:
                need[k] = v
        for k, v in need.items():
            if self.seen[e].get(k, 0) >= v:
                continue
            self.eng[e].wait_ge(self.sem[k[0]], v)
            self.seen[e][k] = v
            self.ninst += 1

    def op(self, e, fn, reads=(), writes=()):
        reads = _norm(reads)
        writes = _norm(writes)
        writes = writes + [r for r in reads if r[0].psum]
        reads = [r for r in reads if not r[0].psum]
        deps = set()
        for b, k in reads:
            deps |= b.wdeps(k)
        for b, k in writes:
            deps |= b.wdeps(k)
            for ev in b.rdeps(k):
                if ev[0][0] != e:
                    deps.add(ev)
        if e not in SAME_ENGINE_SYNC:
            deps = {d for d in deps if d[0][0] != e}
        self._wait(e, deps)
        inst = fn(self.eng[e])
        self.cnt[e] += 1
        inst.then_inc(self.sem[e], 1)
        self.ninst += 1
        ev = ((e, self.phase), self.cnt[e])
        for b, k in reads:
            b.add_reader(k, ev)
        for b, k in writes:
            b.set_writer(k, ev)
        return inst

    def dma(self, out, in_, reads=(), writes=(), q='sp', g=0, **kw):
        reads = _norm(reads)
        writes = _norm(writes)
        deps = set()
        for b, k in reads:
            deps |= b.wdeps(k)
        for b, k in writes:
            same = b.w.get(k, set())
            deps |= {ev for ev in b.wdeps(k) if not (ev in same and ev[0][0].startswith('q'))} | b.rdeps(k)
        self._wait(q, deps)
        sk = f"q{q}{self.rr[q] % self.NQ[q]}"
        self.rr[q] += 1
        kk = (sk, self.phase)
        if self.cnt[sk] > 0 and self.seen[q].get(kk, 0) < self.cnt[sk]:
            self.eng[q].wait_ge(self.sem[sk], self.cnt[sk])
            self.seen[q][kk] = self.cnt[sk]
        inst = self.eng[q].dma_start(out=out, in_=in_, **kw)
        self.cnt[sk] += 16
        inst.then_inc(self.sem[sk], 16)
        self.ninst += 1
        ev = (kk, self.cnt[sk])
        for b, k in reads:
            b.add_reader(k, ev)
        for b, k in writes:
            b.add_dma_writer(k, ev)
        return inst

    def barrier(self, final=False):
        bsem = self.bsems[self.nbar % 2]
        other = self.bsems[(self.nbar + 1) % 2]
        self.nbar += 1
        for k in self.sem:
            if k.startswith('q') and self.cnt[k] > 0:
                self.eng['sp'].wait_ge(self.sem[k], self.cnt[k])
        for e in ENGS:
            if self.cnt[e] > 0:
                self.eng[e].wait_ge(self.sem[e], self.cnt[e])
            self.eng[e].sem_inc(bsem, 1)
        self.eng['sp'].wait_ge(bsem, len(ENGS))
        if not final:
            for k, sm in self.sets[(self.phase + 1) % 2].items():
                self.eng['sp'].sem_clear(sm)
            self.eng['sp'].sem_clear(other)
        self.eng['sp'].sem_inc(bsem, 1)
        for e in ENGS:
            self.eng[e].wait_ge(bsem, len(ENGS) + 1)
        if not final:
            self.phase += 1
            self._new_sems()


PARAMS = [
    ('mod_w', (2, 1024, 6144)), ('mod_b', (2, 6144)), ('mix_norm_pre', (2, 1024)), ('mix_norm_post', (2, 1024)),
    ('ffn_norm_pre', (2, 1024)), ('ffn_norm_post', (2, 1024)), ('w_in', (2, 1024, 2568)), ('w_out', (2, 1024, 1024)),
    ('ssd_conv_w', (2, 5, 512)), ('ssd_conv_b', (2, 512)), ('ssd_dt_bias', (2, 2, 4)), ('ssd_a_log', (2, 2, 4)),
    ('ssd_d', (2, 4)), ('ssd_norm_w', (2, 256)), ('diff_lam_q1', (2, 64)), ('diff_lam_k1', (2, 64)),
    ('diff_lam_q2', (2, 64)), ('diff_lam_k2', (2, 64)), ('diff_subln_w', (2, 128)),
    ('s5_lam_re', (2, 2, 16, 64)), ('s5_lam_im', (2, 2, 16, 64)), ('s5_log_step', (2, 2, 16)),
    ('s5_b_re', (2, 2, 16, 64, 16)), ('s5_b_im', (2, 2, 16, 64, 16)), ('s5_c_re', (2, 2, 16, 16, 64)),
    ('s5_c_im', (2, 2, 16, 16, 64)), ('s5_d', (2, 16, 16)), ('s5_glu_w', (2, 256, 256)), ('s5_glu_b', (2, 256)),
    ('ffn_w_gate', (2, 1024, 2816)), ('ffn_w_up', (2, 1024, 2816)), ('ffn_conv_w', (2, 3, 2816)),
    ('ffn_conv_b', (2, 2816)), ('ffn_w_down', (2, 2816, 1024)),
]


def host_consts():
    r = np.arange(128)
    U = (r[:, None] <= r[None, :]).astype(np.float32)
    c = {}
    c['c_ident'] = np.eye(128, dtype=np.float32)
    c['c_U'] = U
    c['c_UT'] = np.ascontiguousarray(U.T)
    c['c_mf'] = np.where(r[None, :] >= r[:, None], 0.0, -30000.0).astype(np.float32)
    c['c_mb'] = np.where(r[None, :] <= r[:, None], 0.0, -30000.0).astype(np.float32)
    c['c_ones'] = np.ones((128, 128), np.float32)
    t = np.arange(2048)
    rr = (t // 64).astype(np.float32)
    col = (t % 64).astype(np.float32)
    inv = (10000.0 ** (-np.arange(16, dtype=np.float32) / 16)).astype(np.float32)
    ar = rr[:, None] * inv[None, :]
    ac = col[:, None] * inv[None, :]
    c['c_cos'] = np.concatenate([np.cos(ar), np.cos(ac)], 1).astype(np.float32)
    c['c_sin'] = np.concatenate([np.sin(ar), np.sin(ac)], 1).astype(np.float32)
    c['c_iota'] = np.tile(np.arange(256, dtype=np.float32)[None, :], (128, 1))
    return c


import os
_CUT = int(os.environ.get('SSDCUT', '0'))


def ssd_stage(st, nc, s, sbuf, bank, P, D, l, hT, mixT, load_w, rstd_from_ss, tcols, PIECES, C, DB, stop, mkstg, **_):
    wzd = sbuf(st, "wzd", [128, 8, 264], BF16)
    wx = sbuf(st, "wx", [128, 8, 512], BF16)
    with contextlib.ExitStack() as s0:
        stg = mkstg(s0)
        load_w(stg, wzd, 'w_in', l, 0, 256, 0)
        load_w(stg, wzd, 'w_in', l, 768, 8, 256)
        load_w(stg, wx, 'w_in', l, 256, 512, 0)
        s.barrier()
    if stop == 'SSD0':
        return
    zs = sbuf(st, "zs", [128, NT, 256], BF16)
    dtr = sbuf(st, "dtr", [128, NT, 8])
    xbcT = sbuf(st, "xbcT", [128, 4, SEQ], BF16)
    xbc = sbuf(st, "xbc", [128, NT, 512], BF16)
    xraw = sbuf(st, "xraw", [128, 2312])
    xacc = sbuf(st, "xacc", [128, 2312])
    yacc = sbuf(st, "yacc", [128, NT, 256])
    for t in range(NT):
        pb = bank()
        for k in range(8):
            s.op('pe', lambda e, k=k: e.matmul(pb[:, 0:264], lhsT=hT[:, k, tcols(t)], rhs=wzd[:, k, :], start=(k == 0), stop=(k == 7)),
                 reads=[(hT, t), wzd], writes=[pb])
        s.op('act', lambda e: e.activation(out=zs[:, t, :], in_=pb[:, 0:256], func=AF.Silu), reads=[pb], writes=[(zs, t)])
        s.op('dve', lambda e: e.tensor_copy(out=dtr[:, t, :], in_=pb[:, 256:264]), reads=[pb], writes=[dtr])
    if stop == 'SSD1':
        return
    s.op('pool', lambda e: e.memset(xraw[:], 0.0), writes=[xraw])
    for c in range(4):
        for (c0, c1) in PIECES:
            pb = bank()
            n = c1 - c0
            for k in range(8):
                s.op('pe', lambda e, k=k: e.matmul(pb[:, 0:n], lhsT=wx[:, k, c * 128:(c + 1) * 128], rhs=hT[:, k, c0:c1], start=(k == 0), stop=(k == 7)),
                     reads=[hT, wx], writes=[pb])
            off = 2 if c0 == 0 else 6
            s.op('act', lambda e: e.copy(out=xraw[:, off + c0:off + c1], in_=pb[:, 0:n]), reads=[pb], writes=[xraw])
        s.op('act', lambda e: e.activation(out=xacc[:, 2:2310], in_=xraw[:, 2:2310], func=AF.Identity, scale=P['scw'][:, c, 2:3], bias=P['scb'][:, c:c + 1]),
             reads=[xraw, P['scw'], P['scb']], writes=[xacc])
        for j in (0, 1, 3, 4):
            s.op('dve', lambda e, j=j: e.scalar_tensor_tensor(out=xacc[:, 2:2310], in0=xraw[:, j:j + 2308], scalar=P['scw'][:, c, j:j + 1],
                                                             in1=xacc[:, 2:2310], op0=ALU.mult, op1=ALU.add),
                 reads=[xraw, xacc, P['scw']], writes=[xacc])
        s.op('act', lambda e: e.activation(out=xbcT[:, c, 0:256], in_=xacc[:, 2:258], func=AF.Silu), reads=[xacc], writes=[xbcT])
        s.op('act', lambda e: e.activation(out=xbcT[:, c, 256:2304], in_=xacc[:, 262:2310], func=AF.Silu), reads=[xacc], writes=[xbcT])
    if stop == 'SSD2':
        return
    for t in range(NT):
        pb = bank()
        pbb = pb.t[:].bitcast(BF16)
        for c in range(4):
            s.op('pe', lambda e, c=c: e.transpose(out=pbb[:, c * 128:(c + 1) * 128], in_=xbcT[:, c, tcols(t)], identity=C['ident_b'][:]),
                 reads=[xbcT, C['ident_b']], writes=[pb])
        s.op('act', lambda e: e.copy(out=xbc[:, t, :], in_=pbb[:, 0:512]), reads=[pb], writes=[(xbc, t)])
    if stop == 'SSD3':
        return
    dt = sbuf(st, "dt", [128, NT, 8])
    dta = sbuf(st, "dta", [128, NT, 8])
    tA = sbuf(st, "tA", [128, NT, 8])
    tB = sbuf(st, "tB", [128, NT, 8])

    def B8(tb):
        return tb[:].unsqueeze(1).to_broadcast([128, NT, 8])
    s.op('dve', lambda e: e.tensor_tensor(out=dtr[:], in0=dtr[:], in1=B8(P['dtb']), op=ALU.add), reads=[dtr, P['dtb']], writes=[dtr])
    s.op('dve', lambda e: e.tensor_scalar(out=tB[:], in0=dtr[:], scalar1=-1.0, scalar2=None, op0=ALU.mult), reads=[dtr], writes=[tB])
    s.op('dve', lambda e: e.tensor_tensor(out=tA[:], in0=dtr[:], in1=tB[:], op=ALU.max), reads=[dtr, tB], writes=[tA])
    s.op('act', lambda e: e.activation(out=tA[:], in_=tA[:], func=AF.Exp, scale=-1.0), reads=[tA], writes=[tA])
    s.op('act', lambda e: e.activation(out=tA[:], in_=tA[:], func=AF.Ln, bias=1.0, scale=1.0), reads=[tA], writes=[tA])
    s.op('dve', lambda e: e.tensor_single_scalar(out=tB[:], in_=dtr[:], scalar=0.0, op=ALU.max), reads=[dtr], writes=[tB])
    s.op('dve', lambda e: e.tensor_tensor(out=dt[:], in0=tA[:], in1=tB[:], op=ALU.add), reads=[tA, tB], writes=[dt])
    s.op('dve', lambda e: e.tensor_tensor(out=dta[:], in0=dt[:], in1=B8(P['aneg']), op=ALU.mult), reads=[dt, P['aneg']], writes=[dta])
    if stop == 'SSD4':
        return
    for t in range(NT):
        s.op('dve', lambda e, t=t: e.tensor_tensor(out=yacc[:, t, :].rearrange("p (h d) -> p h d", h=4),
                                                  in0=xbc[:, t, 0:256].rearrange("p (h d) -> p h d", h=4),
                                                  in1=P['dsk'][:].unsqueeze(2).to_broadcast([128, 4, 64]), op=ALU.mult),
             reads=[(xbc, t), P['dsk']], writes=[(yacc, t)])
    if stop == 'SSD5':
        return
    sm8 = sbuf(st, "sm8", [128, 8])
    nacs = sbuf(st, "nacs", [128, 4])
    eacs = sbuf(st, "eacs", [128, 4])
    wdec = sbuf(st, "wdec", [128, 4])
    etot = sbuf(st, "etot", [128, 4])
    A1 = sbuf(st, "A1", [128, 4, 128])
    decT = sbuf(st, "decT", [128, 4, 128])
    scT = sbuf(st, "scTs", [128, 4, 128], BF16)
    xdt = sbuf(st, "xdt", [128, 4, 64], BF16)
    xdtw = sbuf(st, "xdtw", [128, 4, 64], BF16)
    state = sbuf(st, "state", [128, 2, 64])
    stateb = sbuf(st, "stateb", [128, 2, 64], BF16)
    ones3 = sbuf(st, "ones3", [128, 4, 128])
    s.op('pool', lambda e: e.memset(ones3[:], 1.0), writes=[ones3])
    for d in range(2):
        Um = C['U_f'] if d == 0 else C['UT_f']
        Mn = C['mf_f'] if d == 0 else C['mb_f']
        order = list(range(NT)) if d == 0 else [1, 0] + list(range(NT - 1, 1, -1))
        s.op('pool', lambda e: e.memset(state[:], 0.0), writes=[state])
        s.op('pool', lambda e: e.memset(stateb[:], 0.0), writes=[stateb])
        for t in order:
            dta_c = dta[:, t, d * 4:(d + 1) * 4]
            dt_c = dt[:, t, d * 4:(d + 1) * 4]
            pbs = bank()
            s.op('pe', lambda e: e.matmul(pbs[:, 0:4], lhsT=Um[:], rhs=dta_c, start=True, stop=True), reads=[Um, dta], writes=[pbs])
            s.op('pe', lambda e: e.matmul(pbs[:, 4:8], lhsT=C['ones_f'][:], rhs=dta_c, start=True, stop=True), reads=[C['ones_f'], dta], writes=[pbs])
            s.op('act', lambda e: e.copy(out=sm8[:], in_=pbs[:, 0:8]), reads=[pbs], writes=[sm8])
            s.op('dve', lambda e: e.tensor_scalar(out=nacs[:], in0=sm8[:, 0:4], scalar1=-1.0, scalar2=None, op0=ALU.mult), reads=[sm8], writes=[nacs])
            s.op('act', lambda e: e.activation(out=eacs[:], in_=sm8[:, 0:4], func=AF.Exp), reads=[sm8], writes=[eacs])
            s.op('dve', lambda e: e.tensor_tensor(out=wdec[:], in0=sm8[:, 4:8], in1=sm8[:, 0:4], op=ALU.subtract), reads=[sm8], writes=[wdec])
            s.op('act', lambda e: e.activation(out=wdec[:], in_=wdec[:], func=AF.Exp), reads=[wdec], writes=[wdec])
            s.op('act', lambda e: e.activation(out=etot[:], in_=sm8[:, 4:8], func=AF.Exp), reads=[sm8], writes=[etot])
            if _CUT == 1:
                return
            s.op('dve', lambda e: e.tensor_tensor(out=A1[:], in0=ones3[:], in1=dta_c.unsqueeze(2).to_broadcast([128, 4, 128]), op=ALU.mult),
                 reads=[ones3, dta], writes=[A1])
            if _CUT == 2:
                return
            pD = bank()
            for h in range(4):
                s.op('pe', lambda e, h=h: e.matmul(pD[:, h * 128:(h + 1) * 128], lhsT=A1[:, h, :], rhs=Um[:], start=True, stop=False),
                     reads=[A1, Um], writes=[pD])
                s.op('pe', lambda e, h=h: e.matmul(pD[:, h * 128:(h + 1) * 128], lhsT=C['ident_f'][:], rhs=Mn[:], start=False, stop=True),
                     reads=[C['ident_f'], Mn], writes=[pD])
            if _CUT == 3:
                return
            for h in range(4):
                s.op('act', lambda e, h=h: e.activation(out=decT[:, h, :], in_=pD[:, h * 128:(h + 1) * 128], func=AF.Exp, bias=nacs[:, h:h + 1], scale=1.0),
                     reads=[pD, nacs], writes=[(decT, h)])
            if _CUT == 4:
                return
            pGs = [bank(), bank()]
            for g in range(2):
                s.op('pe', lambda e, g=g: e.matmul(pGs[g][:, 0:128], lhsT=xbcT[64 * g:64 * g + 64, 2, tcols(t)],
                                                  rhs=xbcT[64 * g:64 * g + 64, 3, tcols(t)], start=True, stop=True), reads=[xbcT], writes=[pGs[g]])
            for h in range(4):
                g = h // 2
                s.op('dve', lambda e, h=h, g=g: e.tensor_tensor(out=scT[:, h, :], in0=pGs[g][:, 0:128], in1=decT[:, h, :], op=ALU.mult),
                     reads=[pGs[g], (decT, h)], writes=[(scT, h)])
            if _CUT == 5:
                return
            xs4 = xbc[:, t, 0:256].rearrange("p (h d) -> p h d", h=4)
            s.op('dve', lambda e: e.tensor_tensor(out=xdt[:], in0=xs4, in1=dt_c.unsqueeze(2).to_broadcast([128, 4, 64]), op=ALU.mult),
                 reads=[(xbc, t), dt], writes=[xdt])
            s.op('dve', lambda e: e.tensor_tensor(out=xdtw[:], in0=xdt[:], in1=wdec[:].unsqueeze(2).to_broadcast([128, 4, 64]), op=ALU.mult),
                 reads=[xdt, wdec], writes=[xdtw])
            if _CUT == 6:
                return
            pY = bank()
            pOs = [bank(), bank()]
            for h in range(4):
                s.op('pe', lambda e, h=h: e.matmul(pY[:, h * 64:(h + 1) * 64], lhsT=scT[:, h, :], rhs=xdt[:, h, :], start=True, stop=True),
                     reads=[(scT, h), xdt], writes=[pY])
            for h in range(4):
                g, hh = h // 2, h % 2
                s.op('pe', lambda e, h=h, g=g, hh=hh: e.matmul(pOs[g][:, hh * 64:(hh + 1) * 64], lhsT=xbcT[64 * g:64 * g + 64, 3, tcols(t)],
                                                              rhs=stateb[64 * g:64 * g + 64, hh, :], start=True, stop=True),
                     reads=[xbcT, stateb], writes=[pOs[g]])
            if _CUT == 7:
                return
            for h in range(4):
                g, hh = h // 2, h % 2
                s.op('dve', lambda e, h=h, g=g, hh=hh: e.scalar_tensor_tensor(out=yacc[:, t, h * 64:(h + 1) * 64], in0=pOs[g][:, hh * 64:(hh + 1) * 64],
                                                                 scalar=eacs[:, h:h + 1], in1=yacc[:, t, h * 64:(h + 1) * 64], op0=ALU.mult, op1=ALU.add),
                     reads=[pOs[g], eacs, (yacc, t)], writes=[(yacc, t)])
            s.op('dve', lambda e: e.tensor_tensor(out=yacc[:, t, :], in0=yacc[:, t, :], in1=pY[:, 0:256], op=ALU.add), reads=[pY, (yacc, t)], writes=[(yacc, t)])
            if _CUT == 8:
                return
            pS = bank()
            s.op('pe', lambda e: e.matmul(pS[:, 0:256], lhsT=xbc[:, t, 256:384], rhs=xdtw[:].rearrange("p h d -> p (h d)"), start=True, stop=True),
                 reads=[(xbc, t), xdtw], writes=[pS])
            for g in range(2):
                for hh in range(2):
                    h = 2 * g + hh
                    s.op('dve', lambda e, g=g, hh=hh, h=h: e.scalar_tensor_tensor(
                        out=state[64 * g:64 * g + 64, hh, :], in0=state[64 * g:64 * g + 64, hh, :], scalar=etot[64 * g:64 * g + 64, h:h + 1],
                        in1=pS[64 * g:64 * g + 64, h * 64:(h + 1) * 64], op0=ALU.mult, op1=ALU.add), reads=[state, etot, pS], writes=[state])
            s.op('act', lambda e: e.copy(out=stateb[:], in_=state[:]), reads=[state], writes=[stateb])
    if stop == 'SSD6':
        return
    gt = sbuf(st, "gt", [128, 256])
    gj = sbuf(st, "gj", [128, 256])
    gb = sbuf(st, "gb", [128, 256], BF16)
    ssg = sbuf(st, "ssg", [128, 1])
    for t in range(NT):
        s.op('dve', lambda e: e.tensor_tensor(out=gt[:], in0=yacc[:, t, :], in1=zs[:, t, :], op=ALU.mult), reads=[(yacc, t), (zs, t)], writes=[gt])
        s.op('act', lambda e: e.activation(out=gj[:], in_=gt[:], func=AF.Square, accum_out=ssg[:]), reads=[gt], writes=[gj, ssg])
        rstd_from_ss(ssg, ssg, 256)
        s.op('dve', lambda e: e.scalar_tensor_tensor(out=gb[:], in0=gt[:], scalar=ssg[:, 0:1], in1=P['snw'][:], op0=ALU.mult, op1=ALU.mult),
             reads=[gt, ssg, P['snw']], writes=[gb])
        pb = bank()
        pbb = pb.t[:].bitcast(BF16)
        for c in range(2):
            s.op('pe', lambda e, c=c: e.transpose(out=pbb[:, c * 128:(c + 1) * 128], in_=gb[:, c * 128:(c + 1) * 128], identity=C['ident_b'][:]),
                 reads=[gb, C['ident_b']], writes=[pb])
        s.op('act', lambda e: e.copy(out=mixT[:, 0:2, tcols(t)], in_=pbb[:, 0:256].rearrange("p (c n) -> p c n", c=2)), reads=[pb], writes=[(mixT, ('ssd', t))])
import os


def s5_stage(st, nc, s, sbuf, bank, P, D, l, hT, mixT, load_w, PIECES, DB, stop, S5SCR, mkstg, **_):
    P = dict(P)
    P['WB'] = sbuf(st, "WBs", [128, 2, 8, 2, 128], BF16)
    P['WC'] = sbuf(st, "WCs", [128, 2, 8, 2, 128])
    P['cosT'] = sbuf(st, "cosTs", [128, 16, 256])
    P['sinT'] = sbuf(st, "sinTs", [128, 16, 256])
    for nm in ['WB', 'WC', 'cosT', 'sinT']:
        dd, db = S5SCR[(nm, l)]
        dst = P[nm][:].rearrange("p a b c d -> p (a b c d)") if nm in ('WB', 'WC') else P[nm][:].rearrange("p a b -> p (a b)")
        s.dma(dst, dd.ap(), reads=[db], writes=[P[nm]], g=2)
    wu = sbuf(st, "wu", [128, 8, 256], BF16)
    with contextlib.ExitStack() as s0:
        load_w(mkstg(s0), wu, 'w_in', l, 2312, 256, 0)
        s.barrier()
    uT = sbuf(st, "uT", [128, 2, SEQ], BF16)
    yT = sbuf(st, "yT", [128, 2, SEQ])
    for c in range(2):
        for (c0, c1) in PIECES:
            pb = bank()
            n = c1 - c0
            for k in range(8):
                s.op('pe', lambda e, k=k: e.matmul(pb[:, 0:n], lhsT=wu[:, k, c * 128:(c + 1) * 128], rhs=hT[:, k, c0:c1], start=(k == 0), stop=(k == 7)),
                     reads=[hT, wu], writes=[pb])
            s.op('act', lambda e: e.copy(out=uT[:, c, c0:c1], in_=pb[:, 0:n]), reads=[pb], writes=[uT])
            s.op('dve', lambda e: e.tensor_scalar(out=yT[:, c, c0:c1], in0=pb[:, 0:n], scalar1=P['s5d'][:, c:c + 1], scalar2=None, op0=ALU.mult),
                 reads=[pb, P['s5d']], writes=[(yT, c)])
    T = 256
    NCH = SEQ // T
    bs = [sbuf(st, f"s5b{i}", [128, 2, T]) for i in range(1)]
    gin = [sbuf(st, f"s5g{i}", [128, 2, T]) for i in range(1)]
    tt = [sbuf(st, f"s5t{i}", [128, 4, T]) for i in range(1)]
    gs = [sbuf(st, f"s5s{i}", [128, 2, T]) for i in range(1)]
    hh_ = [sbuf(st, f"s5h{i}", [128, 2, T]) for i in range(1)]
    t2 = [sbuf(st, f"s5u{i}", [128, 4, T]) for i in range(1)]
    carry = [sbuf(st, f"s5c{q}", [128, 2]) for q in range(8)]
    ctmp = sbuf(st, "s5ct", [128, 4])
    it = 0
    for d in range(2):
        order = list(range(NCH)) if d == 0 else [0] + list(range(NCH - 1, 0, -1))
        for kt in range(2):
            for ci, ch in enumerate(order):
                cols = slice(ch * T, (ch + 1) * T)
                ucols = uT[:, kt, cols] if d == 0 else uT[:, kt, ch * T:(ch + 1) * T][:, ::-1]
                pY = bank()
                for ql in range(4):
                    q = kt * 4 + ql
                    j = d * 8 + q
                    i2 = 0
                    it += 1
                    B_, G_, T_, S_, H_, U_ = bs[i2], gin[i2], tt[i2], gs[i2], hh_[i2], t2[i2]
                    cosj = P['cosT'][:, j, :]
                    sinj = P['sinT'][:, j, :]
                    pB = bank()
                    for ri in range(2):
                        s.op('pe', lambda e, ri=ri: e.matmul(pB[:, ri * T:(ri + 1) * T], lhsT=P['WB'][:, d, q, ri, :], rhs=ucols, start=True, stop=True),
                             reads=[uT, P['WB']], writes=[pB])
                    s.op('act', lambda e: e.copy(out=B_[:].rearrange("p a t -> p (a t)"), in_=pB[:, 0:2 * T]), reads=[pB], writes=[B_])
                    TTp = lambda o, a, b_, op, rd, wr, eng='pool': s.op(eng, lambda e: e.tensor_tensor(out=o, in0=a, in1=b_, op=op), reads=rd, writes=wr)
                    TTp(T_[:, 0, :], B_[:, 0, :], cosj, ALU.mult, [B_, P['cosT']], [(T_, 0)])
                    TTp(T_[:, 1, :], B_[:, 1, :], sinj, ALU.mult, [B_, P['sinT']], [(T_, 1)])
                    TTp(T_[:, 2, :], B_[:, 1, :], cosj, ALU.mult, [B_, P['cosT']], [(T_, 2)])
                    TTp(T_[:, 3, :], B_[:, 0, :], sinj, ALU.mult, [B_, P['sinT']], [(T_, 3)])
                    TTp(G_[:, 0, :], T_[:, 0, :], T_[:, 1, :], ALU.add, [(T_, 0), (T_, 1)], [(G_, 0)])
                    TTp(G_[:, 1, :], T_[:, 2, :], T_[:, 3, :], ALU.subtract, [(T_, 2), (T_, 3)], [(G_, 1)])
                    if ci > 0:
                        cq = carry[q]
                        ar = P['s5ar'][:, j:j + 1]
                        ai = P['s5ai'][:, j:j + 1]
                        s.op('dve', lambda e: e.tensor_scalar(out=ctmp[:, 0:1], in0=cq[:, 1:2], scalar1=ai, scalar2=None, op0=ALU.mult), reads=[cq, P['s5ai']], writes=[ctmp])
                        s.op('dve', lambda e: e.scalar_tensor_tensor(out=ctmp[:, 1:2], in0=cq[:, 0:1], scalar=ar, in1=ctmp[:, 0:1], op0=ALU.mult, op1=ALU.subtract),
                             reads=[cq, ctmp, P['s5ar']], writes=[ctmp])
                        s.op('dve', lambda e: e.tensor_scalar(out=ctmp[:, 2:3], in0=cq[:, 0:1], scalar1=ai, scalar2=None, op0=ALU.mult), reads=[cq, P['s5ai']], writes=[ctmp])
                        s.op('dve', lambda e: e.scalar_tensor_tensor(out=ctmp[:, 3:4], in0=cq[:, 1:2], scalar=ar, in1=ctmp[:, 2:3], op0=ALU.mult, op1=ALU.add),
                             reads=[cq, ctmp, P['s5ar']], writes=[ctmp])
                        s.op('dve', lambda e: e.tensor_tensor(out=G_[:, 0, 0:1], in0=G_[:, 0, 0:1], in1=ctmp[:, 1:2], op=ALU.add), reads=[(G_, 0), ctmp], writes=[(G_, 0)])
                        s.op('dve', lambda e: e.tensor_tensor(out=G_[:, 1, 0:1], in0=G_[:, 1, 0:1], in1=ctmp[:, 3:4], op=ALU.add), reads=[(G_, 1), ctmp], writes=[(G_, 1)])
                    rb = P['s5r'][:, j:j + 1].to_broadcast([128, T])
                    for ri in range(2):
                        s.op('dve', lambda e, ri=ri: e.tensor_tensor_scan(out=S_[:, ri, :], data0=rb, data1=G_[:, ri, :], initial=0.0, op0=ALU.mult, op1=ALU.add),
                             reads=[(G_, ri), P['s5r']], writes=[(S_, ri)])
                    TTd = lambda o, a, b_, op, rd, wr: TTp(o, a, b_, op, rd, wr, 'dve')
                    TTd(U_[:, 0, :], S_[:, 0, :], cosj, ALU.mult, [(S_, 0), P['cosT']], [(U_, 0)])
                    TTd(U_[:, 1, :], S_[:, 1, :], sinj, ALU.mult, [(S_, 1), P['sinT']], [(U_, 1)])
                    TTd(U_[:, 2, :], S_[:, 0, :], sinj, ALU.mult, [(S_, 0), P['sinT']], [(U_, 2)])
                    TTd(U_[:, 3, :], S_[:, 1, :], cosj, ALU.mult, [(S_, 1), P['cosT']], [(U_, 3)])
                    TTd(H_[:, 0, :], U_[:, 0, :], U_[:, 1, :], ALU.subtract, [(U_, 0), (U_, 1)], [(H_, 0)])
                    TTd(H_[:, 1, :], U_[:, 2, :], U_[:, 3, :], ALU.add, [(U_, 2), (U_, 3)], [(H_, 1)])
                    s.op('act', lambda e: e.copy(out=carry[q][:], in_=H_[:, :, T - 1]), reads=[H_], writes=[carry[q]])
                    for ri in range(2):
                        s.op('pe', lambda e, ri=ri, ql=ql: e.matmul(pY[:, 0:T], lhsT=P['WC'][:, d, q, ri, :], rhs=H_[:, ri, :],
                                                                   start=(ql == 0 and ri == 0), stop=(ql == 3 and ri == 1)),
                             reads=[(H_, ri), P['WC']], writes=[pY])
                ycols = yT[:, kt, cols] if d == 0 else yT[:, kt, ch * T:(ch + 1) * T][:, ::-1]
                s.op('dve', lambda e: e.tensor_tensor(out=ycols, in0=ycols, in1=pY[:, 0:T], op=ALU.add), reads=[pY, (yT, kt)], writes=[(yT, kt)])
    if 's5_cos' in DB:
        s.dma(DB['s5_cos'].ap(), P['cosT'][:].rearrange("p a b -> p (a b)"), reads=[P['cosT']], g=5)
        s.dma(DB['s5_sin'].ap(), P['sinT'][:].rearrange("p a b -> p (a b)"), reads=[P['sinT']], g=5)
        s.dma(DB['s5_wc'].ap(), P['WC'][:].rearrange("p a b c d -> p (a b c d)"), reads=[P['WC']], g=5)
    if 's5_y' in DB:
        s.dma(DB['s5_y'].ap(), yT[:].rearrange("p a b -> p (a b)"), reads=[yT], g=5)
        s.dma(DB['s5_u'].ap(), uT[:].rearrange("p a b -> p (a b)"), reads=[uT], q='pool', g=5)
    ygb = uT
    tmp = sbuf(st, "s5tmp", [128, SEQ])
    for c in range(2):
        s.op('dve', lambda e: e.tensor_tensor(out=tmp[:], in0=yT[:, c, :], in1=yT[:, c, :], op=ALU.mult), reads=[(yT, c)], writes=[tmp])
        s.op('dve', lambda e: e.tensor_scalar(out=tmp[:], in0=tmp[:], scalar1=0.044715, scalar2=1.0, op0=ALU.mult, op1=ALU.add), reads=[tmp], writes=[tmp])
        s.op('dve', lambda e: e.tensor_tensor(out=tmp[:], in0=tmp[:], in1=yT[:, c, :], op=ALU.mult), reads=[tmp, (yT, c)], writes=[tmp])
        s.op('act', lambda e: e.activation(out=tmp[:], in_=tmp[:], func=AF.Sigmoid, scale=1.5957691216057308), reads=[tmp], writes=[tmp])
        s.op('dve', lambda e: e.tensor_tensor(out=yT[:, c, :], in0=yT[:, c, :], in1=tmp[:], op=ALU.mult), reads=[tmp, (yT, c)], writes=[(yT, c)])
        s.op('act', lambda e: e.copy(out=ygb[:, c, :], in_=yT[:, c, :]), reads=[(yT, c)], writes=[ygb])
    sg = sbuf(st, "s5sg", [128, 512])
    for c in range(2):
        for (c0, c1) in PIECES:
            pb = bank()
            n = c1 - c0
            for k in range(2):
                s.op('pe', lambda e, k=k: e.matmul(pb[:, 0:n], lhsT=P['gluw'][:, k, c * 128:(c + 1) * 128], rhs=ygb[:, k, c0:c1], start=(k == 0), stop=(k == 1)),
                     reads=[ygb, P['gluw']], writes=[pb])
            s.op('act', lambda e: e.activation(out=sg[:, 0:n], in_=pb[:, 0:n], func=AF.Sigmoid, bias=P['glub'][:, c:c + 1], scale=1.0),
                 reads=[pb, P['glub']], writes=[sg])
            s.op('dve', lambda e: e.tensor_tensor(out=mixT[:, 6 + c, c0:c1], in0=yT[:, c, c0:c1], in1=sg[:, 0:n], op=ALU.mult),
                 reads=[sg, (yT, c)], writes=[(mixT, ('s5', c, c0))])


def attn_stage(st, nc, s, sbuf, bank, P, D, l, hT, mixT, load_w, rstd_from_ss, tcols, need_ctx, C, DB, stop, mkstg, **_):
    wqk = sbuf(st, "wqk", [128, 8, 1024], BF16)
    wv = sbuf(st, "wv", [128, 8, 512], BF16)
    with contextlib.ExitStack() as s0:
        stg = mkstg(s0)
        load_w(stg, wqk, 'w_in', l, 776, 1024, 0)
        load_w(stg, wv, 'w_in', l, 1800, 512, 0)
        s.barrier()
    _AC = int(os.environ.get('ACUT', '0'))
    if _AC == 1:
        return
    QT = sbuf(st, "QT", [128, 4, SEQ], BF16)
    KT = sbuf(st, "KT", [128, 4, SEQ], BF16)
    Vt = sbuf(st, "Vt", [128, NT, 4, 129], BF16)
    s.op('pool', lambda e: e.memset(Vt[:], 1.0), writes=[Vt])
    qkf = sbuf(st, "qkf", [128, 1024])
    qkr = sbuf(st, "qkr", [128, 1024], BF16)
    rt = sbuf(st, "ropet", [128, 4, 16, 2, 16])
    for t in range(NT):
        pq, pk, pv = bank(), bank(), bank()
        for (pb, w, c0) in [(pq, wqk, 0), (pk, wqk, 512), (pv, wv, 0)]:
            for k in range(8):
                s.op('pe', lambda e, k=k, pb=pb, w=w, c0=c0: e.matmul(pb[:, :], lhsT=hT[:, k, tcols(t)], rhs=w[:, k, c0:c0 + 512], start=(k == 0), stop=(k == 7)),
                     reads=[hT, w], writes=[pb])
        s.op('act', lambda e: e.copy(out=Vt[:, t, :, 0:128], in_=pv[:, :].rearrange("p (h e) -> p h e", h=4)), reads=[pv], writes=[(Vt, t)])
        if t < 2:
            s.op('act', lambda e: e.copy(out=qkr[:, 0:512], in_=pq[:, :]), reads=[pq], writes=[qkr])
            s.op('act', lambda e: e.copy(out=qkr[:, 512:1024], in_=pk[:, :]), reads=[pk], writes=[qkr])
        else:
            s.op('act', lambda e: e.copy(out=qkf[:, 0:512], in_=pq[:, :]), reads=[pq], writes=[qkf])
            s.op('act', lambda e: e.copy(out=qkf[:, 512:1024], in_=pk[:, :]), reads=[pk], writes=[qkf])
            tl = t - 2
            xv = qkf[:].rearrange("p (m a b j) -> p m a b j", m=16, a=2, b=2)
            ov = qkr[:].rearrange("p (m a b j) -> p m a b j", m=16, a=2, b=2)
            cosb = C['ropeC'][:, tl, :].rearrange("p (a j) -> p a j", a=2).unsqueeze(1).to_broadcast([128, 16, 2, 16])
            sinb = C['ropeS'][:, tl, :].rearrange("p (a j) -> p a j", a=2).unsqueeze(1).to_broadcast([128, 16, 2, 16])
            PO = lambda o, a, b_, op, rd, wr: s.op('pool', lambda e: e.tensor_tensor(out=o, in0=a, in1=b_, op=op), reads=rd, writes=wr)
            PO(rt[:, 0], xv[:, :, :, 0, :], cosb, ALU.mult, [qkf, C['ropeC']], [(rt, 0)])
            PO(rt[:, 1], xv[:, :, :, 1, :], sinb, ALU.mult, [qkf, C['ropeS']], [(rt, 1)])
            PO(rt[:, 2], xv[:, :, :, 1, :], cosb, ALU.mult, [qkf, C['ropeC']], [(rt, 2)])
            PO(rt[:, 3], xv[:, :, :, 0, :], sinb, ALU.mult, [qkf, C['ropeS']], [(rt, 3)])
            PO(ov[:, :, :, 0, :], rt[:, 0], rt[:, 1], ALU.subtract, [(rt, 0), (rt, 1)], [qkr])
            PO(ov[:, :, :, 1, :], rt[:, 2], rt[:, 3], ALU.add, [(rt, 2), (rt, 3)], [qkr])
        pb = bank()
        pbb = pb.t[:].bitcast(BF16)
        for m in range(8):
            s.op('pe', lambda e, m=m: e.transpose(out=pbb[:, m * 128:(m + 1) * 128], in_=qkr[:, m * 128:(m + 1) * 128], identity=C['ident_b'][:]),
                 reads=[qkr, C['ident_b']], writes=[pb])
        s.op('act', lambda e: e.copy(out=QT[:, :, tcols(t)], in_=pbb[:, 0:512].rearrange("p (h n) -> p h n", h=4)), reads=[pb], writes=[(QT, t)])
        s.op('dve', lambda e: e.tensor_copy(out=KT[:, :, tcols(t)], in_=pbb[:, 512:1024].rearrange("p (h n) -> p h n", h=4)), reads=[pb], writes=[(KT, t)])
    if _AC == 2:
        return
    ET = sbuf(st, "ET", [128, NT, 512], BF16)
    oc = sbuf(st, "oc", [128, 2, 4, 129])
    rr = sbuf(st, "att_rr", [128, 2, 4])
    o_ = sbuf(st, "att_o", [128, 128])
    oj = sbuf(st, "att_oj", [128, 128])
    ob = sbuf(st, "att_ob", [128, 128], BF16)
    sso = sbuf(st, "att_ss", [128, 1])
    groups = [(256 + 512 * i, 512, NT) for i in range(4)]
    if need_ctx:
        groups = [(0, 256, 2)] + groups
    for h in range(4):
        if _AC == 3 and h == 1:
            return
        for (q0, nq, nk) in groups:
            nsub = nq // 128
            for c in range(2):
                for kt in range(nk):
                    pS = bank()
                    s.op('pe', lambda e, kt=kt: e.matmul(pS[:, 0:nq], lhsT=KT[64 * c:64 * c + 64, h, tcols(kt)], rhs=QT[64 * c:64 * c + 64, h, q0:q0 + nq],
                                                        start=True, stop=True), reads=[KT, QT], writes=[pS])
                    s.op('act', lambda e, kt=kt: e.activation(out=ET[:, kt, 0:nq], in_=pS[:, 0:nq], func=AF.Exp, scale=0.125), reads=[pS], writes=[(ET, kt)])
                for qs in range(nsub):
                    pO = bank()
                    for kt in range(nk):
                        s.op('pe', lambda e, kt=kt: e.matmul(pO[:, 0:129], lhsT=ET[:, kt, qs * 128:(qs + 1) * 128], rhs=Vt[:, kt, h, :], start=(kt == 0), stop=(kt == nk - 1)),
                             reads=[(ET, kt), Vt], writes=[pO])
                    s.op('dve', lambda e: e.tensor_copy(out=oc[:, c, qs, :], in_=pO[:, 0:129]), reads=[pO], writes=[(oc, (c, qs))])
            s.op('dve', lambda e: e.reciprocal(out=rr[:, :, 0:nsub], in_=oc[:, :, 0:nsub, 128]), reads=[oc], writes=[rr])
            s.op('dve', lambda e: e.tensor_scalar(out=rr[:, 1, :], in0=rr[:, 1, :], scalar1=P['nlam'][:, 0:1], scalar2=None, op0=ALU.mult), reads=[rr, P['nlam']], writes=[rr])
            for qs in range(nsub):
                s.op('dve', lambda e: e.tensor_scalar(out=o_[:], in0=oc[:, 0, qs, 0:128], scalar1=rr[:, 0, qs:qs + 1], scalar2=None, op0=ALU.mult), reads=[oc, rr], writes=[o_])
                s.op('dve', lambda e: e.scalar_tensor_tensor(out=o_[:], in0=oc[:, 1, qs, 0:128], scalar=rr[:, 1, qs:qs + 1], in1=o_[:], op0=ALU.mult, op1=ALU.add),
                     reads=[oc, rr, o_], writes=[o_])
                s.op('act', lambda e: e.activation(out=oj[:], in_=o_[:], func=AF.Square, accum_out=sso[:]), reads=[o_], writes=[oj, sso])
                rstd_from_ss(sso, sso, 128)
                s.op('dve', lambda e: e.scalar_tensor_tensor(out=ob[:], in0=o_[:], scalar=sso[:, 0:1], in1=P['subw'][:], op0=ALU.mult, op1=ALU.mult),
                     reads=[o_, sso, P['subw']], writes=[ob])
                pb = bank()
                pbb = pb.t[:].bitcast(BF16)
                s.op('pe', lambda e: e.transpose(out=pbb[:, 0:128], in_=ob[:], identity=C['ident_b'][:]), reads=[ob, C['ident_b']], writes=[pb])
                qc = q0 + qs * 128
                s.op('act', lambda e: e.copy(out=mixT[:, 2 + h, qc:qc + 128], in_=pbb[:, 0:128]), reads=[pb], writes=[(mixT, ('att', h, qc))])


NWB = int(os.environ.get('NWB', '2'))


def ffn_stage(st, nc, s, sbuf, bank, P, D, l, b, NB, NCOL, hT, need_ctx, last, xs_d, XS, out_d, OUT, gsc_d, GSC, load_w, rstd_from_ss, tcols, DB, stop, mkstg, load_cast, **_):
    stg = mkstg(st, 4)
    wd = sbuf(st, "wd", [128, 22, 1024], BF16)
    wdsrc = D['ffn_w_down'].ap()[l].rearrange("(f p) c -> p f c", p=128)
    for f in range(22):
        load_cast(stg, wd[:, f, :], wd, wdsrc[:, f, :], [1024], key=f)
    gbc = sbuf(st, "gbcF", [128, 2, 1024])
    s.dma(gbc[:, 0, :], bass.AP(gsc_d, ((l * 2 + 1) * NCOL + b) * 1024, [[0, 128], [1, 1024]]), reads=[GSC], writes=[gbc], g=0)
    s.dma(gbc[:, 1, :], bass.AP(gsc_d, ((l * 2 + 1) * NCOL + NB) * 1024, [[0, 128], [1, 1024]]), reads=[GSC], writes=[gbc], g=0)
    actT = sbuf(st, "actT", [128, 22, 1152], BF16)
    wg = [sbuf(st, f"wg{i}", [128, 8, 128], BF16) for i in range(NWB)]
    wu = [sbuf(st, f"wu{i}", [128, 8, 128], BF16) for i in range(NWB)]
    graw = sbuf(st, "graw", [128, 1160])
    gc = sbuf(st, "gc", [128, 1160])
    ub = sbuf(st, "ub", [128, 1152], BF16)
    xt2 = [sbuf(st, f"xtF{i}", [128, 1024]) for i in range(2)]
    junk = sbuf(st, "junkF", [128, 1024], BF16)
    ss2 = sbuf(st, "ssF", [128, 2])
    rs2 = sbuf(st, "rsF", [128, 1])
    ytmp = sbuf(st, "ytmpF", [128, 1024])
    s.op('pool', lambda e: e.memset(graw[:], 0.0), writes=[graw])
    if stop == 'F0':
        return
    halves = []
    if need_ctx:
        halves.append(dict(gp=[(0, 256, 1), (256, 768, 259), (768, 1153, 771)], up=[(0, 256, 0), (256, 768, 256), (768, 1152, 768)],
                           outs=[(0, 256, 1), (256, 1152, 259)], tiles=list(range(0, 9)), base=0, pads=[0, 257, 258]))
    else:
        halves.append(dict(gp=[(256, 768, 259), (768, 1153, 771)], up=[(256, 768, 256), (768, 1152, 768)],
                           outs=[(256, 1152, 259)], tiles=list(range(2, 9)), base=0, pads=[0, 257, 258]))
    halves.append(dict(gp=[(1151, 1663, 0), (1663, 2175, 512), (2175, 2304, 1024)], up=[(1152, 1664, 0), (1664, 2176, 512), (2176, 2304, 1024)],
                       outs=[(0, 1152, 1)], tiles=list(range(9, 18)), base=1152, pads=[1153]))
    for hv in halves:
        for pcol in hv['pads']:
            s.op('pool', lambda e, pcol=pcol: e.memset(graw[:, pcol:pcol + 1], 0.0), writes=[graw])
        for f in range(int(os.environ.get('FSTART', '0')), int(os.environ.get('FCUT', '22'))):
            g_, u_ = wg[f % NWB], wu[f % NWB]
            load_cast(stg, g_[:], g_, D['ffn_w_gate'].ap()[l].rearrange("(k p) c -> p k c", p=128)[:, :, f * 128:(f + 1) * 128], [8, 128])
            load_cast(stg, u_[:], u_, D['ffn_w_up'].ap()[l].rearrange("(k p) c -> p k c", p=128)[:, :, f * 128:(f + 1) * 128], [8, 128])
            _FS = int(os.environ.get('FSTEP', '0')); _lastf = f == int(os.environ.get('FCUT', '22')) - 1
            if _lastf and _FS == 1:
                return
            for (c0, c1, dst) in hv['gp']:
                pb = bank()
                n = c1 - c0
                for k in range(8):
                    s.op('pe', lambda e, k=k: e.matmul(pb[:, 0:n], lhsT=g_[:, k, :], rhs=hT[:, k, c0:c1], start=(k == 0), stop=(k == 7)), reads=[hT, g_], writes=[pb])
                s.op('act', lambda e: e.copy(out=graw[:, dst:dst + n], in_=pb[:, 0:n]), reads=[pb], writes=[graw])
            if _lastf and _FS == 2:
                return
            for (c0, c1, dst) in hv['up']:
                pb = bank()
                n = c1 - c0
                for k in range(8):
                    s.op('pe', lambda e, k=k: e.matmul(pb[:, 0:n], lhsT=u_[:, k, :], rhs=hT[:, k, c0:c1], start=(k == 0), stop=(k == 7)), reads=[hT, u_], writes=[pb])
                s.op('dve', lambda e: e.tensor_copy(out=ub[:, dst:dst + n], in_=pb[:, 0:n]), reads=[pb], writes=[ub])
            if _lastf and _FS == 3:
                return
            s.op('act', lambda e: e.activation(out=gc[:, 1:1157], in_=graw[:, 1:1157], func=AF.Identity, scale=P['fcw'][:, f, 1:2], bias=P['fcb'][:, f:f + 1]),
                 reads=[graw, P['fcw'], P['fcb']], writes=[gc])
            for j in (0, 2):
                s.op('pool', lambda e, j=j: e.scalar_tensor_tensor(out=gc[:, 1:1157], in0=graw[:, j:j + 1156], scalar=P['fcw'][:, f, j:j + 1], in1=gc[:, 1:1157],
                                                                  op0=ALU.mult, op1=ALU.add), reads=[graw, gc, P['fcw']], writes=[gc]) if False else \
                    s.op('dve', lambda e, j=j: e.scalar_tensor_tensor(out=gc[:, 1:1157], in0=graw[:, j:j + 1156], scalar=P['fcw'][:, f, j:j + 1], in1=gc[:, 1:1157],
                                                                     op0=ALU.mult, op1=ALU.add), reads=[graw, gc, P['fcw']], writes=[gc])
            s.op('act', lambda e: e.activation(out=gc[:, 1:1157], in_=gc[:, 1:1157], func=AF.Silu), reads=[gc], writes=[gc])
            for (a0, a1, src) in hv['outs']:
                n = a1 - a0
                s.op('dve', lambda e: e.tensor_tensor(out=actT[:, f, a0:a1], in0=gc[:, src:src + n], in1=ub[:, a0:a1], op=ALU.mult),
                     reads=[gc, ub], writes=[(actT, f)])
            if stop == 'F1':
                return
        if stop == 'F2':
            return
        for t in hv['tiles']:
            xt = xt2[t % 2]
            s.dma(xt[:], xs_d.ap()[b, tcols(t), :], reads=[(XS, b)], writes=[xt], g=0)
            tc = t * 128 - hv['base']
            pbs = [bank(), bank()]
            for hf in range(2):
                for f in range(22):
                    s.op('pe', lambda e, f=f, hf=hf: e.matmul(pbs[hf][:, :], lhsT=actT[:, f, tc:tc + 128], rhs=wd[:, f, hf * 512:(hf + 1) * 512],
                                                             start=(f == 0), stop=(f == 21)), reads=[actT, wd], writes=[pbs[hf]])
                s.op('act', lambda e, hf=hf: e.activation(out=junk[:, hf * 512:(hf + 1) * 512], in_=pbs[hf][:, :], func=AF.Square, accum_out=ss2[:, hf:hf + 1]),
                     reads=[pbs[hf]], writes=[junk, ss2])
            s.op('dve', lambda e: e.tensor_tensor(out=rs2[:], in0=ss2[:, 0:1], in1=ss2[:, 1:2], op=ALU.add), reads=[ss2], writes=[rs2])
            rstd_from_ss(rs2, rs2, 1024)
            gi = 1 if t < 2 else 0
            for hf in range(2):
                s.op('dve', lambda e, hf=hf: e.scalar_tensor_tensor(out=ytmp[:, hf * 512:(hf + 1) * 512], in0=pbs[hf][:, :], scalar=rs2[:, 0:1],
                                                                   in1=gbc[:, gi, hf * 512:(hf + 1) * 512], op0=ALU.mult, op1=ALU.mult),
                     reads=[pbs[hf], rs2, gbc], writes=[ytmp])
            s.op('pool', lambda e: e.tensor_tensor(out=xt[:], in0=xt[:], in1=ytmp[:], op=ALU.add), reads=[xt, ytmp], writes=[xt])
            if last:
                if t >= 2:
                    s.dma(out_d.ap()[b, (t - 2) * 128:(t - 1) * 128, :], xt[:], reads=[xt], writes=[OUT], g=1)
            else:
                s.dma(xs_d.ap()[b, tcols(t), :], xt[:], reads=[xt], writes=[(XS, b)], g=1)


def build(NB=4, LAYERS=(0, 1), dbg=(), stop=None):
    nc = bass.Bass("TRN2", target_bir_lowering=False)
    D = {}

    def din(name, shape):
        D[name] = nc.dram_tensor(name, list(shape), F32, kind="ExternalInput")
        return D[name]

    x_d = din("x", [NB, 2048, 1024])
    ctx_d = din("ctx", [NB, 256, 1024])
    cc_d = din("cc", [NB + 1, 1024])
    for n, sh in PARAMS:
        din(n, sh)
    for n, a in host_consts().items():
        din(n, a.shape)
    out_d = nc.dram_tensor("out", [NB, 2048, 1024], F32, kind="ExternalOutput")
    xs_d = nc.dram_tensor("xs", [NB, SEQ, 1024], F32)
    gsc_d = nc.dram_tensor("gsc", [2, 2, NB + 1, 1024], F32)
    DB = {}
    for n, sh in dbg:
        DB[n] = nc.dram_tensor("dbg_" + n, list(sh), F32, kind="ExternalOutput")

    NCOL = NB + 1
    with contextlib.ExitStack() as top:
        s = Sched(nc, top)
        top.enter_context(nc.allow_non_contiguous_dma(reason="small param layout loads"))
        top.enter_context(nc.allow_low_precision(reason="bf16 matmul operands"))
        XS = Buf(xs_d, "xs")
        GSC = Buf(gsc_d, "gsc")
        OUT = Buf(out_d, "out")
        DIN = Buf(None, "din")

        uid = [0]

        def sbuf(st, name, shape, dt=F32):
            uid[0] += 1
            name = f"{name}_u{uid[0]}"
            return Buf(st.enter_context(nc.sbuf_tensor(name, list(shape), dt)), name)

        banks = [Buf(top.enter_context(nc.psum_tensor(f"pb{i}", [128, 512], F32)), f"pb{i}", psum=True) for i in range(8)]
        bank_i = [0]

        def bank():
            b = banks[bank_i[0] % 8]
            bank_i[0] += 1
            return b

        def dmp(name, ap, buf, key=None):
            if name in DB:
                s.dma(DB[name].ap() if not isinstance(name, tuple) else None, ap, reads=[(buf, key)], g=5)

        ident_f = sbuf(top, "ident_f", [128, 128])
        ident_b = sbuf(top, "ident_b", [128, 128], BF16)
        U_f = sbuf(top, "U_f", [128, 128])
        UT_f = sbuf(top, "UT_f", [128, 128])
        mf_f = sbuf(top, "mf_f", [128, 128])
        mb_f = sbuf(top, "mb_f", [128, 128])
        ones_f = sbuf(top, "ones_f", [128, 128])
        ropeC = sbuf(top, "ropeC", [128, 16, 32])
        ropeS = sbuf(top, "ropeS", [128, 16, 32])
        for bufc, nm in [(ident_f, 'c_ident'), (U_f, 'c_U'), (UT_f, 'c_UT'), (mf_f, 'c_mf'), (mb_f, 'c_mb'), (ones_f, 'c_ones')]:
            s.dma(bufc[:], D[nm].ap(), writes=[bufc], g=0)
        s.dma(ropeC[:], D['c_cos'].ap().rearrange("(t p) j -> p t j", p=128), writes=[ropeC], g=0)
        s.dma(ropeS[:], D['c_sin'].ap().rearrange("(t p) j -> p t j", p=128), writes=[ropeS], g=0)
        s.op('act', lambda e: e.copy(out=ident_b[:], in_=ident_f[:]), reads=[ident_f], writes=[ident_b])

        LP = {}
        for l in LAYERS:
            P = {}
            P['s1a'] = sbuf(top, f"s1a{l}", [128, 8, NCOL])
            P['sha'] = sbuf(top, f"sha{l}", [128, 8, NCOL])
            P['s1f'] = sbuf(top, f"s1f{l}", [128, 8, NCOL])
            P['shf'] = sbuf(top, f"shf{l}", [128, 8, NCOL])
            P['dtb'] = sbuf(top, f"dtb{l}", [128, 8])
            P['aneg'] = sbuf(top, f"aneg{l}", [128, 8])
            P['dsk'] = sbuf(top, f"dsk{l}", [128, 4])
            P['snw'] = sbuf(top, f"snw{l}", [128, 256])
            P['scw'] = sbuf(top, f"scw{l}", [128, 4, 5])
            P['scb'] = sbuf(top, f"scb{l}", [128, 4])
            P['subw'] = sbuf(top, f"subw{l}", [128, 128])
            P['nlam'] = sbuf(top, f"nlam{l}", [128, 1])
            P['fcw'] = sbuf(top, f"fcw{l}", [128, 22, 3])
            P['fcb'] = sbuf(top, f"fcb{l}", [128, 22])
            P['s5d'] = sbuf(top, f"s5d{l}", [128, 2])
            P['glub'] = sbuf(top, f"glub{l}", [128, 2])
            P['gluw'] = sbuf(top, f"gluw{l}", [128, 2, 256], BF16)
            P['s5r'] = sbuf(top, f"s5r{l}", [128, 16])
            P['s5ar'] = sbuf(top, f"s5ar{l}", [128, 16])
            P['s5ai'] = sbuf(top, f"s5ai{l}", [128, 16])
            LP[l] = P

        S5SCR = {}
        LP['_negpi'] = sbuf(top, "negpi", [128, 1])
        s.op('pool', lambda e: e.memset(LP['_negpi'][:], -math.pi), writes=[LP['_negpi']])
        LP['_Z'] = sbuf(top, "Zt", [128, 8, 128])
        s.op('pool', lambda e: e.memset(LP['_Z'][:], 0.0), writes=[LP['_Z']])
        with contextlib.ExitStack() as st:
            scT = sbuf(st, "scT", [128, 8, NCOL])
            for col in range(NCOL):
                s.dma(scT[:, :, col], bass.AP(cc_d, col * 1024, [[1, 128], [128, 8]]), writes=[scT], g=0)
            s.op('act', lambda e: e.activation(out=scT[:], in_=scT[:], func=AF.Silu), reads=[scT], writes=[scT])
            wm = [sbuf(st, f"wm{i}", [128, 8, 512]) for i in range(2)]
            modT = sbuf(st, "modT", [128, 48, NCOL])
            modb = sbuf(st, "modb", [128, 48])
            nrm = sbuf(st, "nrm", [128, 4, 8])
            tmpg = sbuf(st, "tmpg", [128, 2, 8, NCOL])
            iota = sbuf(st, "iota", [128, 256])
            s.dma(iota[:], D['c_iota'].ap(), writes=[iota], g=0)
            S5D = {}
            for l in LAYERS:
                P = LP[l]
                P['WB'] = sbuf(st, f"WB{l}", [128, 2, 8, 2, 128], BF16)
                P['WC'] = sbuf(st, f"WC{l}", [128, 2, 8, 2, 128])
                P['cosT'] = sbuf(st, f"cosT{l}", [128, 16, 256])
                P['sinT'] = sbuf(st, f"sinT{l}", [128, 16, 256])
                s.dma(modb[:], bass.AP(D['mod_b'], l * 6144, [[1, 128], [128, 48]]), writes=[modb], g=0)
                for i, nm in enumerate(['mix_norm_pre', 'mix_norm_post', 'ffn_norm_pre', 'ffn_norm_post']):
                    s.dma(nrm[:, i, :], bass.AP(D[nm], l * 1024, [[1, 128], [128, 8]]), writes=[nrm], g=0)
                for cch in range(12):
                    w = wm[cch % 2]
                    s.dma(w[:], D['mod_w'].ap()[l].rearrange("(k p) c -> p k c", p=128)[:, :, cch * 512:(cch + 1) * 512],
                          writes=[w], g=1)
                    for jj in range(4):
                        j = cch * 4 + jj
                        pb = bank()
                        for k in range(8):
                            s.op('pe', lambda e, k=k, jj=jj, w=w, pb=pb: e.matmul(pb[:, 0:NCOL], lhsT=w[:, k, jj * 128:(jj + 1) * 128],
                                                                                 rhs=scT[:, k, :], start=(k == 0), stop=(k == 7)),
                                 reads=[w, scT], writes=[pb])
                        s.op('act', lambda e, j=j, pb=pb: e.activation(out=modT[:, j, :], in_=pb[:, 0:NCOL], func=AF.Identity,
                                                                      bias=modb[:, j:j + 1], scale=1.0),
                             reads=[pb, modb], writes=[modT])
                for (s1, sh, o_sh, o_sc, o_g, ipre, ipost, wi) in [(P['s1a'], P['sha'], 0, 8, 16, 0, 1, 0),
                                                                    (P['s1f'], P['shf'], 24, 32, 40, 2, 3, 1)]:
                    s.op('dve', lambda e, s1=s1, o_sc=o_sc: e.tensor_scalar(out=s1[:], in0=modT[:, o_sc:o_sc + 8, :], scalar1=1.0,
                                                                           scalar2=None, op0=ALU.add),
                         reads=[modT], writes=[s1])
                    s.op('dve', lambda e, s1=s1, ipre=ipre: e.tensor_tensor(out=s1[:], in0=s1[:],
                                                                           in1=nrm[:, ipre, :].unsqueeze(2).to_broadcast([128, 8, NCOL]),
                                                                           op=ALU.mult),
                         reads=[s1, nrm], writes=[s1])
                    s.op('dve', lambda e, sh=sh, o_sh=o_sh: e.tensor_copy(out=sh[:], in_=modT[:, o_sh:o_sh + 8, :]),
                         reads=[modT], writes=[sh])
                    s.op('dve', lambda e, wi=wi, o_g=o_g, ipost=ipost: e.tensor_tensor(
                        out=tmpg[:, wi], in0=modT[:, o_g:o_g + 8, :],
                        in1=nrm[:, ipost, :].unsqueeze(2).to_broadcast([128, 8, NCOL]), op=ALU.mult),
                        reads=[modT, nrm], writes=[tmpg])
                    for col in range(NCOL):
                        s.dma(bass.AP(gsc_d, ((l * 2 + wi) * NCOL + col) * 1024, [[1, 128], [128, 8]]), tmpg[:, wi, :, col],
                              reads=[tmpg], writes=[GSC], g=2)

                def bc(name, off, n):
                    return bass.AP(D[name], off, [[0, 128], [1, n]])
                s.dma(P['dtb'][:], bc('ssd_dt_bias', l * 8, 8), writes=[P['dtb']], g=0)
                s.dma(P['aneg'][:], bc('ssd_a_log', l * 8, 8), writes=[P['aneg']], g=0)
                s.op('act', lambda e, P=P: e.activation(out=P['aneg'][:], in_=P['aneg'][:], func=AF.Exp), reads=[P['aneg']], writes=[P['aneg']])
                s.op('dve', lambda e, P=P: e.tensor_scalar(out=P['aneg'][:], in0=P['aneg'][:], scalar1=-1.0, scalar2=None, op0=ALU.mult),
                     reads=[P['aneg']], writes=[P['aneg']])
                s.dma(P['dsk'][:], bc('ssd_d', l * 4, 4), writes=[P['dsk']], g=0)
                s.dma(P['snw'][:], bc('ssd_norm_w', l * 256, 256), writes=[P['snw']], g=0)
                for j in range(5):
                    s.dma(P['scw'][:, :, j], bass.AP(D['ssd_conv_w'], l * 2560 + j * 512, [[1, 128], [128, 4]]), writes=[P['scw']], g=0)
                s.dma(P['scb'][:], bass.AP(D['ssd_conv_b'], l * 512, [[1, 128], [128, 4]]), writes=[P['scb']], g=0)
                for j in range(3):
                    s.dma(P['fcw'][:, :, j], bass.AP(D['ffn_conv_w'], (l * 3 + j) * 2816, [[1, 128], [128, 22]]), writes=[P['fcw']], g=0)
                s.dma(P['fcb'][:], bass.AP(D['ffn_conv_b'], l * 2816, [[1, 128], [128, 22]]), writes=[P['fcb']], g=0)
                s.dma(P['s5d'][:], bass.AP(D['s5_d'], l * 256, [[1, 128], [128, 2]]), writes=[P['s5d']], g=0)
                s.dma(P['glub'][:], bass.AP(D['s5_glu_b'], l * 256, [[1, 128], [128, 2]]), writes=[P['glub']], g=0)
                gst = sbuf(st, f"gst{l}", [128, 2, 256])
                s.dma(gst[:], D['s5_glu_w'].ap()[l].rearrange("(k p) c -> p k c", p=128), writes=[gst], g=3)
                s.op('pool', lambda e, P=P, gst=gst: e.tensor_copy(out=P['gluw'][:], in_=gst[:]), reads=[gst], writes=[P['gluw']])
                lam_init = 0.8 - 0.6 * math.exp(-0.3 * l)
                s.dma(P['subw'][:], bc('diff_subln_w', l * 128, 128), writes=[P['subw']], g=0)
                s.op('dve', lambda e, P=P, li=lam_init: e.tensor_scalar(out=P['subw'][:], in0=P['subw'][:], scalar1=1.0 - li, scalar2=None,
                                                                      op0=ALU.mult), reads=[P['subw']], writes=[P['subw']])
                lt = sbuf(st, f"lt{l}", [128, 4, 64])
                lsum = sbuf(st, f"lsum{l}", [128, 4])
                for i, nm in enumerate(['diff_lam_q1', 'diff_lam_k1', 'diff_lam_q2', 'diff_lam_k2']):
                    s.dma(lt[:, i, :], bc(nm, l * 64, 64), writes=[lt], g=0)
                s.op('dve', lambda e, lt=lt: e.tensor_tensor(out=lt[:, 0, :], in0=lt[:, 0, :], in1=lt[:, 1, :], op=ALU.mult), reads=[lt], writes=[lt])
                s.op('dve', lambda e, lt=lt: e.tensor_tensor(out=lt[:, 2, :], in0=lt[:, 2, :], in1=lt[:, 3, :], op=ALU.mult), reads=[lt], writes=[lt])
                s.op('dve', lambda e, lt=lt, lsum=lsum: e.reduce_sum(out=lsum[:, 0:1], in_=lt[:, 0, :], axis=mybir.AxisListType.X), reads=[lt], writes=[lsum])
                s.op('dve', lambda e, lt=lt, lsum=lsum: e.reduce_sum(out=lsum[:, 1:2], in_=lt[:, 2, :], axis=mybir.AxisListType.X), reads=[lt], writes=[lsum])
                s.op('act', lambda e, lsum=lsum: e.activation(out=lsum[:, 2:4], in_=lsum[:, 0:2], func=AF.Exp), reads=[lsum], writes=[lsum])
                s.op('dve', lambda e, lsum=lsum, P=P, li=lam_init: e.scalar_tensor_tensor(out=P['nlam'][:], in0=lsum[:, 3:4], scalar=-li,
                                                                                        in1=lsum[:, 2:3], op0=ALU.add, op1=ALU.subtract),
                     reads=[lsum], writes=[P['nlam']])

                lre = sbuf(st, f"lre{l}", [128, 16])
                lim = sbuf(st, f"lim{l}", [128, 16])
                stp = sbuf(st, f"stp{l}", [128, 16])
                for d in range(2):
                    s.dma(lre[:, d * 8:(d + 1) * 8], bass.AP(D['s5_lam_re'], (l * 2 + d) * 1024, [[1, 128], [128, 8]]), writes=[lre], g=0)
                    s.dma(lim[:, d * 8:(d + 1) * 8], bass.AP(D['s5_lam_im'], (l * 2 + d) * 1024, [[1, 128], [128, 8]]), writes=[lim], g=0)
                    for gl in range(2):
                        s.dma(stp[64 * gl:64 * gl + 64, d * 8:(d + 1) * 8],
                              bass.AP(D['s5_log_step'], (l * 2 + d) * 16 + gl, [[0, 64], [2, 8]]), writes=[stp], g=0)
                w16 = [sbuf(st, f"w16_{l}_{i}", [128, 16]) for i in range(10)]
                lrs, lis, mag, sn, cs, nr, den, cre, cim, t16 = w16
                r_, ar_, ai_ = P['s5r'], P['s5ar'], P['s5ai']

                def V(fn, rd, wr, eng='dve'):
                    s.op(eng, fn, reads=rd, writes=wr)
                V(lambda e: e.activation(out=stp[:], in_=stp[:], func=AF.Exp), [stp], [stp], 'act')
                V(lambda e: e.tensor_tensor(out=lrs[:], in0=lre[:], in1=stp[:], op=ALU.mult), [lre, stp], [lrs])
                V(lambda e: e.tensor_tensor(out=lis[:], in0=lim[:], in1=stp[:], op=ALU.mult), [lim, stp], [lis])
                V(lambda e: e.activation(out=r_[:], in_=lrs[:], func=AF.Exp), [lrs], [r_], 'act')
                ki16 = sbuf(st, f"ki16_{l}", [128, 16], mybir.dt.int32)
                kf16 = sbuf(st, f"kf16_{l}", [128, 16])

                def sin16(out, phase):
                    V(lambda e: e.tensor_scalar(out=t16[:], in0=lis[:], scalar1=phase, scalar2=None, op0=ALU.add), [lis], [t16])
                    V(lambda e: e.tensor_scalar(out=ki16[:], in0=t16[:], scalar1=1.0 / TWO_PI, scalar2=None, op0=ALU.mult), [t16], [ki16])
                    V(lambda e: e.tensor_copy(out=kf16[:], in_=ki16[:]), [ki16], [kf16])
                    V(lambda e: e.scalar_tensor_tensor(out=t16[:], in0=kf16[:], scalar=-TWO_PI, in1=t16[:], op0=ALU.mult, op1=ALU.add), [kf16, t16], [t16])
                    V(lambda e: e.tensor_scalar(out=t16[:], in0=t16[:], scalar1=-3.14159, scalar2=3.14159, op0=ALU.max, op1=ALU.min), [t16], [t16])
                    V(lambda e: e.activation(out=out[:], in_=t16[:], func=AF.Sin), [t16], [out], 'act')
                sin16(sn, 0.0)
                sin16(cs, 0.5 * math.pi)
                negpi = LP['_negpi']
                Zt = LP['_Z']
                V(lambda e: e.tensor_tensor(out=ar_[:], in0=r_[:], in1=cs[:], op=ALU.mult), [r_, cs], [ar_])
                V(lambda e: e.tensor_tensor(out=ai_[:], in0=r_[:], in1=sn[:], op=ALU.mult), [r_, sn], [ai_])
                V(lambda e: e.tensor_scalar(out=nr[:], in0=ar_[:], scalar1=-1.0, scalar2=None, op0=ALU.add), [ar_], [nr])
                V(lambda e: e.tensor_tensor(out=den[:], in0=lre[:], in1=lre[:], op=ALU.mult), [lre], [den])
                V(lambda e: e.tensor_tensor(out=t16[:], in0=lim[:], in1=lim[:], op=ALU.mult), [lim], [t16])
                V(lambda e: e.tensor_tensor(out=den[:], in0=den[:], in1=t16[:], op=ALU.add), [den, t16], [den])
                V(lambda e: e.reciprocal(out=den[:], in_=den[:]), [den], [den])
                V(lambda e: e.tensor_tensor(out=cre[:], in0=nr[:], in1=lre[:], op=ALU.mult), [nr, lre], [cre])
                V(lambda e: e.tensor_tensor(out=t16[:], in0=ai_[:], in1=lim[:], op=ALU.mult), [ai_, lim], [t16])
                V(lambda e: e.tensor_tensor(out=cre[:], in0=cre[:], in1=t16[:], op=ALU.add), [cre, t16], [cre])
                V(lambda e: e.tensor_tensor(out=cre[:], in0=cre[:], in1=den[:], op=ALU.mult), [cre, den], [cre])
                V(lambda e: e.tensor_tensor(out=cim[:], in0=ai_[:], in1=lre[:], op=ALU.mult), [ai_, lre], [cim])
                V(lambda e: e.tensor_tensor(out=t16[:], in0=nr[:], in1=lim[:], op=ALU.mult), [nr, lim], [t16])
                V(lambda e: e.tensor_tensor(out=cim[:], in0=cim[:], in1=t16[:], op=ALU.subtract), [cim, t16], [cim])
                V(lambda e: e.tensor_tensor(out=cim[:], in0=cim[:], in1=den[:], op=ALU.mult), [cim, den], [cim])
                rt = sbuf(st, f"rt{l}", [128, 256])
                rki = sbuf(st, f"rki{l}", [128, 256], mybir.dt.int32)
                rkf = sbuf(st, f"rkf{l}", [128, 256])
                for j in range(16):
                    for (tab, ph) in [(P['cosT'], 0.5 * math.pi), (P['sinT'], 0.0)]:
                        V(lambda e, j=j, ph=ph: e.tensor_scalar(out=rt[:], in0=iota[:], scalar1=lis[:, j:j + 1], scalar2=ph, op0=ALU.mult, op1=ALU.add),
                          [iota, lis], [rt])
                        V(lambda e: e.tensor_scalar(out=rki[:], in0=rt[:], scalar1=1.0 / TWO_PI, scalar2=None, op0=ALU.mult), [rt], [rki])
                        V(lambda e: e.tensor_copy(out=rkf[:], in_=rki[:]), [rki], [rkf])
                        V(lambda e: e.scalar_tensor_tensor(out=rt[:], in0=rkf[:], scalar=-TWO_PI, in1=rt[:], op0=ALU.mult, op1=ALU.add), [rkf, rt], [rt])
                        V(lambda e: e.tensor_scalar(out=rt[:], in0=rt[:], scalar1=-3.14159, scalar2=3.14159, op0=ALU.max, op1=ALU.min), [rt], [rt])
                        V(lambda e, j=j, tab=tab: e.activation(out=tab[:, j, :], in_=rt[:], func=AF.Sin), [rt], [(tab, j)], 'act')
                bre = sbuf(st, f"bre{l}", [128, 16, 16])
                bim = sbuf(st, f"bim{l}", [128, 16, 16])
                bbr = sbuf(st, f"bbr{l}", [128, 16, 16])
                bbi = sbuf(st, f"bbi{l}", [128, 16, 16])
                btm = sbuf(st, f"btm{l}", [128, 16, 16])
                for d in range(2):
                    s.dma(bre[:, d * 8:(d + 1) * 8, :], bass.AP(D['s5_b_re'], (l * 2 + d) * 16384, [[16, 128], [2048, 8], [1, 16]]), writes=[bre], g=0)
                    s.dma(bim[:, d * 8:(d + 1) * 8, :], bass.AP(D['s5_b_im'], (l * 2 + d) * 16384, [[16, 128], [2048, 8], [1, 16]]), writes=[bim], g=0)

                def B3(t):
                    return t[:].unsqueeze(2).to_broadcast([128, 16, 16])
                V(lambda e: e.tensor_tensor(out=bbr[:], in0=bre[:], in1=B3(cre), op=ALU.mult), [bre, cre], [bbr])
                V(lambda e: e.tensor_tensor(out=btm[:], in0=bim[:], in1=B3(cim), op=ALU.mult), [bim, cim], [btm])
                V(lambda e: e.tensor_tensor(out=bbr[:], in0=bbr[:], in1=btm[:], op=ALU.subtract), [bbr, btm], [bbr])
                V(lambda e: e.tensor_tensor(out=bbi[:], in0=bim[:], in1=B3(cre), op=ALU.mult), [bim, cre], [bbi])
                V(lambda e: e.tensor_tensor(out=btm[:], in0=bre[:], in1=B3(cim), op=ALU.mult), [bre, cim], [btm])
                V(lambda e: e.tensor_tensor(out=bbi[:], in0=bbi[:], in1=btm[:], op=ALU.add), [bbi, btm], [bbi])
                for d in range(2):
                    for q in range(8):
                        j = d * 8 + q
                        ql = q % 4
                        for ri, bb in enumerate([bbr, bbi]):
                            zi = ri * 4 + ql
                            for gl in range(2):
                                V(lambda e, gl=gl, zi=zi, ql=ql, bb=bb, j=j: e.tensor_copy(
                                    out=Zt[64 * gl:64 * gl + 64, zi, 32 * ql + 16 * gl:32 * ql + 16 * gl + 16], in_=bb[64 * gl:64 * gl + 64, j, :]),
                                  [bb], [(Zt, zi)])
                            pb = bank()
                            s.op('pe', lambda e, pb=pb, zi=zi: e.transpose(out=pb[:, 0:128], in_=Zt[:, zi, :], identity=ident_f[:]),
                                 reads=[(Zt, zi), ident_f], writes=[pb])
                            s.op('act', lambda e, pb=pb, d=d, q=q, ri=ri: e.copy(out=P['WB'][:, d, q, ri, :], in_=pb[:, 0:128]),
                                 reads=[pb], writes=[(P['WB'], (d, q, ri))])
                s.op('pool', lambda e: e.memset(P['WC'][:], 0.0), writes=[P['WC']])
                for d in range(2):
                    for q in range(8):
                        ql = q % 4
                        for gl in range(2):
                            g = 2 * q + gl
                            for ri, nm in enumerate(['s5_c_re', 's5_c_im']):
                                s.dma(P['WC'][64 * gl:64 * gl + 64, d, q, ri, 32 * ql + 16 * gl:32 * ql + 16 * gl + 16],
                                      bass.AP(D[nm], ((l * 2 + d) * 16 + g) * 1024, [[1, 64], [64, 16]]), writes=[P['WC']], g=4)
                V(lambda e: e.tensor_scalar(out=P['WC'][:, :, :, 1, :], in0=P['WC'][:, :, :, 1, :], scalar1=-1.0, scalar2=None, op0=ALU.mult),
                  [P['WC']], [P['WC']])
                for nm in ['WB', 'WC', 'cosT', 'sinT']:
                    dt_ = BF16 if nm == 'WB' else F32
                    dd = nc.dram_tensor(f"s5scr_{nm}{l}", [128, 4096], dt_)
                    S5SCR[(nm, l)] = (dd, Buf(dd, f"s5scr_{nm}{l}"))
                    src = P[nm][:].rearrange("p a b c d -> p (a b c d)") if nm in ('WB', 'WC') else P[nm][:].rearrange("p a b -> p (a b)")
                    s.dma(dd.ap(), src, reads=[P[nm]], writes=[S5SCR[(nm, l)][1]], g=2)
                    P[nm] = None
            s.barrier()

        def tcols(t):
            return slice(t * 128, (t + 1) * 128)

        PIECES = [(0, 256), (256, 768), (768, 1280), (1280, 1792), (1792, 2304)]

        def mkstg(st, n=3):
            return dict(t=[sbuf(st, "stg", [128, 1024]) for _ in range(n)], i=[0])

        def load_cast(stg, dst_ap, dst_buf, src_ap, shape, key=None, eng='pool'):
            t = stg['t'][stg['i'][0] % len(stg['t'])]
            stg['i'][0] += 1
            n = int(np.prod(shape))
            tv = t[:, 0:n]
            if len(shape) == 2:
                tv = tv.rearrange("p (a b) -> p a b", a=shape[0])
            s.dma(tv, src_ap, writes=[t], g=3)
            s.op(eng, lambda e: e.tensor_copy(out=dst_ap, in_=tv), reads=[t], writes=[(dst_buf, key)])

        def load_w_bf16(stg, wbuf, name, l, c0, n, dst0=0, nk=8):
            src = D[name].ap()[l].rearrange("(k p) c -> p k c", p=128)
            kk = max(1, 1024 // n)
            for k0 in range(0, nk, kk):
                k1 = min(nk, k0 + kk)
                if k1 - k0 == 1:
                    load_cast(stg, wbuf[:, k0, dst0:dst0 + n], wbuf, src[:, k0, c0:c0 + n], [n])
                else:
                    load_cast(stg, wbuf[:, k0:k1, dst0:dst0 + n], wbuf, src[:, k0:k1, c0:c0 + n], [k1 - k0, n])

        def rstd_from_ss(ss, rstd, n):
            s.op('act', lambda e: e.activation(out=rstd[:], in_=ss[:], func=AF.Sqrt, bias=EPS, scale=1.0 / n), reads=[ss], writes=[rstd])
            s.op('dve', lambda e: e.reciprocal(out=rstd[:], in_=rstd[:]), reads=[rstd], writes=[rstd])

        def norm_mod_transpose(st_bufs, xt, hT, t, s1, sh, col):
            junk, ss, rstd, xn = st_bufs
            s.op('act', lambda e: e.activation(out=junk[:], in_=xt[:], func=AF.Square, accum_out=ss[:]), reads=[xt], writes=[junk, ss])
            rstd_from_ss(ss, rstd, 1024)
            s.op('dve', lambda e: e.tensor_scalar(out=xn[:], in0=xt[:], scalar1=rstd[:, 0:1], scalar2=None, op0=ALU.mult),
                 reads=[xt, rstd], writes=[xn])
            pb = bank()
            pbb = pb.t[:].bitcast(BF16)
            for k in range(8):
                s.op('pe', lambda e, k=k: e.transpose(out=pbb[:, k * 128:(k + 1) * 128], in_=xn[:, k * 128:(k + 1) * 128], identity=ident_b[:]),
                     reads=[xn, ident_b], writes=[pb])
            for k in range(8):
                s.op('act', lambda e, k=k: e.activation(out=hT[:, k, tcols(t)], in_=pbb[:, k * 128:(k + 1) * 128], func=AF.Identity,
                                                       scale=s1[:, k, col:col + 1], bias=sh[:, k, col:col + 1]),
                     reads=[pb, s1, sh], writes=[(hT, t)])

        for b in range(NB):
            s.dma(xs_d.ap()[b, 0:256, :], ctx_d.ap()[b], writes=[(XS, b)], g=0)
            s.dma(xs_d.ap()[b, 256:2304, :], x_d.ap()[b], writes=[(XS, b)], g=0)
            for l in LAYERS:
                P = LP[l]
                last = (l == LAYERS[-1])
                need_ctx = not last
                with contextlib.ExitStack() as shs:
                  hT = sbuf(shs, "hT", [128, 8, SEQ], BF16)
                  early = False
                  with contextlib.ExitStack() as sm:
                    mixT = sbuf(sm, "mixT", [128, 8, SEQ], BF16)
                    with contextlib.ExitStack() as st:
                        xt2 = [sbuf(st, f"xt{i}", [128, 1024]) for i in range(2)]
                        nb_ = (sbuf(st, "junk", [128, 1024], BF16), sbuf(st, "ssA", [128, 1]), sbuf(st, "rstdA", [128, 1]),
                               sbuf(st, "xnA", [128, 1024], BF16))
                        for t in range(NT):
                            xt = xt2[t % 2]
                            s.dma(xt[:], xs_d.ap()[b, tcols(t), :], reads=[(XS, b)], writes=[xt], g=0)
                            norm_mod_transpose(nb_, xt, hT, t, P['s1a'], P['sha'], NB if t < 2 else b)
                        s.barrier()
                    CONST = dict(ident_b=ident_b, ident_f=ident_f, U_f=U_f, UT_f=UT_f, mf_f=mf_f, mb_f=mb_f, ones_f=ones_f,
                                 ropeC=ropeC, ropeS=ropeS)
                    ENV = dict(S5SCR=S5SCR, nc=nc, s=s, sbuf=sbuf, bank=bank, P=P, D=D, l=l, b=b, NB=NB, NCOL=NCOL, hT=hT, mixT=mixT,
                               load_w=load_w_bf16, mkstg=mkstg, load_cast=load_cast, rstd_from_ss=rstd_from_ss, tcols=tcols, PIECES=PIECES, C=CONST, DB=DB, stop=stop,
                               need_ctx=need_ctx, last=last, xs_d=xs_d, XS=XS, out_d=out_d, OUT=OUT, gsc_d=gsc_d, GSC=GSC)
                    if stop == 'A':
                        s.dma(DB['hT'].ap(), hT[:], reads=[hT], q='pool', g=5)
                        early = True
                    if not early:
                        with contextlib.ExitStack() as st:
                            ssd_stage(st, **ENV)
                            s.barrier()
                        early = stop is not None and stop.startswith('SSD')
                    if not early:
                        with contextlib.ExitStack() as st:
                            s5_stage(st, **ENV)
                            s.barrier()
                        early = stop is not None and stop.startswith('S5')
                        if 'mix2' in DB:
                            s.dma(DB['mix2'].ap(), mixT[:], reads=[mixT], q='pool', g=5)
                            s.barrier()
                    if not early:
                        with contextlib.ExitStack() as st:
                            attn_stage(st, **ENV)
                            s.barrier()
                        early = stop is not None and stop.startswith('ATT')
                    if stop is not None and stop[:2] in ('SS', 'S5', 'AT') and 'mixT' in DB:
                        s.dma(DB['mixT'].ap(), mixT[:], reads=[mixT], q='pool', g=5)
                    tiles = list(range(NT)) if need_ctx else list(range(2, NT))
                    if not early:
                      with contextlib.ExitStack() as st:
                        wo = sbuf(st, "wo", [128, 8, 1024], BF16)
                        stgE = mkstg(st)
                        load_w_bf16(stgE, wo, 'w_out', l, 0, 1024)
                        gbc = sbuf(st, "gbc", [128, 2, 1024])
                        s.dma(gbc[:, 0, :], bass.AP(gsc_d, ((l * 2 + 0) * NCOL + b) * 1024, [[0, 128], [1, 1024]]), reads=[GSC], writes=[gbc], g=0)
                        s.dma(gbc[:, 1, :], bass.AP(gsc_d, ((l * 2 + 0) * NCOL + NB) * 1024, [[0, 128], [1, 1024]]), reads=[GSC], writes=[gbc], g=0)
                        xt2 = [sbuf(st, f"xtE{i}", [128, 1024]) for i in range(2)]
                        nb_ = (sbuf(st, "junkE", [128, 1024], BF16), sbuf(st, "ssE", [128, 1]), sbuf(st, "rstdE", [128, 1]),
                               sbuf(st, "xnE", [128, 1024], BF16))
                        ss2 = sbuf(st, "ss2", [128, 2])
                        rs2 = sbuf(st, "rs2", [128, 1])
                        ytmp = sbuf(st, "ytmp", [128, 1024])
                        for t in tiles:
                            xt = xt2[t % 2]
                            s.dma(xt[:], xs_d.ap()[b, tcols(t), :], reads=[(XS, b)], writes=[xt], g=0)
                            pbs = [bank(), bank()]
                            for hf in range(2):
                                for k in range(8):
                                    s.op('pe', lambda e, k=k, hf=hf: e.matmul(pbs[hf][:, :], lhsT=mixT[:, k, tcols(t)], rhs=wo[:, k, hf * 512:(hf + 1) * 512],
                                                                             start=(k == 0), stop=(k == 7)), reads=[mixT, wo], writes=[pbs[hf]])
                                s.op('act', lambda e, hf=hf: e.activation(out=nb_[0][:, hf * 512:(hf + 1) * 512], in_=pbs[hf][:, :], func=AF.Square,
                                                                         accum_out=ss2[:, hf:hf + 1]), reads=[pbs[hf]], writes=[nb_[0], ss2])
                            s.op('dve', lambda e: e.tensor_tensor(out=rs2[:], in0=ss2[:, 0:1], in1=ss2[:, 1:2], op=ALU.add), reads=[ss2], writes=[rs2])
                            rstd_from_ss(rs2, rs2, 1024)
                            gi = 1 if t < 2 else 0
                            for hf in range(2):
                                s.op('dve', lambda e, hf=hf, gi=gi: e.scalar_tensor_tensor(out=ytmp[:, hf * 512:(hf + 1) * 512], in0=pbs[hf][:, :], scalar=rs2[:, 0:1],
                                                                                        in1=gbc[:, gi, hf * 512:(hf + 1) * 512], op0=ALU.mult, op1=ALU.mult),
                                     reads=[pbs[hf], rs2, gbc], writes=[ytmp])
                            if t == 5 and 'e_wo' in DB:
                                s.dma(DB['e_wo'].ap(), wo[:].rearrange("p a b -> p (a b)"), reads=[wo], q='pool', g=5)
                                s.dma(DB['e_mix'].ap(), mixT[:].rearrange("p a b -> p (a b)"), reads=[mixT], q='pool', g=5)
                            if t == 5 and 'e_ytmp' in DB:
                                s.dma(DB['e_ytmp'].ap(), ytmp[:], reads=[ytmp], g=5)
                                s.dma(DB['e_ss'].ap(), ss2[:], reads=[ss2], g=5)
                                s.dma(DB['e_rs'].ap(), rs2[:], reads=[rs2], g=5)
                                s.dma(DB['e_gbc'].ap(), gbc[:].rearrange("p a b -> p (a b)"), reads=[gbc], g=5)
                                s.dma(DB['e_x0'].ap(), xt[:], reads=[xt], g=5)
                            s.op('pool', lambda e, xt=xt: e.tensor_tensor(out=xt[:], in0=xt[:], in1=ytmp[:], op=ALU.add), reads=[xt, ytmp], writes=[xt])
                            s.dma(xs_d.ap()[b, tcols(t), :], xt[:], reads=[xt], writes=[(XS, b)], g=1)
                            norm_mod_transpose(nb_, xt, hT, t, P['s1f'], P['shf'], NB if t < 2 else b)
                        s.barrier()
                  if stop == 'E':
                    early = True
                  if not early:
                    with contextlib.ExitStack() as st:
                        ffn_stage(st, **ENV)
                        s.barrier()
                    if stop is not None and stop.startswith('F'):
                        early = True
                  if stop in ('E', 'L0'):
                    s.dma(DB['xs'].ap(), xs_d.ap()[b], reads=[(XS, b)], g=5)
                    if stop == 'E':
                        s.dma(DB['hT'].ap(), hT[:], reads=[hT], q='pool', g=5)
                    early = True
                  if early:
                    break
            if stop is not None:
                break
        s.barrier(final=True)
    return nc


def shard_inputs(inputs, NB, ncores):
    consts = host_consts()
    maps = []
    for i in range(ncores):
        m = {}
        m['x'] = np.ascontiguousarray(inputs['x'][i * NB:(i + 1) * NB], dtype=np.float32)
        m['ctx'] = np.ascontiguousarray(inputs['ctx'][i * NB:(i + 1) * NB], dtype=np.float32)
        m['cc'] = np.ascontiguousarray(np.concatenate([inputs['c'][i * NB:(i + 1) * NB], inputs['c_ctx'][None, :]], 0), dtype=np.float32)
        for n, sh in PARAMS:
            m[n] = np.ascontiguousarray(inputs[n], dtype=np.float32)
        m.update(consts)
        maps.append(m)
    return maps


def kernel(**inputs):
    inputs = {k: np.asarray(v) for k, v in inputs.items()}
    NB = 4
    nc = build(NB=NB, LAYERS=(0, 1))
    maps = shard_inputs(inputs, NB, 8)
    res = run_bass_kernel_spmd(nc, maps, core_ids=list(range(8)))
    return np.concatenate([r["out"] for r in res.results], axis=0).astype(np.float32)
```

```python
import contextlib
import math
import numpy as np
import concourse.bass as bass
import concourse.mybir as mybir
from concourse.bass_utils import run_bass_kernel_spmd

F32 = mybir.dt.float32
BF16 = mybir.dt.bfloat16
AF = mybir.ActivationFunctionType
ALU = mybir.AluOpType

ENGS = ('pe', 'act', 'dve', 'pool', 'sp')
SAME_ENGINE_SYNC = ('act', 'dve', 'pool')
NDSEM = 6
EPS = 1e-6
NT = 18
SEQ = 2304
TWO_PI = 2.0 * math.pi


class Buf:
    def __init__(self, t, name, psum=False):
        self.t = t
        self.name = name
        self.psum = psum
        self.w = {}
        self.r = {}

    def _keys(self, key):
        if key is None:
            return list(set(self.w) | set(self.r) | {None})
        return [key, None]

    def wdeps(self, key):
        d = set()
        for k in self._keys(key):
            d |= self.w.get(k, set())
        return d

    def rdeps(self, key):
        d = set()
        for k in self._keys(key):
            d |= set(self.r.get(k, {}).items())
        return d

    def add_reader(self, key, ev):
        rr = self.r.setdefault(key, {})
        if rr.get(ev[0], 0) < ev[1]:
            rr[ev[0]] = ev[1]

    def set_writer(self, key, ev):
        if key is None:
            self.w = {None: {ev}}
            self.r = {}
        else:
            self.w[key] = {ev}
            self.r[key] = {}

    def add_dma_writer(self, key, ev):
        cur = self.w.get(key, set())
        cur = {e for e in cur if e[0][1] == ev[0][1] and e[0][0].startswith('q') and not (e[0] == ev[0] and e[1] <= ev[1])}
        cur.add(ev)
        if key is None:
            self.w = {None: cur}
            self.r = {}
        else:
            self.w[key] = cur
            self.r[key] = {}

    def __getitem__(self, idx):
        return self.t[idx]


def _norm(lst):
    return [(x, None) if isinstance(x, Buf) else x for x in lst]


class Sched:
    NQ = {'sp': 14, 'pool': 3}

    def __init__(self, nc, stack):
        self.nc = nc
        self.stack = stack
        self.eng = {'pe': nc.tensor, 'act': nc.scalar, 'dve': nc.vector, 'pool': nc.gpsimd, 'sp': nc.sync}
        self.phase = 0
        self.ninst = 0
        self.sets = []
        for i in range(2):
            d = {}
            for e in ENGS:
                d[e] = stack.enter_context(nc.semaphore(f"s_{e}_{i}"))
            for q, n in self.NQ.items():
                for j in range(n):
                    d[f"q{q}{j}"] = stack.enter_context(nc.semaphore(f"s_q{q}{j}_{i}"))
            self.sets.append(d)
        self.bsems = [stack.enter_context(nc.semaphore(f"s_bar{i}")) for i in range(2)]
        self.nbar = 0
        self._new_sems()

    def _new_sems(self):
        self.sem = self.sets[self.phase % 2]
        self.cnt = {k: 0 for k in self.sem}
        self.seen = {e: {} for e in ENGS}
        self.rr = {q: 0 for q in self.NQ}

    def _wait(self, e, deps):
        need = {}
        for (k, v) in deps:
            if k[1] != self.phase:
                continue
            if need.get(k, 0) < v:
                need[k] = v
        for k, v in need.items():
            if self.seen[e].get(k, 0) >= v:
                continue
            self.eng[e].wait_ge(self.sem[k[0]], v)
            self.seen[e][k] = v
            self.ninst += 1

    def op(self, e, fn, reads=(), writes=()):
        reads = _norm(reads)
        writes = _norm(writes)
        writes = writes + [r for r in reads if r[0].psum]
        reads = [r for r in reads if not r[0].psum]
        deps = set()
        for b, k in reads:
            deps |= b.wdeps(k)
        for b, k in writes:
            deps |= b.wdeps(k)
            for ev in b.rdeps(k):
                if ev[0][0] != e:
                    deps.add(ev)
        if e not in SAME_ENGINE_SYNC:
            deps = {d for d in deps if d[0][0] != e}
        self._wait(e, deps)
        inst = fn(self.eng[e])
        self.cnt[e] += 1
        inst.then_inc(self.sem[e], 1)
        self.ninst += 1
        ev = ((e, self.phase), self.cnt[e])
        for b, k in reads:
            b.add_reader(k, ev)
        for b, k in writes:
            b.set_writer(k, ev)
        return inst

    def dma(self, out, in_, reads=(), writes=(), q='sp', g=0, **kw):
        reads = _norm(reads)
        writes = _norm(writes)
        deps = set()
        for b, k in reads:
            deps |= b.wdeps(k)
        for b, k in writes:
            same = b.w.get(k, set())
            deps |= {ev for ev in b.wdeps(k) if not (ev in same and ev[0][0].startswith('q'))} | b.rdeps(k)
        self._wait(q, deps)
        sk = f"q{q}{self.rr[q] % self.NQ[q]}"
        self.rr[q] += 1
        kk = (sk, self.phase)
        if self.cnt[sk] > 0 and self.seen[q].get(kk, 0) < self.cnt[sk]:
            self.eng[q].wait_ge(self.sem[sk], self.cnt[sk])
            self.seen[q][kk] = self.cnt[sk]
        inst = self.eng[q].dma_start(out=out, in_=in_, **kw)
        self.cnt[sk] += 16
        inst.then_inc(self.sem[sk], 16)
        self.ninst += 1
        ev = (kk, self.cnt[sk])
        for b, k in reads:
            b.add_reader(k, ev)
        for b, k in writes:
            b.add_dma_writer(k, ev)
        return inst

    def barrier(self, final=False):
        bsem = self.bsems[self.nbar % 2]
        other = self.bsems[(self.nbar + 1) % 2]
        self.nbar += 1
        for k in self.sem:
            if k.startswith('q') and self.cnt[k] > 0:
                self.eng['sp'].wait_ge(self.sem[k], self.cnt[k])
        for e in ENGS:
            if self.cnt[e] > 0:
                self.eng[e].wait_ge(self.sem[e], self.cnt[e])
            self.eng[e].sem_inc(bsem, 1)
        self.eng['sp'].wait_ge(bsem, len(ENGS))
        if not final:
            for k, sm in self.sets[(self.phase + 1) % 2].items():
                self.eng['sp'].sem_clear(sm)
            self.eng['sp'].sem_clear(other)
        self.eng['sp'].sem_inc(bsem, 1)
        for e in ENGS:
            self.eng[e].wait_ge(bsem, len(ENGS) + 1)
        if not final:
            self.phase += 1
            self._new_sems()


PARAMS = [
    ('mod_w', (2, 1024, 6144)), ('mod_b', (2, 6144)), ('mix_norm_pre', (2, 1024)), ('mix_norm_post', (2, 1024)),
    ('ffn_norm_pre', (2, 1024)), ('ffn_norm_post', (2, 1024)), ('w_in', (2, 1024, 2568)), ('w_out', (2, 1024, 1024)),
    ('ssd_conv_w', (2, 5, 512)), ('ssd_conv_b', (2, 512)), ('ssd_dt_bias', (2, 2, 4)), ('ssd_a_log', (2, 2, 4)),
    ('ssd_d', (2, 4)), ('ssd_norm_w', (2, 256)), ('diff_lam_q1', (2, 64)), ('diff_lam_k1', (2, 64)),
    ('diff_lam_q2', (2, 64)), ('diff_lam_k2', (2, 64)), ('diff_subln_w', (2, 128)),
    ('s5_lam_re', (2, 2, 16, 64)), ('s5_lam_im', (2, 2, 16, 64)), ('s5_log_step', (2, 2, 16)),
    ('s5_b_re', (2, 2, 16, 64, 16)), ('s5_b_im', (2, 2, 16, 64, 16)), ('s5_c_re', (2, 2, 16, 16, 64)),
    ('s5_c_im', (2, 2, 16, 16, 64)), ('s5_d', (2, 16, 16)), ('s5_glu_w', (2, 256, 256)), ('s5_glu_b', (2, 256)),
    ('ffn_w_gate', (2, 1024, 2816)), ('ffn_w_up', (2, 1024, 2816)), ('ffn_conv_w', (2, 3, 2816)),
    ('ffn_conv_b', (2, 2816)), ('ffn_w_down', (2, 2816, 1024)),
]


def host_consts():
    r = np.arange(128)
    U = (r[:, None] <= r[None, :]).astype(np.float32)
    c = {}
    c['c_ident'] = np.eye(128, dtype=np.float32)
    c['c_U'] = U
    c['c_UT'] = np.ascontiguousarray(U.T)
    c['c_mf'] = np.where(r[None, :] >= r[:, None], 0.0, -30000.0).astype(np.float32)
    c['c_mb'] = np.where(r[None, :] <= r[:, None], 0.0, -30000.0).astype(np.float32)
    c['c_ones'] = np.ones((128, 128), np.float32)
    t = np.arange(2048)
    rr = (t // 64).astype(np.float32)
    col = (t % 64).astype(np.float32)
    inv = (10000.0 ** (-np.arange(16, dtype=np.float32) / 16)).astype(np.float32)
    ar = rr[:, None] * inv[None, :]
    ac = col[:, None] * inv[None, :]
    c['c_cos'] = np.concatenate([np.cos(ar), np.cos(ac)], 1).astype(np.float32)
    c['c_sin'] = np.concatenate([np.sin(ar), np.sin(ac)], 1).astype(np.float32)
    c['c_iota'] = np.tile(np.arange(256, dtype=np.float32)[None, :], (128, 1))
    return c


import os
_CUT = int(os.environ.get('SSDCUT', '0'))


def ssd_stage(st, nc, s, sbuf, bank, P, D, l, hT, mixT, load_w, rstd_from_ss, tcols, PIECES, C, DB, stop, mkstg, **_):
    wzd = sbuf(st, "wzd", [128, 8, 264], BF16)
    wx = sbuf(st, "wx", [128, 8, 512], BF16)
    with contextlib.ExitStack() as s0:
        stg = mkstg(s0)
        load_w(stg, wzd, 'w_in', l, 0, 256, 0)
        load_w(stg, wzd, 'w_in', l, 768, 8, 256)
        load_w(stg, wx, 'w_in', l, 256, 512, 0)
        s.barrier()
    if stop == 'SSD0':
        return
    zs = sbuf(st, "zs", [128, NT, 256], BF16)
    dtr = sbuf(st, "dtr", [128, NT, 8])
    xbcT = sbuf(st, "xbcT", [128, 4, SEQ], BF16)
    xbc = sbuf(st, "xbc", [128, NT, 512], BF16)
    xraw = sbuf(st, "xraw", [128, 2312])
    xacc = sbuf(st, "xacc", [128, 2312])
    yacc = sbuf(st, "yacc", [128, NT, 256])
    for t in range(NT):
        pb = bank()
        for k in range(8):
            s.op('pe', lambda e, k=k: e.matmul(pb[:, 0:264], lhsT=hT[:, k, tcols(t)], rhs=wzd[:, k, :], start=(k == 0), stop=(k == 7)),
                 reads=[(hT, t), wzd], writes=[pb])
        s.op('act', lambda e: e.activation(out=zs[:, t, :], in_=pb[:, 0:256], func=AF.Silu), reads=[pb], writes=[(zs, t)])
        s.op('dve', lambda e: e.tensor_copy(out=dtr[:, t, :], in_=pb[:, 256:264]), reads=[pb], writes=[dtr])
    if stop == 'SSD1':
        return
    s.op('pool', lambda e: e.memset(xraw[:], 0.0), writes=[xraw])
    for c in range(4):
        for (c0, c1) in PIECES:
            pb = bank()
            n = c1 - c0
            for k in range(8):
                s.op('pe', lambda e, k=k: e.matmul(pb[:, 0:n], lhsT=wx[:, k, c * 128:(c + 1) * 128], rhs=hT[:, k, c0:c1], start=(k == 0), stop=(k == 7)),
                     reads=[hT, wx], writes=[pb])
            off = 2 if c0 == 0 else 6
            s.op('act', lambda e: e.copy(out=xraw[:, off + c0:off + c1], in_=pb[:, 0:n]), reads=[pb], writes=[xraw])
        s.op('act', lambda e: e.activation(out=xacc[:, 2:2310], in_=xraw[:, 2:2310], func=AF.Identity, scale=P['scw'][:, c, 2:3], bias=P['scb'][:, c:c + 1]),
             reads=[xraw, P['scw'], P['scb']], writes=[xacc])
        for j in (0, 1, 3, 4):
            s.op('dve', lambda e, j=j: e.scalar_tensor_tensor(out=xacc[:, 2:2310], in0=xraw[:, j:j + 2308], scalar=P['scw'][:, c, j:j + 1],
                                                             in1=xacc[:, 2:2310], op0=ALU.mult, op1=ALU.add),
                 reads=[xraw, xacc, P['scw']], writes=[xacc])
        s.op('act', lambda e: e.activation(out=xbcT[:, c, 0:256], in_=xacc[:, 2:258], func=AF.Silu), reads=[xacc], writes=[xbcT])
        s.op('act', lambda e: e.activation(out=xbcT[:, c, 256:2304], in_=xacc[:, 262:2310], func=AF.Silu), reads=[xacc], writes=[xbcT])
    if stop == 'SSD2':
        return
    for t in range(NT):
        pb = bank()
        pbb = pb.t[:].bitcast(BF16)
        for c in range(4):
            s.op('pe', lambda e, c=c: e.transpose(out=pbb[:, c * 128:(c + 1) * 128], in_=xbcT[:, c, tcols(t)], identity=C['ident_b'][:]),
                 reads=[xbcT, C['ident_b']], writes=[pb])
        s.op('act', lambda e: e.copy(out=xbc[:, t, :], in_=pbb[:, 0:512]), reads=[pb], writes=[(xbc, t)])
    if stop == 'SSD3':
        return
    dt = sbuf(st, "dt", [128, NT, 8])
    dta = sbuf(st, "dta", [128, NT, 8])
    tA = sbuf(st, "tA", [128, NT, 8])
    tB = sbuf(st, "tB", [128, NT, 8])

    def B8(tb):
        return tb[:].unsqueeze(1).to_broadcast([128, NT, 8])
    s.op('dve', lambda e: e.tensor_tensor(out=dtr[:], in0=dtr[:], in1=B8(P['dtb']), op=ALU.add), reads=[dtr, P['dtb']], writes=[dtr])
    s.op('dve', lambda e: e.tensor_scalar(out=tB[:], in0=dtr[:], scalar1=-1.0, scalar2=None, op0=ALU.mult), reads=[dtr], writes=[tB])
    s.op('dve', lambda e: e.tensor_tensor(out=tA[:], in0=dtr[:], in1=tB[:], op=ALU.max), reads=[dtr, tB], writes=[tA])
    s.op('act', lambda e: e.activation(out=tA[:], in_=tA[:], func=AF.Exp, scale=-1.0), reads=[tA], writes=[tA])
    s.op('act', lambda e: e.activation(out=tA[:], in_=tA[:], func=AF.Ln, bias=1.0, scale=1.0), reads=[tA], writes=[tA])
    s.op('dve', lambda e: e.tensor_single_scalar(out=tB[:], in_=dtr[:], scalar=0.0, op=ALU.max), reads=[dtr], writes=[tB])
    s.op('dve', lambda e: e.tensor_tensor(out=dt[:], in0=tA[:], in1=tB[:], op=ALU.add), reads=[tA, tB], writes=[dt])
    s.op('dve', lambda e: e.tensor_tensor(out=dta[:], in0=dt[:], in1=B8(P['aneg']), op=ALU.mult), reads=[dt, P['aneg']], writes=[dta])
    if stop == 'SSD4':
        return
    for t in range(NT):
        s.op('dve', lambda e, t=t: e.tensor_tensor(out=yacc[:, t, :].rearrange("p (h d) -> p h d", h=4),
                                                  in0=xbc[:, t, 0:256].rearrange("p (h d) -> p h d", h=4),
                                                  in1=P['dsk'][:].unsqueeze(2).to_broadcast([128, 4, 64]), op=ALU.mult),
             reads=[(xbc, t), P['dsk']], writes=[(yacc, t)])
    if stop == 'SSD5':
        return
    sm8 = sbuf(st, "sm8", [128, 8])
    nacs = sbuf(st, "nacs", [128, 4])
    eacs = sbuf(st, "eacs", [128, 4])
    wdec = sbuf(st, "wdec", [128, 4])
    etot = sbuf(st, "etot", [128, 4])
    A1 = sbuf(st, "A1", [128, 4, 128])
    decT = sbuf(st, "decT", [128, 4, 128])
    scT = sbuf(st, "scTs", [128, 4, 128], BF16)
    xdt = sbuf(st, "xdt", [128, 4, 64], BF16)
    xdtw = sbuf(st, "xdtw", [128, 4, 64], BF16)
    state = sbuf(st, "state", [128, 2, 64])
    stateb = sbuf(st, "stateb", [128, 2, 64], BF16)
    ones3 = sbuf(st, "ones3", [128, 4, 128])
    s.op('pool', lambda e: e.memset(ones3[:], 1.0), writes=[ones3])
    for d in range(2):
        Um = C['U_f'] if d == 0 else C['UT_f']
        Mn = C['mf_f'] if d == 0 else C['mb_f']
        order = list(range(NT)) if d == 0 else [1, 0] + list(range(NT - 1, 1, -1))
        s.op('pool', lambda e: e.memset(state[:], 0.0), writes=[state])
        s.op('pool', lambda e: e.memset(stateb[:], 0.0), writes=[stateb])
        for t in order:
            dta_c = dta[:, t, d * 4:(d + 1) * 4]
            dt_c = dt[:, t, d * 4:(d + 1) * 4]
            pbs = bank()
            s.op('pe', lambda e: e.matmul(pbs[:, 0:4], lhsT=Um[:], rhs=dta_c, start=True, stop=True), reads=[Um, dta], writes=[pbs])
            s.op('pe', lambda e: e.matmul(pbs[:, 4:8], lhsT=C['ones_f'][:], rhs=dta_c, start=True, stop=True), reads=[C['ones_f'], dta], writes=[pbs])
            s.op('act', lambda e: e.copy(out=sm8[:], in_=pbs[:, 0:8]), reads=[pbs], writes=[sm8])
            s.op('dve', lambda e: e.tensor_scalar(out=nacs[:], in0=sm8[:, 0:4], scalar1=-1.0, scalar2=None, op0=ALU.mult), reads=[sm8], writes=[nacs])
            s.op('act', lambda e: e.activation(out=eacs[:], in_=sm8[:, 0:4], func=AF.Exp), reads=[sm8], writes=[eacs])
            s.op('dve', lambda e: e.tensor_tensor(out=wdec[:], in0=sm8[:, 4:8], in1=sm8[:, 0:4], op=ALU.subtract), reads=[sm8], writes=[wdec])
            s.op('act', lambda e: e.activation(out=wdec[:], in_=wdec[:], func=AF.Exp), reads=[wdec], writes=[wdec])
            s.op('act', lambda e: e.activation(out=etot[:], in_=sm8[:, 4:8], func=AF.Exp), reads=[sm8], writes=[etot])
            if _CUT == 1:
                return
            s.op('dve', lambda e: e.tensor_tensor(out=A1[:], in0=ones3[:], in1=dta_c.unsqueeze(2).to_broadcast([128, 4, 128]), op=ALU.mult),
                 reads=[ones3, dta], writes=[A1])
            if _CUT == 2:
                return
            pD = bank()
            for h in range(4):
                s.op('pe', lambda e, h=h: e.matmul(pD[:, h * 128:(h + 1) * 128], lhsT=A1[:, h, :], rhs=Um[:], start=True, stop=False),
                     reads=[A1, Um], writes=[pD])
                s.op('pe', lambda e, h=h: e.matmul(pD[:, h * 128:(h + 1) * 128], lhsT=C['ident_f'][:], rhs=Mn[:], start=False, stop=True),
                     reads=[C['ident_f'], Mn], writes=[pD])
            if _CUT == 3:
                return
            for h in range(4):
                s.op('act', lambda e, h=h: e.activation(out=decT[:, h, :], in_=pD[:, h * 128:(h + 1) * 128], func=AF.Exp, bias=nacs[:, h:h + 1], scale=1.0),
                     reads=[pD, nacs], writes=[(decT, h)])
            if _CUT == 4:
                return
            pGs = [bank(), bank()]
            for g in range(2):
                s.op('pe', lambda e, g=g: e.matmul(pGs[g][:, 0:128], lhsT=xbcT[64 * g:64 * g + 64, 2, tcols(t)],
                                                  rhs=xbcT[64 * g:64 * g + 64, 3, tcols(t)], start=True, stop=True), reads=[xbcT], writes=[pGs[g]])
            for h in range(4):
                g = h // 2
                s.op('dve', lambda e, h=h, g=g: e.tensor_tensor(out=scT[:, h, :], in0=pGs[g][:, 0:128], in1=decT[:, h, :], op=ALU.mult),
                     reads=[pGs[g], (decT, h)], writes=[(scT, h)])
            if _CUT == 5:
                return
            xs4 = xbc[:, t, 0:256].rearrange("p (h d) -> p h d", h=4)
            s.op('dve', lambda e: e.tensor_tensor(out=xdt[:], in0=xs4, in1=dt_c.unsqueeze(2).to_broadcast([128, 4, 64]), op=ALU.mult),
                 reads=[(xbc, t), dt], writes=[xdt])
            s.op('dve', lambda e: e.tensor_tensor(out=xdtw[:], in0=xdt[:], in1=wdec[:].unsqueeze(2).to_broadcast([128, 4, 64]), op=ALU.mult),
                 reads=[xdt, wdec], writes=[xdtw])
            if _CUT == 6:
                return
            pY = bank()
            pOs = [bank(), bank()]
            for h in range(4):
                s.op('pe', lambda e, h=h: e.matmul(pY[:, h * 64:(h + 1) * 64], lhsT=scT[:, h, :], rhs=xdt[:, h, :], start=True, stop=True),
                     reads=[(scT, h), xdt], writes=[pY])
            for h in range(4):
                g, hh = h // 2, h % 2
                s.op('pe', lambda e, h=h, g=g, hh=hh: e.matmul(pOs[g][:, hh * 64:(hh + 1) * 64], lhsT=xbcT[64 * g:64 * g + 64, 3, tcols(t)],
                                                              rhs=stateb[64 * g:64 * g + 64, hh, :], start=True, stop=True),
                     reads=[xbcT, stateb], writes=[pOs[g]])
            if _CUT == 7:
                return
            for h in range(4):
                g, hh = h // 2, h % 2
                s.op('dve', lambda e, h=h, g=g, hh=hh: e.scalar_tensor_tensor(out=yacc[:, t, h * 64:(h + 1) * 64], in0=pOs[g][:, hh * 64:(hh + 1) * 64],
                                                                 scalar=eacs[:, h:h + 1], in1=yacc[:, t, h * 64:(h + 1) * 64], op0=ALU.mult, op1=ALU.add),
                     reads=[pOs[g], eacs, (yacc, t)], writes=[(yacc, t)])
            s.op('dve', lambda e: e.tensor_tensor(out=yacc[:, t, :], in0=yacc[:, t, :], in1=pY[:, 0:256], op=ALU.add), reads=[pY, (yacc, t)], writes=[(yacc, t)])
            if _CUT == 8:
                return
            pS = bank()
            s.op('pe', lambda e: e.matmul(pS[:, 0:256], lhsT=xbc[:, t, 256:384], rhs=xdtw[:].rearrange("p h d -> p (h d)"), start=True, stop=True),
                 reads=[(xbc, t), xdtw], writes=[pS])
            for g in range(2):
                for hh in range(2):
                    h = 2 * g + hh
                    s.op('dve', lambda e, g=g, hh=hh, h=h: e.scalar_tensor_tensor(
                        out=state[64 * g:64 * g + 64, hh, :], in0=state[64 * g:64 * g + 64, hh, :], scalar=etot[64 * g:64 * g + 64, h:h + 1],
                        in1=pS[64 * g:64 * g + 64, h * 64:(h + 1) * 64], op0=ALU.mult, op1=ALU.add), reads=[state, etot, pS], writes=[state])
            s.op('act', lambda e: e.copy(out=stateb[:], in_=state[:]), reads=[state], writes=[stateb])
    if stop == 'SSD6':
        return
    gt = sbuf(st, "gt", [128, 256])
    gj = sbuf(st, "gj", [128, 256])
    gb = sbuf(st, "gb", [128, 256], BF16)
    ssg = sbuf(st, "ssg", [128, 1])
    for t in range(NT):
        s.op('dve', lambda e: e.tensor_tensor(out=gt[:], in0=yacc[:, t, :], in1=zs[:, t, :], op=ALU.mult), reads=[(yacc, t), (zs, t)], writes=[gt])
        s.op('act', lambda e: e.activation(out=gj[:], in_=gt[:], func=AF.Square, accum_out=ssg[:]), reads=[gt], writes=[gj, ssg])
        rstd_from_ss(ssg, ssg, 256)
        s.op('dve', lambda e: e.scalar_tensor_tensor(out=gb[:], in0=gt[:], scalar=ssg[:, 0:1], in1=P['snw'][:], op0=ALU.mult, op1=ALU.mult),
             reads=[gt, ssg, P['snw']], writes=[gb])
        pb = bank()
        pbb = pb.t[:].bitcast(BF16)
        for c in range(2):
            s.op('pe', lambda e, c=c: e.transpose(out=pbb[:, c * 128:(c + 1) * 128], in_=gb[:, c * 128:(c + 1) * 128], identity=C['ident_b'][:]),
                 reads=[gb, C['ident_b']], writes=[pb])
        s.op('act', lambda e: e.copy(out=mixT[:, 0:2, tcols(t)], in_=pbb[:, 0:256].rearrange("p (c n) -> p c n", c=2)), reads=[pb], writes=[(mixT, ('ssd', t))])
import os


def s5_stage(st, nc, s, sbuf, bank, P, D, l, hT, mixT, load_w, PIECES, DB, stop, S5SCR, mkstg, **_):
    P = dict(P)
    P['WB'] = sbuf(st, "WBs", [128, 2, 8, 2, 128], BF16)
    P['WC'] = sbuf(st, "WCs", [128, 2, 8, 2, 128])
    P['cosT'] = sbuf(st, "cosTs", [128, 16, 256])
    P['sinT'] = sbuf(st, "sinTs", [128, 16, 256])
    for nm in ['WB', 'WC', 'cosT', 'sinT']:
        dd, db = S5SCR[(nm, l)]
        dst = P[nm][:].rearrange("p a b c d -> p (a b c d)") if nm in ('WB', 'WC') else P[nm][:].rearrange("p a b -> p (a b)")
        s.dma(dst, dd.ap(), reads=[db], writes=[P[nm]], g=2)
    wu = sbuf(st, "wu", [128, 8, 256], BF16)
    with contextlib.ExitStack() as s0:
        load_w(mkstg(s0), wu, 'w_in', l, 2312, 256, 0)
        s.barrier()
    uT = sbuf(st, "uT", [128, 2, SEQ], BF16)
    yT = sbuf(st, "yT", [128, 2, SEQ])
    for c in range(2):
        for (c0, c1) in PIECES:
            pb = bank()
            n = c1 - c0
            for k in range(8):
                s.op('pe', lambda e, k=k: e.matmul(pb[:, 0:n], lhsT=wu[:, k, c * 128:(c + 1) * 128], rhs=hT[:, k, c0:c1], start=(k == 0), stop=(k == 7)),
                     reads=[hT, wu], writes=[pb])
            s.op('act', lambda e: e.copy(out=uT[:, c, c0:c1], in_=pb[:, 0:n]), reads=[pb], writes=[uT])
            s.op('dve', lambda e: e.tensor_scalar(out=yT[:, c, c0:c1], in0=pb[:, 0:n], scalar1=P['s5d'][:, c:c + 1], scalar2=None, op0=ALU.mult),
                 reads=[pb, P['s5d']], writes=[(yT, c)])
    T = 256
    NCH = SEQ // T
    bs = [sbuf(st, f"s5b{i}", [128, 2, T]) for i in range(2)]
    gin = [sbuf(st, f"s5g{i}", [128, 2, T]) for i in range(2)]
    tt = [sbuf(st, f"s5t{i}", [128, 4, T]) for i in range(2)]
    gs = [sbuf(st, f"s5s{i}", [128, 2, T]) for i in range(2)]
    hh_ = [sbuf(st, f"s5h{i}", [128, 2, T]) for i in range(2)]
    t2 = tt
    ini = [sbuf(st, f"s5c{q}", [128, 2]) for q in range(8)]
    ctmp = [sbuf(st, f"s5ct{i}", [128, 2]) for i in range(2)]
    items = []
    for d in range(2):
        order = list(range(NCH)) if d == 0 else [0] + list(range(NCH - 1, 0, -1))
        for kt in range(2):
            for ci, ch in enumerate(order):
                for ql in range(4):
                    items.append((d, kt, ci, ch, ql, len(order)))
    pYs = {}

    def front(i):
        d, kt, ci, ch, ql, nord = items[i]
        q = kt * 4 + ql
        j = d * 8 + q
        i2 = i % 2
        B_, G_, T_ = bs[i2], gin[i2], tt[i2]
        ucols = uT[:, kt, ch * T:(ch + 1) * T] if d == 0 else uT[:, kt, ch * T:(ch + 1) * T][:, ::-1]
        cosj = P['cosT'][:, j, :]
        sinj = P['sinT'][:, j, :]
        pB = bank(2 + (i % 6))
        for ri in range(2):
            s.op('pe', lambda e, ri=ri: e.matmul(pB[:, ri * T:(ri + 1) * T], lhsT=P['WB'][:, d, q, ri, :], rhs=ucols, start=True, stop=True),
                 reads=[uT, P['WB']], writes=[pB])
        s.op('act', lambda e: e.copy(out=B_[:].rearrange("p a t -> p (a t)"), in_=pB[:, 0:2 * T]), reads=[pB], writes=[B_])
        TTp = lambda o, a, b_, op, rd, wr: s.op('pool', lambda e: e.tensor_tensor(out=o, in0=a, in1=b_, op=op), reads=rd, writes=wr)
        TTp(T_[:, 0, :], B_[:, 0, :], cosj, ALU.mult, [B_, P['cosT']], [(T_, 0)])
        TTp(T_[:, 1, :], B_[:, 1, :], sinj, ALU.mult, [B_, P['sinT']], [(T_, 1)])
        TTp(T_[:, 2, :], B_[:, 1, :], cosj, ALU.mult, [B_, P['cosT']], [(T_, 2)])
        TTp(T_[:, 3, :], B_[:, 0, :], sinj, ALU.mult, [B_, P['sinT']], [(T_, 3)])
        TTp(G_[:, 0, :], T_[:, 0, :], T_[:, 1, :], ALU.add, [(T_, 0), (T_, 1)], [(G_, 0)])
        TTp(G_[:, 1, :], T_[:, 2, :], T_[:, 3, :], ALU.subtract, [(T_, 2), (T_, 3)], [(G_, 1)])

    def back(i):
        d, kt, ci, ch, ql, nord = items[i]
        q = kt * 4 + ql
        j = d * 8 + q
        i2 = i % 2
        G_, S_, H_, U_ = gin[i2], gs[i2], hh_[i2], t2[i2]
        cosj = P['cosT'][:, j, :]
        sinj = P['sinT'][:, j, :]
        if ql == 0:
            pYs['cur'] = bank(len(pYs.setdefault('n', [])) % 2)
            pYs['n'].append(0)
        pY = pYs['cur']
        rb = P['s5r'][:, j:j + 1].to_broadcast([128, T])
        for ri in range(2):
            init = ini[q][:, ri:ri + 1] if ci > 0 else 0.0
            s.op('dve', lambda e, ri=ri, init=init: e.tensor_tensor_scan(out=S_[:, ri, :], data0=rb, data1=G_[:, ri, :], initial=init, op0=ALU.mult, op1=ALU.add),
                 reads=[(G_, ri), P['s5r'], ini[q]], writes=[(S_, ri)])
        TTd = lambda o, a, b_, op, rd, wr: s.op('dve', lambda e: e.tensor_tensor(out=o, in0=a, in1=b_, op=op), reads=rd, writes=wr)
        TTd(U_[:, 0, :], S_[:, 0, :], cosj, ALU.mult, [(S_, 0), P['cosT']], [(U_, 0)])
        TTd(U_[:, 1, :], S_[:, 1, :], sinj, ALU.mult, [(S_, 1), P['sinT']], [(U_, 1)])
        TTd(U_[:, 2, :], S_[:, 0, :], sinj, ALU.mult, [(S_, 0), P['sinT']], [(U_, 2)])
        TTd(U_[:, 3, :], S_[:, 1, :], cosj, ALU.mult, [(S_, 1), P['cosT']], [(U_, 3)])
        TTd(H_[:, 0, :], U_[:, 0, :], U_[:, 1, :], ALU.subtract, [(U_, 0), (U_, 1)], [(H_, 0)])
        TTd(H_[:, 1, :], U_[:, 2, :], U_[:, 3, :], ALU.add, [(U_, 2), (U_, 3)], [(H_, 1)])
        if ci < nord - 1:
            c1 = P['cosT'][:, j, 1:2]
            s1_ = P['sinT'][:, j, 1:2]
            ct = ctmp[i2]
            hre_l = H_[:, 0, T - 1:T]
            him_l = H_[:, 1, T - 1:T]
            s.op('dve', lambda e: e.tensor_scalar(out=ct[:, 0:1], in0=him_l, scalar1=s1_, scalar2=None, op0=ALU.mult), reads=[(H_, 1), P['sinT']], writes=[ct])
            s.op('dve', lambda e: e.tensor_scalar(out=ct[:, 1:2], in0=hre_l, scalar1=s1_, scalar2=None, op0=ALU.mult), reads=[(H_, 0), P['sinT']], writes=[ct])
            s.op('dve', lambda e: e.scalar_tensor_tensor(out=ini[q][:, 0:1], in0=hre_l, scalar=c1, in1=ct[:, 0:1], op0=ALU.mult, op1=ALU.subtract),
                 reads=[(H_, 0), ct, P['cosT']], writes=[ini[q]])
            s.op('dve', lambda e: e.scalar_tensor_tensor(out=ini[q][:, 1:2], in0=him_l, scalar=c1, in1=ct[:, 1:2], op0=ALU.mult, op1=ALU.add),
                 reads=[(H_, 1), ct, P['cosT']], writes=[ini[q]])
        for ri in range(2):
            s.op('pe', lambda e, ri=ri: e.matmul(pY[:, 0:T], lhsT=P['WC'][:, d, q, ri, :], rhs=H_[:, ri, :],
                                                start=(ql == 0 and ri == 0), stop=(ql == 3 and ri == 1)),
                 reads=[(H_, ri), P['WC']], writes=[pY])
        if ql == 3:
            ycols = yT[:, kt, ch * T:(ch + 1) * T] if d == 0 else yT[:, kt, ch * T:(ch + 1) * T][:, ::-1]
            s.op('dve', lambda e: e.tensor_tensor(out=ycols, in0=ycols, in1=pY[:, 0:T], op=ALU.add), reads=[pY, (yT, kt)], writes=[(yT, kt)])

    front(0)
    for i in range(len(items)):
        if i + 1 < len(items):
            front(i + 1)
        back(i)
    if 's5_cos' in DB:
        s.dma(DB['s5_cos'].ap(), P['cosT'][:].rearrange("p a b -> p (a b)"), reads=[P['cosT']], g=5)
        s.dma(DB['s5_sin'].ap(), P['sinT'][:].rearrange("p a b -> p (a b)"), reads=[P['sinT']], g=5)
        s.dma(DB['s5_wc'].ap(), P['WC'][:].rearrange("p a b c d -> p (a b c d)"), reads=[P['WC']], g=5)
    if 's5_y' in DB:
        s.dma(DB['s5_y'].ap(), yT[:].rearrange("p a b -> p (a b)"), reads=[yT], g=5)
        s.dma(DB['s5_u'].ap(), uT[:].rearrange("p a b -> p (a b)"), reads=[uT], q='pool', g=5)
    ygb = uT
    fi = 0
    for c in range(2):
        for (g0, g1) in [(0, 1024), (1024, 2048), (2048, 2304)]:
            tb = tt[fi % 2]
            fi += 1
            n = g1 - g0
            tmp = tb[:].rearrange("p a t -> p (a t)")[:, 0:n]
            yv = yT[:, c, g0:g1]
            s.op('pool', lambda e: e.tensor_tensor(out=tmp, in0=yv, in1=yv, op=ALU.mult), reads=[(yT, c)], writes=[tb])
            s.op('pool', lambda e: e.tensor_scalar(out=tmp, in0=tmp, scalar1=0.044715, scalar2=1.0, op0=ALU.mult, op1=ALU.add), reads=[tb], writes=[tb])
            s.op('dve', lambda e: e.tensor_tensor(out=tmp, in0=tmp, in1=yv, op=ALU.mult), reads=[tb, (yT, c)], writes=[tb])
            s.op('act', lambda e: e.activation(out=tmp, in_=tmp, func=AF.Sigmoid, scale=1.5957691216057308), reads=[tb], writes=[tb])
            s.op('dve', lambda e: e.tensor_tensor(out=yv, in0=yv, in1=tmp, op=ALU.mult), reads=[tb, (yT, c)], writes=[(yT, c)])
            s.op('act', lambda e: e.copy(out=ygb[:, c, g0:g1], in_=yv), reads=[(yT, c)], writes=[ygb])
    sg = sbuf(st, "s5sg", [128, 512])
    for c in range(2):
        for (c0, c1) in PIECES:
            pb = bank()
            n = c1 - c0
            for k in range(2):
                s.op('pe', lambda e, k=k: e.matmul(pb[:, 0:n], lhsT=P['gluw'][:, k, c * 128:(c + 1) * 128], rhs=ygb[:, k, c0:c1], start=(k == 0), stop=(k == 1)),
                     reads=[ygb, P['gluw']], writes=[pb])
            s.op('act', lambda e: e.activation(out=sg[:, 0:n], in_=pb[:, 0:n], func=AF.Sigmoid, bias=P['glub'][:, c:c + 1], scale=1.0),
                 reads=[pb, P['glub']], writes=[sg])
            s.op('dve', lambda e: e.tensor_tensor(out=mixT[:, 6 + c, c0:c1], in0=yT[:, c, c0:c1], in1=sg[:, 0:n], op=ALU.mult),
                 reads=[sg, (yT, c)], writes=[(mixT, ('s5', c, c0))])


def attn_stage(st, nc, s, sbuf, bank, P, D, l, hT, mixT, load_w, rstd_from_ss, tcols, need_ctx, C, DB, stop, mkstg, **_):
    wqk = sbuf(st, "wqk", [128, 8, 1024], BF16)
    wv = sbuf(st, "wv", [128, 8, 512], BF16)
    with contextlib.ExitStack() as s0:
        stg = mkstg(s0)
        load_w(stg, wqk, 'w_in', l, 776, 1024, 0)
        load_w(stg, wv, 'w_in', l, 1800, 512, 0)
        s.barrier()
    _AC = int(os.environ.get('ACUT', '0'))
    if _AC == 1:
        return
    QT = sbuf(st, "QT", [128, 4, SEQ], BF16)
    KT = sbuf(st, "KT", [128, 4, SEQ], BF16)
    Vt = sbuf(st, "Vt", [128, NT, 4, 129], BF16)
    s.op('pool', lambda e: e.memset(Vt[:], 1.0), writes=[Vt])
    qkf = sbuf(st, "qkf", [128, 1024])
    qkr = sbuf(st, "qkr", [128, 1024], BF16)
    rt = sbuf(st, "ropet", [128, 4, 16, 2, 16])
    for t in range(NT):
        pq, pk, pv = bank(), bank(), bank()
        for (pb, w, c0) in [(pq, wqk, 0), (pk, wqk, 512), (pv, wv, 0)]:
            for k in range(8):
                s.op('pe', lambda e, k=k, pb=pb, w=w, c0=c0: e.matmul(pb[:, :], lhsT=hT[:, k, tcols(t)], rhs=w[:, k, c0:c0 + 512], start=(k == 0), stop=(k == 7)),
                     reads=[hT, w], writes=[pb])
        s.op('act', lambda e: e.copy(out=Vt[:, t, :, 0:128], in_=pv[:, :].rearrange("p (h e) -> p h e", h=4)), reads=[pv], writes=[(Vt, t)])
        if t < 2:
            s.op('act', lambda e: e.copy(out=qkr[:, 0:512], in_=pq[:, :]), reads=[pq], writes=[qkr])
            s.op('act', lambda e: e.copy(out=qkr[:, 512:1024], in_=pk[:, :]), reads=[pk], writes=[qkr])
        else:
            s.op('act', lambda e: e.copy(out=qkf[:, 0:512], in_=pq[:, :]), reads=[pq], writes=[qkf])
            s.op('act', lambda e: e.copy(out=qkf[:, 512:1024], in_=pk[:, :]), reads=[pk], writes=[qkf])
            tl = t - 2
            xv = qkf[:].rearrange("p (m a b j) -> p m a b j", m=16, a=2, b=2)
            ov = qkr[:].rearrange("p (m a b j) -> p m a b j", m=16, a=2, b=2)
            cosb = C['ropeC'][:, tl, :].rearrange("p (a j) -> p a j", a=2).unsqueeze(1).to_broadcast([128, 16, 2, 16])
            sinb = C['ropeS'][:, tl, :].rearrange("p (a j) -> p a j", a=2).unsqueeze(1).to_broadcast([128, 16, 2, 16])
            PO = lambda o, a, b_, op, rd, wr: s.op('pool', lambda e: e.tensor_tensor(out=o, in0=a, in1=b_, op=op), reads=rd, writes=wr)
            PO(rt[:, 0], xv[:, :, :, 0, :], cosb, ALU.mult, [qkf, C['ropeC']], [(rt, 0)])
            PO(rt[:, 1], xv[:, :, :, 1, :], sinb, ALU.mult, [qkf, C['ropeS']], [(rt, 1)])
            PO(rt[:, 2], xv[:, :, :, 1, :], cosb, ALU.mult, [qkf, C['ropeC']], [(rt, 2)])
            PO(rt[:, 3], xv[:, :, :, 0, :], sinb, ALU.mult, [qkf, C['ropeS']], [(rt, 3)])
            PO(ov[:, :, :, 0, :], rt[:, 0], rt[:, 1], ALU.subtract, [(rt, 0), (rt, 1)], [qkr])
            PO(ov[:, :, :, 1, :], rt[:, 2], rt[:, 3], ALU.add, [(rt, 2), (rt, 3)], [qkr])
        pb = bank()
        pbb = pb.t[:].bitcast(BF16)
        for m in range(8):
            s.op('pe', lambda e, m=m: e.transpose(out=pbb[:, m * 128:(m + 1) * 128], in_=qkr[:, m * 128:(m + 1) * 128], identity=C['ident_b'][:]),
                 reads=[qkr, C['ident_b']], writes=[pb])
        s.op('act', lambda e: e.copy(out=QT[:, :, tcols(t)], in_=pbb[:, 0:512].rearrange("p (h n) -> p h n", h=4)), reads=[pb], writes=[(QT, t)])
        s.op('dve', lambda e: e.tensor_copy(out=KT[:, :, tcols(t)], in_=pbb[:, 512:1024].rearrange("p (h n) -> p h n", h=4)), reads=[pb], writes=[(KT, t)])
    if _AC == 2:
        return
    ET = sbuf(st, "ET", [128, NT, 512], BF16)
    oc = sbuf(st, "oc", [128, 2, 4, 129])
    rr = sbuf(st, "att_rr", [128, 2, 4])
    o_ = sbuf(st, "att_o", [128, 128])
    oj = sbuf(st, "att_oj", [128, 128])
    ob = sbuf(st, "att_ob", [128, 128], BF16)
    sso = sbuf(st, "att_ss", [128, 1])
    groups = [(256 + 512 * i, 512, NT) for i in range(4)]
    if need_ctx:
        groups = [(0, 256, 2)] + groups
    for h in range(4):
        if _AC == 3 and h == 1:
            return
        for (q0, nq, nk) in groups:
            nsub = nq // 128
            for c in range(2):
                for kt in range(nk):
                    pS = bank()
                    s.op('pe', lambda e, kt=kt: e.matmul(pS[:, 0:nq], lhsT=KT[64 * c:64 * c + 64, h, tcols(kt)], rhs=QT[64 * c:64 * c + 64, h, q0:q0 + nq],
                                                        start=True, stop=True), reads=[KT, QT], writes=[pS])
                    s.op('act', lambda e, kt=kt: e.activation(out=ET[:, kt, 0:nq], in_=pS[:, 0:nq], func=AF.Exp, scale=0.125), reads=[pS], writes=[(ET, kt)])
                for qs in range(nsub):
                    pO = bank()
                    for kt in range(nk):
                        s.op('pe', lambda e, kt=kt: e.matmul(pO[:, 0:129], lhsT=ET[:, kt, qs * 128:(qs + 1) * 128], rhs=Vt[:, kt, h, :], start=(kt == 0), stop=(kt == nk - 1)),
                             reads=[(ET, kt), Vt], writes=[pO])
                    s.op('dve', lambda e: e.tensor_copy(out=oc[:, c, qs, :], in_=pO[:, 0:129]), reads=[pO], writes=[(oc, (c, qs))])
            s.op('dve', lambda e: e.reciprocal(out=rr[:, :, 0:nsub], in_=oc[:, :, 0:nsub, 128]), reads=[oc], writes=[rr])
            s.op('dve', lambda e: e.tensor_scalar(out=rr[:, 1, :], in0=rr[:, 1, :], scalar1=P['nlam'][:, 0:1], scalar2=None, op0=ALU.mult), reads=[rr, P['nlam']], writes=[rr])
            for qs in range(nsub):
                s.op('dve', lambda e: e.tensor_scalar(out=o_[:], in0=oc[:, 0, qs, 0:128], scalar1=rr[:, 0, qs:qs + 1], scalar2=None, op0=ALU.mult), reads=[oc, rr], writes=[o_])
                s.op('dve', lambda e: e.scalar_tensor_tensor(out=o_[:], in0=oc[:, 1, qs, 0:128], scalar=rr[:, 1, qs:qs + 1], in1=o_[:], op0=ALU.mult, op1=ALU.add),
                     reads=[oc, rr, o_], writes=[o_])
                s.op('act', lambda e: e.activation(out=oj[:], in_=o_[:], func=AF.Square, accum_out=sso[:]), reads=[o_], writes=[oj, sso])
                rstd_from_ss(sso, sso, 128)
                s.op('dve', lambda e: e.scalar_tensor_tensor(out=ob[:], in0=o_[:], scalar=sso[:, 0:1], in1=P['subw'][:], op0=ALU.mult, op1=ALU.mult),
                     reads=[o_, sso, P['subw']], writes=[ob])
                pb = bank()
                pbb = pb.t[:].bitcast(BF16)
                s.op('pe', lambda e: e.transpose(out=pbb[:, 0:128], in_=ob[:], identity=C['ident_b'][:]), reads=[ob, C['ident_b']], writes=[pb])
                qc = q0 + qs * 128
                s.op('act', lambda e: e.copy(out=mixT[:, 2 + h, qc:qc + 128], in_=pbb[:, 0:128]), reads=[pb], writes=[(mixT, ('att', h, qc))])


NWB = int(os.environ.get('NWB', '2'))


def ffn_stage(st, nc, s, sbuf, bank, P, D, l, b, NB, NCOL, hT, need_ctx, last, xs_d, XS, out_d, OUT, gsc_d, GSC, load_w, rstd_from_ss, tcols, DB, stop, mkstg, load_cast, **_):
    stg = mkstg(st, 4)
    wd = sbuf(st, "wd", [128, 22, 1024], BF16)
    wdsrc = D['ffn_w_down'].ap()[l].rearrange("(f p) c -> p f c", p=128)
    for f in range(22):
        load_cast(stg, wd[:, f, :], wd, wdsrc[:, f, :], [1024], key=f)
    gbc = sbuf(st, "gbcF", [128, 2, 1024])
    s.dma(gbc[:, 0, :], bass.AP(gsc_d, ((l * 2 + 1) * NCOL + b) * 1024, [[0, 128], [1, 1024]]), reads=[GSC], writes=[gbc], g=0)
    s.dma(gbc[:, 1, :], bass.AP(gsc_d, ((l * 2 + 1) * NCOL + NB) * 1024, [[0, 128], [1, 1024]]), reads=[GSC], writes=[gbc], g=0)
    actT = sbuf(st, "actT", [128, 22, 1152], BF16)
    wg = [sbuf(st, f"wg{i}", [128, 8, 128], BF16) for i in range(NWB)]
    wu = [sbuf(st, f"wu{i}", [128, 8, 128], BF16) for i in range(NWB)]
    graw = sbuf(st, "graw", [128, 1160])
    gc = sbuf(st, "gc", [128, 1160])
    ub = sbuf(st, "ub", [128, 1152], BF16)
    xt2 = [sbuf(st, f"xtF{i}", [128, 1024]) for i in range(2)]
    junk = sbuf(st, "junkF", [128, 1024], BF16)
    ss2 = sbuf(st, "ssF", [128, 2])
    rs2 = sbuf(st, "rsF", [128, 1])
    ytmp = sbuf(st, "ytmpF", [128, 1024])
    s.op('pool', lambda e: e.memset(graw[:], 0.0), writes=[graw])
    if stop == 'F0':
        return
    halves = []
    if need_ctx:
        halves.append(dict(gp=[(0, 256, 1), (256, 768, 259), (768, 1153, 771)], up=[(0, 256, 0), (256, 768, 256), (768, 1152, 768)],
                           outs=[(0, 256, 1), (256, 1152, 259)], tiles=list(range(0, 9)), base=0, pads=[0, 257, 258]))
    else:
        halves.append(dict(gp=[(256, 768, 259), (768, 1153, 771)], up=[(256, 768, 256), (768, 1152, 768)],
                           outs=[(256, 1152, 259)], tiles=list(range(2, 9)), base=0, pads=[0, 257, 258]))
    halves.append(dict(gp=[(1151, 1663, 0), (1663, 2175, 512), (2175, 2304, 1024)], up=[(1152, 1664, 0), (1664, 2176, 512), (2176, 2304, 1024)],
                       outs=[(0, 1152, 1)], tiles=list(range(9, 18)), base=1152, pads=[1153]))
    for hv in halves:
        for pcol in hv['pads']:
            s.op('pool', lambda e, pcol=pcol: e.memset(graw[:, pcol:pcol + 1], 0.0), writes=[graw])
        for f in range(int(os.environ.get('FSTART', '0')), int(os.environ.get('FCUT', '22'))):
            g_, u_ = wg[f % NWB], wu[f % NWB]
            load_cast(stg, g_[:], g_, D['ffn_w_gate'].ap()[l].rearrange("(k p) c -> p k c", p=128)[:, :, f * 128:(f + 1) * 128], [8, 128])
            load_cast(stg, u_[:], u_, D['ffn_w_up'].ap()[l].rearrange("(k p) c -> p k c", p=128)[:, :, f * 128:(f + 1) * 128], [8, 128])
            _FS = int(os.environ.get('FSTEP', '0')); _lastf = f == int(os.environ.get('FCUT', '22')) - 1
            if _lastf and _FS == 1:
                return
            for (c0, c1, dst) in hv['gp']:
                pb = bank()
                n = c1 - c0
                for k in range(8):
                    s.op('pe', lambda e, k=k: e.matmul(pb[:, 0:n], lhsT=g_[:, k, :], rhs=hT[:, k, c0:c1], start=(k == 0), stop=(k == 7)), reads=[hT, g_], writes=[pb])
                s.op('act', lambda e: e.copy(out=graw[:, dst:dst + n], in_=pb[:, 0:n]), reads=[pb], writes=[graw])
            if _lastf and _FS == 2:
                return
            for (c0, c1, dst) in hv['up']:
                pb = bank()
                n = c1 - c0
                for k in range(8):
                    s.op('pe', lambda e, k=k: e.matmul(pb[:, 0:n], lhsT=u_[:, k, :], rhs=hT[:, k, c0:c1], start=(k == 0), stop=(k == 7)), reads=[hT, u_], writes=[pb])
                s.op('dve', lambda e: e.tensor_copy(out=ub[:, dst:dst + n], in_=pb[:, 0:n]), reads=[pb], writes=[ub])
            if _lastf and _FS == 3:
                return
            s.op('act', lambda e: e.activation(out=gc[:, 1:1157], in_=graw[:, 1:1157], func=AF.Identity, scale=P['fcw'][:, f, 1:2], bias=P['fcb'][:, f:f + 1]),
                 reads=[graw, P['fcw'], P['fcb']], writes=[gc])
            for j in (0, 2):
                s.op('pool', lambda e, j=j: e.scalar_tensor_tensor(out=gc[:, 1:1157], in0=graw[:, j:j + 1156], scalar=P['fcw'][:, f, j:j + 1], in1=gc[:, 1:1157],
                                                                  op0=ALU.mult, op1=ALU.add), reads=[graw, gc, P['fcw']], writes=[gc]) if False else \
                    s.op('dve', lambda e, j=j: e.scalar_tensor_tensor(out=gc[:, 1:1157], in0=graw[:, j:j + 1156], scalar=P['fcw'][:, f, j:j + 1], in1=gc[:, 1:1157],
                                                                     op0=ALU.mult, op1=ALU.add), reads=[graw, gc, P['fcw']], writes=[gc])
            s.op('act', lambda e: e.activation(out=gc[:, 1:1157], in_=gc[:, 1:1157], func=AF.Silu), reads=[gc], writes=[gc])
            for (a0, a1, src) in hv['outs']:
                n = a1 - a0
                s.op('dve', lambda e: e.tensor_tensor(out=actT[:, f, a0:a1], in0=gc[:, src:src + n], in1=ub[:, a0:a1], op=ALU.mult),
                     reads=[gc, ub], writes=[(actT, f)])
            if stop == 'F1':
                return
        if stop == 'F2':
            return
        for t in hv['tiles']:
            xt = xt2[t % 2]
            s.dma(xt[:], xs_d.ap()[b, tcols(t), :], reads=[(XS, b)], writes=[xt], g=0)
            tc = t * 128 - hv['base']
            pbs = [bank(), bank()]
            for hf in range(2):
                for f in range(22):
                    s.op('pe', lambda e, f=f, hf=hf: e.matmul(pbs[hf][:, :], lhsT=actT[:, f, tc:tc + 128], rhs=wd[:, f, hf * 512:(hf + 1) * 512],
                                                             start=(f == 0), stop=(f == 21)), reads=[actT, wd], writes=[pbs[hf]])
                s.op('act', lambda e, hf=hf: e.activation(out=junk[:, hf * 512:(hf + 1) * 512], in_=pbs[hf][:, :], func=AF.Square, accum_out=ss2[:, hf:hf + 1]),
                     reads=[pbs[hf]], writes=[junk, ss2])
            s.op('dve', lambda e: e.tensor_tensor(out=rs2[:], in0=ss2[:, 0:1], in1=ss2[:, 1:2], op=ALU.add), reads=[ss2], writes=[rs2])
            rstd_from_ss(rs2, rs2, 1024)
            gi = 1 if t < 2 else 0
            for hf in range(2):
                s.op('dve', lambda e, hf=hf: e.scalar_tensor_tensor(out=ytmp[:, hf * 512:(hf + 1) * 512], in0=pbs[hf][:, :], scalar=rs2[:, 0:1],
                                                                   in1=gbc[:, gi, hf * 512:(hf + 1) * 512], op0=ALU.mult, op1=ALU.mult),
                     reads=[pbs[hf], rs2, gbc], writes=[ytmp])
            s.op('pool', lambda e: e.tensor_tensor(out=xt[:], in0=xt[:], in1=ytmp[:], op=ALU.add), reads=[xt, ytmp], writes=[xt])
            if last:
                if t >= 2:
                    s.dma(out_d.ap()[b, (t - 2) * 128:(t - 1) * 128, :], xt[:], reads=[xt], writes=[OUT], g=1)
            else:
                s.dma(xs_d.ap()[b, tcols(t), :], xt[:], reads=[xt], writes=[(XS, b)], g=1)


def build(NB=4, LAYERS=(0, 1), dbg=(), stop=None):
    nc = bass.Bass("TRN2", target_bir_lowering=False)
    D = {}

    def din(name, shape):
        D[name] = nc.dram_tensor(name, list(shape), F32, kind="ExternalInput")
        return D[name]

    x_d = din("x", [NB, 2048, 1024])
    ctx_d = din("ctx", [NB, 256, 1024])
    cc_d = din("cc", [NB + 1, 1024])
    for n, sh in PARAMS:
        din(n, sh)
    for n, a in host_consts().items():
        din(n, a.shape)
    out_d = nc.dram_tensor("out", [NB, 2048, 1024], F32, kind="ExternalOutput")
    xs_d = nc.dram_tensor("xs", [NB, SEQ, 1024], F32)
    gsc_d = nc.dram_tensor("gsc", [2, 2, NB + 1, 1024], F32)
    DB = {}
    for n, sh in dbg:
        DB[n] = nc.dram_tensor("dbg_" + n, list(sh), F32, kind="ExternalOutput")

    NCOL = NB + 1
    with contextlib.ExitStack() as top:
        s = Sched(nc, top)
        top.enter_context(nc.allow_non_contiguous_dma(reason="small param layout loads"))
        top.enter_context(nc.allow_low_precision(reason="bf16 matmul operands"))
        XS = Buf(xs_d, "xs")
        GSC = Buf(gsc_d, "gsc")
        OUT = Buf(out_d, "out")
        DIN = Buf(None, "din")

        uid = [0]

        def sbuf(st, name, shape, dt=F32):
            uid[0] += 1
            name = f"{name}_u{uid[0]}"
            return Buf(st.enter_context(nc.sbuf_tensor(name, list(shape), dt)), name)

        banks = [Buf(top.enter_context(nc.psum_tensor(f"pb{i}", [128, 512], F32)), f"pb{i}", psum=True) for i in range(8)]
        bank_i = [0]

        def bank(idx=None):
            if idx is not None:
                return banks[idx]
            b = banks[bank_i[0] % 8]
            bank_i[0] += 1
            return b

        def dmp(name, ap, buf, key=None):
            if name in DB:
                s.dma(DB[name].ap() if not isinstance(name, tuple) else None, ap, reads=[(buf, key)], g=5)

        ident_f = sbuf(top, "ident_f", [128, 128])
        ident_b = sbuf(top, "ident_b", [128, 128], BF16)
        U_f = sbuf(top, "U_f", [128, 128])
        UT_f = sbuf(top, "UT_f", [128, 128])
        mf_f = sbuf(top, "mf_f", [128, 128])
        mb_f = sbuf(top, "mb_f", [128, 128])
        ones_f = sbuf(top, "ones_f", [128, 128])
        ropeC = sbuf(top, "ropeC", [128, 16, 32])
        ropeS = sbuf(top, "ropeS", [128, 16, 32])
        for bufc, nm in [(ident_f, 'c_ident'), (U_f, 'c_U'), (UT_f, 'c_UT'), (mf_f, 'c_mf'), (mb_f, 'c_mb'), (ones_f, 'c_ones')]:
            s.dma(bufc[:], D[nm].ap(), writes=[bufc], g=0)
        s.dma(ropeC[:], D['c_cos'].ap().rearrange("(t p) j -> p t j", p=128), writes=[ropeC], g=0)
        s.dma(ropeS[:], D['c_sin'].ap().rearrange("(t p) j -> p t j", p=128), writes=[ropeS], g=0)
        s.op('act', lambda e: e.copy(out=ident_b[:], in_=ident_f[:]), reads=[ident_f], writes=[ident_b])

        LP = {}
        for l in LAYERS:
            P = {}
            P['s1a'] = sbuf(top, f"s1a{l}", [128, 8, NCOL])
            P['sha'] = sbuf(top, f"sha{l}", [128, 8, NCOL])
            P['s1f'] = sbuf(top, f"s1f{l}", [128, 8, NCOL])
            P['shf'] = sbuf(top, f"shf{l}", [128, 8, NCOL])
            P['dtb'] = sbuf(top, f"dtb{l}", [128, 8])
            P['aneg'] = sbuf(top, f"aneg{l}", [128, 8])
            P['dsk'] = sbuf(top, f"dsk{l}", [128, 4])
            P['snw'] = sbuf(top, f"snw{l}", [128, 256])
            P['scw'] = sbuf(top, f"scw{l}", [128, 4, 5])
            P['scb'] = sbuf(top, f"scb{l}", [128, 4])
            P['subw'] = sbuf(top, f"subw{l}", [128, 128])
            P['nlam'] = sbuf(top, f"nlam{l}", [128, 1])
            P['fcw'] = sbuf(top, f"fcw{l}", [128, 22, 3])
            P['fcb'] = sbuf(top, f"fcb{l}", [128, 22])
            P['s5d'] = sbuf(top, f"s5d{l}", [128, 2])
            P['glub'] = sbuf(top, f"glub{l}", [128, 2])
            P['gluw'] = sbuf(top, f"gluw{l}", [128, 2, 256], BF16)
            P['s5r'] = sbuf(top, f"s5r{l}", [128, 16])
            P['s5ar'] = sbuf(top, f"s5ar{l}", [128, 16])
            P['s5ai'] = sbuf(top, f"s5ai{l}", [128, 16])
            LP[l] = P

        S5SCR = {}
        LP['_negpi'] = sbuf(top, "negpi", [128, 1])
        s.op('pool', lambda e: e.memset(LP['_negpi'][:], -math.pi), writes=[LP['_negpi']])
        LP['_Z'] = sbuf(top, "Zt", [128, 8, 128])
        s.op('pool', lambda e: e.memset(LP['_Z'][:], 0.0), writes=[LP['_Z']])
        with contextlib.ExitStack() as st:
            scT = sbuf(st, "scT", [128, 8, NCOL])
            for col in range(NCOL):
                s.dma(scT[:, :, col], bass.AP(cc_d, col * 1024, [[1, 128], [128, 8]]), writes=[scT], g=0)
            s.op('act', lambda e: e.activation(out=scT[:], in_=scT[:], func=AF.Silu), reads=[scT], writes=[scT])
            wm = [sbuf(st, f"wm{i}", [128, 8, 512]) for i in range(2)]
            modT = sbuf(st, "modT", [128, 48, NCOL])
            modb = sbuf(st, "modb", [128, 48])
            nrm = sbuf(st, "nrm", [128, 4, 8])
            tmpg = sbuf(st, "tmpg", [128, 2, 8, NCOL])
            iota = sbuf(st, "iota", [128, 256])
            s.dma(iota[:], D['c_iota'].ap(), writes=[iota], g=0)
            S5D = {}
            for l in LAYERS:
                P = LP[l]
                P['WB'] = sbuf(st, f"WB{l}", [128, 2, 8, 2, 128], BF16)
                P['WC'] = sbuf(st, f"WC{l}", [128, 2, 8, 2, 128])
                P['cosT'] = sbuf(st, f"cosT{l}", [128, 16, 256])
                P['sinT'] = sbuf(st, f"sinT{l}", [128, 16, 256])
                s.dma(modb[:], bass.AP(D['mod_b'], l * 6144, [[1, 128], [128, 48]]), writes=[modb], g=0)
                for i, nm in enumerate(['mix_norm_pre', 'mix_norm_post', 'ffn_norm_pre', 'ffn_norm_post']):
                    s.dma(nrm[:, i, :], bass.AP(D[nm], l * 1024, [[1, 128], [128, 8]]), writes=[nrm], g=0)
                for cch in range(12):
                    w = wm[cch % 2]
                    s.dma(w[:], D['mod_w'].ap()[l].rearrange("(k p) c -> p k c", p=128)[:, :, cch * 512:(cch + 1) * 512],
                          writes=[w], g=1)
                    for jj in range(4):
                        j = cch * 4 + jj
                        pb = bank()
                        for k in range(8):
                            s.op('pe', lambda e, k=k, jj=jj, w=w, pb=pb: e.matmul(pb[:, 0:NCOL], lhsT=w[:, k, jj * 128:(jj + 1) * 128],
                                                                                 rhs=scT[:, k, :], start=(k == 0), stop=(k == 7)),
                                 reads=[w, scT], writes=[pb])
                        s.op('act', lambda e, j=j, pb=pb: e.activation(out=modT[:, j, :], in_=pb[:, 0:NCOL], func=AF.Identity,
                                                                      bias=modb[:, j:j + 1], scale=1.0),
                             reads=[pb, modb], writes=[modT])
                for (s1, sh, o_sh, o_sc, o_g, ipre, ipost, wi) in [(P['s1a'], P['sha'], 0, 8, 16, 0, 1, 0),
                                                                    (P['s1f'], P['shf'], 24, 32, 40, 2, 3, 1)]:
                    s.op('dve', lambda e, s1=s1, o_sc=o_sc: e.tensor_scalar(out=s1[:], in0=modT[:, o_sc:o_sc + 8, :], scalar1=1.0,
                                                                           scalar2=None, op0=ALU.add),
                         reads=[modT], writes=[s1])
                    s.op('dve', lambda e, s1=s1, ipre=ipre: e.tensor_tensor(out=s1[:], in0=s1[:],
                                                                           in1=nrm[:, ipre, :].unsqueeze(2).to_broadcast([128, 8, NCOL]),
                                                                           op=ALU.mult),
                         reads=[s1, nrm], writes=[s1])
                    s.op('dve', lambda e, sh=sh, o_sh=o_sh: e.tensor_copy(out=sh[:], in_=modT[:, o_sh:o_sh + 8, :]),
                         reads=[modT], writes=[sh])
                    s.op('dve', lambda e, wi=wi, o_g=o_g, ipost=ipost: e.tensor_tensor(
                        out=tmpg[:, wi], in0=modT[:, o_g:o_g + 8, :],
                        in1=nrm[:, ipost, :].unsqueeze(2).to_broadcast([128, 8, NCOL]), op=ALU.mult),
                        reads=[modT, nrm], writes=[tmpg])
                    for col in range(NCOL):
                        s.dma(bass.AP(gsc_d, ((l * 2 + wi) * NCOL + col) * 1024, [[1, 128], [128, 8]]), tmpg[:, wi, :, col],
                              reads=[tmpg], writes=[GSC], g=2)

                def bc(name, off, n):
                    return bass.AP(D[name], off, [[0, 128], [1, n]])
                s.dma(P['dtb'][:], bc('ssd_dt_bias', l * 8, 8), writes=[P['dtb']], g=0)
                s.dma(P['aneg'][:], bc('ssd_a_log', l * 8, 8), writes=[P['aneg']], g=0)
                s.op('act', lambda e, P=P: e.activation(out=P['aneg'][:], in_=P['aneg'][:], func=AF.Exp), reads=[P['aneg']], writes=[P['aneg']])
                s.op('dve', lambda e, P=P: e.tensor_scalar(out=P['aneg'][:], in0=P['aneg'][:], scalar1=-1.0, scalar2=None, op0=ALU.mult),
                     reads=[P['aneg']], writes=[P['aneg']])
                s.dma(P['dsk'][:], bc('ssd_d', l * 4, 4), writes=[P['dsk']], g=0)
                s.dma(P['snw'][:], bc('ssd_norm_w', l * 256, 256), writes=[P['snw']], g=0)
                for j in range(5):
                    s.dma(P['scw'][:, :, j], bass.AP(D['ssd_conv_w'], l * 2560 + j * 512, [[1, 128], [128, 4]]), writes=[P['scw']], g=0)
                s.dma(P['scb'][:], bass.AP(D['ssd_conv_b'], l * 512, [[1, 128], [128, 4]]), writes=[P['scb']], g=0)
                for j in range(3):
                    s.dma(P['fcw'][:, :, j], bass.AP(D['ffn_conv_w'], (l * 3 + j) * 2816, [[1, 128], [128, 22]]), writes=[P['fcw']], g=0)
                s.dma(P['fcb'][:], bass.AP(D['ffn_conv_b'], l * 2816, [[1, 128], [128, 22]]), writes=[P['fcb']], g=0)
                s.dma(P['s5d'][:], bass.AP(D['s5_d'], l * 256, [[1, 128], [128, 2]]), writes=[P['s5d']], g=0)
                s.dma(P['glub'][:], bass.AP(D['s5_glu_b'], l * 256, [[1, 128], [128, 2]]), writes=[P['glub']], g=0)
                gst = sbuf(st, f"gst{l}", [128, 2, 256])
                s.dma(gst[:], D['s5_glu_w'].ap()[l].rearrange("(k p) c -> p k c", p=128), writes=[gst], g=3)
                s.op('pool', lambda e, P=P, gst=gst: e.tensor_copy(out=P['gluw'][:], in_=gst[:]), reads=[gst], writes=[P['gluw']])
                lam_init = 0.8 - 0.6 * math.exp(-0.3 * l)
                s.dma(P['subw'][:], bc('diff_subln_w', l * 128, 128), writes=[P['subw']], g=0)
                s.op('dve', lambda e, P=P, li=lam_init: e.tensor_scalar(out=P['subw'][:], in0=P['subw'][:], scalar1=1.0 - li, scalar2=None,
                                                                      op0=ALU.mult), reads=[P['subw']], writes=[P['subw']])
                lt = sbuf(st, f"lt{l}", [128, 4, 64])
                lsum = sbuf(st, f"lsum{l}", [128, 4])
                for i, nm in enumerate(['diff_lam_q1', 'diff_lam_k1', 'diff_lam_q2', 'diff_lam_k2']):
                    s.dma(lt[:, i, :], bc(nm, l * 64, 64), writes=[lt], g=0)
                s.op('dve', lambda e, lt=lt: e.tensor_tensor(out=lt[:, 0, :], in0=lt[:, 0, :], in1=lt[:, 1, :], op=ALU.mult), reads=[lt], writes=[lt])
                s.op('dve', lambda e, lt=lt: e.tensor_tensor(out=lt[:, 2, :], in0=lt[:, 2, :], in1=lt[:, 3, :], op=ALU.mult), reads=[lt], writes=[lt])
                s.op('dve', lambda e, lt=lt, lsum=lsum: e.reduce_sum(out=lsum[:, 0:1], in_=lt[:, 0, :], axis=mybir.AxisListType.X), reads=[lt], writes=[lsum])
                s.op('dve', lambda e, lt=lt, lsum=lsum: e.reduce_sum(out=lsum[:, 1:2], in_=lt[:, 2, :], axis=mybir.AxisListType.X), reads=[lt], writes=[lsum])
                s.op('act', lambda e, lsum=lsum: e.activation(out=lsum[:, 2:4], in_=lsum[:, 0:2], func=AF.Exp), reads=[lsum], writes=[lsum])
                s.op('dve', lambda e, lsum=lsum, P=P, li=lam_init: e.scalar_tensor_tensor(out=P['nlam'][:], in0=lsum[:, 3:4], scalar=-li,
                                                                                        in1=lsum[:, 2:3], op0=ALU.add, op1=ALU.subtract),
                     reads=[lsum], writes=[P['nlam']])

                lre = sbuf(st, f"lre{l}", [128, 16])
                lim = sbuf(st, f"lim{l}", [128, 16])
                stp = sbuf(st, f"stp{l}", [128, 16])
                for d in range(2):
                    s.dma(lre[:, d * 8:(d + 1) * 8], bass.AP(D['s5_lam_re'], (l * 2 + d) * 1024, [[1, 128], [128, 8]]), writes=[lre], g=0)
                    s.dma(lim[:, d * 8:(d + 1) * 8], bass.AP(D['s5_lam_im'], (l * 2 + d) * 1024, [[1, 128], [128, 8]]), writes=[lim], g=0)
                    for gl in range(2):
                        s.dma(stp[64 * gl:64 * gl + 64, d * 8:(d + 1) * 8],
                              bass.AP(D['s5_log_step'], (l * 2 + d) * 16 + gl, [[0, 64], [2, 8]]), writes=[stp], g=0)
                w16 = [sbuf(st, f"w16_{l}_{i}", [128, 16]) for i in range(10)]
                lrs, lis, mag, sn, cs, nr, den, cre, cim, t16 = w16
                r_, ar_, ai_ = P['s5r'], P['s5ar'], P['s5ai']

                def V(fn, rd, wr, eng='dve'):
                    s.op(eng, fn, reads=rd, writes=wr)
                V(lambda e: e.activation(out=stp[:], in_=stp[:], func=AF.Exp), [stp], [stp], 'act')
                V(lambda e: e.tensor_tensor(out=lrs[:], in0=lre[:], in1=stp[:], op=ALU.mult), [lre, stp], [lrs])
                V(lambda e: e.tensor_tensor(out=lis[:], in0=lim[:], in1=stp[:], op=ALU.mult), [lim, stp], [lis])
                V(lambda e: e.activation(out=r_[:], in_=lrs[:], func=AF.Exp), [lrs], [r_], 'act')
                ki16 = sbuf(st, f"ki16_{l}", [128, 16], mybir.dt.int32)
                kf16 = sbuf(st, f"kf16_{l}", [128, 16])

                def sin16(out, phase):
                    V(lambda e: e.tensor_scalar(out=t16[:], in0=lis[:], scalar1=phase, scalar2=None, op0=ALU.add), [lis], [t16])
                    V(lambda e: e.tensor_scalar(out=ki16[:], in0=t16[:], scalar1=1.0 / TWO_PI, scalar2=None, op0=ALU.mult), [t16], [ki16])
                    V(lambda e: e.tensor_copy(out=kf16[:], in_=ki16[:]), [ki16], [kf16])
                    V(lambda e: e.scalar_tensor_tensor(out=t16[:], in0=kf16[:], scalar=-TWO_PI, in1=t16[:], op0=ALU.mult, op1=ALU.add), [kf16, t16], [t16])
                    V(lambda e: e.tensor_scalar(out=t16[:], in0=t16[:], scalar1=-3.14159, scalar2=3.14159, op0=ALU.max, op1=ALU.min), [t16], [t16])
                    V(lambda e: e.activation(out=out[:], in_=t16[:], func=AF.Sin), [t16], [out], 'act')
                sin16(sn, 0.0)
                sin16(cs, 0.5 * math.pi)
                negpi = LP['_negpi']
                Zt = LP['_Z']
                V(lambda e: e.tensor_tensor(out=ar_[:], in0=r_[:], in1=cs[:], op=ALU.mult), [r_, cs], [ar_])
                V(lambda e: e.tensor_tensor(out=ai_[:], in0=r_[:], in1=sn[:], op=ALU.mult), [r_, sn], [ai_])
                V(lambda e: e.tensor_scalar(out=nr[:], in0=ar_[:], scalar1=-1.0, scalar2=None, op0=ALU.add), [ar_], [nr])
                V(lambda e: e.tensor_tensor(out=den[:], in0=lre[:], in1=lre[:], op=ALU.mult), [lre], [den])
                V(lambda e: e.tensor_tensor(out=t16[:], in0=lim[:], in1=lim[:], op=ALU.mult), [lim], [t16])
                V(lambda e: e.tensor_tensor(out=den[:], in0=den[:], in1=t16[:], op=ALU.add), [den, t16], [den])
                V(lambda e: e.reciprocal(out=den[:], in_=den[:]), [den], [den])
                V(lambda e: e.tensor_tensor(out=cre[:], in0=nr[:], in1=lre[:], op=ALU.mult), [nr, lre], [cre])
                V(lambda e: e.tensor_tensor(out=t16[:], in0=ai_[:], in1=lim[:], op=ALU.mult), [ai_, lim], [t16])
                V(lambda e: e.tensor_tensor(out=cre[:], in0=cre[:], in1=t16[:], op=ALU.add), [cre, t16], [cre])
                V(lambda e: e.tensor_tensor(out=cre[:], in0=cre[:], in1=den[:], op=ALU.mult), [cre, den], [cre])
                V(lambda e: e.tensor_tensor(out=cim[:], in0=ai_[:], in1=lre[:], op=ALU.mult), [ai_, lre], [cim])
                V(lambda e: e.tensor_tensor(out=t16[:], in0=nr[:], in1=lim[:], op=ALU.mult), [nr, lim], [t16])
                V(lambda e: e.tensor_tensor(out=cim[:], in0=cim[:], in1=t16[:], op=ALU.subtract), [cim, t16], [cim])
                V(lambda e: e.tensor_tensor(out=cim[:], in0=cim[:], in1=den[:], op=ALU.mult), [cim, den], [cim])
                rt = sbuf(st, f"rt{l}", [128, 256])
                rki = sbuf(st, f"rki{l}", [128, 256], mybir.dt.int32)
                rkf = sbuf(st, f"rkf{l}", [128, 256])
                for j in range(16):
                    for (tab, ph) in [(P['cosT'], 0.5 * math.pi), (P['sinT'], 0.0)]:
                        V(lambda e, j=j, ph=ph: e.tensor_scalar(out=rt[:], in0=iota[:], scalar1=lis[:, j:j + 1], scalar2=ph, op0=ALU.mult, op1=ALU.add),
                          [iota, lis], [rt])
                        V(lambda e: e.tensor_scalar(out=rki[:], in0=rt[:], scalar1=1.0 / TWO_PI, scalar2=None, op0=ALU.mult), [rt], [rki])
                        V(lambda e: e.tensor_copy(out=rkf[:], in_=rki[:]), [rki], [rkf])
                        V(lambda e: e.scalar_tensor_tensor(out=rt[:], in0=rkf[:], scalar=-TWO_PI, in1=rt[:], op0=ALU.mult, op1=ALU.add), [rkf, rt], [rt])
                        V(lambda e: e.tensor_scalar(out=rt[:], in0=rt[:], scalar1=-3.14159, scalar2=3.14159, op0=ALU.max, op1=ALU.min), [rt], [rt])
                        V(lambda e, j=j, tab=tab: e.activation(out=tab[:, j, :], in_=rt[:], func=AF.Sin), [rt], [(tab, j)], 'act')
                bre = sbuf(st, f"bre{l}", [128, 16, 16])
                bim = sbuf(st, f"bim{l}", [128, 16, 16])
                bbr = sbuf(st, f"bbr{l}", [128, 16, 16])
                bbi = sbuf(st, f"bbi{l}", [128, 16, 16])
                btm = sbuf(st, f"btm{l}", [128, 16, 16])
                for d in range(2):
                    s.dma(bre[:, d * 8:(d + 1) * 8, :], bass.AP(D['s5_b_re'], (l * 2 + d) * 16384, [[16, 128], [2048, 8], [1, 16]]), writes=[bre], g=0)
                    s.dma(bim[:, d * 8:(d + 1) * 8, :], bass.AP(D['s5_b_im'], (l * 2 + d) * 16384, [[16, 128], [2048, 8], [1, 16]]), writes=[bim], g=0)

                def B3(t):
                    return t[:].unsqueeze(2).to_broadcast([128, 16, 16])
                V(lambda e: e.tensor_tensor(out=bbr[:], in0=bre[:], in1=B3(cre), op=ALU.mult), [bre, cre], [bbr])
                V(lambda e: e.tensor_tensor(out=btm[:], in0=bim[:], in1=B3(cim), op=ALU.mult), [bim, cim], [btm])
                V(lambda e: e.tensor_tensor(out=bbr[:], in0=bbr[:], in1=btm[:], op=ALU.subtract), [bbr, btm], [bbr])
                V(lambda e: e.tensor_tensor(out=bbi[:], in0=bim[:], in1=B3(cre), op=ALU.mult), [bim, cre], [bbi])
                V(lambda e: e.tensor_tensor(out=btm[:], in0=bre[:], in1=B3(cim), op=ALU.mult), [bre, cim], [btm])
                V(lambda e: e.tensor_tensor(out=bbi[:], in0=bbi[:], in1=btm[:], op=ALU.add), [bbi, btm], [bbi])
                for d in range(2):
                    for q in range(8):
                        j = d * 8 + q
                        ql = q % 4
                        for ri, bb in enumerate([bbr, bbi]):
                            zi = ri * 4 + ql
                            for gl in range(2):
                                V(lambda e, gl=gl, zi=zi, ql=ql, bb=bb, j=j: e.tensor_copy(
                                    out=Zt[64 * gl:64 * gl + 64, zi, 32 * ql + 16 * gl:32 * ql + 16 * gl + 16], in_=bb[64 * gl:64 * gl + 64, j, :]),
                                  [bb], [(Zt, zi)])
                            pb = bank()
                            s.op('pe', lambda e, pb=pb, zi=zi: e.transpose(out=pb[:, 0:128], in_=Zt[:, zi, :], identity=ident_f[:]),
                                 reads=[(Zt, zi), ident_f], writes=[pb])
                            s.op('act', lambda e, pb=pb, d=d, q=q, ri=ri: e.copy(out=P['WB'][:, d, q, ri, :], in_=pb[:, 0:128]),
                                 reads=[pb], writes=[(P['WB'], (d, q, ri))])
                s.op('pool', lambda e: e.memset(P['WC'][:], 0.0), writes=[P['WC']])
                for d in range(2):
                    for q in range(8):
                        ql = q % 4
                        for gl in range(2):
                            g = 2 * q + gl
                            for ri, nm in enumerate(['s5_c_re', 's5_c_im']):
                                s.dma(P['WC'][64 * gl:64 * gl + 64, d, q, ri, 32 * ql + 16 * gl:32 * ql + 16 * gl + 16],
                                      bass.AP(D[nm], ((l * 2 + d) * 16 + g) * 1024, [[1, 64], [64, 16]]), writes=[P['WC']], g=4)
                V(lambda e: e.tensor_scalar(out=P['WC'][:, :, :, 1, :], in0=P['WC'][:, :, :, 1, :], scalar1=-1.0, scalar2=None, op0=ALU.mult),
                  [P['WC']], [P['WC']])
                for nm in ['WB', 'WC', 'cosT', 'sinT']:
                    dt_ = BF16 if nm == 'WB' else F32
                    dd = nc.dram_tensor(f"s5scr_{nm}{l}", [128, 4096], dt_)
                    S5SCR[(nm, l)] = (dd, Buf(dd, f"s5scr_{nm}{l}"))
                    src = P[nm][:].rearrange("p a b c d -> p (a b c d)") if nm in ('WB', 'WC') else P[nm][:].rearrange("p a b -> p (a b)")
                    s.dma(dd.ap(), src, reads=[P[nm]], writes=[S5SCR[(nm, l)][1]], g=2)
                    P[nm] = None
            s.barrier()

        def tcols(t):
            return slice(t * 128, (t + 1) * 128)

        PIECES = [(0, 256), (256, 768), (768, 1280), (1280, 1792), (1792, 2304)]

        def mkstg(st, n=3):
            return dict(t=[sbuf(st, "stg", [128, 1024]) for _ in range(n)], i=[0])

        def load_cast(stg, dst_ap, dst_buf, src_ap, shape, key=None, eng='pool'):
            t = stg['t'][stg['i'][0] % len(stg['t'])]
            stg['i'][0] += 1
            n = int(np.prod(shape))
            tv = t[:, 0:n]
            if len(shape) == 2:
                tv = tv.rearrange("p (a b) -> p a b", a=shape[0])
            s.dma(tv, src_ap, writes=[t], g=3)
            s.op(eng, lambda e: e.tensor_copy(out=dst_ap, in_=tv), reads=[t], writes=[(dst_buf, key)])

        def load_w_bf16(stg, wbuf, name, l, c0, n, dst0=0, nk=8):
            src = D[name].ap()[l].rearrange("(k p) c -> p k c", p=128)
            kk = max(1, 1024 // n)
            for k0 in range(0, nk, kk):
                k1 = min(nk, k0 + kk)
                if k1 - k0 == 1:
                    load_cast(stg, wbuf[:, k0, dst0:dst0 + n], wbuf, src[:, k0, c0:c0 + n], [n])
                else:
                    load_cast(stg, wbuf[:, k0:k1, dst0:dst0 + n], wbuf, src[:, k0:k1, c0:c0 + n], [k1 - k0, n])

        def rstd_from_ss(ss, rstd, n):
            s.op('act', lambda e: e.activation(out=rstd[:], in_=ss[:], func=AF.Sqrt, bias=EPS, scale=1.0 / n), reads=[ss], writes=[rstd])
            s.op('dve', lambda e: e.reciprocal(out=rstd[:], in_=rstd[:]), reads=[rstd], writes=[rstd])

        def norm_mod_transpose(st_bufs, xt, hT, t, s1, sh, col):
            junk, ss, rstd, xn = st_bufs
            s.op('act', lambda e: e.activation(out=junk[:], in_=xt[:], func=AF.Square, accum_out=ss[:]), reads=[xt], writes=[junk, ss])
            rstd_from_ss(ss, rstd, 1024)
            s.op('dve', lambda e: e.tensor_scalar(out=xn[:], in0=xt[:], scalar1=rstd[:, 0:1], scalar2=None, op0=ALU.mult),
                 reads=[xt, rstd], writes=[xn])
            pb = bank()
            pbb = pb.t[:].bitcast(BF16)
            for k in range(8):
                s.op('pe', lambda e, k=k: e.transpose(out=pbb[:, k * 128:(k + 1) * 128], in_=xn[:, k * 128:(k + 1) * 128], identity=ident_b[:]),
                     reads=[xn, ident_b], writes=[pb])
            for k in range(8):
                s.op('act', lambda e, k=k: e.activation(out=hT[:, k, tcols(t)], in_=pbb[:, k * 128:(k + 1) * 128], func=AF.Identity,
                                                       scale=s1[:, k, col:col + 1], bias=sh[:, k, col:col + 1]),
                     reads=[pb, s1, sh], writes=[(hT, t)])

        for b in range(NB):
            s.dma(xs_d.ap()[b, 0:256, :], ctx_d.ap()[b], writes=[(XS, b)], g=0)
            s.dma(xs_d.ap()[b, 256:2304, :], x_d.ap()[b], writes=[(XS, b)], g=0)
            for l in LAYERS:
                P = LP[l]
                last = (l == LAYERS[-1])
                need_ctx = not last
                with contextlib.ExitStack() as shs:
                  hT = sbuf(shs, "hT", [128, 8, SEQ], BF16)
                  early = False
                  with contextlib.ExitStack() as sm:
                    mixT = sbuf(sm, "mixT", [128, 8, SEQ], BF16)
                    with contextlib.ExitStack() as st:
                        xt2 = [sbuf(st, f"xt{i}", [128, 1024]) for i in range(2)]
                        nb_ = (sbuf(st, "junk", [128, 1024], BF16), sbuf(st, "ssA", [128, 1]), sbuf(st, "rstdA", [128, 1]),
                               sbuf(st, "xnA", [128, 1024], BF16))
                        for t in range(NT):
                            xt = xt2[t % 2]
                            s.dma(xt[:], xs_d.ap()[b, tcols(t), :], reads=[(XS, b)], writes=[xt], g=0)
                            norm_mod_transpose(nb_, xt, hT, t, P['s1a'], P['sha'], NB if t < 2 else b)
                        s.barrier()
                    CONST = dict(ident_b=ident_b, ident_f=ident_f, U_f=U_f, UT_f=UT_f, mf_f=mf_f, mb_f=mb_f, ones_f=ones_f,
                                 ropeC=ropeC, ropeS=ropeS)
                    ENV = dict(S5SCR=S5SCR, nc=nc, s=s, sbuf=sbuf, bank=bank, P=P, D=D, l=l, b=b, NB=NB, NCOL=NCOL, hT=hT, mixT=mixT,
                               load_w=load_w_bf16, mkstg=mkstg, load_cast=load_cast, rstd_from_ss=rstd_from_ss, tcols=tcols, PIECES=PIECES, C=CONST, DB=DB, stop=stop,
                               need_ctx=need_ctx, last=last, xs_d=xs_d, XS=XS, out_d=out_d, OUT=OUT, gsc_d=gsc_d, GSC=GSC)
                    if stop == 'A':
                        s.dma(DB['hT'].ap(), hT[:], reads=[hT], q='pool', g=5)
                        early = True
                    if not early:
                        with contextlib.ExitStack() as st:
                            ssd_stage(st, **ENV)
                            s.barrier()
                        early = stop is not None and stop.startswith('SSD')
                    if not early:
                        with contextlib.ExitStack() as st:
                            s5_stage(st, **ENV)
                            s.barrier()
                        early = stop is not None and stop.startswith('S5')
                        if 'mix2' in DB:
                            s.dma(DB['mix2'].ap(), mixT[:], reads=[mixT], q='pool', g=5)
                            s.barrier()
                    if not early:
                        with contextlib.ExitStack() as st:
                            attn_stage(st, **ENV)
                            s.barrier()
                        early = stop is not None and stop.startswith('ATT')
                    if stop is not None and stop[:2] in ('SS', 'S5', 'AT') and 'mixT' in DB:
                        s.dma(DB['mixT'].ap(), mixT[:], reads=[mixT], q='pool', g=5)
                    tiles = list(range(NT)) if need_ctx else list(range(2, NT))
                    if not early:
                      with contextlib.ExitStack() as st:
                        wo = sbuf(st, "wo", [128, 8, 1024], BF16)
                        stgE = mkstg(st)
                        load_w_bf16(stgE, wo, 'w_out', l, 0, 1024)
                        gbc = sbuf(st, "gbc", [128, 2, 1024])
                        s.dma(gbc[:, 0, :], bass.AP(gsc_d, ((l * 2 + 0) * NCOL + b) * 1024, [[0, 128], [1, 1024]]), reads=[GSC], writes=[gbc], g=0)
                        s.dma(gbc[:, 1, :], bass.AP(gsc_d, ((l * 2 + 0) * NCOL + NB) * 1024, [[0, 128], [1, 1024]]), reads=[GSC], writes=[gbc], g=0)
                        xt2 = [sbuf(st, f"xtE{i}", [128, 1024]) for i in range(2)]
                        nb_ = (sbuf(st, "junkE", [128, 1024], BF16), sbuf(st, "ssE", [128, 1]), sbuf(st, "rstdE", [128, 1]),
                               sbuf(st, "xnE", [128, 1024], BF16))
                        ss2 = sbuf(st, "ss2", [128, 2])
                        rs2 = sbuf(st, "rs2", [128, 1])
                        ytmp = sbuf(st, "ytmp", [128, 1024])
                        for t in tiles:
                            xt = xt2[t % 2]
                            s.dma(xt[:], xs_d.ap()[b, tcols(t), :], reads=[(XS, b)], writes=[xt], g=0)
                            pbs = [bank(), bank()]
                            for hf in range(2):
                                for k in range(8):
                                    s.op('pe', lambda e, k=k, hf=hf: e.matmul(pbs[hf][:, :], lhsT=mixT[:, k, tcols(t)], rhs=wo[:, k, hf * 512:(hf + 1) * 512],
                                                                             start=(k == 0), stop=(k == 7)), reads=[mixT, wo], writes=[pbs[hf]])
                                s.op('act', lambda e, hf=hf: e.activation(out=nb_[0][:, hf * 512:(hf + 1) * 512], in_=pbs[hf][:, :], func=AF.Square,
                                                                         accum_out=ss2[:, hf:hf + 1]), reads=[pbs[hf]], writes=[nb_[0], ss2])
                            s.op('dve', lambda e: e.tensor_tensor(out=rs2[:], in0=ss2[:, 0:1], in1=ss2[:, 1:2], op=ALU.add), reads=[ss2], writes=[rs2])
                            rstd_from_ss(rs2, rs2, 1024)
                            gi = 1 if t < 2 else 0
                            for hf in range(2):
                                s.op('dve', lambda e, hf=hf, gi=gi: e.scalar_tensor_tensor(out=ytmp[:, hf * 512:(hf + 1) * 512], in0=pbs[hf][:, :], scalar=rs2[:, 0:1],
                                                                                        in1=gbc[:, gi, hf * 512:(hf + 1) * 512], op0=ALU.mult, op1=ALU.mult),
                                     reads=[pbs[hf], rs2, gbc], writes=[ytmp])
                            if t == 5 and 'e_wo' in DB:
                                s.dma(DB['e_wo'].ap(), wo[:].rearrange("p a b -> p (a b)"), reads=[wo], q='pool', g=5)
                                s.dma(DB['e_mix'].ap(), mixT[:].rearrange("p a b -> p (a b)"), reads=[mixT], q='pool', g=5)
                            if t == 5 and 'e_ytmp' in DB:
                                s.dma(DB['e_ytmp'].ap(), ytmp[:], reads=[ytmp], g=5)
                                s.dma(DB['e_ss'].ap(), ss2[:], reads=[ss2], g=5)
                                s.dma(DB['e_rs'].ap(), rs2[:], reads=[rs2], g=5)
                                s.dma(DB['e_gbc'].ap(), gbc[:].rearrange("p a b -> p (a b)"), reads=[gbc], g=5)
                                s.dma(DB['e_x0'].ap(), xt[:], reads=[xt], g=5)
                            s.op('pool', lambda e, xt=xt: e.tensor_tensor(out=xt[:], in0=xt[:], in1=ytmp[:], op=ALU.add), reads=[xt, ytmp], writes=[xt])
                            s.dma(xs_d.ap()[b, tcols(t), :], xt[:], reads=[xt], writes=[(XS, b)], g=1)
                            norm_mod_transpose(nb_, xt, hT, t, P['s1f'], P['shf'], NB if t < 2 else b)
                        s.barrier()
                  if stop == 'E':
                    early = True
                  if not early:
                    with contextlib.ExitStack() as st:
                        ffn_stage(st, **ENV)
                        s.barrier()
                    if stop is not None and stop.startswith('F'):
                        early = True
                  if stop in ('E', 'L0'):
                    s.dma(DB['xs'].ap(), xs_d.ap()[b], reads=[(XS, b)], g=5)
                    if stop == 'E':
                        s.dma(DB['hT'].ap(), hT[:], reads=[hT], q='pool', g=5)
                    early = True
                  if early:
                    break
            if stop is not None:
                break
        s.barrier(final=True)
    return nc


def shard_inputs(inputs, NB, ncores):
    consts = host_consts()
    maps = []
    for i in range(ncores):
        m = {}
        m['x'] = np.ascontiguousarray(inputs['x'][i * NB:(i + 1) * NB], dtype=np.float32)
        m['ctx'] = np.ascontiguousarray(inputs['ctx'][i * NB:(i + 1) * NB], dtype=np.float32)
        m['cc'] = np.ascontiguousarray(np.concatenate([inputs['c'][i * NB:(i + 1) * NB], inputs['c_ctx'][None, :]], 0), dtype=np.float32)
        for n, sh in PARAMS:
            m[n] = np.ascontiguousarray(inputs[n], dtype=np.float32)
        m.update(consts)
        maps.append(m)
    return maps


def kernel(**inputs):
    inputs = {k: np.asarray(v) for k, v in inputs.items()}
    NB = 4
    nc = build(NB=NB, LAYERS=(0, 1))
    maps = shard_inputs(inputs, NB, 8)
    res = run_bass_kernel_spmd(nc, maps, core_ids=list(range(8)))
    return np.concatenate([r["out"] for r in res.results], axis=0).astype(np.float32)
```

```python
import contextlib
import math
import numpy as np
import concourse.bass as bass
import concourse.mybir as mybir
from concourse.bass_utils import run_bass_kernel_spmd

F32 = mybir.dt.float32
BF16 = mybir.dt.bfloat16
AF = mybir.ActivationFunctionType
ALU = mybir.AluOpType

ENGS = ('pe', 'act', 'dve', 'pool', 'sp')
SAME_ENGINE_SYNC = ('act', 'dve', 'pool')
NDSEM = 6
EPS = 1e-6
NT = 18
SEQ = 2304
TWO_PI = 2.0 * math.pi


class Buf:
    def __init__(self, t, name, psum=False):
        self.t = t
        self.name = name
        self.psum = psum
        self.w = {}
        self.r = {}

    def _keys(self, key):
        if key is None:
            return list(set(self.w) | set(self.r) | {None})
        return [key, None]

    def wdeps(self, key):
        d = set()
        for k in self._keys(key):
            d |= self.w.get(k, set())
        return d

    def rdeps(self, key):
        d = set()
        for k in self._keys(key):
            d |= set(self.r.get(k, {}).items())
        return d

    def add_reader(self, key, ev):
        rr = self.r.setdefault(key, {})
        if rr.get(ev[0], 0) < ev[1]:
            rr[ev[0]] = ev[1]

    def set_writer(self, key, ev):
        if key is None:
            self.w = {None: {ev}}
            self.r = {}
        else:
            self.w[key] = {ev}
            self.r[key] = {}

    def add_dma_writer(self, key, ev):
        cur = self.w.get(key, set())
        cur = {e for e in cur if e[0][1] == ev[0][1] and e[0][0].startswith('q') and not (e[0] == ev[0] and e[1] <= ev[1])}
        cur.add(ev)
        if key is None:
            self.w = {None: cur}
            self.r = {}
        else:
            self.w[key] = cur
            self.r[key] = {}

    def __getitem__(self, idx):
        return self.t[idx]


def _norm(lst):
    return [(x, None) if isinstance(x, Buf) else x for x in lst]


class Sched:
    NQ = {'sp': 14, 'pool': 3}

    def __init__(self, nc, stack):
        self.nc = nc
        self.stack = stack
        self.eng = {'pe': nc.tensor, 'act': nc.scalar, 'dve': nc.vector, 'pool': nc.gpsimd, 'sp': nc.sync}
        self.phase = 0
        self.ninst = 0
        self.sets = []
        for i in range(2):
            d = {}
            for e in ENGS:
                d[e] = stack.enter_context(nc.semaphore(f"s_{e}_{i}"))
            for q, n in self.NQ.items():
                for j in range(n):
                    d[f"q{q}{j}"] = stack.enter_context(nc.semaphore(f"s_q{q}{j}_{i}"))
            self.sets.append(d)
        self.bsems = [stack.enter_context(nc.semaphore(f"s_bar{i}")) for i in range(2)]
        self.nbar = 0
        self._new_sems()

    def _new_sems(self):
        self.sem = self.sets[self.phase % 2]
        self.cnt = {k: 0 for k in self.sem}
        self.seen = {e: {} for e in ENGS}
        self.rr = {q: 0 for q in self.NQ}

    def _wait(self, e, deps):
        need = {}
        for (k, v) in deps:
            if k[1] != self.phase:
                continue
            if need.get(k, 0) < v:
                need[k] = v
        for k, v in need.items():
            if self.seen[e].get(k, 0) >= v:
                continue
            self.eng[e].wait_ge(self.sem[k[0]], v)
            self.seen[e][k] = v
            self.ninst += 1

    def op(self, e, fn, reads=(), writes=()):
        reads = _norm(reads)
        writes = _norm(writes)
        writes = writes + [r for r in reads if r[0].psum]
        reads = [r for r in reads if not r[0].psum]
        deps = set()
        for b, k in reads:
            deps |= b.wdeps(k)
        for b, k in writes:
            deps |= b.wdeps(k)
            for ev in b.rdeps(k):
                if ev[0][0] != e:
                    deps.add(ev)
        if e not in SAME_ENGINE_SYNC:
            deps = {d for d in deps if d[0][0] != e}
        self._wait(e, deps)
        inst = fn(self.eng[e])
        self.cnt[e] += 1
        inst.then_inc(self.sem[e], 1)
        self.ninst += 1
        ev = ((e, self.phase), self.cnt[e])
        for b, k in reads:
            b.add_reader(k, ev)
        for b, k in writes:
            b.set_writer(k, ev)
        return inst

    def dma(self, out, in_, reads=(), writes=(), q='sp', g=0, **kw):
        reads = _norm(reads)
        writes = _norm(writes)
        deps = set()
        for b, k in reads:
            deps |= b.wdeps(k)
        for b, k in writes:
            same = b.w.get(k, set())
            deps |= {ev for ev in b.wdeps(k) if not (ev in same and ev[0][0].startswith('q'))} | b.rdeps(k)
        self._wait(q, deps)
        sk = f"q{q}{self.rr[q] % self.NQ[q]}"
        self.rr[q] += 1
        kk = (sk, self.phase)
        if self.cnt[sk] > 0 and self.seen[q].get(kk, 0) < self.cnt[sk]:
            self.eng[q].wait_ge(self.sem[sk], self.cnt[sk])
            self.seen[q][kk] = self.cnt[sk]
        inst = self.eng[q].dma_start(out=out, in_=in_, **kw)
        self.cnt[sk] += 16
        inst.then_inc(self.sem[sk], 16)
        self.ninst += 1
        ev = (kk, self.cnt[sk])
        for b, k in reads:
            b.add_reader(k, ev)
        for b, k in writes:
            b.add_dma_writer(k, ev)
        return inst

    def barrier(self, final=False):
        bsem = self.bsems[self.nbar % 2]
        other = self.bsems[(self.nbar + 1) % 2]
        self.nbar += 1
        for k in self.sem:
            if k.startswith('q') and self.cnt[k] > 0:
                self.eng['sp'].wait_ge(self.sem[k], self.cnt[k])
        for e in ENGS:
            if self.cnt[e] > 0:
                self.eng[e].wait_ge(self.sem[e], self.cnt[e])
            self.eng[e].sem_inc(bsem, 1)
        self.eng['sp'].wait_ge(bsem, len(ENGS))
        if not final:
            for k, sm in self.sets[(self.phase + 1) % 2].items():
                self.eng['sp'].sem_clear(sm)
            self.eng['sp'].sem_clear(other)
        self.eng['sp'].sem_inc(bsem, 1)
        for e in ENGS:
            self.eng[e].wait_ge(bsem, len(ENGS) + 1)
        if not final:
            self.phase += 1
            self._new_sems()


PARAMS = [
    ('mod_w', (2, 1024, 6144)), ('mod_b', (2, 6144)), ('mix_norm_pre', (2, 1024)), ('mix_norm_post', (2, 1024)),
    ('ffn_norm_pre', (2, 1024)), ('ffn_norm_post', (2, 1024)), ('w_in', (2, 1024, 2568)), ('w_out', (2, 1024, 1024)),
    ('ssd_conv_w', (2, 5, 512)), ('ssd_conv_b', (2, 512)), ('ssd_dt_bias', (2, 2, 4)), ('ssd_a_log', (2, 2, 4)),
    ('ssd_d', (2, 4)), ('ssd_norm_w', (2, 256)), ('diff_lam_q1', (2, 64)), ('diff_lam_k1', (2, 64)),
    ('diff_lam_q2', (2, 64)), ('diff_lam_k2', (2, 64)), ('diff_subln_w', (2, 128)),
    ('s5_lam_re', (2, 2, 16, 64)), ('s5_lam_im', (2, 2, 16, 64)), ('s5_log_step', (2, 2, 16)),
    ('s5_b_re', (2, 2, 16, 64, 16)), ('s5_b_im', (2, 2, 16, 64, 16)), ('s5_c_re', (2, 2, 16, 16, 64)),
    ('s5_c_im', (2, 2, 16, 16, 64)), ('s5_d', (2, 16, 16)), ('s5_glu_w', (2, 256, 256)), ('s5_glu_b', (2, 256)),
    ('ffn_w_gate', (2, 1024, 2816)), ('ffn_w_up', (2, 1024, 2816)), ('ffn_conv_w', (2, 3, 2816)),
    ('ffn_conv_b', (2, 2816)), ('ffn_w_down', (2, 2816, 1024)),
]


def host_consts():
    r = np.arange(128)
    U = (r[:, None] <= r[None, :]).astype(np.float32)
    c = {}
    c['c_ident'] = np.eye(128, dtype=np.float32)
    c['c_U'] = U
    c['c_UT'] = np.ascontiguousarray(U.T)
    c['c_mf'] = np.where(r[None, :] >= r[:, None], 0.0, -30000.0).astype(np.float32)
    c['c_mb'] = np.where(r[None, :] <= r[:, None], 0.0, -30000.0).astype(np.float32)
    c['c_ones'] = np.ones((128, 128), np.float32)
    t = np.arange(2048)
    rr = (t // 64).astype(np.float32)
    col = (t % 64).astype(np.float32)
    inv = (10000.0 ** (-np.arange(16, dtype=np.float32) / 16)).astype(np.float32)
    ar = rr[:, None] * inv[None, :]
    ac = col[:, None] * inv[None, :]
    c['c_cos'] = np.concatenate([np.cos(ar), np.cos(ac)], 1).astype(np.float32)
    c['c_sin'] = np.concatenate([np.sin(ar), np.sin(ac)], 1).astype(np.float32)
    c['c_iota'] = np.tile(np.arange(256, dtype=np.float32)[None, :], (128, 1))
    return c


import os
_CUT = int(os.environ.get('SSDCUT', '0'))


def ssd_stage(st, nc, s, sbuf, bank, P, D, l, hT, mixT, load_w, rstd_from_ss, tcols, PIECES, C, DB, stop, mkstg, **_):
    wzd = sbuf(st, "wzd", [128, 8, 264], BF16)
    wx = sbuf(st, "wx", [128, 8, 512], BF16)
    with contextlib.ExitStack() as s0:
        stg = mkstg(s0)
        load_w(stg, wzd, 'w_in', l, 0, 256, 0)
        load_w(stg, wzd, 'w_in', l, 768, 8, 256)
        load_w(stg, wx, 'w_in', l, 256, 512, 0)
        s.barrier()
    if stop == 'SSD0':
        return
    zs = sbuf(st, "zs", [128, NT, 256], BF16)
    dtr = sbuf(st, "dtr", [128, NT, 8])
    xbcT = sbuf(st, "xbcT", [128, 4, SEQ], BF16)
    xbc = sbuf(st, "xbc", [128, NT, 512], BF16)
    yacc = sbuf(st, "yacc", [128, NT, 256])
    for t in range(NT):
        pb = bank()
        for k in range(8):
            s.op('pe', lambda e, k=k: e.matmul(pb[:, 0:264], lhsT=hT[:, k, tcols(t)], rhs=wzd[:, k, :], start=(k == 0), stop=(k == 7)),
                 reads=[(hT, t), wzd], writes=[pb])
        s.op('act', lambda e: e.activation(out=zs[:, t, :], in_=pb[:, 0:256], func=AF.Silu), reads=[pb], writes=[(zs, t)])
        s.op('dve', lambda e: e.tensor_copy(out=dtr[:, t, :], in_=pb[:, 256:264]), reads=[pb], writes=[dtr])
    if stop == 'SSD1':
        return
    with contextlib.ExitStack() as sx:
        xraw = sbuf(sx, "xraw", [128, 2312])
        xacc = sbuf(sx, "xacc", [128, 2312])
        s.op('pool', lambda e: e.memset(xraw[:], 0.0), writes=[xraw])
        for c in range(4):
            for (c0, c1) in PIECES:
                pb = bank()
                n = c1 - c0
                for k in range(8):
                    s.op('pe', lambda e, k=k: e.matmul(pb[:, 0:n], lhsT=wx[:, k, c * 128:(c + 1) * 128], rhs=hT[:, k, c0:c1], start=(k == 0), stop=(k == 7)),
                         reads=[hT, wx], writes=[pb])
                off = 2 if c0 == 0 else 6
                s.op('act', lambda e: e.copy(out=xraw[:, off + c0:off + c1], in_=pb[:, 0:n]), reads=[pb], writes=[xraw])
            s.op('act', lambda e: e.activation(out=xacc[:, 2:2310], in_=xraw[:, 2:2310], func=AF.Identity, scale=P['scw'][:, c, 2:3], bias=P['scb'][:, c:c + 1]),
                 reads=[xraw, P['scw'], P['scb']], writes=[xacc])
            for j in (0, 1, 3, 4):
                s.op('dve', lambda e, j=j: e.scalar_tensor_tensor(out=xacc[:, 2:2310], in0=xraw[:, j:j + 2308], scalar=P['scw'][:, c, j:j + 1],
                                                                 in1=xacc[:, 2:2310], op0=ALU.mult, op1=ALU.add),
                     reads=[xraw, xacc, P['scw']], writes=[xacc])
            s.op('act', lambda e: e.activation(out=xbcT[:, c, 0:256], in_=xacc[:, 2:258], func=AF.Silu), reads=[xacc], writes=[xbcT])
            s.op('act', lambda e: e.activation(out=xbcT[:, c, 256:2304], in_=xacc[:, 262:2310], func=AF.Silu), reads=[xacc], writes=[xbcT])

        s.barrier()
    if stop == 'SSD2':
        return
    for t in range(NT):
        pb = bank()
        pbb = pb.t[:].bitcast(BF16)
        for c in range(4):
            s.op('pe', lambda e, c=c: e.transpose(out=pbb[:, c * 128:(c + 1) * 128], in_=xbcT[:, c, tcols(t)], identity=C['ident_b'][:]),
                 reads=[xbcT, C['ident_b']], writes=[pb])
        s.op('act', lambda e: e.copy(out=xbc[:, t, :], in_=pbb[:, 0:512]), reads=[pb], writes=[(xbc, t)])
    if stop == 'SSD3':
        return
    dt = sbuf(st, "dt", [128, NT, 8])
    dta = sbuf(st, "dta", [128, NT, 8])
    tA = sbuf(st, "tA", [128, NT, 8])
    tB = sbuf(st, "tB", [128, NT, 8])

    def B8(tb):
        return tb[:].unsqueeze(1).to_broadcast([128, NT, 8])
    s.op('dve', lambda e: e.tensor_tensor(out=dtr[:], in0=dtr[:], in1=B8(P['dtb']), op=ALU.add), reads=[dtr, P['dtb']], writes=[dtr])
    s.op('dve', lambda e: e.tensor_scalar(out=tB[:], in0=dtr[:], scalar1=-1.0, scalar2=None, op0=ALU.mult), reads=[dtr], writes=[tB])
    s.op('dve', lambda e: e.tensor_tensor(out=tA[:], in0=dtr[:], in1=tB[:], op=ALU.max), reads=[dtr, tB], writes=[tA])
    s.op('act', lambda e: e.activation(out=tA[:], in_=tA[:], func=AF.Exp, scale=-1.0), reads=[tA], writes=[tA])
    s.op('act', lambda e: e.activation(out=tA[:], in_=tA[:], func=AF.Ln, bias=1.0, scale=1.0), reads=[tA], writes=[tA])
    s.op('dve', lambda e: e.tensor_single_scalar(out=tB[:], in_=dtr[:], scalar=0.0, op=ALU.max), reads=[dtr], writes=[tB])
    s.op('dve', lambda e: e.tensor_tensor(out=dt[:], in0=tA[:], in1=tB[:], op=ALU.add), reads=[tA, tB], writes=[dt])
    s.op('dve', lambda e: e.tensor_tensor(out=dta[:], in0=dt[:], in1=B8(P['aneg']), op=ALU.mult), reads=[dt, P['aneg']], writes=[dta])
    if stop == 'SSD4':
        return
    for t in range(NT):
        s.op('dve', lambda e, t=t: e.tensor_tensor(out=yacc[:, t, :].rearrange("p (h d) -> p h d", h=4),
                                                  in0=xbc[:, t, 0:256].rearrange("p (h d) -> p h d", h=4),
                                                  in1=P['dsk'][:].unsqueeze(2).to_broadcast([128, 4, 64]), op=ALU.mult),
             reads=[(xbc, t), P['dsk']], writes=[(yacc, t)])
    if stop == 'SSD5':
        return
    def two(name, shape, dt=F32):
        return [sbuf(st, f"{name}{i}", shape, dt) for i in range(2)]
    sm8s, nacss, eacss, wdecs, etots = two("sm8", [128, 8]), two("nacs", [128, 4]), two("eacs", [128, 4]), two("wdec", [128, 4]), two("etot", [128, 4])
    A1 = sbuf(st, "A1", [128, 4, 128])
    A1hs, A1ls = two("A1h", [128, 4, 128], BF16), two("A1l", [128, 4, 128], BF16)
    Ub = {}
    for nm in ['U_f', 'UT_f', 'mf_f', 'mb_f']:
        Ub[nm] = sbuf(st, nm + "_b", [128, 128], BF16)
        s.op('act', lambda e, nm=nm: e.copy(out=Ub[nm][:], in_=C[nm][:]), reads=[C[nm]], writes=[Ub[nm]])
    decTs = two("decT", [128, 4, 128])
    scTs = two("scTs", [128, 4, 128], BF16)
    xdts, xdtws = two("xdt", [128, 4, 64], BF16), two("xdtw", [128, 4, 64], BF16)
    state = sbuf(st, "state", [128, 2, 64])
    stateb = sbuf(st, "stateb", [128, 2, 64], BF16)
    ones3 = sbuf(st, "ones3", [128, 4, 128])
    s.op('pool', lambda e: e.memset(ones3[:], 1.0), writes=[ones3])
    seq = []
    for d in range(2):
        order = list(range(NT)) if d == 0 else [1, 0] + list(range(NT - 1, 1, -1))
        for oi, t in enumerate(order):
            seq.append((d, t, oi == 0))

    def front(i):
        d, t, first = seq[i]
        k2 = i % 2
        sm8, nacs, eacs, wdec, etot = sm8s[k2], nacss[k2], eacss[k2], wdecs[k2], etots[k2]
        A1h, A1l, decT, scT, xdt, xdtw = A1hs[k2], A1ls[k2], decTs[k2], scTs[k2], xdts[k2], xdtws[k2]
        Um = C['U_f'] if d == 0 else C['UT_f']
        Umb = Ub['U_f'] if d == 0 else Ub['UT_f']
        Mnb = Ub['mf_f'] if d == 0 else Ub['mb_f']
        dta_c = dta[:, t, d * 4:(d + 1) * 4]
        dt_c = dt[:, t, d * 4:(d + 1) * 4]
        pbs = bank(0)
        s.op('pe', lambda e: e.matmul(pbs[:, 0:4], lhsT=Um[:], rhs=dta_c, start=True, stop=True), reads=[Um, dta], writes=[pbs])
        s.op('pe', lambda e: e.matmul(pbs[:, 4:8], lhsT=C['ones_f'][:], rhs=dta_c, start=True, stop=True), reads=[C['ones_f'], dta], writes=[pbs])
        s.op('act', lambda e: e.copy(out=sm8[:], in_=pbs[:, 0:8]), reads=[pbs], writes=[sm8])
        s.op('dve', lambda e: e.tensor_scalar(out=nacs[:], in0=sm8[:, 0:4], scalar1=-1.0, scalar2=None, op0=ALU.mult), reads=[sm8], writes=[nacs])
        s.op('act', lambda e: e.activation(out=eacs[:], in_=sm8[:, 0:4], func=AF.Exp), reads=[sm8], writes=[eacs])
        s.op('dve', lambda e: e.tensor_tensor(out=wdec[:], in0=sm8[:, 4:8], in1=sm8[:, 0:4], op=ALU.subtract), reads=[sm8], writes=[wdec])
        s.op('act', lambda e: e.activation(out=wdec[:], in_=wdec[:], func=AF.Exp), reads=[wdec], writes=[wdec])
        s.op('act', lambda e: e.activation(out=etot[:], in_=sm8[:, 4:8], func=AF.Exp), reads=[sm8], writes=[etot])
        s.op('dve', lambda e: e.tensor_tensor(out=A1[:], in0=ones3[:], in1=dta_c.unsqueeze(2).to_broadcast([128, 4, 128]), op=ALU.mult),
             reads=[ones3, dta], writes=[A1])
        s.op('act', lambda e: e.copy(out=A1h[:], in_=A1[:]), reads=[A1], writes=[A1h])
        s.op('dve', lambda e: e.tensor_tensor(out=A1l[:], in0=A1[:], in1=A1h[:], op=ALU.subtract), reads=[A1, A1h], writes=[A1l])
        pD = bank(1)
        for h in range(4):
            s.op('pe', lambda e, h=h: e.matmul(pD[:, h * 128:(h + 1) * 128], lhsT=A1h[:, h, :], rhs=Umb[:], start=True, stop=False),
                 reads=[A1h, Umb], writes=[pD])
            s.op('pe', lambda e, h=h: e.matmul(pD[:, h * 128:(h + 1) * 128], lhsT=A1l[:, h, :], rhs=Umb[:], start=False, stop=False),
                 reads=[A1l, Umb], writes=[pD])
            s.op('pe', lambda e, h=h: e.matmul(pD[:, h * 128:(h + 1) * 128], lhsT=C['ident_b'][:], rhs=Mnb[:], start=False, stop=True),
                 reads=[C['ident_b'], Mnb], writes=[pD])
        for h in range(4):
            s.op('act', lambda e, h=h: e.activation(out=decT[:, h, :], in_=pD[:, h * 128:(h + 1) * 128], func=AF.Exp, bias=nacs[:, h:h + 1], scale=1.0),
                 reads=[pD, nacs], writes=[(decT, h)])
        pGs = [bank(2), bank(3)]
        for g in range(2):
            s.op('pe', lambda e, g=g: e.matmul(pGs[g][:, 0:128], lhsT=xbcT[64 * g:64 * g + 64, 2, tcols(t)],
                                              rhs=xbcT[64 * g:64 * g + 64, 3, tcols(t)], start=True, stop=True), reads=[xbcT], writes=[pGs[g]])
        for h in range(4):
            g = h // 2
            s.op('dve', lambda e, h=h, g=g: e.tensor_tensor(out=scT[:, h, :], in0=pGs[g][:, 0:128], in1=decT[:, h, :], op=ALU.mult),
                 reads=[pGs[g], (decT, h)], writes=[(scT, h)])
        xs4 = xbc[:, t, 0:256].rearrange("p (h d) -> p h d", h=4)
        s.op('dve', lambda e: e.tensor_tensor(out=xdt[:], in0=xs4, in1=dt_c.unsqueeze(2).to_broadcast([128, 4, 64]), op=ALU.mult),
             reads=[(xbc, t), dt], writes=[xdt])
        s.op('dve', lambda e: e.tensor_tensor(out=xdtw[:], in0=xdt[:], in1=wdec[:].unsqueeze(2).to_broadcast([128, 4, 64]), op=ALU.mult),
             reads=[xdt, wdec], writes=[xdtw])
        pY = bank(4)
        for h in range(4):
            s.op('pe', lambda e, h=h: e.matmul(pY[:, h * 64:(h + 1) * 64], lhsT=scT[:, h, :], rhs=xdt[:, h, :], start=True, stop=True),
                 reads=[(scT, h), xdt], writes=[pY])
        s.op('dve', lambda e: e.tensor_tensor(out=yacc[:, t, :], in0=yacc[:, t, :], in1=pY[:, 0:256], op=ALU.add), reads=[pY, (yacc, t)], writes=[(yacc, t)])

    def back(i):
        d, t, first = seq[i]
        k2 = i % 2
        eacs, etot, xdtw = eacss[k2], etots[k2], xdtws[k2]
        if first:
            s.op('pool', lambda e: e.memset(state[:], 0.0), writes=[state])
            s.op('pool', lambda e: e.memset(stateb[:], 0.0), writes=[stateb])
        pOs = [bank(5), bank(6)]
        for h in range(4):
            g, hh = h // 2, h % 2
            s.op('pe', lambda e, h=h, g=g, hh=hh: e.matmul(pOs[g][:, hh * 64:(hh + 1) * 64], lhsT=xbcT[64 * g:64 * g + 64, 3, tcols(t)],
                                                          rhs=stateb[64 * g:64 * g + 64, hh, :], start=True, stop=True),
                 reads=[xbcT, stateb], writes=[pOs[g]])
        pS = bank(7)
        s.op('pe', lambda e: e.matmul(pS[:, 0:256], lhsT=xbc[:, t, 256:384], rhs=xdtw[:].rearrange("p h d -> p (h d)"), start=True, stop=True),
             reads=[(xbc, t), xdtw], writes=[pS])
        for g in range(2):
            for hh in range(2):
                h = 2 * g + hh
                s.op('dve', lambda e, g=g, hh=hh, h=h: e.scalar_tensor_tensor(
                    out=state[64 * g:64 * g + 64, hh, :], in0=state[64 * g:64 * g + 64, hh, :], scalar=etot[64 * g:64 * g + 64, h:h + 1],
                    in1=pS[64 * g:64 * g + 64, h * 64:(h + 1) * 64], op0=ALU.mult, op1=ALU.add), reads=[state, etot, pS], writes=[state])
        s.op('act', lambda e: e.copy(out=stateb[:], in_=state[:]), reads=[state], writes=[stateb])
        for h in range(4):
            g, hh = h // 2, h % 2
            s.op('dve', lambda e, h=h, g=g, hh=hh: e.scalar_tensor_tensor(out=yacc[:, t, h * 64:(h + 1) * 64], in0=pOs[g][:, hh * 64:(hh + 1) * 64],
                                                             scalar=eacs[:, h:h + 1], in1=yacc[:, t, h * 64:(h + 1) * 64], op0=ALU.mult, op1=ALU.add),
                 reads=[pOs[g], eacs, (yacc, t)], writes=[(yacc, t)])

    front(0)
    for i in range(len(seq)):
        if i + 1 < len(seq):
            front(i + 1)
        back(i)
    if stop == 'SSD6':
        return
    gt = sbuf(st, "gt", [128, 256])
    gj = sbuf(st, "gj", [128, 256])
    gb = sbuf(st, "gb", [128, 256], BF16)
    ssg = sbuf(st, "ssg", [128, 1])
    for t in range(NT):
        s.op('dve', lambda e: e.tensor_tensor(out=gt[:], in0=yacc[:, t, :], in1=zs[:, t, :], op=ALU.mult), reads=[(yacc, t), (zs, t)], writes=[gt])
        s.op('act', lambda e: e.activation(out=gj[:], in_=gt[:], func=AF.Square, accum_out=ssg[:]), reads=[gt], writes=[gj, ssg])
        rstd_from_ss(ssg, ssg, 256)
        s.op('dve', lambda e: e.scalar_tensor_tensor(out=gb[:], in0=gt[:], scalar=ssg[:, 0:1], in1=P['snw'][:], op0=ALU.mult, op1=ALU.mult),
             reads=[gt, ssg, P['snw']], writes=[gb])
        pb = bank()
        pbb = pb.t[:].bitcast(BF16)
        for c in range(2):
            s.op('pe', lambda e, c=c: e.transpose(out=pbb[:, c * 128:(c + 1) * 128], in_=gb[:, c * 128:(c + 1) * 128], identity=C['ident_b'][:]),
                 reads=[gb, C['ident_b']], writes=[pb])
        s.op('act', lambda e: e.copy(out=mixT[:, 0:2, tcols(t)], in_=pbb[:, 0:256].rearrange("p (c n) -> p c n", c=2)), reads=[pb], writes=[(mixT, ('ssd', t))])
import os


def s5_stage(st, nc, s, sbuf, bank, P, D, l, hT, mixT, load_w, PIECES, DB, stop, S5SCR, mkstg, **_):
    P = dict(P)
    P['WB'] = sbuf(st, "WBs", [128, 2, 8, 2, 128], BF16)
    P['WC'] = sbuf(st, "WCs", [128, 2, 8, 2, 128])
    P['cosT'] = sbuf(st, "cosTs", [128, 16, 256])
    P['sinT'] = sbuf(st, "sinTs", [128, 16, 256])
    for nm in ['WB', 'WC', 'cosT', 'sinT']:
        dd, db = S5SCR[(nm, l)]
        dst = P[nm][:].rearrange("p a b c d -> p (a b c d)") if nm in ('WB', 'WC') else P[nm][:].rearrange("p a b -> p (a b)")
        s.dma(dst, dd.ap(), reads=[db], writes=[P[nm]], g=2)
    wu = sbuf(st, "wu", [128, 8, 256], BF16)
    with contextlib.ExitStack() as s0:
        load_w(mkstg(s0), wu, 'w_in', l, 2312, 256, 0)
        s.barrier()
    uT = sbuf(st, "uT", [128, 2, SEQ], BF16)
    yT = sbuf(st, "yT", [128, 2, SEQ])
    for c in range(2):
        for (c0, c1) in PIECES:
            pb = bank()
            n = c1 - c0
            for k in range(8):
                s.op('pe', lambda e, k=k: e.matmul(pb[:, 0:n], lhsT=wu[:, k, c * 128:(c + 1) * 128], rhs=hT[:, k, c0:c1], start=(k == 0), stop=(k == 7)),
                     reads=[hT, wu], writes=[pb])
            s.op('act', lambda e: e.copy(out=uT[:, c, c0:c1], in_=pb[:, 0:n]), reads=[pb], writes=[uT])
            s.op('dve', lambda e: e.tensor_scalar(out=yT[:, c, c0:c1], in0=pb[:, 0:n], scalar1=P['s5d'][:, c:c + 1], scalar2=None, op0=ALU.mult),
                 reads=[pb, P['s5d']], writes=[(yT, c)])
    T = 256
    NCH = SEQ // T
    bs = [sbuf(st, f"s5b{i}", [128, 2, T]) for i in range(2)]
    gin = [sbuf(st, f"s5g{i}", [128, 2, T]) for i in range(2)]
    tt = [sbuf(st, f"s5t{i}", [128, 4, T]) for i in range(2)]
    gs = [sbuf(st, f"s5s{i}", [128, 2, T]) for i in range(2)]
    hh_ = [sbuf(st, f"s5h{i}", [128, 2, T]) for i in range(2)]
    t2 = tt
    ini = [sbuf(st, f"s5c{q}", [128, 2]) for q in range(8)]
    ctmp = [sbuf(st, f"s5ct{i}", [128, 2]) for i in range(2)]
    nsin1 = sbuf(st, "s5nsin1", [128, 16])
    s.op('dve', lambda e: e.tensor_scalar(out=nsin1[:], in0=P['sinT'][:, :, 1], scalar1=-1.0, scalar2=None, op0=ALU.mult), reads=[P['sinT']], writes=[nsin1])
    items = []
    for d in range(2):
        order = list(range(NCH)) if d == 0 else [0] + list(range(NCH - 1, 0, -1))
        for kt in range(2):
            for ci, ch in enumerate(order):
                for ql in range(4):
                    items.append((d, kt, ci, ch, ql, len(order)))
    pYs = {}

    def front(i):
        d, kt, ci, ch, ql, nord = items[i]
        q = kt * 4 + ql
        j = d * 8 + q
        i2 = i % 2
        B_, G_, T_ = bs[i2], gin[i2], tt[i2]
        ucols = uT[:, kt, ch * T:(ch + 1) * T] if d == 0 else uT[:, kt, ch * T:(ch + 1) * T][:, ::-1]
        cosj = P['cosT'][:, j, :]
        sinj = P['sinT'][:, j, :]
        pB = bank(2 + (i % 6))
        for ri in range(2):
            s.op('pe', lambda e, ri=ri: e.matmul(pB[:, ri * T:(ri + 1) * T], lhsT=P['WB'][:, d, q, ri, :], rhs=ucols, start=True, stop=True),
                 reads=[uT, P['WB']], writes=[pB])
        s.op('act', lambda e: e.copy(out=B_[:].rearrange("p a t -> p (a t)"), in_=pB[:, 0:2 * T]), reads=[pB], writes=[B_])
        TTp = lambda o, a, b_, op, rd, wr: s.op('pool', lambda e: e.tensor_tensor(out=o, in0=a, in1=b_, op=op), reads=rd, writes=wr)
        TTp(T_[:, 0, :], B_[:, 0, :], cosj, ALU.mult, [B_, P['cosT']], [(T_, 0)])
        TTp(T_[:, 1, :], B_[:, 1, :], sinj, ALU.mult, [B_, P['sinT']], [(T_, 1)])
        TTp(T_[:, 2, :], B_[:, 1, :], cosj, ALU.mult, [B_, P['cosT']], [(T_, 2)])
        TTp(T_[:, 3, :], B_[:, 0, :], sinj, ALU.mult, [B_, P['sinT']], [(T_, 3)])
        TTp(G_[:, 0, :], T_[:, 0, :], T_[:, 1, :], ALU.add, [(T_, 0), (T_, 1)], [(G_, 0)])
        TTp(G_[:, 1, :], T_[:, 2, :], T_[:, 3, :], ALU.subtract, [(T_, 2), (T_, 3)], [(G_, 1)])

    def back(i):
        d, kt, ci, ch, ql, nord = items[i]
        q = kt * 4 + ql
        j = d * 8 + q
        i2 = i % 2
        G_, S_, H_, U_ = gin[i2], gs[i2], hh_[i2], t2[i2]
        cosj = P['cosT'][:, j, :]
        sinj = P['sinT'][:, j, :]
        if ql == 0:
            pYs['cur'] = bank(len(pYs.setdefault('n', [])) % 2)
            pYs['n'].append(0)
        pY = pYs['cur']
        rb = P['s5r'][:, j:j + 1].to_broadcast([128, T])
        for ri in range(2):
            init = ini[q][:, ri:ri + 1] if ci > 0 else 0.0
            s.op('dve', lambda e, ri=ri, init=init: e.tensor_tensor_scan(out=S_[:, ri, :], data0=rb, data1=G_[:, ri, :], initial=init, op0=ALU.mult, op1=ALU.add),
                 reads=[(G_, ri), P['s5r'], ini[q]], writes=[(S_, ri)])
        TTd = lambda o, a, b_, op, rd, wr: s.op('dve', lambda e: e.tensor_tensor(out=o, in0=a, in1=b_, op=op), reads=rd, writes=wr)
        TTd(U_[:, 0, :], S_[:, 0, :], cosj, ALU.mult, [(S_, 0), P['cosT']], [(U_, 0)])
        TTd(U_[:, 1, :], S_[:, 1, :], sinj, ALU.mult, [(S_, 1), P['sinT']], [(U_, 1)])
        TTd(U_[:, 2, :], S_[:, 0, :], sinj, ALU.mult, [(S_, 0), P['sinT']], [(U_, 2)])
        TTd(U_[:, 3, :], S_[:, 1, :], cosj, ALU.mult, [(S_, 1), P['cosT']], [(U_, 3)])
        TTd(H_[:, 0, :], U_[:, 0, :], U_[:, 1, :], ALU.subtract, [(U_, 0), (U_, 1)], [(H_, 0)])
        TTd(H_[:, 1, :], U_[:, 2, :], U_[:, 3, :], ALU.add, [(U_, 2), (U_, 3)], [(H_, 1)])
        if ci < nord - 1:
            c1 = P['cosT'][:, j, 1:2]
            s1_ = P['sinT'][:, j, 1:2]
            ns1 = nsin1[:, j:j + 1]
            ct = ctmp[i2]
            hre_l = H_[:, 0, T - 1:T]
            him_l = H_[:, 1, T - 1:T]
            s.op('act', lambda e: e.activation(out=ct[:, 0:1], in_=him_l, func=AF.Copy, scale=ns1), reads=[(H_, 1), nsin1], writes=[ct])
            s.op('act', lambda e: e.activation(out=ct[:, 1:2], in_=hre_l, func=AF.Copy, scale=s1_), reads=[(H_, 0), P['sinT']], writes=[ct])
            s.op('act', lambda e: e.activation(out=ini[q][:, 0:1], in_=hre_l, func=AF.Identity, scale=c1, bias=ct[:, 0:1]),
                 reads=[(H_, 0), ct, P['cosT']], writes=[ini[q]])
            s.op('act', lambda e: e.activation(out=ini[q][:, 1:2], in_=him_l, func=AF.Identity, scale=c1, bias=ct[:, 1:2]),
                 reads=[(H_, 1), ct, P['cosT']], writes=[ini[q]])
        for ri in range(2):
            s.op('pe', lambda e, ri=ri: e.matmul(pY[:, 0:T], lhsT=P['WC'][:, d, q, ri, :], rhs=H_[:, ri, :],
                                                start=(ql == 0 and ri == 0), stop=(ql == 3 and ri == 1)),
                 reads=[(H_, ri), P['WC']], writes=[pY])
        if ql == 3:
            ycols = yT[:, kt, ch * T:(ch + 1) * T] if d == 0 else yT[:, kt, ch * T:(ch + 1) * T][:, ::-1]
            s.op('dve', lambda e: e.tensor_tensor(out=ycols, in0=ycols, in1=pY[:, 0:T], op=ALU.add), reads=[pY, (yT, kt)], writes=[(yT, kt)])

    front(0)
    for i in range(len(items)):
        if i + 1 < len(items):
            front(i + 1)
        back(i)
    if 's5_cos' in DB:
        s.dma(DB['s5_cos'].ap(), P['cosT'][:].rearrange("p a b -> p (a b)"), reads=[P['cosT']], g=5)
        s.dma(DB['s5_sin'].ap(), P['sinT'][:].rearrange("p a b -> p (a b)"), reads=[P['sinT']], g=5)
        s.dma(DB['s5_wc'].ap(), P['WC'][:].rearrange("p a b c d -> p (a b c d)"), reads=[P['WC']], g=5)
    if 's5_y' in DB:
        s.dma(DB['s5_y'].ap(), yT[:].rearrange("p a b -> p (a b)"), reads=[yT], g=5)
        s.dma(DB['s5_u'].ap(), uT[:].rearrange("p a b -> p (a b)"), reads=[uT], q='pool', g=5)
    ygb = uT
    fi = 0
    for c in range(2):
        for (g0, g1) in [(0, 1024), (1024, 2048), (2048, 2304)]:
            tb = tt[fi % 2]
            fi += 1
            n = g1 - g0
            tmp = tb[:].rearrange("p a t -> p (a t)")[:, 0:n]
            yv = yT[:, c, g0:g1]
            s.op('pool', lambda e: e.tensor_tensor(out=tmp, in0=yv, in1=yv, op=ALU.mult), reads=[(yT, c)], writes=[tb])
            s.op('pool', lambda e: e.tensor_scalar(out=tmp, in0=tmp, scalar1=0.044715, scalar2=1.0, op0=ALU.mult, op1=ALU.add), reads=[tb], writes=[tb])
            s.op('dve', lambda e: e.tensor_tensor(out=tmp, in0=tmp, in1=yv, op=ALU.mult), reads=[tb, (yT, c)], writes=[tb])
            s.op('act', lambda e: e.activation(out=tmp, in_=tmp, func=AF.Sigmoid, scale=1.5957691216057308), reads=[tb], writes=[tb])
            s.op('dve', lambda e: e.tensor_tensor(out=yv, in0=yv, in1=tmp, op=ALU.mult), reads=[tb, (yT, c)], writes=[(yT, c)])
            s.op('act', lambda e: e.copy(out=ygb[:, c, g0:g1], in_=yv), reads=[(yT, c)], writes=[ygb])
    sg = sbuf(st, "s5sg", [128, 512])
    for c in range(2):
        for (c0, c1) in PIECES:
            pb = bank()
            n = c1 - c0
            for k in range(2):
                s.op('pe', lambda e, k=k: e.matmul(pb[:, 0:n], lhsT=P['gluw'][:, k, c * 128:(c + 1) * 128], rhs=ygb[:, k, c0:c1], start=(k == 0), stop=(k == 1)),
                     reads=[ygb, P['gluw']], writes=[pb])
            s.op('act', lambda e: e.activation(out=sg[:, 0:n], in_=pb[:, 0:n], func=AF.Sigmoid, bias=P['glub'][:, c:c + 1], scale=1.0),
                 reads=[pb, P['glub']], writes=[sg])
            s.op('dve', lambda e: e.tensor_tensor(out=mixT[:, 6 + c, c0:c1], in0=yT[:, c, c0:c1], in1=sg[:, 0:n], op=ALU.mult),
                 reads=[sg, (yT, c)], writes=[(mixT, ('s5', c, c0))])


def attn_stage(st, nc, s, sbuf, bank, P, D, l, hT, mixT, load_w, rstd_from_ss, tcols, need_ctx, C, DB, stop, mkstg, **_):
    wqk = sbuf(st, "wqk", [128, 8, 1024], BF16)
    wv = sbuf(st, "wv", [128, 8, 512], BF16)
    with contextlib.ExitStack() as s0:
        stg = mkstg(s0)
        load_w(stg, wqk, 'w_in', l, 776, 1024, 0)
        load_w(stg, wv, 'w_in', l, 1800, 512, 0)
        s.barrier()
    _AC = int(os.environ.get('ACUT', '0'))
    if _AC == 1:
        return
    QT = sbuf(st, "QT", [128, 4, SEQ], BF16)
    KT = sbuf(st, "KT", [128, 4, SEQ], BF16)
    Vt = sbuf(st, "Vt", [128, NT, 4, 129], BF16)
    s.op('pool', lambda e: e.memset(Vt[:], 1.0), writes=[Vt])
    qkf = sbuf(st, "qkf", [128, 1024])
    qkr = sbuf(st, "qkr", [128, 1024], BF16)
    rt = sbuf(st, "ropet", [128, 4, 16, 2, 16])
    for t in range(NT):
        pq, pk, pv = bank(), bank(), bank()
        for (pb, w, c0) in [(pq, wqk, 0), (pk, wqk, 512), (pv, wv, 0)]:
            for k in range(8):
                s.op('pe', lambda e, k=k, pb=pb, w=w, c0=c0: e.matmul(pb[:, :], lhsT=hT[:, k, tcols(t)], rhs=w[:, k, c0:c0 + 512], start=(k == 0), stop=(k == 7)),
                     reads=[hT, w], writes=[pb])
        s.op('act', lambda e: e.copy(out=Vt[:, t, :, 0:128], in_=pv[:, :].rearrange("p (h e) -> p h e", h=4)), reads=[pv], writes=[(Vt, t)])
        if t < 2:
            s.op('act', lambda e: e.copy(out=qkr[:, 0:512], in_=pq[:, :]), reads=[pq], writes=[qkr])
            s.op('act', lambda e: e.copy(out=qkr[:, 512:1024], in_=pk[:, :]), reads=[pk], writes=[qkr])
        else:
            s.op('act', lambda e: e.copy(out=qkf[:, 0:512], in_=pq[:, :]), reads=[pq], writes=[qkf])
            s.op('act', lambda e: e.copy(out=qkf[:, 512:1024], in_=pk[:, :]), reads=[pk], writes=[qkf])
            tl = t - 2
            xv = qkf[:].rearrange("p (m a b j) -> p m a b j", m=16, a=2, b=2)
            ov = qkr[:].rearrange("p (m a b j) -> p m a b j", m=16, a=2, b=2)
            cosb = C['ropeC'][:, tl, :].rearrange("p (a j) -> p a j", a=2).unsqueeze(1).to_broadcast([128, 16, 2, 16])
            sinb = C['ropeS'][:, tl, :].rearrange("p (a j) -> p a j", a=2).unsqueeze(1).to_broadcast([128, 16, 2, 16])
            PO = lambda o, a, b_, op, rd, wr: s.op('pool', lambda e: e.tensor_tensor(out=o, in0=a, in1=b_, op=op), reads=rd, writes=wr)
            PO(rt[:, 0], xv[:, :, :, 0, :], cosb, ALU.mult, [qkf, C['ropeC']], [(rt, 0)])
            PO(rt[:, 1], xv[:, :, :, 1, :], sinb, ALU.mult, [qkf, C['ropeS']], [(rt, 1)])
            PO(rt[:, 2], xv[:, :, :, 1, :], cosb, ALU.mult, [qkf, C['ropeC']], [(rt, 2)])
            PO(rt[:, 3], xv[:, :, :, 0, :], sinb, ALU.mult, [qkf, C['ropeS']], [(rt, 3)])
            PO(ov[:, :, :, 0, :], rt[:, 0], rt[:, 1], ALU.subtract, [(rt, 0), (rt, 1)], [qkr])
            PO(ov[:, :, :, 1, :], rt[:, 2], rt[:, 3], ALU.add, [(rt, 2), (rt, 3)], [qkr])
        pb = bank()
        pbb = pb.t[:].bitcast(BF16)
        for m in range(8):
            s.op('pe', lambda e, m=m: e.transpose(out=pbb[:, m * 128:(m + 1) * 128], in_=qkr[:, m * 128:(m + 1) * 128], identity=C['ident_b'][:]),
                 reads=[qkr, C['ident_b']], writes=[pb])
        s.op('act', lambda e: e.copy(out=QT[:, :, tcols(t)], in_=pbb[:, 0:512].rearrange("p (h n) -> p h n", h=4)), reads=[pb], writes=[(QT, t)])
        s.op('dve', lambda e: e.tensor_copy(out=KT[:, :, tcols(t)], in_=pbb[:, 512:1024].rearrange("p (h n) -> p h n", h=4)), reads=[pb], writes=[(KT, t)])
    if _AC == 2:
        return
    ET = sbuf(st, "ET", [128, NT, 512], BF16)
    oc = sbuf(st, "oc", [128, 2, 4, 129])
    rr = sbuf(st, "att_rr", [128, 2, 4])
    o_ = sbuf(st, "att_o", [128, 128])
    oj = sbuf(st, "att_oj", [128, 128])
    ob = sbuf(st, "att_ob", [128, 128], BF16)
    sso = sbuf(st, "att_ss", [128, 1])
    groups = [(256 + 512 * i, 512, NT) for i in range(4)]
    if need_ctx:
        groups = [(0, 256, 2)] + groups
    for h in range(4):
        if _AC == 3 and h == 1:
            return
        for (q0, nq, nk) in groups:
            nsub = nq // 128
            for c in range(2):
                for kt in range(nk):
                    pS = bank()
                    s.op('pe', lambda e, kt=kt: e.matmul(pS[:, 0:nq], lhsT=KT[64 * c:64 * c + 64, h, tcols(kt)], rhs=QT[64 * c:64 * c + 64, h, q0:q0 + nq],
                                                        start=True, stop=True), reads=[KT, QT], writes=[pS])
                    s.op('act', lambda e, kt=kt: e.activation(out=ET[:, kt, 0:nq], in_=pS[:, 0:nq], func=AF.Exp, scale=0.125), reads=[pS], writes=[(ET, kt)])
                for qs in range(nsub):
                    pO = bank()
                    for kt in range(nk):
                        s.op('pe', lambda e, kt=kt: e.matmul(pO[:, 0:129], lhsT=ET[:, kt, qs * 128:(qs + 1) * 128], rhs=Vt[:, kt, h, :], start=(kt == 0), stop=(kt == nk - 1)),
                             reads=[(ET, kt), Vt], writes=[pO])
                    s.op('dve', lambda e: e.tensor_copy(out=oc[:, c, qs, :], in_=pO[:, 0:129]), reads=[pO], writes=[(oc, (c, qs))])
            s.op('dve', lambda e: e.reciprocal(out=rr[:, :, 0:nsub], in_=oc[:, :, 0:nsub, 128]), reads=[oc], writes=[rr])
            s.op('dve', lambda e: e.tensor_scalar(out=rr[:, 1, :], in0=rr[:, 1, :], scalar1=P['nlam'][:, 0:1], scalar2=None, op0=ALU.mult), reads=[rr, P['nlam']], writes=[rr])
            for qs in range(nsub):
                s.op('dve', lambda e: e.tensor_scalar(out=o_[:], in0=oc[:, 0, qs, 0:128], scalar1=rr[:, 0, qs:qs + 1], scalar2=None, op0=ALU.mult), reads=[oc, rr], writes=[o_])
                s.op('dve', lambda e: e.scalar_tensor_tensor(out=o_[:], in0=oc[:, 1, qs, 0:128], scalar=rr[:, 1, qs:qs + 1], in1=o_[:], op0=ALU.mult, op1=ALU.add),
                     reads=[oc, rr, o_], writes=[o_])
                s.op('act', lambda e: e.activation(out=oj[:], in_=o_[:], func=AF.Square, accum_out=sso[:]), reads=[o_], writes=[oj, sso])
                rstd_from_ss(sso, sso, 128)
                s.op('dve', lambda e: e.scalar_tensor_tensor(out=ob[:], in0=o_[:], scalar=sso[:, 0:1], in1=P['subw'][:], op0=ALU.mult, op1=ALU.mult),
                     reads=[o_, sso, P['subw']], writes=[ob])
                pb = bank()
                pbb = pb.t[:].bitcast(BF16)
                s.op('pe', lambda e: e.transpose(out=pbb[:, 0:128], in_=ob[:], identity=C['ident_b'][:]), reads=[ob, C['ident_b']], writes=[pb])
                qc = q0 + qs * 128
                s.op('act', lambda e: e.copy(out=mixT[:, 2 + h, qc:qc + 128], in_=pbb[:, 0:128]), reads=[pb], writes=[(mixT, ('att', h, qc))])


NWB = int(os.environ.get('NWB', '2'))


def ffn_stage(st, nc, s, sbuf, bank, P, D, l, b, NB, NCOL, hT, need_ctx, last, xs_d, XS, out_d, OUT, gsc_d, GSC, load_w, rstd_from_ss, tcols, DB, stop, mkstg, load_cast, **_):
    stg = mkstg(st, 4)
    wd = sbuf(st, "wd", [128, 22, 1024], BF16)
    wdsrc = D['ffn_w_down'].ap()[l].rearrange("(f p) c -> p f c", p=128)
    for f in range(22):
        load_cast(stg, wd[:, f, :], wd, wdsrc[:, f, :], [1024], key=f)
    gbc = sbuf(st, "gbcF", [128, 2, 1024])
    s.dma(gbc[:, 0, :], bass.AP(gsc_d, ((l * 2 + 1) * NCOL + b) * 1024, [[0, 128], [1, 1024]]), reads=[GSC], writes=[gbc], g=0)
    s.dma(gbc[:, 1, :], bass.AP(gsc_d, ((l * 2 + 1) * NCOL + NB) * 1024, [[0, 128], [1, 1024]]), reads=[GSC], writes=[gbc], g=0)
    actT = sbuf(st, "actT", [128, 22, 1152], BF16)
    wg = [sbuf(st, f"wg{i}", [128, 8, 128], BF16) for i in range(NWB)]
    wu = [sbuf(st, f"wu{i}", [128, 8, 128], BF16) for i in range(NWB)]
    graw = sbuf(st, "graw", [128, 1160])
    gc = sbuf(st, "gc", [128, 1160])
    ub = sbuf(st, "ub", [128, 1152], BF16)
    xt2 = [sbuf(st, f"xtF{i}", [128, 1024]) for i in range(2)]
    junk = sbuf(st, "junkF", [128, 1024], BF16)
    ss2 = sbuf(st, "ssF", [128, 2])
    rs2 = sbuf(st, "rsF", [128, 1])
    ytmp = sbuf(st, "ytmpF", [128, 1024])
    s.op('pool', lambda e: e.memset(graw[:], 0.0), writes=[graw])
    if stop == 'F0':
        return
    halves = []
    if need_ctx:
        halves.append(dict(gp=[(0, 256, 1), (256, 768, 259), (768, 1153, 771)], up=[(0, 256, 0), (256, 768, 256), (768, 1152, 768)],
                           outs=[(0, 256, 1), (256, 1152, 259)], tiles=list(range(0, 9)), base=0, pads=[0, 257, 258]))
    else:
        halves.append(dict(gp=[(256, 768, 259), (768, 1153, 771)], up=[(256, 768, 256), (768, 1152, 768)],
                           outs=[(256, 1152, 259)], tiles=list(range(2, 9)), base=0, pads=[0, 257, 258]))
    halves.append(dict(gp=[(1151, 1663, 0), (1663, 2175, 512), (2175, 2304, 1024)], up=[(1152, 1664, 0), (1664, 2176, 512), (2176, 2304, 1024)],
                       outs=[(0, 1152, 1)], tiles=list(range(9, 18)), base=1152, pads=[1153]))
    for hv in halves:
        for pcol in hv['pads']:
            s.op('pool', lambda e, pcol=pcol: e.memset(graw[:, pcol:pcol + 1], 0.0), writes=[graw])
        for f in range(int(os.environ.get('FSTART', '0')), int(os.environ.get('FCUT', '22'))):
            g_, u_ = wg[f % NWB], wu[f % NWB]
            load_cast(stg, g_[:], g_, D['ffn_w_gate'].ap()[l].rearrange("(k p) c -> p k c", p=128)[:, :, f * 128:(f + 1) * 128], [8, 128])
            load_cast(stg, u_[:], u_, D['ffn_w_up'].ap()[l].rearrange("(k p) c -> p k c", p=128)[:, :, f * 128:(f + 1) * 128], [8, 128])
            _FS = int(os.environ.get('FSTEP', '0')); _lastf = f == int(os.environ.get('FCUT', '22')) - 1
            if _lastf and _FS == 1:
                return
            for (c0, c1, dst) in hv['gp']:
                pb = bank()
                n = c1 - c0
                for k in range(8):
                    s.op('pe', lambda e, k=k: e.matmul(pb[:, 0:n], lhsT=g_[:, k, :], rhs=hT[:, k, c0:c1], start=(k == 0), stop=(k == 7)), reads=[hT, g_], writes=[pb])
                s.op('act', lambda e: e.copy(out=graw[:, dst:dst + n], in_=pb[:, 0:n]), reads=[pb], writes=[graw])
            if _lastf and _FS == 2:
                return
            for (c0, c1, dst) in hv['up']:
                pb = bank()
                n = c1 - c0
                for k in range(8):
                    s.op('pe', lambda e, k=k: e.matmul(pb[:, 0:n], lhsT=u_[:, k, :], rhs=hT[:, k, c0:c1], start=(k == 0), stop=(k == 7)), reads=[hT, u_], writes=[pb])
                s.op('dve', lambda e: e.tensor_copy(out=ub[:, dst:dst + n], in_=pb[:, 0:n]), reads=[pb], writes=[ub])
            if _lastf and _FS == 3:
                return
            s.op('act', lambda e: e.activation(out=gc[:, 1:1157], in_=graw[:, 1:1157], func=AF.Identity, scale=P['fcw'][:, f, 1:2], bias=P['fcb'][:, f:f + 1]),
                 reads=[graw, P['fcw'], P['fcb']], writes=[gc])
            for j in (0, 2):
                s.op('pool', lambda e, j=j: e.scalar_tensor_tensor(out=gc[:, 1:1157], in0=graw[:, j:j + 1156], scalar=P['fcw'][:, f, j:j + 1], in1=gc[:, 1:1157],
                                                                  op0=ALU.mult, op1=ALU.add), reads=[graw, gc, P['fcw']], writes=[gc]) if False else \
                    s.op('dve', lambda e, j=j: e.scalar_tensor_tensor(out=gc[:, 1:1157], in0=graw[:, j:j + 1156], scalar=P['fcw'][:, f, j:j + 1], in1=gc[:, 1:1157],
                                                                     op0=ALU.mult, op1=ALU.add), reads=[graw, gc, P['fcw']], writes=[gc])
            s.op('act', lambda e: e.activation(out=gc[:, 1:1157], in_=gc[:, 1:1157], func=AF.Silu), reads=[gc], writes=[gc])
            for (a0, a1, src) in hv['outs']:
                n = a1 - a0
                s.op('dve', lambda e: e.tensor_tensor(out=actT[:, f, a0:a1], in0=gc[:, src:src + n], in1=ub[:, a0:a1], op=ALU.mult),
                     reads=[gc, ub], writes=[(actT, f)])
            if stop == 'F1':
                return
        if stop == 'F2':
            return
        htl = hv['tiles']

        def ldF(i):
            s.dma(xt2[i % 2][:], xs_d.ap()[b, tcols(htl[i]), :], reads=[(XS, b)], writes=[xt2[i % 2]], g=0)
        ldF(0)
        for ti, t in enumerate(htl):
            if ti + 1 < len(htl):
                ldF(ti + 1)
            xt = xt2[ti % 2]
            tc = t * 128 - hv['base']
            pbs = [bank(), bank()]
            for hf in range(2):
                for f in range(22):
                    s.op('pe', lambda e, f=f, hf=hf: e.matmul(pbs[hf][:, :], lhsT=actT[:, f, tc:tc + 128], rhs=wd[:, f, hf * 512:(hf + 1) * 512],
                                                             start=(f == 0), stop=(f == 21)), reads=[actT, wd], writes=[pbs[hf]])
                s.op('act', lambda e, hf=hf: e.activation(out=junk[:, hf * 512:(hf + 1) * 512], in_=pbs[hf][:, :], func=AF.Square, accum_out=ss2[:, hf:hf + 1]),
                     reads=[pbs[hf]], writes=[junk, ss2])
            s.op('dve', lambda e: e.tensor_tensor(out=rs2[:], in0=ss2[:, 0:1], in1=ss2[:, 1:2], op=ALU.add), reads=[ss2], writes=[rs2])
            rstd_from_ss(rs2, rs2, 1024)
            gi = 1 if t < 2 else 0
            for hf in range(2):
                s.op('dve', lambda e, hf=hf: e.scalar_tensor_tensor(out=ytmp[:, hf * 512:(hf + 1) * 512], in0=pbs[hf][:, :], scalar=rs2[:, 0:1],
                                                                   in1=gbc[:, gi, hf * 512:(hf + 1) * 512], op0=ALU.mult, op1=ALU.mult),
                     reads=[pbs[hf], rs2, gbc], writes=[ytmp])
            s.op('pool', lambda e: e.tensor_tensor(out=xt[:], in0=xt[:], in1=ytmp[:], op=ALU.add), reads=[xt, ytmp], writes=[xt])
            if last:
                if t >= 2:
                    s.dma(out_d.ap()[b, (t - 2) * 128:(t - 1) * 128, :], xt[:], reads=[xt], writes=[OUT], g=1)
            else:
                s.dma(xs_d.ap()[b, tcols(t), :], xt[:], reads=[xt], writes=[(XS, b)], g=1)


def build(NB=4, LAYERS=(0, 1), dbg=(), stop=None):
    nc = bass.Bass("TRN2", target_bir_lowering=False)
    D = {}

    def din(name, shape):
        D[name] = nc.dram_tensor(name, list(shape), F32, kind="ExternalInput")
        return D[name]

    x_d = din("x", [NB, 2048, 1024])
    ctx_d = din("ctx", [NB, 256, 1024])
    cc_d = din("cc", [NB + 1, 1024])
    for n, sh in PARAMS:
        din(n, sh)
    for n, a in host_consts().items():
        din(n, a.shape)
    out_d = nc.dram_tensor("out", [NB, 2048, 1024], F32, kind="ExternalOutput")
    xs_d = nc.dram_tensor("xs", [NB, SEQ, 1024], F32)
    gsc_d = nc.dram_tensor("gsc", [2, 2, NB + 1, 1024], F32)
    DB = {}
    for n, sh in dbg:
        DB[n] = nc.dram_tensor("dbg_" + n, list(sh), F32, kind="ExternalOutput")

    NCOL = NB + 1
    with contextlib.ExitStack() as top:
        s = Sched(nc, top)
        top.enter_context(nc.allow_non_contiguous_dma(reason="small param layout loads"))
        top.enter_context(nc.allow_low_precision(reason="bf16 matmul operands"))
        XS = Buf(xs_d, "xs")
        GSC = Buf(gsc_d, "gsc")
        OUT = Buf(out_d, "out")
        DIN = Buf(None, "din")

        uid = [0]

        def sbuf(st, name, shape, dt=F32):
            uid[0] += 1
            name = f"{name}_u{uid[0]}"
            return Buf(st.enter_context(nc.sbuf_tensor(name, list(shape), dt)), name)

        banks = [Buf(top.enter_context(nc.psum_tensor(f"pb{i}", [128, 512], F32)), f"pb{i}", psum=True) for i in range(8)]
        bank_i = [0]

        def bank(idx=None):
            if idx is not None:
                return banks[idx]
            b = banks[bank_i[0] % 8]
            bank_i[0] += 1
            return b

        def dmp(name, ap, buf, key=None):
            if name in DB:
                s.dma(DB[name].ap() if not isinstance(name, tuple) else None, ap, reads=[(buf, key)], g=5)

        ident_f = sbuf(top, "ident_f", [128, 128])
        ident_b = sbuf(top, "ident_b", [128, 128], BF16)
        U_f = sbuf(top, "U_f", [128, 128])
        UT_f = sbuf(top, "UT_f", [128, 128])
        mf_f = sbuf(top, "mf_f", [128, 128])
        mb_f = sbuf(top, "mb_f", [128, 128])
        ones_f = sbuf(top, "ones_f", [128, 128])
        ropeC = sbuf(top, "ropeC", [128, 16, 32])
        ropeS = sbuf(top, "ropeS", [128, 16, 32])
        for bufc, nm in [(ident_f, 'c_ident'), (U_f, 'c_U'), (UT_f, 'c_UT'), (mf_f, 'c_mf'), (mb_f, 'c_mb'), (ones_f, 'c_ones')]:
            s.dma(bufc[:], D[nm].ap(), writes=[bufc], g=0)
        s.dma(ropeC[:], D['c_cos'].ap().rearrange("(t p) j -> p t j", p=128), writes=[ropeC], g=0)
        s.dma(ropeS[:], D['c_sin'].ap().rearrange("(t p) j -> p t j", p=128), writes=[ropeS], g=0)
        s.op('act', lambda e: e.copy(out=ident_b[:], in_=ident_f[:]), reads=[ident_f], writes=[ident_b])

        LP = {}
        for l in LAYERS:
            P = {}
            P['s1a'] = sbuf(top, f"s1a{l}", [128, 8, NCOL])
            P['sha'] = sbuf(top, f"sha{l}", [128, 8, NCOL])
            P['s1f'] = sbuf(top, f"s1f{l}", [128, 8, NCOL])
            P['shf'] = sbuf(top, f"shf{l}", [128, 8, NCOL])
            P['dtb'] = sbuf(top, f"dtb{l}", [128, 8])
            P['aneg'] = sbuf(top, f"aneg{l}", [128, 8])
            P['dsk'] = sbuf(top, f"dsk{l}", [128, 4])
            P['snw'] = sbuf(top, f"snw{l}", [128, 256])
            P['scw'] = sbuf(top, f"scw{l}", [128, 4, 5])
            P['scb'] = sbuf(top, f"scb{l}", [128, 4])
            P['subw'] = sbuf(top, f"subw{l}", [128, 128])
            P['nlam'] = sbuf(top, f"nlam{l}", [128, 1])
            P['fcw'] = sbuf(top, f"fcw{l}", [128, 22, 3])
            P['fcb'] = sbuf(top, f"fcb{l}", [128, 22])
            P['s5d'] = sbuf(top, f"s5d{l}", [128, 2])
            P['glub'] = sbuf(top, f"glub{l}", [128, 2])
            P['gluw'] = sbuf(top, f"gluw{l}", [128, 2, 256], BF16)
            P['s5r'] = sbuf(top, f"s5r{l}", [128, 16])
            P['s5ar'] = sbuf(top, f"s5ar{l}", [128, 16])
            P['s5ai'] = sbuf(top, f"s5ai{l}", [128, 16])
            LP[l] = P

        S5SCR = {}
        LP['_negpi'] = sbuf(top, "negpi", [128, 1])
        s.op('pool', lambda e: e.memset(LP['_negpi'][:], -math.pi), writes=[LP['_negpi']])
        LP['_Z'] = sbuf(top, "Zt", [128, 8, 128])
        s.op('pool', lambda e: e.memset(LP['_Z'][:], 0.0), writes=[LP['_Z']])
        with contextlib.ExitStack() as st:
            scT = sbuf(st, "scT", [128, 8, NCOL])
            for col in range(NCOL):
                s.dma(scT[:, :, col], bass.AP(cc_d, col * 1024, [[1, 128], [128, 8]]), writes=[scT], g=0)
            s.op('act', lambda e: e.activation(out=scT[:], in_=scT[:], func=AF.Silu), reads=[scT], writes=[scT])
            wm = [sbuf(st, f"wm{i}", [128, 8, 512]) for i in range(2)]
            modT = sbuf(st, "modT", [128, 48, NCOL])
            modb = sbuf(st, "modb", [128, 48])
            nrm = sbuf(st, "nrm", [128, 4, 8])
            tmpg = sbuf(st, "tmpg", [128, 2, 8, NCOL])
            iota = sbuf(st, "iota", [128, 256])
            s.dma(iota[:], D['c_iota'].ap(), writes=[iota], g=0)
            S5D = {}
            for l in LAYERS:
                P = LP[l]
                P['WB'] = sbuf(st, f"WB{l}", [128, 2, 8, 2, 128], BF16)
                P['WC'] = sbuf(st, f"WC{l}", [128, 2, 8, 2, 128])
                P['cosT'] = sbuf(st, f"cosT{l}", [128, 16, 256])
                P['sinT'] = sbuf(st, f"sinT{l}", [128, 16, 256])
                s.dma(modb[:], bass.AP(D['mod_b'], l * 6144, [[1, 128], [128, 48]]), writes=[modb], g=0)
                for i, nm in enumerate(['mix_norm_pre', 'mix_norm_post', 'ffn_norm_pre', 'ffn_norm_post']):
                    s.dma(nrm[:, i, :], bass.AP(D[nm], l * 1024, [[1, 128], [128, 8]]), writes=[nrm], g=0)
                for cch in range(12):
                    w = wm[cch % 2]
                    s.dma(w[:], D['mod_w'].ap()[l].rearrange("(k p) c -> p k c", p=128)[:, :, cch * 512:(cch + 1) * 512],
                          writes=[w], g=1)
                    for jj in range(4):
                        j = cch * 4 + jj
                        pb = bank()
                        for k in range(8):
                            s.op('pe', lambda e, k=k, jj=jj, w=w, pb=pb: e.matmul(pb[:, 0:NCOL], lhsT=w[:, k, jj * 128:(jj + 1) * 128],
                                                                                 rhs=scT[:, k, :], start=(k == 0), stop=(k == 7)),
                                 reads=[w, scT], writes=[pb])
                        s.op('act', lambda e, j=j, pb=pb: e.activation(out=modT[:, j, :], in_=pb[:, 0:NCOL], func=AF.Identity,
                                                                      bias=modb[:, j:j + 1], scale=1.0),
                             reads=[pb, modb], writes=[modT])
                for (s1, sh, o_sh, o_sc, o_g, ipre, ipost, wi) in [(P['s1a'], P['sha'], 0, 8, 16, 0, 1, 0),
                                                                    (P['s1f'], P['shf'], 24, 32, 40, 2, 3, 1)]:
                    s.op('dve', lambda e, s1=s1, o_sc=o_sc: e.tensor_scalar(out=s1[:], in0=modT[:, o_sc:o_sc + 8, :], scalar1=1.0,
                                                                           scalar2=None, op0=ALU.add),
                         reads=[modT], writes=[s1])
                    s.op('dve', lambda e, s1=s1, ipre=ipre: e.tensor_tensor(out=s1[:], in0=s1[:],
                                                                           in1=nrm[:, ipre, :].unsqueeze(2).to_broadcast([128, 8, NCOL]),
                                                                           op=ALU.mult),
                         reads=[s1, nrm], writes=[s1])
                    s.op('dve', lambda e, sh=sh, o_sh=o_sh: e.tensor_copy(out=sh[:], in_=modT[:, o_sh:o_sh + 8, :]),
                         reads=[modT], writes=[sh])
                    s.op('dve', lambda e, wi=wi, o_g=o_g, ipost=ipost: e.tensor_tensor(
                        out=tmpg[:, wi], in0=modT[:, o_g:o_g + 8, :],
                        in1=nrm[:, ipost, :].unsqueeze(2).to_broadcast([128, 8, NCOL]), op=ALU.mult),
                        reads=[modT, nrm], writes=[tmpg])
                    for col in range(NCOL):
                        s.dma(bass.AP(gsc_d, ((l * 2 + wi) * NCOL + col) * 1024, [[1, 128], [128, 8]]), tmpg[:, wi, :, col],
                              reads=[tmpg], writes=[GSC], g=2)

                def bc(name, off, n):
                    return bass.AP(D[name], off, [[0, 128], [1, n]])
                s.dma(P['dtb'][:], bc('ssd_dt_bias', l * 8, 8), writes=[P['dtb']], g=0)
                s.dma(P['aneg'][:], bc('ssd_a_log', l * 8, 8), writes=[P['aneg']], g=0)
                s.op('act', lambda e, P=P: e.activation(out=P['aneg'][:], in_=P['aneg'][:], func=AF.Exp), reads=[P['aneg']], writes=[P['aneg']])
                s.op('dve', lambda e, P=P: e.tensor_scalar(out=P['aneg'][:], in0=P['aneg'][:], scalar1=-1.0, scalar2=None, op0=ALU.mult),
                     reads=[P['aneg']], writes=[P['aneg']])
                s.dma(P['dsk'][:], bc('ssd_d', l * 4, 4), writes=[P['dsk']], g=0)
                s.dma(P['snw'][:], bc('ssd_norm_w', l * 256, 256), writes=[P['snw']], g=0)
                for j in range(5):
                    s.dma(P['scw'][:, :, j], bass.AP(D['ssd_conv_w'], l * 2560 + j * 512, [[1, 128], [128, 4]]), writes=[P['scw']], g=0)
                s.dma(P['scb'][:], bass.AP(D['ssd_conv_b'], l * 512, [[1, 128], [128, 4]]), writes=[P['scb']], g=0)
                for j in range(3):
                    s.dma(P['fcw'][:, :, j], bass.AP(D['ffn_conv_w'], (l * 3 + j) * 2816, [[1, 128], [128, 22]]), writes=[P['fcw']], g=0)
                s.dma(P['fcb'][:], bass.AP(D['ffn_conv_b'], l * 2816, [[1, 128], [128, 22]]), writes=[P['fcb']], g=0)
                s.dma(P['s5d'][:], bass.AP(D['s5_d'], l * 256, [[1, 128], [128, 2]]), writes=[P['s5d']], g=0)
                s.dma(P['glub'][:], bass.AP(D['s5_glu_b'], l * 256, [[1, 128], [128, 2]]), writes=[P['glub']], g=0)
                gst = sbuf(st, f"gst{l}", [128, 2, 256])
                s.dma(gst[:], D['s5_glu_w'].ap()[l].rearrange("(k p) c -> p k c", p=128), writes=[gst], g=3)
                s.op('pool', lambda e, P=P, gst=gst: e.tensor_copy(out=P['gluw'][:], in_=gst[:]), reads=[gst], writes=[P['gluw']])
                lam_init = 0.8 - 0.6 * math.exp(-0.3 * l)
                s.dma(P['subw'][:], bc('diff_subln_w', l * 128, 128), writes=[P['subw']], g=0)
                s.op('dve', lambda e, P=P, li=lam_init: e.tensor_scalar(out=P['subw'][:], in0=P['subw'][:], scalar1=1.0 - li, scalar2=None,
                                                                      op0=ALU.mult), reads=[P['subw']], writes=[P['subw']])
                lt = sbuf(st, f"lt{l}", [128, 4, 64])
                lsum = sbuf(st, f"lsum{l}", [128, 4])
                for i, nm in enumerate(['diff_lam_q1', 'diff_lam_k1', 'diff_lam_q2', 'diff_lam_k2']):
                    s.dma(lt[:, i, :], bc(nm, l * 64, 64), writes=[lt], g=0)
                s.op('dve', lambda e, lt=lt: e.tensor_tensor(out=lt[:, 0, :], in0=lt[:, 0, :], in1=lt[:, 1, :], op=ALU.mult), reads=[lt], writes=[lt])
                s.op('dve', lambda e, lt=lt: e.tensor_tensor(out=lt[:, 2, :], in0=lt[:, 2, :], in1=lt[:, 3, :], op=ALU.mult), reads=[lt], writes=[lt])
                s.op('dve', lambda e, lt=lt, lsum=lsum: e.reduce_sum(out=lsum[:, 0:1], in_=lt[:, 0, :], axis=mybir.AxisListType.X), reads=[lt], writes=[lsum])
                s.op('dve', lambda e, lt=lt, lsum=lsum: e.reduce_sum(out=lsum[:, 1:2], in_=lt[:, 2, :], axis=mybir.AxisListType.X), reads=[lt], writes=[lsum])
                s.op('act', lambda e, lsum=lsum: e.activation(out=lsum[:, 2:4], in_=lsum[:, 0:2], func=AF.Exp), reads=[lsum], writes=[lsum])
                s.op('dve', lambda e, lsum=lsum, P=P, li=lam_init: e.scalar_tensor_tensor(out=P['nlam'][:], in0=lsum[:, 3:4], scalar=-li,
                                                                                        in1=lsum[:, 2:3], op0=ALU.add, op1=ALU.subtract),
                     reads=[lsum], writes=[P['nlam']])

                lre = sbuf(st, f"lre{l}", [128, 16])
                lim = sbuf(st, f"lim{l}", [128, 16])
                stp = sbuf(st, f"stp{l}", [128, 16])
                for d in range(2):
                    s.dma(lre[:, d * 8:(d + 1) * 8], bass.AP(D['s5_lam_re'], (l * 2 + d) * 1024, [[1, 128], [128, 8]]), writes=[lre], g=0)
                    s.dma(lim[:, d * 8:(d + 1) * 8], bass.AP(D['s5_lam_im'], (l * 2 + d) * 1024, [[1, 128], [128, 8]]), writes=[lim], g=0)
                    for gl in range(2):
                        s.dma(stp[64 * gl:64 * gl + 64, d * 8:(d + 1) * 8],
                              bass.AP(D['s5_log_step'], (l * 2 + d) * 16 + gl, [[0, 64], [2, 8]]), writes=[stp], g=0)
                w16 = [sbuf(st, f"w16_{l}_{i}", [128, 16]) for i in range(10)]
                lrs, lis, mag, sn, cs, nr, den, cre, cim, t16 = w16
                r_, ar_, ai_ = P['s5r'], P['s5ar'], P['s5ai']

                def V(fn, rd, wr, eng='dve'):
                    s.op(eng, fn, reads=rd, writes=wr)
                V(lambda e: e.activation(out=stp[:], in_=stp[:], func=AF.Exp), [stp], [stp], 'act')
                V(lambda e: e.tensor_tensor(out=lrs[:], in0=lre[:], in1=stp[:], op=ALU.mult), [lre, stp], [lrs])
                V(lambda e: e.tensor_tensor(out=lis[:], in0=lim[:], in1=stp[:], op=ALU.mult), [lim, stp], [lis])
                V(lambda e: e.activation(out=r_[:], in_=lrs[:], func=AF.Exp), [lrs], [r_], 'act')
                ki16 = sbuf(st, f"ki16_{l}", [128, 16], mybir.dt.int32)
                kf16 = sbuf(st, f"kf16_{l}", [128, 16])

                def sin16(out, phase):
                    V(lambda e: e.tensor_scalar(out=t16[:], in0=lis[:], scalar1=phase, scalar2=None, op0=ALU.add), [lis], [t16])
                    V(lambda e: e.tensor_scalar(out=ki16[:], in0=t16[:], scalar1=1.0 / TWO_PI, scalar2=None, op0=ALU.mult), [t16], [ki16])
                    V(lambda e: e.tensor_copy(out=kf16[:], in_=ki16[:]), [ki16], [kf16])
                    V(lambda e: e.scalar_tensor_tensor(out=t16[:], in0=kf16[:], scalar=-TWO_PI, in1=t16[:], op0=ALU.mult, op1=ALU.add), [kf16, t16], [t16])
                    V(lambda e: e.tensor_scalar(out=t16[:], in0=t16[:], scalar1=-3.14159, scalar2=3.14159, op0=ALU.max, op1=ALU.min), [t16], [t16])
                    V(lambda e: e.activation(out=out[:], in_=t16[:], func=AF.Sin), [t16], [out], 'act')
                sin16(sn, 0.0)
                sin16(cs, 0.5 * math.pi)
                negpi = LP['_negpi']
                Zt = LP['_Z']
                V(lambda e: e.tensor_tensor(out=ar_[:], in0=r_[:], in1=cs[:], op=ALU.mult), [r_, cs], [ar_])
                V(lambda e: e.tensor_tensor(out=ai_[:], in0=r_[:], in1=sn[:], op=ALU.mult), [r_, sn], [ai_])
                V(lambda e: e.tensor_scalar(out=nr[:], in0=ar_[:], scalar1=-1.0, scalar2=None, op0=ALU.add), [ar_], [nr])
                V(lambda e: e.tensor_tensor(out=den[:], in0=lre[:], in1=lre[:], op=ALU.mult), [lre], [den])
                V(lambda e: e.tensor_tensor(out=t16[:], in0=lim[:], in1=lim[:], op=ALU.mult), [lim], [t16])
                V(lambda e: e.tensor_tensor(out=den[:], in0=den[:], in1=t16[:], op=ALU.add), [den, t16], [den])
                V(lambda e: e.reciprocal(out=den[:], in_=den[:]), [den], [den])
                V(lambda e: e.tensor_tensor(out=cre[:], in0=nr[:], in1=lre[:], op=ALU.mult), [nr, lre], [cre])
                V(lambda e: e.tensor_tensor(out=t16[:], in0=ai_[:], in1=lim[:], op=ALU.mult), [ai_, lim], [t16])
                V(lambda e: e.tensor_tensor(out=cre[:], in0=cre[:], in1=t16[:], op=ALU.add), [cre, t16], [cre])
                V(lambda e: e.tensor_tensor(out=cre[:], in0=cre[:], in1=den[:], op=ALU.mult), [cre, den], [cre])
                V(lambda e: e.tensor_tensor(out=cim[:], in0=ai_[:], in1=lre[:], op=ALU.mult), [ai_, lre], [cim])
                V(lambda e: e.tensor_tensor(out=t16[:], in0=nr[:], in1=lim[:], op=ALU.mult), [nr, lim], [t16])
                V(lambda e: e.tensor_tensor(out=cim[:], in0=cim[:], in1=t16[:], op=ALU.subtract), [cim, t16], [cim])
                V(lambda e: e.tensor_tensor(out=cim[:], in0=cim[:], in1=den[:], op=ALU.mult), [cim, den], [cim])
                rt = sbuf(st, f"rt{l}", [128, 256])
                rki = sbuf(st, f"rki{l}", [128, 256], mybir.dt.int32)
                rkf = sbuf(st, f"rkf{l}", [128, 256])
                for j in range(16):
                    for (tab, ph) in [(P['cosT'], 0.5 * math.pi), (P['sinT'], 0.0)]:
                        V(lambda e, j=j, ph=ph: e.tensor_scalar(out=rt[:], in0=iota[:], scalar1=lis[:, j:j + 1], scalar2=ph, op0=ALU.mult, op1=ALU.add),
                          [iota, lis], [rt])
                        V(lambda e: e.tensor_scalar(out=rki[:], in0=rt[:], scalar1=1.0 / TWO_PI, scalar2=None, op0=ALU.mult), [rt], [rki])
                        V(lambda e: e.tensor_copy(out=rkf[:], in_=rki[:]), [rki], [rkf])
                        V(lambda e: e.scalar_tensor_tensor(out=rt[:], in0=rkf[:], scalar=-TWO_PI, in1=rt[:], op0=ALU.mult, op1=ALU.add), [rkf, rt], [rt])
                        V(lambda e: e.tensor_scalar(out=rt[:], in0=rt[:], scalar1=-3.14159, scalar2=3.14159, op0=ALU.max, op1=ALU.min), [rt], [rt])
                        V(lambda e, j=j, tab=tab: e.activation(out=tab[:, j, :], in_=rt[:], func=AF.Sin), [rt], [(tab, j)], 'act')
                bre = sbuf(st, f"bre{l}", [128, 16, 16])
                bim = sbuf(st, f"bim{l}", [128, 16, 16])
                bbr = sbuf(st, f"bbr{l}", [128, 16, 16])
                bbi = sbuf(st, f"bbi{l}", [128, 16, 16])
                btm = sbuf(st, f"btm{l}", [128, 16, 16])
                for d in range(2):
                    s.dma(bre[:, d * 8:(d + 1) * 8, :], bass.AP(D['s5_b_re'], (l * 2 + d) * 16384, [[16, 128], [2048, 8], [1, 16]]), writes=[bre], g=0)
                    s.dma(bim[:, d * 8:(d + 1) * 8, :], bass.AP(D['s5_b_im'], (l * 2 + d) * 16384, [[16, 128], [2048, 8], [1, 16]]), writes=[bim], g=0)

                def B3(t):
                    return t[:].unsqueeze(2).to_broadcast([128, 16, 16])
                V(lambda e: e.tensor_tensor(out=bbr[:], in0=bre[:], in1=B3(cre), op=ALU.mult), [bre, cre], [bbr])
                V(lambda e: e.tensor_tensor(out=btm[:], in0=bim[:], in1=B3(cim), op=ALU.mult), [bim, cim], [btm])
                V(lambda e: e.tensor_tensor(out=bbr[:], in0=bbr[:], in1=btm[:], op=ALU.subtract), [bbr, btm], [bbr])
                V(lambda e: e.tensor_tensor(out=bbi[:], in0=bim[:], in1=B3(cre), op=ALU.mult), [bim, cre], [bbi])
                V(lambda e: e.tensor_tensor(out=btm[:], in0=bre[:], in1=B3(cim), op=ALU.mult), [bre, cim], [btm])
                V(lambda e: e.tensor_tensor(out=bbi[:], in0=bbi[:], in1=btm[:], op=ALU.add), [bbi, btm], [bbi])
                for d in range(2):
                    for q in range(8):
                        j = d * 8 + q
                        ql = q % 4
                        for ri, bb in enumerate([bbr, bbi]):
                            zi = ri * 4 + ql
                            for gl in range(2):
                                V(lambda e, gl=gl, zi=zi, ql=ql, bb=bb, j=j: e.tensor_copy(
                                    out=Zt[64 * gl:64 * gl + 64, zi, 32 * ql + 16 * gl:32 * ql + 16 * gl + 16], in_=bb[64 * gl:64 * gl + 64, j, :]),
                                  [bb], [(Zt, zi)])
                            pb = bank()
                            s.op('pe', lambda e, pb=pb, zi=zi: e.transpose(out=pb[:, 0:128], in_=Zt[:, zi, :], identity=ident_f[:]),
                                 reads=[(Zt, zi), ident_f], writes=[pb])
                            s.op('act', lambda e, pb=pb, d=d, q=q, ri=ri: e.copy(out=P['WB'][:, d, q, ri, :], in_=pb[:, 0:128]),
                                 reads=[pb], writes=[(P['WB'], (d, q, ri))])
                s.op('pool', lambda e: e.memset(P['WC'][:], 0.0), writes=[P['WC']])
                for d in range(2):
                    for q in range(8):
                        ql = q % 4
                        for gl in range(2):
                            g = 2 * q + gl
                            for ri, nm in enumerate(['s5_c_re', 's5_c_im']):
                                s.dma(P['WC'][64 * gl:64 * gl + 64, d, q, ri, 32 * ql + 16 * gl:32 * ql + 16 * gl + 16],
                                      bass.AP(D[nm], ((l * 2 + d) * 16 + g) * 1024, [[1, 64], [64, 16]]), writes=[P['WC']], g=4)
                V(lambda e: e.tensor_scalar(out=P['WC'][:, :, :, 1, :], in0=P['WC'][:, :, :, 1, :], scalar1=-1.0, scalar2=None, op0=ALU.mult),
                  [P['WC']], [P['WC']])
                for nm in ['WB', 'WC', 'cosT', 'sinT']:
                    dt_ = BF16 if nm == 'WB' else F32
                    dd = nc.dram_tensor(f"s5scr_{nm}{l}", [128, 4096], dt_)
                    S5SCR[(nm, l)] = (dd, Buf(dd, f"s5scr_{nm}{l}"))
                    src = P[nm][:].rearrange("p a b c d -> p (a b c d)") if nm in ('WB', 'WC') else P[nm][:].rearrange("p a b -> p (a b)")
                    s.dma(dd.ap(), src, reads=[P[nm]], writes=[S5SCR[(nm, l)][1]], g=2)
                    P[nm] = None
            s.barrier()

        def tcols(t):
            return slice(t * 128, (t + 1) * 128)

        PIECES = [(0, 256), (256, 768), (768, 1280), (1280, 1792), (1792, 2304)]

        def mkstg(st, n=3):
            return dict(t=[sbuf(st, "stg", [128, 1024]) for _ in range(n)], i=[0])

        def load_cast(stg, dst_ap, dst_buf, src_ap, shape, key=None, eng='pool'):
            t = stg['t'][stg['i'][0] % len(stg['t'])]
            stg['i'][0] += 1
            n = int(np.prod(shape))
            tv = t[:, 0:n]
            if len(shape) == 2:
                tv = tv.rearrange("p (a b) -> p a b", a=shape[0])
            s.dma(tv, src_ap, writes=[t], g=3)
            s.op(eng, lambda e: e.tensor_copy(out=dst_ap, in_=tv), reads=[t], writes=[(dst_buf, key)])

        def load_w_bf16(stg, wbuf, name, l, c0, n, dst0=0, nk=8):
            src = D[name].ap()[l].rearrange("(k p) c -> p k c", p=128)
            kk = max(1, 1024 // n)
            for k0 in range(0, nk, kk):
                k1 = min(nk, k0 + kk)
                if k1 - k0 == 1:
                    load_cast(stg, wbuf[:, k0, dst0:dst0 + n], wbuf, src[:, k0, c0:c0 + n], [n])
                else:
                    load_cast(stg, wbuf[:, k0:k1, dst0:dst0 + n], wbuf, src[:, k0:k1, c0:c0 + n], [k1 - k0, n])

        def rstd_from_ss(ss, rstd, n):
            s.op('act', lambda e: e.activation(out=rstd[:], in_=ss[:], func=AF.Sqrt, bias=EPS, scale=1.0 / n), reads=[ss], writes=[rstd])
            s.op('dve', lambda e: e.reciprocal(out=rstd[:], in_=rstd[:]), reads=[rstd], writes=[rstd])

        def nmt_front(st_bufs, xt):
            junk, ss, rstd, xn = st_bufs
            s.op('act', lambda e: e.activation(out=junk[:], in_=xt[:], func=AF.Square, accum_out=ss[:]), reads=[xt], writes=[junk, ss])
            rstd_from_ss(ss, rstd, 1024)
            s.op('dve', lambda e: e.tensor_scalar(out=xn[:], in0=xt[:], scalar1=rstd[:, 0:1], scalar2=None, op0=ALU.mult),
                 reads=[xt, rstd], writes=[xn])

        def nmt_back(st_bufs, hT, t, s1, sh, col):
            junk, ss, rstd, xn = st_bufs
            pb = bank()
            pbb = pb.t[:].bitcast(BF16)
            for k in range(8):
                s.op('pe', lambda e, k=k: e.transpose(out=pbb[:, k * 128:(k + 1) * 128], in_=xn[:, k * 128:(k + 1) * 128], identity=ident_b[:]),
                     reads=[xn, ident_b], writes=[pb])
            for k in range(8):
                s.op('act', lambda e, k=k: e.activation(out=hT[:, k, tcols(t)], in_=pbb[:, k * 128:(k + 1) * 128], func=AF.Identity,
                                                       scale=s1[:, k, col:col + 1], bias=sh[:, k, col:col + 1]),
                     reads=[pb, s1, sh], writes=[(hT, t)])

        def norm_mod_transpose(st_bufs, xt, hT, t, s1, sh, col):
            nmt_front(st_bufs, xt)
            nmt_back(st_bufs, hT, t, s1, sh, col)

        for b in range(NB):
            s.dma(xs_d.ap()[b, 0:256, :], ctx_d.ap()[b], writes=[(XS, b)], g=0)
            s.dma(xs_d.ap()[b, 256:2304, :], x_d.ap()[b], writes=[(XS, b)], g=0)
            for l in LAYERS:
                P = LP[l]
                last = (l == LAYERS[-1])
                need_ctx = not last
                with contextlib.ExitStack() as shs:
                  hT = sbuf(shs, "hT", [128, 8, SEQ], BF16)
                  early = False
                  with contextlib.ExitStack() as sm:
                    mixT = sbuf(sm, "mixT", [128, 8, SEQ], BF16)
                    with contextlib.ExitStack() as st:
                        xt2 = [sbuf(st, f"xt{i}", [128, 1024]) for i in range(3)]
                        nbs = [(sbuf(st, "junk", [128, 1024], BF16), sbuf(st, "ssA", [128, 1]), sbuf(st, "rstdA", [128, 1]),
                                sbuf(st, "xnA", [128, 1024], BF16)) for _ in range(2)]
                        def frontA(t):
                            xt = xt2[t % 3]
                            s.dma(xt[:], xs_d.ap()[b, tcols(t), :], reads=[(XS, b)], writes=[xt], g=0)
                            nmt_front(nbs[t % 2], xt)
                        frontA(0)
                        for t in range(NT):
                            if t + 1 < NT:
                                frontA(t + 1)
                            nmt_back(nbs[t % 2], hT, t, P['s1a'], P['sha'], NB if t < 2 else b)
                        s.barrier()
                    CONST = dict(ident_b=ident_b, ident_f=ident_f, U_f=U_f, UT_f=UT_f, mf_f=mf_f, mb_f=mb_f, ones_f=ones_f,
                                 ropeC=ropeC, ropeS=ropeS)
                    ENV = dict(S5SCR=S5SCR, nc=nc, s=s, sbuf=sbuf, bank=bank, P=P, D=D, l=l, b=b, NB=NB, NCOL=NCOL, hT=hT, mixT=mixT,
                               load_w=load_w_bf16, mkstg=mkstg, load_cast=load_cast, rstd_from_ss=rstd_from_ss, tcols=tcols, PIECES=PIECES, C=CONST, DB=DB, stop=stop,
                               need_ctx=need_ctx, last=last, xs_d=xs_d, XS=XS, out_d=out_d, OUT=OUT, gsc_d=gsc_d, GSC=GSC)
                    if stop == 'A':
                        s.dma(DB['hT'].ap(), hT[:], reads=[hT], q='pool', g=5)
                        early = True
                    if not early:
                        with contextlib.ExitStack() as st:
                            ssd_stage(st, **ENV)
                            s.barrier()
                        early = stop is not None and stop.startswith('SSD')
                    if not early:
                        with contextlib.ExitStack() as st:
                            s5_stage(st, **ENV)
                            s.barrier()
                        early = stop is not None and stop.startswith('S5')
                        if 'mix2' in DB:
                            s.dma(DB['mix2'].ap(), mixT[:], reads=[mixT], q='pool', g=5)
                            s.barrier()
                    if not early:
                        with contextlib.ExitStack() as st:
                            attn_stage(st, **ENV)
                            s.barrier()
                        early = stop is not None and stop.startswith('ATT')
                    if stop is not None and stop[:2] in ('SS', 'S5', 'AT') and 'mixT' in DB:
                        s.dma(DB['mixT'].ap(), mixT[:], reads=[mixT], q='pool', g=5)
                    tiles = list(range(NT)) if need_ctx else list(range(2, NT))
                    if not early:
                      with contextlib.ExitStack() as st:
                        wo = sbuf(st, "wo", [128, 8, 1024], BF16)
                        stgE = mkstg(st)
                        load_w_bf16(stgE, wo, 'w_out', l, 0, 1024)
                        gbc = sbuf(st, "gbc", [128, 2, 1024])
                        s.dma(gbc[:, 0, :], bass.AP(gsc_d, ((l * 2 + 0) * NCOL + b) * 1024, [[0, 128], [1, 1024]]), reads=[GSC], writes=[gbc], g=0)
                        s.dma(gbc[:, 1, :], bass.AP(gsc_d, ((l * 2 + 0) * NCOL + NB) * 1024, [[0, 128], [1, 1024]]), reads=[GSC], writes=[gbc], g=0)
                        NXT = 5
                        xt2 = [sbuf(st, f"xtE{i}", [128, 1024]) for i in range(NXT)]
                        junk1 = [sbuf(st, "junkE1", [128, 1024], BF16) for _ in range(2)]
                        nbs = [(sbuf(st, "junkE", [128, 1024], BF16), sbuf(st, "ssE", [128, 1]), sbuf(st, "rstdE", [128, 1]),
                                sbuf(st, "xnE", [128, 1024], BF16)) for _ in range(2)]
                        ss2s = [sbuf(st, "ss2", [128, 2]) for _ in range(2)]
                        rs2s = [sbuf(st, "rs2", [128, 1]) for _ in range(2)]
                        ytmps = [sbuf(st, "ytmp", [128, 1024]) for _ in range(2)]
                        def ldE(i):
                            s.dma(xt2[i % NXT][:], xs_d.ap()[b, tcols(tiles[i]), :], reads=[(XS, b)], writes=[xt2[i % NXT]], g=0)
                        def s1E(ti):
                            t = tiles[ti]
                            if ti + 3 < len(tiles):
                                ldE(ti + 3)
                            xt = xt2[ti % NXT]
                            ss2, rs2, ytmp, jk = ss2s[ti % 2], rs2s[ti % 2], ytmps[ti % 2], junk1[ti % 2]
                            pbs = [bank(), bank()]
                            for hf in range(2):
                                for k in range(8):
                                    s.op('pe', lambda e, k=k, hf=hf: e.matmul(pbs[hf][:, :], lhsT=mixT[:, k, tcols(t)], rhs=wo[:, k, hf * 512:(hf + 1) * 512],
                                                                             start=(k == 0), stop=(k == 7)), reads=[mixT, wo], writes=[pbs[hf]])
                                s.op('act', lambda e, hf=hf: e.activation(out=jk[:, hf * 512:(hf + 1) * 512], in_=pbs[hf][:, :], func=AF.Square,
                                                                         accum_out=ss2[:, hf:hf + 1]), reads=[pbs[hf]], writes=[jk, ss2])
                            s.op('dve', lambda e: e.tensor_tensor(out=rs2[:], in0=ss2[:, 0:1], in1=ss2[:, 1:2], op=ALU.add), reads=[ss2], writes=[rs2])
                            rstd_from_ss(rs2, rs2, 1024)
                            gi = 1 if t < 2 else 0
                            for hf in range(2):
                                s.op('dve', lambda e, hf=hf, gi=gi: e.scalar_tensor_tensor(out=ytmp[:, hf * 512:(hf + 1) * 512], in0=pbs[hf][:, :], scalar=rs2[:, 0:1],
                                                                                        in1=gbc[:, gi, hf * 512:(hf + 1) * 512], op0=ALU.mult, op1=ALU.mult),
                                     reads=[pbs[hf], rs2, gbc], writes=[ytmp])
                            s.op('pool', lambda e, xt=xt: e.tensor_tensor(out=xt[:], in0=xt[:], in1=ytmp[:], op=ALU.add), reads=[xt, ytmp], writes=[xt])
                            s.dma(xs_d.ap()[b, tcols(t), :], xt[:], reads=[xt], writes=[(XS, b)], g=1)

                        def s2E(ti):
                            nmt_front(nbs[ti % 2], xt2[ti % NXT])

                        def s3E(ti):
                            t = tiles[ti]
                            nmt_back(nbs[ti % 2], hT, t, P['s1f'], P['shf'], NB if t < 2 else b)

                        nE = len(tiles)
                        for i0 in range(min(3, nE)):
                            ldE(i0)
                        s1E(0)
                        if nE > 1:
                            s1E(1)
                        s2E(0)
                        for ti in range(nE):
                            if ti + 2 < nE:
                                s1E(ti + 2)
                            if ti + 1 < nE:
                                s2E(ti + 1)
                            s3E(ti)
                        s.barrier()
                  if stop == 'E':
                    early = True
                  if not early:
                    with contextlib.ExitStack() as st:
                        ffn_stage(st, **ENV)
                        s.barrier()
                    if stop is not None and stop.startswith('F'):
                        early = True
                  if stop in ('E', 'L0'):
                    s.dma(DB['xs'].ap(), xs_d.ap()[b], reads=[(XS, b)], g=5)
                    if stop == 'E':
                        s.dma(DB['hT'].ap(), hT[:], reads=[hT], q='pool', g=5)
                    early = True
                  if early:
                    break
            if stop is not None:
                break
        s.barrier(final=True)
    return nc


def shard_inputs(inputs, NB, ncores):
    consts = host_consts()
    maps = []
    for i in range(ncores):
        m = {}
        m['x'] = np.ascontiguousarray(inputs['x'][i * NB:(i + 1) * NB], dtype=np.float32)
        m['ctx'] = np.ascontiguousarray(inputs['ctx'][i * NB:(i + 1) * NB], dtype=np.float32)
        m['cc'] = np.ascontiguousarray(np.concatenate([inputs['c'][i * NB:(i + 1) * NB], inputs['c_ctx'][None, :]], 0), dtype=np.float32)
        for n, sh in PARAMS:
            m[n] = np.ascontiguousarray(inputs[n], dtype=np.float32)
        m.update(consts)
        maps.append(m)
    return maps


def kernel(**inputs):
    inputs = {k: np.asarray(v) for k, v in inputs.items()}
    NB = 4
    nc = build(NB=NB, LAYERS=(0, 1))
    maps = shard_inputs(inputs, NB, 8)
    res = run_bass_kernel_spmd(nc, maps, core_ids=list(range(8)))
    return np.concatenate([r["out"] for r in res.results], axis=0).astype(np.float32)
```

```python
import contextlib
import math
import numpy as np
import concourse.bass as bass
import concourse.mybir as mybir
from concourse.bass_utils import run_bass_kernel_spmd

F32 = mybir.dt.float32
BF16 = mybir.dt.bfloat16
AF = mybir.ActivationFunctionType
ALU = mybir.AluOpType

ENGS = ('pe', 'act', 'dve', 'pool', 'sp')
SAME_ENGINE_SYNC = ('act', 'dve', 'pool')
NDSEM = 6
EPS = 1e-6
NT = 18
SEQ = 2304
TWO_PI = 2.0 * math.pi


class Buf:
    def __init__(self, t, name, psum=False):
        self.t = t
        self.name = name
        self.psum = psum
        self.w = {}
        self.r = {}

    def _keys(self, key):
        if key is None:
            return list(set(self.w) | set(self.r) | {None})
        return [key, None]

    def wdeps(self, key):
        d = set()
        for k in self._keys(key):
            d |= self.w.get(k, set())
        return d

    def rdeps(self, key):
        d = set()
        for k in self._keys(key):
            d |= set(self.r.get(k, {}).items())
        return d

    def add_reader(self, key, ev):
        rr = self.r.setdefault(key, {})
        if rr.get(ev[0], 0) < ev[1]:
            rr[ev[0]] = ev[1]

    def set_writer(self, key, ev):
        if key is None:
            self.w = {None: {ev}}
            self.r = {}
        else:
            self.w[key] = {ev}
            self.r[key] = {}

    def add_dma_writer(self, key, ev):
        cur = self.w.get(key, set())
        cur = {e for e in cur if e[0][1] == ev[0][1] and e[0][0].startswith('q') and not (e[0] == ev[0] and e[1] <= ev[1])}
        cur.add(ev)
        if key is None:
            self.w = {None: cur}
            self.r = {}
        else:
            self.w[key] = cur
            self.r[key] = {}

    def __getitem__(self, idx):
        return self.t[idx]


def _norm(lst):
    return [(x, None) if isinstance(x, Buf) else x for x in lst]


class Sched:
    NQ = {'sp': 14, 'pool': 3}

    def __init__(self, nc, stack):
        self.nc = nc
        self.stack = stack
        self.eng = {'pe': nc.tensor, 'act': nc.scalar, 'dve': nc.vector, 'pool': nc.gpsimd, 'sp': nc.sync}
        self.phase = 0
        self.ninst = 0
        self.sets = []
        for i in range(2):
            d = {}
            for e in ENGS:
                d[e] = stack.enter_context(nc.semaphore(f"s_{e}_{i}"))
            for q, n in self.NQ.items():
                for j in range(n):
                    d[f"q{q}{j}"] = stack.enter_context(nc.semaphore(f"s_q{q}{j}_{i}"))
            self.sets.append(d)
        self.bsems = [stack.enter_context(nc.semaphore(f"s_bar{i}")) for i in range(2)]
        self.nbar = 0
        self._new_sems()

    def _new_sems(self):
        self.sem = self.sets[self.phase % 2]
        self.cnt = {k: 0 for k in self.sem}
        self.seen = {e: {} for e in ENGS}
        self.rr = {q: 0 for q in self.NQ}

    def _wait(self, e, deps):
        need = {}
        for (k, v) in deps:
            if k[1] != self.phase:
                continue
            if need.get(k, 0) < v:
                need[k] = v
        for k, v in need.items():
            if self.seen[e].get(k, 0) >= v:
                continue
            self.eng[e].wait_ge(self.sem[k[0]], v)
            self.seen[e][k] = v
            self.ninst += 1

    def op(self, e, fn, reads=(), writes=()):
        reads = _norm(reads)
        writes = _norm(writes)
        writes = writes + [r for r in reads if r[0].psum]
        reads = [r for r in reads if not r[0].psum]
        deps = set()
        for b, k in reads:
            deps |= b.wdeps(k)
        for b, k in writes:
            deps |= b.wdeps(k)
            for ev in b.rdeps(k):
                if ev[0][0] != e:
                    deps.add(ev)
        if e not in SAME_ENGINE_SYNC:
            deps = {d for d in deps if d[0][0] != e}
        self._wait(e, deps)
        inst = fn(self.eng[e])
        self.cnt[e] += 1
        inst.then_inc(self.sem[e], 1)
        self.ninst += 1
        ev = ((e, self.phase), self.cnt[e])
        for b, k in reads:
            b.add_reader(k, ev)
        for b, k in writes:
            b.set_writer(k, ev)
        return inst

    def dma(self, out, in_, reads=(), writes=(), q='sp', g=0, **kw):
        reads = _norm(reads)
        writes = _norm(writes)
        deps = set()
        for b, k in reads:
            deps |= b.wdeps(k)
        for b, k in writes:
            same = b.w.get(k, set())
            deps |= {ev for ev in b.wdeps(k) if not (ev in same and ev[0][0].startswith('q'))} | b.rdeps(k)
        self._wait(q, deps)
        sk = f"q{q}{self.rr[q] % self.NQ[q]}"
        self.rr[q] += 1
        kk = (sk, self.phase)
        if self.cnt[sk] > 0 and self.seen[q].get(kk, 0) < self.cnt[sk]:
            self.eng[q].wait_ge(self.sem[sk], self.cnt[sk])
            self.seen[q][kk] = self.cnt[sk]
        inst = self.eng[q].dma_start(out=out, in_=in_, **kw)
        self.cnt[sk] += 16
        inst.then_inc(self.sem[sk], 16)
        self.ninst += 1
        ev = (kk, self.cnt[sk])
        for b, k in reads:
            b.add_reader(k, ev)
        for b, k in writes:
            b.add_dma_writer(k, ev)
        return inst

    def barrier(self, final=False):
        bsem = self.bsems[self.nbar % 2]
        other = self.bsems[(self.nbar + 1) % 2]
        self.nbar += 1
        for k in self.sem:
            if k.startswith('q') and self.cnt[k] > 0:
                self.eng['sp'].wait_ge(self.sem[k], self.cnt[k])
        for e in ENGS:
            if self.cnt[e] > 0:
                self.eng[e].wait_ge(self.sem[e], self.cnt[e])
            self.eng[e].sem_inc(bsem, 1)
        self.eng['sp'].wait_ge(bsem, len(ENGS))
        if not final:
            for k, sm in self.sets[(self.phase + 1) % 2].items():
                self.eng['sp'].sem_clear(sm)
            self.eng['sp'].sem_clear(other)
        self.eng['sp'].sem_inc(bsem, 1)
        for e in ENGS:
            self.eng[e].wait_ge(bsem, len(ENGS) + 1)
        if not final:
            self.phase += 1
            self._new_sems()


PARAMS = [
    ('mod_w', (2, 1024, 6144)), ('mod_b', (2, 6144)), ('mix_norm_pre', (2, 1024)), ('mix_norm_post', (2, 1024)),
    ('ffn_norm_pre', (2, 1024)), ('ffn_norm_post', (2, 1024)), ('w_in', (2, 1024, 2568)), ('w_out', (2, 1024, 1024)),
    ('ssd_conv_w', (2, 5, 512)), ('ssd_conv_b', (2, 512)), ('ssd_dt_bias', (2, 2, 4)), ('ssd_a_log', (2, 2, 4)),
    ('ssd_d', (2, 4)), ('ssd_norm_w', (2, 256)), ('diff_lam_q1', (2, 64)), ('diff_lam_k1', (2, 64)),
    ('diff_lam_q2', (2, 64)), ('diff_lam_k2', (2, 64)), ('diff_subln_w', (2, 128)),
    ('s5_lam_re', (2, 2, 16, 64)), ('s5_lam_im', (2, 2, 16, 64)), ('s5_log_step', (2, 2, 16)),
    ('s5_b_re', (2, 2, 16, 64, 16)), ('s5_b_im', (2, 2, 16, 64, 16)), ('s5_c_re', (2, 2, 16, 16, 64)),
    ('s5_c_im', (2, 2, 16, 16, 64)), ('s5_d', (2, 16, 16)), ('s5_glu_w', (2, 256, 256)), ('s5_glu_b', (2, 256)),
    ('ffn_w_gate', (2, 1024, 2816)), ('ffn_w_up', (2, 1024, 2816)), ('ffn_conv_w', (2, 3, 2816)),
    ('ffn_conv_b', (2, 2816)), ('ffn_w_down', (2, 2816, 1024)),
]


def host_consts():
    r = np.arange(128)
    U = (r[:, None] <= r[None, :]).astype(np.float32)
    c = {}
    c['c_ident'] = np.eye(128, dtype=np.float32)
    c['c_U'] = U
    c['c_UT'] = np.ascontiguousarray(U.T)
    c['c_mf'] = np.where(r[None, :] >= r[:, None], 0.0, -30000.0).astype(np.float32)
    c['c_mb'] = np.where(r[None, :] <= r[:, None], 0.0, -30000.0).astype(np.float32)
    c['c_ones'] = np.ones((128, 128), np.float32)
    t = np.arange(2048)
    rr = (t // 64).astype(np.float32)
    col = (t % 64).astype(np.float32)
    inv = (10000.0 ** (-np.arange(16, dtype=np.float32) / 16)).astype(np.float32)
    ar = rr[:, None] * inv[None, :]
    ac = col[:, None] * inv[None, :]
    c['c_cos'] = np.concatenate([np.cos(ar), np.cos(ac)], 1).astype(np.float32)
    c['c_sin'] = np.concatenate([np.sin(ar), np.sin(ac)], 1).astype(np.float32)
    c['c_iota'] = np.tile(np.arange(256, dtype=np.float32)[None, :], (128, 1))
    return c


import os
_CUT = int(os.environ.get('SSDCUT', '0'))


def ssd_stage(st, nc, s, sbuf, bank, P, D, l, hT, mixT, load_w, rstd_from_ss, tcols, PIECES, C, DB, stop, mkstg, **_):
    wzd = sbuf(st, "wzd", [128, 8, 264], BF16)
    wx = sbuf(st, "wx", [128, 8, 512], BF16)
    with contextlib.ExitStack() as s0:
        stg = mkstg(s0)
        load_w(stg, wzd, 'w_in', l, 0, 256, 0)
        load_w(stg, wzd, 'w_in', l, 768, 8, 256)
        load_w(stg, wx, 'w_in', l, 256, 512, 0)
        s.barrier()
    if stop == 'SSD0':
        return
    zs = sbuf(st, "zs", [128, NT, 256], BF16)
    dtr = sbuf(st, "dtr", [128, NT, 8])
    xbcT = sbuf(st, "xbcT", [128, 4, SEQ], BF16)
    xbc = sbuf(st, "xbc", [128, NT, 512], BF16)
    yacc = sbuf(st, "yacc", [128, NT, 256])
    for t in range(NT):
        pb = bank()
        for k in range(8):
            s.op('pe', lambda e, k=k: e.matmul(pb[:, 0:264], lhsT=hT[:, k, tcols(t)], rhs=wzd[:, k, :], start=(k == 0), stop=(k == 7)),
                 reads=[(hT, t), wzd], writes=[pb])
        s.op('act', lambda e: e.activation(out=zs[:, t, :], in_=pb[:, 0:256], func=AF.Silu), reads=[pb], writes=[(zs, t)])
        s.op('dve', lambda e: e.tensor_copy(out=dtr[:, t, :], in_=pb[:, 256:264]), reads=[pb], writes=[dtr])
    if stop == 'SSD1':
        return
    with contextlib.ExitStack() as sx:
        xraw = sbuf(sx, "xraw", [128, 2312])
        xacc = sbuf(sx, "xacc", [128, 2312])
        s.op('pool', lambda e: e.memset(xraw[:], 0.0), writes=[xraw])
        for c in range(4):
            for (c0, c1) in PIECES:
                pb = bank()
                n = c1 - c0
                for k in range(8):
                    s.op('pe', lambda e, k=k: e.matmul(pb[:, 0:n], lhsT=wx[:, k, c * 128:(c + 1) * 128], rhs=hT[:, k, c0:c1], start=(k == 0), stop=(k == 7)),
                         reads=[hT, wx], writes=[pb])
                off = 2 if c0 == 0 else 6
                s.op('act', lambda e: e.copy(out=xraw[:, off + c0:off + c1], in_=pb[:, 0:n]), reads=[pb], writes=[xraw])
            s.op('act', lambda e: e.activation(out=xacc[:, 2:2310], in_=xraw[:, 2:2310], func=AF.Identity, scale=P['scw'][:, c, 2:3], bias=P['scb'][:, c:c + 1]),
                 reads=[xraw, P['scw'], P['scb']], writes=[xacc])
            for j in (0, 1, 3, 4):
                s.op('dve', lambda e, j=j: e.scalar_tensor_tensor(out=xacc[:, 2:2310], in0=xraw[:, j:j + 2308], scalar=P['scw'][:, c, j:j + 1],
                                                                 in1=xacc[:, 2:2310], op0=ALU.mult, op1=ALU.add),
                     reads=[xraw, xacc, P['scw']], writes=[xacc])
            s.op('act', lambda e: e.activation(out=xbcT[:, c, 0:256], in_=xacc[:, 2:258], func=AF.Silu), reads=[xacc], writes=[xbcT])
            s.op('act', lambda e: e.activation(out=xbcT[:, c, 256:2304], in_=xacc[:, 262:2310], func=AF.Silu), reads=[xacc], writes=[xbcT])

        s.barrier()
    if stop == 'SSD2':
        return
    for t in range(NT):
        pb = bank()
        pbb = pb.t[:].bitcast(BF16)
        for c in range(4):
            s.op('pe', lambda e, c=c: e.transpose(out=pbb[:, c * 128:(c + 1) * 128], in_=xbcT[:, c, tcols(t)], identity=C['ident_b'][:]),
                 reads=[xbcT, C['ident_b']], writes=[pb])
        s.op('act', lambda e: e.copy(out=xbc[:, t, :], in_=pbb[:, 0:512]), reads=[pb], writes=[(xbc, t)])
    if stop == 'SSD3':
        return
    dt = sbuf(st, "dt", [128, NT, 8])
    dta = sbuf(st, "dta", [128, NT, 8])
    tA = sbuf(st, "tA", [128, NT, 8])
    tB = sbuf(st, "tB", [128, NT, 8])

    def B8(tb):
        return tb[:].unsqueeze(1).to_broadcast([128, NT, 8])
    s.op('dve', lambda e: e.tensor_tensor(out=dtr[:], in0=dtr[:], in1=B8(P['dtb']), op=ALU.add), reads=[dtr, P['dtb']], writes=[dtr])
    s.op('dve', lambda e: e.tensor_scalar(out=tB[:], in0=dtr[:], scalar1=-1.0, scalar2=None, op0=ALU.mult), reads=[dtr], writes=[tB])
    s.op('dve', lambda e: e.tensor_tensor(out=tA[:], in0=dtr[:], in1=tB[:], op=ALU.max), reads=[dtr, tB], writes=[tA])
    s.op('act', lambda e: e.activation(out=tA[:], in_=tA[:], func=AF.Exp, scale=-1.0), reads=[tA], writes=[tA])
    s.op('act', lambda e: e.activation(out=tA[:], in_=tA[:], func=AF.Ln, bias=1.0, scale=1.0), reads=[tA], writes=[tA])
    s.op('dve', lambda e: e.tensor_single_scalar(out=tB[:], in_=dtr[:], scalar=0.0, op=ALU.max), reads=[dtr], writes=[tB])
    s.op('dve', lambda e: e.tensor_tensor(out=dt[:], in0=tA[:], in1=tB[:], op=ALU.add), reads=[tA, tB], writes=[dt])
    s.op('dve', lambda e: e.tensor_tensor(out=dta[:], in0=dt[:], in1=B8(P['aneg']), op=ALU.mult), reads=[dt, P['aneg']], writes=[dta])
    if stop == 'SSD4':
        return
    for t in range(NT):
        s.op('dve', lambda e, t=t: e.tensor_tensor(out=yacc[:, t, :].rearrange("p (h d) -> p h d", h=4),
                                                  in0=xbc[:, t, 0:256].rearrange("p (h d) -> p h d", h=4),
                                                  in1=P['dsk'][:].unsqueeze(2).to_broadcast([128, 4, 64]), op=ALU.mult),
             reads=[(xbc, t), P['dsk']], writes=[(yacc, t)])
    if stop == 'SSD5':
        return
    def two(name, shape, dt=F32):
        return [sbuf(st, f"{name}{i}", shape, dt) for i in range(2)]
    sm8s, nacss, eacss, wdecs, etots = two("sm8", [128, 8]), two("nacs", [128, 4]), two("eacs", [128, 4]), two("wdec", [128, 4]), two("etot", [128, 4])
    A1 = sbuf(st, "A1", [128, 4, 128])
    A1hs, A1ls = two("A1h", [128, 4, 128], BF16), two("A1l", [128, 4, 128], BF16)
    Ub = {}
    for nm in ['U_f', 'UT_f', 'mf_f', 'mb_f']:
        Ub[nm] = sbuf(st, nm + "_b", [128, 128], BF16)
        s.op('act', lambda e, nm=nm: e.copy(out=Ub[nm][:], in_=C[nm][:]), reads=[C[nm]], writes=[Ub[nm]])
    decTs = two("decT", [128, 4, 128])
    scTs = two("scTs", [128, 4, 128], BF16)
    xdts, xdtws = two("xdt", [128, 4, 64], BF16), two("xdtw", [128, 4, 64], BF16)
    state = sbuf(st, "state", [128, 2, 64])
    stateb = sbuf(st, "stateb", [128, 2, 64], BF16)
    ones3 = sbuf(st, "ones3", [128, 4, 128])
    s.op('pool', lambda e: e.memset(ones3[:], 1.0), writes=[ones3])
    seq = []
    for d in range(2):
        order = list(range(NT)) if d == 0 else [1, 0] + list(range(NT - 1, 1, -1))
        for oi, t in enumerate(order):
            seq.append((d, t, oi == 0))

    def front(i):
        d, t, first = seq[i]
        k2 = i % 2
        sm8, nacs, eacs, wdec, etot = sm8s[k2], nacss[k2], eacss[k2], wdecs[k2], etots[k2]
        A1h, A1l, decT, scT, xdt, xdtw = A1hs[k2], A1ls[k2], decTs[k2], scTs[k2], xdts[k2], xdtws[k2]
        Um = C['U_f'] if d == 0 else C['UT_f']
        Umb = Ub['U_f'] if d == 0 else Ub['UT_f']
        Mnb = Ub['mf_f'] if d == 0 else Ub['mb_f']
        dta_c = dta[:, t, d * 4:(d + 1) * 4]
        dt_c = dt[:, t, d * 4:(d + 1) * 4]
        pbs = bank(0)
        s.op('pe', lambda e: e.matmul(pbs[:, 0:4], lhsT=Um[:], rhs=dta_c, start=True, stop=True), reads=[Um, dta], writes=[pbs])
        s.op('pe', lambda e: e.matmul(pbs[:, 4:8], lhsT=C['ones_f'][:], rhs=dta_c, start=True, stop=True), reads=[C['ones_f'], dta], writes=[pbs])
        s.op('act', lambda e: e.copy(out=sm8[:], in_=pbs[:, 0:8]), reads=[pbs], writes=[sm8])
        s.op('dve', lambda e: e.tensor_scalar(out=nacs[:], in0=sm8[:, 0:4], scalar1=-1.0, scalar2=None, op0=ALU.mult), reads=[sm8], writes=[nacs])
        s.op('act', lambda e: e.activation(out=eacs[:], in_=sm8[:, 0:4], func=AF.Exp), reads=[sm8], writes=[eacs])
        s.op('dve', lambda e: e.tensor_tensor(out=wdec[:], in0=sm8[:, 4:8], in1=sm8[:, 0:4], op=ALU.subtract), reads=[sm8], writes=[wdec])
        s.op('act', lambda e: e.activation(out=wdec[:], in_=wdec[:], func=AF.Exp), reads=[wdec], writes=[wdec])
        s.op('act', lambda e: e.activation(out=etot[:], in_=sm8[:, 4:8], func=AF.Exp), reads=[sm8], writes=[etot])
        s.op('dve', lambda e: e.tensor_tensor(out=A1[:], in0=ones3[:], in1=dta_c.unsqueeze(2).to_broadcast([128, 4, 128]), op=ALU.mult),
             reads=[ones3, dta], writes=[A1])
        s.op('act', lambda e: e.copy(out=A1h[:], in_=A1[:]), reads=[A1], writes=[A1h])
        s.op('dve', lambda e: e.tensor_tensor(out=A1l[:], in0=A1[:], in1=A1h[:], op=ALU.subtract), reads=[A1, A1h], writes=[A1l])
        pD = bank(1)
        for h in range(4):
            s.op('pe', lambda e, h=h: e.matmul(pD[:, h * 128:(h + 1) * 128], lhsT=A1h[:, h, :], rhs=Umb[:], start=True, stop=False),
                 reads=[A1h, Umb], writes=[pD])
            s.op('pe', lambda e, h=h: e.matmul(pD[:, h * 128:(h + 1) * 128], lhsT=A1l[:, h, :], rhs=Umb[:], start=False, stop=False),
                 reads=[A1l, Umb], writes=[pD])
            s.op('pe', lambda e, h=h: e.matmul(pD[:, h * 128:(h + 1) * 128], lhsT=C['ident_b'][:], rhs=Mnb[:], start=False, stop=True),
                 reads=[C['ident_b'], Mnb], writes=[pD])
        for h in range(4):
            s.op('act', lambda e, h=h: e.activation(out=decT[:, h, :], in_=pD[:, h * 128:(h + 1) * 128], func=AF.Exp, bias=nacs[:, h:h + 1], scale=1.0),
                 reads=[pD, nacs], writes=[(decT, h)])
        pGs = [bank(2), bank(3)]
        for g in range(2):
            s.op('pe', lambda e, g=g: e.matmul(pGs[g][:, 0:128], lhsT=xbcT[64 * g:64 * g + 64, 2, tcols(t)],
                                              rhs=xbcT[64 * g:64 * g + 64, 3, tcols(t)], start=True, stop=True), reads=[xbcT], writes=[pGs[g]])
        for h in range(4):
            g = h // 2
            s.op('dve', lambda e, h=h, g=g: e.tensor_tensor(out=scT[:, h, :], in0=pGs[g][:, 0:128], in1=decT[:, h, :], op=ALU.mult),
                 reads=[pGs[g], (decT, h)], writes=[(scT, h)])
        xs4 = xbc[:, t, 0:256].rearrange("p (h d) -> p h d", h=4)
        s.op('dve', lambda e: e.tensor_tensor(out=xdt[:], in0=xs4, in1=dt_c.unsqueeze(2).to_broadcast([128, 4, 64]), op=ALU.mult),
             reads=[(xbc, t), dt], writes=[xdt])
        s.op('dve', lambda e: e.tensor_tensor(out=xdtw[:], in0=xdt[:], in1=wdec[:].unsqueeze(2).to_broadcast([128, 4, 64]), op=ALU.mult),
             reads=[xdt, wdec], writes=[xdtw])
        pY = bank(4)
        for h in range(4):
            s.op('pe', lambda e, h=h: e.matmul(pY[:, h * 64:(h + 1) * 64], lhsT=scT[:, h, :], rhs=xdt[:, h, :], start=True, stop=True),
                 reads=[(scT, h), xdt], writes=[pY])
        s.op('dve', lambda e: e.tensor_tensor(out=yacc[:, t, :], in0=yacc[:, t, :], in1=pY[:, 0:256], op=ALU.add), reads=[pY, (yacc, t)], writes=[(yacc, t)])

    def back(i):
        d, t, first = seq[i]
        k2 = i % 2
        eacs, etot, xdtw = eacss[k2], etots[k2], xdtws[k2]
        if first:
            s.op('pool', lambda e: e.memset(state[:], 0.0), writes=[state])
            s.op('pool', lambda e: e.memset(stateb[:], 0.0), writes=[stateb])
        pOs = [bank(5), bank(6)]
        for h in range(4):
            g, hh = h // 2, h % 2
            s.op('pe', lambda e, h=h, g=g, hh=hh: e.matmul(pOs[g][:, hh * 64:(hh + 1) * 64], lhsT=xbcT[64 * g:64 * g + 64, 3, tcols(t)],
                                                          rhs=stateb[64 * g:64 * g + 64, hh, :], start=True, stop=True),
                 reads=[xbcT, stateb], writes=[pOs[g]])
        pS = bank(7)
        s.op('pe', lambda e: e.matmul(pS[:, 0:256], lhsT=xbc[:, t, 256:384], rhs=xdtw[:].rearrange("p h d -> p (h d)"), start=True, stop=True),
             reads=[(xbc, t), xdtw], writes=[pS])
        for g in range(2):
            for hh in range(2):
                h = 2 * g + hh
                s.op('dve', lambda e, g=g, hh=hh, h=h: e.scalar_tensor_tensor(
                    out=state[64 * g:64 * g + 64, hh, :], in0=state[64 * g:64 * g + 64, hh, :], scalar=etot[64 * g:64 * g + 64, h:h + 1],
                    in1=pS[64 * g:64 * g + 64, h * 64:(h + 1) * 64], op0=ALU.mult, op1=ALU.add), reads=[state, etot, pS], writes=[state])
        s.op('act', lambda e: e.copy(out=stateb[:], in_=state[:]), reads=[state], writes=[stateb])
        for h in range(4):
            g, hh = h // 2, h % 2
            s.op('dve', lambda e, h=h, g=g, hh=hh: e.scalar_tensor_tensor(out=yacc[:, t, h * 64:(h + 1) * 64], in0=pOs[g][:, hh * 64:(hh + 1) * 64],
                                                             scalar=eacs[:, h:h + 1], in1=yacc[:, t, h * 64:(h + 1) * 64], op0=ALU.mult, op1=ALU.add),
                 reads=[pOs[g], eacs, (yacc, t)], writes=[(yacc, t)])

    front(0)
    for i in range(len(seq)):
        if i + 1 < len(seq):
            front(i + 1)
        back(i)
    if stop == 'SSD6':
        return
    gt = sbuf(st, "gt", [128, 256])
    gj = sbuf(st, "gj", [128, 256])
    gb = sbuf(st, "gb", [128, 256], BF16)
    ssg = sbuf(st, "ssg", [128, 1])
    for t in range(NT):
        s.op('dve', lambda e: e.tensor_tensor(out=gt[:], in0=yacc[:, t, :], in1=zs[:, t, :], op=ALU.mult), reads=[(yacc, t), (zs, t)], writes=[gt])
        s.op('act', lambda e: e.activation(out=gj[:], in_=gt[:], func=AF.Square, accum_out=ssg[:]), reads=[gt], writes=[gj, ssg])
        rstd_from_ss(ssg, ssg, 256)
        s.op('dve', lambda e: e.scalar_tensor_tensor(out=gb[:], in0=gt[:], scalar=ssg[:, 0:1], in1=P['snw'][:], op0=ALU.mult, op1=ALU.mult),
             reads=[gt, ssg, P['snw']], writes=[gb])
        pb = bank()
        pbb = pb.t[:].bitcast(BF16)
        for c in range(2):
            s.op('pe', lambda e, c=c: e.transpose(out=pbb[:, c * 128:(c + 1) * 128], in_=gb[:, c * 128:(c + 1) * 128], identity=C['ident_b'][:]),
                 reads=[gb, C['ident_b']], writes=[pb])
        s.op('act', lambda e: e.copy(out=mixT[:, 0:2, tcols(t)], in_=pbb[:, 0:256].rearrange("p (c n) -> p c n", c=2)), reads=[pb], writes=[(mixT, ('ssd', t))])
import os


def s5_stage(st, nc, s, sbuf, bank, P, D, l, hT, mixT, load_w, PIECES, DB, stop, S5SCR, mkstg, **_):
    P = dict(P)
    P['WB'] = sbuf(st, "WBs", [128, 2, 8, 2, 128], BF16)
    P['WC'] = sbuf(st, "WCs", [128, 2, 8, 2, 128])
    P['cosT'] = sbuf(st, "cosTs", [128, 16, 256])
    P['sinT'] = sbuf(st, "sinTs", [128, 16, 256])
    for nm in ['WB', 'WC', 'cosT', 'sinT']:
        dd, db = S5SCR[(nm, l)]
        dst = P[nm][:].rearrange("p a b c d -> p (a b c d)") if nm in ('WB', 'WC') else P[nm][:].rearrange("p a b -> p (a b)")
        s.dma(dst, dd.ap(), reads=[db], writes=[P[nm]], g=2)
    wu = sbuf(st, "wu", [128, 8, 256], BF16)
    with contextlib.ExitStack() as s0:
        load_w(mkstg(s0), wu, 'w_in', l, 2312, 256, 0)
        s.barrier()
    uT = sbuf(st, "uT", [128, 2, SEQ], BF16)
    yT = sbuf(st, "yT", [128, 2, SEQ])
    for c in range(2):
        for (c0, c1) in PIECES:
            pb = bank()
            n = c1 - c0
            for k in range(8):
                s.op('pe', lambda e, k=k: e.matmul(pb[:, 0:n], lhsT=wu[:, k, c * 128:(c + 1) * 128], rhs=hT[:, k, c0:c1], start=(k == 0), stop=(k == 7)),
                     reads=[hT, wu], writes=[pb])
            s.op('act', lambda e: e.copy(out=uT[:, c, c0:c1], in_=pb[:, 0:n]), reads=[pb], writes=[uT])
            s.op('dve', lambda e: e.tensor_scalar(out=yT[:, c, c0:c1], in0=pb[:, 0:n], scalar1=P['s5d'][:, c:c + 1], scalar2=None, op0=ALU.mult),
                 reads=[pb, P['s5d']], writes=[(yT, c)])
    T = 256
    NCH = SEQ // T
    bs = [sbuf(st, f"s5b{i}", [128, 2, T]) for i in range(2)]
    gin = [sbuf(st, f"s5g{i}", [128, 2, T]) for i in range(2)]
    tt = [sbuf(st, f"s5t{i}", [128, 4, T]) for i in range(2)]
    gs = [sbuf(st, f"s5s{i}", [128, 2, T]) for i in range(2)]
    hh_ = [sbuf(st, f"s5h{i}", [128, 2, T]) for i in range(2)]
    t2 = tt
    ini = [sbuf(st, f"s5c{q}", [128, 2]) for q in range(8)]
    ctmp = [sbuf(st, f"s5ct{i}", [128, 2]) for i in range(2)]
    nsin1 = sbuf(st, "s5nsin1", [128, 16])
    s.op('dve', lambda e: e.tensor_scalar(out=nsin1[:], in0=P['sinT'][:, :, 1], scalar1=-1.0, scalar2=None, op0=ALU.mult), reads=[P['sinT']], writes=[nsin1])
    items = []
    for d in range(2):
        order = list(range(NCH)) if d == 0 else [0] + list(range(NCH - 1, 0, -1))
        for kt in range(2):
            for ci, ch in enumerate(order):
                for ql in range(4):
                    items.append((d, kt, ci, ch, ql, len(order)))
    pYs = {}

    def front(i):
        d, kt, ci, ch, ql, nord = items[i]
        q = kt * 4 + ql
        j = d * 8 + q
        i2 = i % 2
        B_, G_, T_ = bs[i2], gin[i2], tt[i2]
        ucols = uT[:, kt, ch * T:(ch + 1) * T] if d == 0 else uT[:, kt, ch * T:(ch + 1) * T][:, ::-1]
        cosj = P['cosT'][:, j, :]
        sinj = P['sinT'][:, j, :]
        pB = bank(2 + (i % 6))
        for ri in range(2):
            s.op('pe', lambda e, ri=ri: e.matmul(pB[:, ri * T:(ri + 1) * T], lhsT=P['WB'][:, d, q, ri, :], rhs=ucols, start=True, stop=True),
                 reads=[uT, P['WB']], writes=[pB])
        s.op('act', lambda e: e.copy(out=B_[:].rearrange("p a t -> p (a t)"), in_=pB[:, 0:2 * T]), reads=[pB], writes=[B_])
        TTp = lambda o, a, b_, op, rd, wr: s.op('pool', lambda e: e.tensor_tensor(out=o, in0=a, in1=b_, op=op), reads=rd, writes=wr)
        TTp(T_[:, 0, :], B_[:, 0, :], cosj, ALU.mult, [B_, P['cosT']], [(T_, 0)])
        TTp(T_[:, 1, :], B_[:, 1, :], sinj, ALU.mult, [B_, P['sinT']], [(T_, 1)])
        TTp(T_[:, 2, :], B_[:, 1, :], cosj, ALU.mult, [B_, P['cosT']], [(T_, 2)])
        TTp(T_[:, 3, :], B_[:, 0, :], sinj, ALU.mult, [B_, P['sinT']], [(T_, 3)])
        TTp(G_[:, 0, :], T_[:, 0, :], T_[:, 1, :], ALU.add, [(T_, 0), (T_, 1)], [(G_, 0)])
        TTp(G_[:, 1, :], T_[:, 2, :], T_[:, 3, :], ALU.subtract, [(T_, 2), (T_, 3)], [(G_, 1)])

    def back(i):
        d, kt, ci, ch, ql, nord = items[i]
        q = kt * 4 + ql
        j = d * 8 + q
        i2 = i % 2
        G_, S_, H_, U_ = gin[i2], gs[i2], hh_[i2], t2[i2]
        cosj = P['cosT'][:, j, :]
        sinj = P['sinT'][:, j, :]
        if ql == 0:
            pYs['cur'] = bank(len(pYs.setdefault('n', [])) % 2)
            pYs['n'].append(0)
        pY = pYs['cur']
        rb = P['s5r'][:, j:j + 1].to_broadcast([128, T])
        for ri in range(2):
            init = ini[q][:, ri:ri + 1] if ci > 0 else 0.0
            s.op('dve', lambda e, ri=ri, init=init: e.tensor_tensor_scan(out=S_[:, ri, :], data0=rb, data1=G_[:, ri, :], initial=init, op0=ALU.mult, op1=ALU.add),
                 reads=[(G_, ri), P['s5r'], ini[q]], writes=[(S_, ri)])
        TTd = lambda o, a, b_, op, rd, wr: s.op('dve', lambda e: e.tensor_tensor(out=o, in0=a, in1=b_, op=op), reads=rd, writes=wr)
        TTd(U_[:, 0, :], S_[:, 0, :], cosj, ALU.mult, [(S_, 0), P['cosT']], [(U_, 0)])
        TTd(U_[:, 1, :], S_[:, 1, :], sinj, ALU.mult, [(S_, 1), P['sinT']], [(U_, 1)])
        TTd(U_[:, 2, :], S_[:, 0, :], sinj, ALU.mult, [(S_, 0), P['sinT']], [(U_, 2)])
        TTd(U_[:, 3, :], S_[:, 1, :], cosj, ALU.mult, [(S_, 1), P['cosT']], [(U_, 3)])
        TTd(H_[:, 0, :], U_[:, 0, :], U_[:, 1, :], ALU.subtract, [(U_, 0), (U_, 1)], [(H_, 0)])
        TTd(H_[:, 1, :], U_[:, 2, :], U_[:, 3, :], ALU.add, [(U_, 2), (U_, 3)], [(H_, 1)])
        if ci < nord - 1:
            c1 = P['cosT'][:, j, 1:2]
            s1_ = P['sinT'][:, j, 1:2]
            ns1 = nsin1[:, j:j + 1]
            ct = ctmp[i2]
            hre_l = H_[:, 0, T - 1:T]
            him_l = H_[:, 1, T - 1:T]
            s.op('act', lambda e: e.activation(out=ct[:, 0:1], in_=him_l, func=AF.Copy, scale=ns1), reads=[(H_, 1), nsin1], writes=[ct])
            s.op('act', lambda e: e.activation(out=ct[:, 1:2], in_=hre_l, func=AF.Copy, scale=s1_), reads=[(H_, 0), P['sinT']], writes=[ct])
            s.op('act', lambda e: e.activation(out=ini[q][:, 0:1], in_=hre_l, func=AF.Identity, scale=c1, bias=ct[:, 0:1]),
                 reads=[(H_, 0), ct, P['cosT']], writes=[ini[q]])
            s.op('act', lambda e: e.activation(out=ini[q][:, 1:2], in_=him_l, func=AF.Identity, scale=c1, bias=ct[:, 1:2]),
                 reads=[(H_, 1), ct, P['cosT']], writes=[ini[q]])
        for ri in range(2):
            s.op('pe', lambda e, ri=ri: e.matmul(pY[:, 0:T], lhsT=P['WC'][:, d, q, ri, :], rhs=H_[:, ri, :],
                                                start=(ql == 0 and ri == 0), stop=(ql == 3 and ri == 1)),
                 reads=[(H_, ri), P['WC']], writes=[pY])
        if ql == 3:
            ycols = yT[:, kt, ch * T:(ch + 1) * T] if d == 0 else yT[:, kt, ch * T:(ch + 1) * T][:, ::-1]
            s.op('dve', lambda e: e.tensor_tensor(out=ycols, in0=ycols, in1=pY[:, 0:T], op=ALU.add), reads=[pY, (yT, kt)], writes=[(yT, kt)])

    front(0)
    for i in range(len(items)):
        if i + 1 < len(items):
            front(i + 1)
        back(i)
    if 's5_cos' in DB:
        s.dma(DB['s5_cos'].ap(), P['cosT'][:].rearrange("p a b -> p (a b)"), reads=[P['cosT']], g=5)
        s.dma(DB['s5_sin'].ap(), P['sinT'][:].rearrange("p a b -> p (a b)"), reads=[P['sinT']], g=5)
        s.dma(DB['s5_wc'].ap(), P['WC'][:].rearrange("p a b c d -> p (a b c d)"), reads=[P['WC']], g=5)
    if 's5_y' in DB:
        s.dma(DB['s5_y'].ap(), yT[:].rearrange("p a b -> p (a b)"), reads=[yT], g=5)
        s.dma(DB['s5_u'].ap(), uT[:].rearrange("p a b -> p (a b)"), reads=[uT], q='pool', g=5)
    ygb = uT
    fi = 0
    for c in range(2):
        for (g0, g1) in [(0, 1024), (1024, 2048), (2048, 2304)]:
            tb = tt[fi % 2]
            fi += 1
            n = g1 - g0
            tmp = tb[:].rearrange("p a t -> p (a t)")[:, 0:n]
            yv = yT[:, c, g0:g1]
            s.op('pool', lambda e: e.tensor_tensor(out=tmp, in0=yv, in1=yv, op=ALU.mult), reads=[(yT, c)], writes=[tb])
            s.op('pool', lambda e: e.tensor_scalar(out=tmp, in0=tmp, scalar1=0.044715, scalar2=1.0, op0=ALU.mult, op1=ALU.add), reads=[tb], writes=[tb])
            s.op('dve', lambda e: e.tensor_tensor(out=tmp, in0=tmp, in1=yv, op=ALU.mult), reads=[tb, (yT, c)], writes=[tb])
            s.op('act', lambda e: e.activation(out=tmp, in_=tmp, func=AF.Sigmoid, scale=1.5957691216057308), reads=[tb], writes=[tb])
            s.op('dve', lambda e: e.tensor_tensor(out=yv, in0=yv, in1=tmp, op=ALU.mult), reads=[tb, (yT, c)], writes=[(yT, c)])
            s.op('act', lambda e: e.copy(out=ygb[:, c, g0:g1], in_=yv), reads=[(yT, c)], writes=[ygb])
    sg = sbuf(st, "s5sg", [128, 512])
    for c in range(2):
        for (c0, c1) in PIECES:
            pb = bank()
            n = c1 - c0
            for k in range(2):
                s.op('pe', lambda e, k=k: e.matmul(pb[:, 0:n], lhsT=P['gluw'][:, k, c * 128:(c + 1) * 128], rhs=ygb[:, k, c0:c1], start=(k == 0), stop=(k == 1)),
                     reads=[ygb, P['gluw']], writes=[pb])
            s.op('act', lambda e: e.activation(out=sg[:, 0:n], in_=pb[:, 0:n], func=AF.Sigmoid, bias=P['glub'][:, c:c + 1], scale=1.0),
                 reads=[pb, P['glub']], writes=[sg])
            s.op('dve', lambda e: e.tensor_tensor(out=mixT[:, 6 + c, c0:c1], in0=yT[:, c, c0:c1], in1=sg[:, 0:n], op=ALU.mult),
                 reads=[sg, (yT, c)], writes=[(mixT, ('s5', c, c0))])


def attn_stage(st, nc, s, sbuf, bank, P, D, l, hT, mixT, load_w, rstd_from_ss, tcols, need_ctx, C, DB, stop, mkstg, **_):
    wqk = sbuf(st, "wqk", [128, 8, 1024], BF16)
    wv = sbuf(st, "wv", [128, 8, 512], BF16)
    with contextlib.ExitStack() as s0:
        stg = mkstg(s0)
        load_w(stg, wqk, 'w_in', l, 776, 1024, 0)
        load_w(stg, wv, 'w_in', l, 1800, 512, 0)
        s.barrier()
    _AC = int(os.environ.get('ACUT', '0'))
    if _AC == 1:
        return
    QT = sbuf(st, "QT", [128, 4, SEQ], BF16)
    KT = sbuf(st, "KT", [128, 4, SEQ], BF16)
    Vt = sbuf(st, "Vt", [128, NT, 4, 129], BF16)
    s.op('pool', lambda e: e.memset(Vt[:], 1.0), writes=[Vt])
    qkf = sbuf(st, "qkf", [128, 1024])
    qkr = sbuf(st, "qkr", [128, 1024], BF16)
    rt = sbuf(st, "ropet", [128, 4, 16, 2, 16])
    for t in range(NT):
        pq, pk, pv = bank(), bank(), bank()
        for (pb, w, c0) in [(pq, wqk, 0), (pk, wqk, 512), (pv, wv, 0)]:
            for k in range(8):
                s.op('pe', lambda e, k=k, pb=pb, w=w, c0=c0: e.matmul(pb[:, :], lhsT=hT[:, k, tcols(t)], rhs=w[:, k, c0:c0 + 512], start=(k == 0), stop=(k == 7)),
                     reads=[hT, w], writes=[pb])
        s.op('act', lambda e: e.copy(out=Vt[:, t, :, 0:128], in_=pv[:, :].rearrange("p (h e) -> p h e", h=4)), reads=[pv], writes=[(Vt, t)])
        if t < 2:
            s.op('act', lambda e: e.copy(out=qkr[:, 0:512], in_=pq[:, :]), reads=[pq], writes=[qkr])
            s.op('act', lambda e: e.copy(out=qkr[:, 512:1024], in_=pk[:, :]), reads=[pk], writes=[qkr])
        else:
            s.op('act', lambda e: e.copy(out=qkf[:, 0:512], in_=pq[:, :]), reads=[pq], writes=[qkf])
            s.op('act', lambda e: e.copy(out=qkf[:, 512:1024], in_=pk[:, :]), reads=[pk], writes=[qkf])
            tl = t - 2
            xv = qkf[:].rearrange("p (m a b j) -> p m a b j", m=16, a=2, b=2)
            ov = qkr[:].rearrange("p (m a b j) -> p m a b j", m=16, a=2, b=2)
            cosb = C['ropeC'][:, tl, :].rearrange("p (a j) -> p a j", a=2).unsqueeze(1).to_broadcast([128, 16, 2, 16])
            sinb = C['ropeS'][:, tl, :].rearrange("p (a j) -> p a j", a=2).unsqueeze(1).to_broadcast([128, 16, 2, 16])
            PO = lambda o, a, b_, op, rd, wr: s.op('pool', lambda e: e.tensor_tensor(out=o, in0=a, in1=b_, op=op), reads=rd, writes=wr)
            PO(rt[:, 0], xv[:, :, :, 0, :], cosb, ALU.mult, [qkf, C['ropeC']], [(rt, 0)])
            PO(rt[:, 1], xv[:, :, :, 1, :], sinb, ALU.mult, [qkf, C['ropeS']], [(rt, 1)])
            PO(rt[:, 2], xv[:, :, :, 1, :], cosb, ALU.mult, [qkf, C['ropeC']], [(rt, 2)])
            PO(rt[:, 3], xv[:, :, :, 0, :], sinb, ALU.mult, [qkf, C['ropeS']], [(rt, 3)])
            PO(ov[:, :, :, 0, :], rt[:, 0], rt[:, 1], ALU.subtract, [(rt, 0), (rt, 1)], [qkr])
            PO(ov[:, :, :, 1, :], rt[:, 2], rt[:, 3], ALU.add, [(rt, 2), (rt, 3)], [qkr])
        pb = bank()
        pbb = pb.t[:].bitcast(BF16)
        for m in range(8):
            s.op('pe', lambda e, m=m: e.transpose(out=pbb[:, m * 128:(m + 1) * 128], in_=qkr[:, m * 128:(m + 1) * 128], identity=C['ident_b'][:]),
                 reads=[qkr, C['ident_b']], writes=[pb])
        s.op('act', lambda e: e.copy(out=QT[:, :, tcols(t)], in_=pbb[:, 0:512].rearrange("p (h n) -> p h n", h=4)), reads=[pb], writes=[(QT, t)])
        s.op('dve', lambda e: e.tensor_copy(out=KT[:, :, tcols(t)], in_=pbb[:, 512:1024].rearrange("p (h n) -> p h n", h=4)), reads=[pb], writes=[(KT, t)])
    if _AC == 2:
        return
    ET = sbuf(st, "ET", [128, NT, 512], BF16)
    oc = sbuf(st, "oc", [128, 2, 4, 129])
    rr = sbuf(st, "att_rr", [128, 2, 4])
    o_ = sbuf(st, "att_o", [128, 128])
    oj = sbuf(st, "att_oj", [128, 128])
    ob = sbuf(st, "att_ob", [128, 128], BF16)
    sso = sbuf(st, "att_ss", [128, 1])
    groups = [(256 + 512 * i, 512, NT) for i in range(4)]
    if need_ctx:
        groups = [(0, 256, 2)] + groups
    for h in range(4):
        if _AC == 3 and h == 1:
            return
        for (q0, nq, nk) in groups:
            nsub = nq // 128
            for c in range(2):
                for kt in range(nk):
                    pS = bank()
                    s.op('pe', lambda e, kt=kt: e.matmul(pS[:, 0:nq], lhsT=KT[64 * c:64 * c + 64, h, tcols(kt)], rhs=QT[64 * c:64 * c + 64, h, q0:q0 + nq],
                                                        start=True, stop=True), reads=[KT, QT], writes=[pS])
                    s.op('act', lambda e, kt=kt: e.activation(out=ET[:, kt, 0:nq], in_=pS[:, 0:nq], func=AF.Exp, scale=0.125), reads=[pS], writes=[(ET, kt)])
                for qs in range(nsub):
                    pO = bank()
                    for kt in range(nk):
                        s.op('pe', lambda e, kt=kt: e.matmul(pO[:, 0:129], lhsT=ET[:, kt, qs * 128:(qs + 1) * 128], rhs=Vt[:, kt, h, :], start=(kt == 0), stop=(kt == nk - 1)),
                             reads=[(ET, kt), Vt], writes=[pO])
                    s.op('dve', lambda e: e.tensor_copy(out=oc[:, c, qs, :], in_=pO[:, 0:129]), reads=[pO], writes=[(oc, (c, qs))])
            s.op('dve', lambda e: e.reciprocal(out=rr[:, :, 0:nsub], in_=oc[:, :, 0:nsub, 128]), reads=[oc], writes=[rr])
            s.op('dve', lambda e: e.tensor_scalar(out=rr[:, 1, :], in0=rr[:, 1, :], scalar1=P['nlam'][:, 0:1], scalar2=None, op0=ALU.mult), reads=[rr, P['nlam']], writes=[rr])
            for qs in range(nsub):
                s.op('dve', lambda e: e.tensor_scalar(out=o_[:], in0=oc[:, 0, qs, 0:128], scalar1=rr[:, 0, qs:qs + 1], scalar2=None, op0=ALU.mult), reads=[oc, rr], writes=[o_])
                s.op('dve', lambda e: e.scalar_tensor_tensor(out=o_[:], in0=oc[:, 1, qs, 0:128], scalar=rr[:, 1, qs:qs + 1], in1=o_[:], op0=ALU.mult, op1=ALU.add),
                     reads=[oc, rr, o_], writes=[o_])
                s.op('act', lambda e: e.activation(out=oj[:], in_=o_[:], func=AF.Square, accum_out=sso[:]), reads=[o_], writes=[oj, sso])
                s.op('act', lambda e: e.activation(out=sso[:], in_=sso[:], func=AF.Ln, bias=EPS, scale=1.0 / 128), reads=[sso], writes=[sso])
                s.op('act', lambda e: e.activation(out=sso[:], in_=sso[:], func=AF.Exp, scale=-0.5), reads=[sso], writes=[sso])
                s.op('dve', lambda e: e.scalar_tensor_tensor(out=ob[:], in0=o_[:], scalar=sso[:, 0:1], in1=P['subw'][:], op0=ALU.mult, op1=ALU.mult),
                     reads=[o_, sso, P['subw']], writes=[ob])
                pb = bank()
                pbb = pb.t[:].bitcast(BF16)
                s.op('pe', lambda e: e.transpose(out=pbb[:, 0:128], in_=ob[:], identity=C['ident_b'][:]), reads=[ob, C['ident_b']], writes=[pb])
                qc = q0 + qs * 128
                s.op('act', lambda e: e.copy(out=mixT[:, 2 + h, qc:qc + 128], in_=pbb[:, 0:128]), reads=[pb], writes=[(mixT, ('att', h, qc))])


NWB = int(os.environ.get('NWB', '2'))


def ffn_stage(st, nc, s, sbuf, bank, P, D, l, b, NB, NCOL, hT, need_ctx, last, xs_d, XS, out_d, OUT, gsc_d, GSC, load_w, rstd_from_ss, tcols, DB, stop, mkstg, load_cast, **_):
    stg = mkstg(st, 4)
    wd = sbuf(st, "wd", [128, 22, 1024], BF16)
    wdsrc = D['ffn_w_down'].ap()[l].rearrange("(f p) c -> p f c", p=128)
    for f in range(22):
        load_cast(stg, wd[:, f, :], wd, wdsrc[:, f, :], [1024], key=f)
    gbc = sbuf(st, "gbcF", [128, 2, 1024])
    s.dma(gbc[:, 0, :], bass.AP(gsc_d, ((l * 2 + 1) * NCOL + b) * 1024, [[0, 128], [1, 1024]]), reads=[GSC], writes=[gbc], g=0)
    s.dma(gbc[:, 1, :], bass.AP(gsc_d, ((l * 2 + 1) * NCOL + NB) * 1024, [[0, 128], [1, 1024]]), reads=[GSC], writes=[gbc], g=0)
    actT = sbuf(st, "actT", [128, 22, 1152], BF16)
    wg = [sbuf(st, f"wg{i}", [128, 8, 128], BF16) for i in range(NWB)]
    wu = [sbuf(st, f"wu{i}", [128, 8, 128], BF16) for i in range(NWB)]
    graw = sbuf(st, "graw", [128, 1160])
    gc = sbuf(st, "gc", [128, 1160])
    ub = sbuf(st, "ub", [128, 1152], BF16)
    xt2 = [sbuf(st, f"xtF{i}", [128, 1024]) for i in range(2)]
    junk = sbuf(st, "junkF", [128, 1024], BF16)
    ss2 = sbuf(st, "ssF", [128, 2])
    rs2 = sbuf(st, "rsF", [128, 1])
    ytmp = sbuf(st, "ytmpF", [128, 1024])
    s.op('pool', lambda e: e.memset(graw[:], 0.0), writes=[graw])
    if stop == 'F0':
        return
    halves = []
    if need_ctx:
        halves.append(dict(gp=[(0, 256, 1), (256, 768, 259), (768, 1153, 771)], up=[(0, 256, 0), (256, 768, 256), (768, 1152, 768)],
                           outs=[(0, 256, 1), (256, 1152, 259)], tiles=list(range(0, 9)), base=0, pads=[0, 257, 258]))
    else:
        halves.append(dict(gp=[(256, 768, 259), (768, 1153, 771)], up=[(256, 768, 256), (768, 1152, 768)],
                           outs=[(256, 1152, 259)], tiles=list(range(2, 9)), base=0, pads=[0, 257, 258]))
    halves.append(dict(gp=[(1151, 1663, 0), (1663, 2175, 512), (2175, 2304, 1024)], up=[(1152, 1664, 0), (1664, 2176, 512), (2176, 2304, 1024)],
                       outs=[(0, 1152, 1)], tiles=list(range(9, 18)), base=1152, pads=[1153]))
    for hv in halves:
        for pcol in hv['pads']:
            s.op('pool', lambda e, pcol=pcol: e.memset(graw[:, pcol:pcol + 1], 0.0), writes=[graw])
        for f in range(int(os.environ.get('FSTART', '0')), int(os.environ.get('FCUT', '22'))):
            g_, u_ = wg[f % NWB], wu[f % NWB]
            load_cast(stg, g_[:], g_, D['ffn_w_gate'].ap()[l].rearrange("(k p) c -> p k c", p=128)[:, :, f * 128:(f + 1) * 128], [8, 128], eng='pool')
            load_cast(stg, u_[:], u_, D['ffn_w_up'].ap()[l].rearrange("(k p) c -> p k c", p=128)[:, :, f * 128:(f + 1) * 128], [8, 128], eng='pool')
            _FS = int(os.environ.get('FSTEP', '0')); _lastf = f == int(os.environ.get('FCUT', '22')) - 1
            if _lastf and _FS == 1:
                return
            for (c0, c1, dst) in hv['gp']:
                pb = bank()
                n = c1 - c0
                for k in range(8):
                    s.op('pe', lambda e, k=k: e.matmul(pb[:, 0:n], lhsT=g_[:, k, :], rhs=hT[:, k, c0:c1], start=(k == 0), stop=(k == 7)), reads=[hT, g_], writes=[pb])
                s.op('act', lambda e: e.copy(out=graw[:, dst:dst + n], in_=pb[:, 0:n]), reads=[pb], writes=[graw])
            if _lastf and _FS == 2:
                return
            for (c0, c1, dst) in hv['up']:
                pb = bank()
                n = c1 - c0
                for k in range(8):
                    s.op('pe', lambda e, k=k: e.matmul(pb[:, 0:n], lhsT=u_[:, k, :], rhs=hT[:, k, c0:c1], start=(k == 0), stop=(k == 7)), reads=[hT, u_], writes=[pb])
                s.op('dve', lambda e: e.tensor_copy(out=ub[:, dst:dst + n], in_=pb[:, 0:n]), reads=[pb], writes=[ub])
            if _lastf and _FS == 3:
                return
            s.op('act', lambda e: e.activation(out=gc[:, 1:1157], in_=graw[:, 1:1157], func=AF.Identity, scale=P['fcw'][:, f, 1:2], bias=P['fcb'][:, f:f + 1]),
                 reads=[graw, P['fcw'], P['fcb']], writes=[gc])
            for j in (0, 2):
                s.op('pool', lambda e, j=j: e.scalar_tensor_tensor(out=gc[:, 1:1157], in0=graw[:, j:j + 1156], scalar=P['fcw'][:, f, j:j + 1], in1=gc[:, 1:1157],
                                                                  op0=ALU.mult, op1=ALU.add), reads=[graw, gc, P['fcw']], writes=[gc]) if False else \
                    s.op('dve', lambda e, j=j: e.scalar_tensor_tensor(out=gc[:, 1:1157], in0=graw[:, j:j + 1156], scalar=P['fcw'][:, f, j:j + 1], in1=gc[:, 1:1157],
                                                                     op0=ALU.mult, op1=ALU.add), reads=[graw, gc, P['fcw']], writes=[gc])
            s.op('act', lambda e: e.activation(out=gc[:, 1:1157], in_=gc[:, 1:1157], func=AF.Silu), reads=[gc], writes=[gc])
            for (a0, a1, src) in hv['outs']:
                n = a1 - a0
                s.op('dve', lambda e: e.tensor_tensor(out=actT[:, f, a0:a1], in0=gc[:, src:src + n], in1=ub[:, a0:a1], op=ALU.mult),
                     reads=[gc, ub], writes=[(actT, f)])
            if stop == 'F1':
                return
        if stop == 'F2':
            return
        htl = hv['tiles']

        def ldF(i):
            s.dma(xt2[i % 2][:], xs_d.ap()[b, tcols(htl[i]), :], reads=[(XS, b)], writes=[xt2[i % 2]], g=0)
        ldF(0)
        for ti, t in enumerate(htl):
            if ti + 1 < len(htl):
                ldF(ti + 1)
            xt = xt2[ti % 2]
            tc = t * 128 - hv['base']
            pbs = [bank(), bank()]
            for hf in range(2):
                for f in range(22):
                    s.op('pe', lambda e, f=f, hf=hf: e.matmul(pbs[hf][:, :], lhsT=actT[:, f, tc:tc + 128], rhs=wd[:, f, hf * 512:(hf + 1) * 512],
                                                             start=(f == 0), stop=(f == 21)), reads=[actT, wd], writes=[pbs[hf]])
                s.op('act', lambda e, hf=hf: e.activation(out=junk[:, hf * 512:(hf + 1) * 512], in_=pbs[hf][:, :], func=AF.Square, accum_out=ss2[:, hf:hf + 1]),
                     reads=[pbs[hf]], writes=[junk, ss2])
            s.op('dve', lambda e: e.tensor_tensor(out=rs2[:], in0=ss2[:, 0:1], in1=ss2[:, 1:2], op=ALU.add), reads=[ss2], writes=[rs2])
            rstd_from_ss(rs2, rs2, 1024)
            gi = 1 if t < 2 else 0
            for hf in range(2):
                s.op('dve', lambda e, hf=hf: e.scalar_tensor_tensor(out=ytmp[:, hf * 512:(hf + 1) * 512], in0=pbs[hf][:, :], scalar=rs2[:, 0:1],
                                                                   in1=gbc[:, gi, hf * 512:(hf + 1) * 512], op0=ALU.mult, op1=ALU.mult),
                     reads=[pbs[hf], rs2, gbc], writes=[ytmp])
            s.op('pool', lambda e: e.tensor_tensor(out=xt[:], in0=xt[:], in1=ytmp[:], op=ALU.add), reads=[xt, ytmp], writes=[xt])
            if last:
                if t >= 2:
                    s.dma(out_d.ap()[b, (t - 2) * 128:(t - 1) * 128, :], xt[:], reads=[xt], writes=[OUT], g=1)
            else:
                s.dma(xs_d.ap()[b, tcols(t), :], xt[:], reads=[xt], writes=[(XS, b)], g=1)


def build(NB=4, LAYERS=(0, 1), dbg=(), stop=None):
    nc = bass.Bass("TRN2", target_bir_lowering=False)
    D = {}

    def din(name, shape):
        D[name] = nc.dram_tensor(name, list(shape), F32, kind="ExternalInput")
        return D[name]

    x_d = din("x", [NB, 2048, 1024])
    ctx_d = din("ctx", [NB, 256, 1024])
    cc_d = din("cc", [NB + 1, 1024])
    for n, sh in PARAMS:
        din(n, sh)
    for n, a in host_consts().items():
        din(n, a.shape)
    out_d = nc.dram_tensor("out", [NB, 2048, 1024], F32, kind="ExternalOutput")
    xs_d = nc.dram_tensor("xs", [NB, SEQ, 1024], F32)
    gsc_d = nc.dram_tensor("gsc", [2, 2, NB + 1, 1024], F32)
    DB = {}
    for n, sh in dbg:
        DB[n] = nc.dram_tensor("dbg_" + n, list(sh), F32, kind="ExternalOutput")

    NCOL = NB + 1
    with contextlib.ExitStack() as top:
        s = Sched(nc, top)
        top.enter_context(nc.allow_non_contiguous_dma(reason="small param layout loads"))
        top.enter_context(nc.allow_low_precision(reason="bf16 matmul operands"))
        XS = Buf(xs_d, "xs")
        GSC = Buf(gsc_d, "gsc")
        OUT = Buf(out_d, "out")
        DIN = Buf(None, "din")

        uid = [0]

        def sbuf(st, name, shape, dt=F32):
            uid[0] += 1
            name = f"{name}_u{uid[0]}"
            return Buf(st.enter_context(nc.sbuf_tensor(name, list(shape), dt)), name)

        banks = [Buf(top.enter_context(nc.psum_tensor(f"pb{i}", [128, 512], F32)), f"pb{i}", psum=True) for i in range(8)]
        bank_i = [0]

        def bank(idx=None):
            if idx is not None:
                return banks[idx]
            b = banks[bank_i[0] % 8]
            bank_i[0] += 1
            return b

        def dmp(name, ap, buf, key=None):
            if name in DB:
                s.dma(DB[name].ap() if not isinstance(name, tuple) else None, ap, reads=[(buf, key)], g=5)

        ident_f = sbuf(top, "ident_f", [128, 128])
        ident_b = sbuf(top, "ident_b", [128, 128], BF16)
        U_f = sbuf(top, "U_f", [128, 128])
        UT_f = sbuf(top, "UT_f", [128, 128])
        mf_f = sbuf(top, "mf_f", [128, 128])
        mb_f = sbuf(top, "mb_f", [128, 128])
        ones_f = sbuf(top, "ones_f", [128, 128])
        ropeC = sbuf(top, "ropeC", [128, 16, 32])
        ropeS = sbuf(top, "ropeS", [128, 16, 32])
        for bufc, nm in [(ident_f, 'c_ident'), (U_f, 'c_U'), (UT_f, 'c_UT'), (mf_f, 'c_mf'), (mb_f, 'c_mb'), (ones_f, 'c_ones')]:
            s.dma(bufc[:], D[nm].ap(), writes=[bufc], g=0)
        s.dma(ropeC[:], D['c_cos'].ap().rearrange("(t p) j -> p t j", p=128), writes=[ropeC], g=0)
        s.dma(ropeS[:], D['c_sin'].ap().rearrange("(t p) j -> p t j", p=128), writes=[ropeS], g=0)
        s.op('act', lambda e: e.copy(out=ident_b[:], in_=ident_f[:]), reads=[ident_f], writes=[ident_b])
        ones_b = sbuf(top, "ones_b", [128, 128], BF16)
        s.op('act', lambda e: e.copy(out=ones_b[:], in_=ones_f[:]), reads=[ones_f], writes=[ones_b])

        LP = {}
        for l in LAYERS:
            P = {}
            P['s1a'] = sbuf(top, f"s1a{l}", [128, 8, NCOL])
            P['sha'] = sbuf(top, f"sha{l}", [128, 8, NCOL])
            P['s1f'] = sbuf(top, f"s1f{l}", [128, 8, NCOL])
            P['shf'] = sbuf(top, f"shf{l}", [128, 8, NCOL])
            P['dtb'] = sbuf(top, f"dtb{l}", [128, 8])
            P['aneg'] = sbuf(top, f"aneg{l}", [128, 8])
            P['dsk'] = sbuf(top, f"dsk{l}", [128, 4])
            P['snw'] = sbuf(top, f"snw{l}", [128, 256])
            P['scw'] = sbuf(top, f"scw{l}", [128, 4, 5])
            P['scb'] = sbuf(top, f"scb{l}", [128, 4])
            P['subw'] = sbuf(top, f"subw{l}", [128, 128])
            P['nlam'] = sbuf(top, f"nlam{l}", [128, 1])
            P['subwc'] = sbuf(top, f"subwc{l}", [128, 1])
            P['fcw'] = sbuf(top, f"fcw{l}", [128, 22, 3])
            P['fcb'] = sbuf(top, f"fcb{l}", [128, 22])
            P['s5d'] = sbuf(top, f"s5d{l}", [128, 2])
            P['glub'] = sbuf(top, f"glub{l}", [128, 2])
            P['gluw'] = sbuf(top, f"gluw{l}", [128, 2, 256], BF16)
            P['s5r'] = sbuf(top, f"s5r{l}", [128, 16])
            P['s5ar'] = sbuf(top, f"s5ar{l}", [128, 16])
            P['s5ai'] = sbuf(top, f"s5ai{l}", [128, 16])
            LP[l] = P

        S5SCR = {}
        LP['_negpi'] = sbuf(top, "negpi", [128, 1])
        s.op('pool', lambda e: e.memset(LP['_negpi'][:], -math.pi), writes=[LP['_negpi']])
        LP['_Z'] = sbuf(top, "Zt", [128, 8, 128])
        s.op('pool', lambda e: e.memset(LP['_Z'][:], 0.0), writes=[LP['_Z']])
        with contextlib.ExitStack() as st:
            scT = sbuf(st, "scT", [128, 8, NCOL])
            for col in range(NCOL):
                s.dma(scT[:, :, col], bass.AP(cc_d, col * 1024, [[1, 128], [128, 8]]), writes=[scT], g=0)
            s.op('act', lambda e: e.activation(out=scT[:], in_=scT[:], func=AF.Silu), reads=[scT], writes=[scT])
            wm = [sbuf(st, f"wm{i}", [128, 8, 512]) for i in range(2)]
            modT = sbuf(st, "modT", [128, 48, NCOL])
            modb = sbuf(st, "modb", [128, 48])
            nrm = sbuf(st, "nrm", [128, 4, 8])
            tmpg = sbuf(st, "tmpg", [128, 2, 8, NCOL])
            iota = sbuf(st, "iota", [128, 256])
            s.dma(iota[:], D['c_iota'].ap(), writes=[iota], g=0)
            S5D = {}
            for l in LAYERS:
                P = LP[l]
                P['WB'] = sbuf(st, f"WB{l}", [128, 2, 8, 2, 128], BF16)
                P['WC'] = sbuf(st, f"WC{l}", [128, 2, 8, 2, 128])
                P['cosT'] = sbuf(st, f"cosT{l}", [128, 16, 256])
                P['sinT'] = sbuf(st, f"sinT{l}", [128, 16, 256])
                s.dma(modb[:], bass.AP(D['mod_b'], l * 6144, [[1, 128], [128, 48]]), writes=[modb], g=0)
                for i, nm in enumerate(['mix_norm_pre', 'mix_norm_post', 'ffn_norm_pre', 'ffn_norm_post']):
                    s.dma(nrm[:, i, :], bass.AP(D[nm], l * 1024, [[1, 128], [128, 8]]), writes=[nrm], g=0)
                for cch in range(12):
                    w = wm[cch % 2]
                    s.dma(w[:], D['mod_w'].ap()[l].rearrange("(k p) c -> p k c", p=128)[:, :, cch * 512:(cch + 1) * 512],
                          writes=[w], g=1)
                    for jj in range(4):
                        j = cch * 4 + jj
                        pb = bank()
                        for k in range(8):
                            s.op('pe', lambda e, k=k, jj=jj, w=w, pb=pb: e.matmul(pb[:, 0:NCOL], lhsT=w[:, k, jj * 128:(jj + 1) * 128],
                                                                                 rhs=scT[:, k, :], start=(k == 0), stop=(k == 7)),
                                 reads=[w, scT], writes=[pb])
                        s.op('act', lambda e, j=j, pb=pb: e.activation(out=modT[:, j, :], in_=pb[:, 0:NCOL], func=AF.Identity,
                                                                      bias=modb[:, j:j + 1], scale=1.0),
                             reads=[pb, modb], writes=[modT])
                for (s1, sh, o_sh, o_sc, o_g, ipre, ipost, wi) in [(P['s1a'], P['sha'], 0, 8, 16, 0, 1, 0),
                                                                    (P['s1f'], P['shf'], 24, 32, 40, 2, 3, 1)]:
                    s.op('dve', lambda e, s1=s1, o_sc=o_sc: e.tensor_scalar(out=s1[:], in0=modT[:, o_sc:o_sc + 8, :], scalar1=1.0,
                                                                           scalar2=None, op0=ALU.add),
                         reads=[modT], writes=[s1])
                    s.op('dve', lambda e, s1=s1, ipre=ipre: e.tensor_tensor(out=s1[:], in0=s1[:],
                                                                           in1=nrm[:, ipre, :].unsqueeze(2).to_broadcast([128, 8, NCOL]),
                                                                           op=ALU.mult),
                         reads=[s1, nrm], writes=[s1])
                    s.op('dve', lambda e, sh=sh, o_sh=o_sh: e.tensor_copy(out=sh[:], in_=modT[:, o_sh:o_sh + 8, :]),
                         reads=[modT], writes=[sh])
                    s.op('dve', lambda e, wi=wi, o_g=o_g, ipost=ipost: e.tensor_tensor(
                        out=tmpg[:, wi], in0=modT[:, o_g:o_g + 8, :],
                        in1=nrm[:, ipost, :].unsqueeze(2).to_broadcast([128, 8, NCOL]), op=ALU.mult),
                        reads=[modT, nrm], writes=[tmpg])
                    for col in range(NCOL):
                        s.dma(bass.AP(gsc_d, ((l * 2 + wi) * NCOL + col) * 1024, [[1, 128], [128, 8]]), tmpg[:, wi, :, col],
                              reads=[tmpg], writes=[GSC], g=2)

                def bc(name, off, n):
                    return bass.AP(D[name], off, [[0, 128], [1, n]])
                s.dma(P['dtb'][:], bc('ssd_dt_bias', l * 8, 8), writes=[P['dtb']], g=0)
                s.dma(P['aneg'][:], bc('ssd_a_log', l * 8, 8), writes=[P['aneg']], g=0)
                s.op('act', lambda e, P=P: e.activation(out=P['aneg'][:], in_=P['aneg'][:], func=AF.Exp), reads=[P['aneg']], writes=[P['aneg']])
                s.op('dve', lambda e, P=P: e.tensor_scalar(out=P['aneg'][:], in0=P['aneg'][:], scalar1=-1.0, scalar2=None, op0=ALU.mult),
                     reads=[P['aneg']], writes=[P['aneg']])
                s.dma(P['dsk'][:], bc('ssd_d', l * 4, 4), writes=[P['dsk']], g=0)
                s.dma(P['snw'][:], bc('ssd_norm_w', l * 256, 256), writes=[P['snw']], g=0)
                for j in range(5):
                    s.dma(P['scw'][:, :, j], bass.AP(D['ssd_conv_w'], l * 2560 + j * 512, [[1, 128], [128, 4]]), writes=[P['scw']], g=0)
                s.dma(P['scb'][:], bass.AP(D['ssd_conv_b'], l * 512, [[1, 128], [128, 4]]), writes=[P['scb']], g=0)
                for j in range(3):
                    s.dma(P['fcw'][:, :, j], bass.AP(D['ffn_conv_w'], (l * 3 + j) * 2816, [[1, 128], [128, 22]]), writes=[P['fcw']], g=0)
                s.dma(P['fcb'][:], bass.AP(D['ffn_conv_b'], l * 2816, [[1, 128], [128, 22]]), writes=[P['fcb']], g=0)
                s.dma(P['s5d'][:], bass.AP(D['s5_d'], l * 256, [[1, 128], [128, 2]]), writes=[P['s5d']], g=0)
                s.dma(P['glub'][:], bass.AP(D['s5_glu_b'], l * 256, [[1, 128], [128, 2]]), writes=[P['glub']], g=0)
                gst = sbuf(st, f"gst{l}", [128, 2, 256])
                s.dma(gst[:], D['s5_glu_w'].ap()[l].rearrange("(k p) c -> p k c", p=128), writes=[gst], g=3)
                s.op('pool', lambda e, P=P, gst=gst: e.tensor_copy(out=P['gluw'][:], in_=gst[:]), reads=[gst], writes=[P['gluw']])
                lam_init = 0.8 - 0.6 * math.exp(-0.3 * l)
                s.dma(P['subw'][:], bc('diff_subln_w', l * 128, 128), writes=[P['subw']], g=0)
                s.dma(P['subwc'][:], bass.AP(D['diff_subln_w'], l * 128, [[1, 128], [1, 1]]), writes=[P['subwc']], g=0)
                s.op('dve', lambda e, P=P, li=lam_init: e.tensor_scalar(out=P['subwc'][:], in0=P['subwc'][:], scalar1=1.0 - li, scalar2=None,
                                                                      op0=ALU.mult), reads=[P['subwc']], writes=[P['subwc']])
                s.op('dve', lambda e, P=P, li=lam_init: e.tensor_scalar(out=P['subw'][:], in0=P['subw'][:], scalar1=1.0 - li, scalar2=None,
                                                                      op0=ALU.mult), reads=[P['subw']], writes=[P['subw']])
                lt = sbuf(st, f"lt{l}", [128, 4, 64])
                lsum = sbuf(st, f"lsum{l}", [128, 4])
                for i, nm in enumerate(['diff_lam_q1', 'diff_lam_k1', 'diff_lam_q2', 'diff_lam_k2']):
                    s.dma(lt[:, i, :], bc(nm, l * 64, 64), writes=[lt], g=0)
                s.op('dve', lambda e, lt=lt: e.tensor_tensor(out=lt[:, 0, :], in0=lt[:, 0, :], in1=lt[:, 1, :], op=ALU.mult), reads=[lt], writes=[lt])
                s.op('dve', lambda e, lt=lt: e.tensor_tensor(out=lt[:, 2, :], in0=lt[:, 2, :], in1=lt[:, 3, :], op=ALU.mult), reads=[lt], writes=[lt])
                s.op('dve', lambda e, lt=lt, lsum=lsum: e.reduce_sum(out=lsum[:, 0:1], in_=lt[:, 0, :], axis=mybir.AxisListType.X), reads=[lt], writes=[lsum])
                s.op('dve', lambda e, lt=lt, lsum=lsum: e.reduce_sum(out=lsum[:, 1:2], in_=lt[:, 2, :], axis=mybir.AxisListType.X), reads=[lt], writes=[lsum])
                s.op('act', lambda e, lsum=lsum: e.activation(out=lsum[:, 2:4], in_=lsum[:, 0:2], func=AF.Exp), reads=[lsum], writes=[lsum])
                s.op('dve', lambda e, lsum=lsum, P=P, li=lam_init: e.scalar_tensor_tensor(out=P['nlam'][:], in0=lsum[:, 3:4], scalar=-li,
                                                                                        in1=lsum[:, 2:3], op0=ALU.add, op1=ALU.subtract),
                     reads=[lsum], writes=[P['nlam']])

                lre = sbuf(st, f"lre{l}", [128, 16])
                lim = sbuf(st, f"lim{l}", [128, 16])
                stp = sbuf(st, f"stp{l}", [128, 16])
                for d in range(2):
                    s.dma(lre[:, d * 8:(d + 1) * 8], bass.AP(D['s5_lam_re'], (l * 2 + d) * 1024, [[1, 128], [128, 8]]), writes=[lre], g=0)
                    s.dma(lim[:, d * 8:(d + 1) * 8], bass.AP(D['s5_lam_im'], (l * 2 + d) * 1024, [[1, 128], [128, 8]]), writes=[lim], g=0)
                    for gl in range(2):
                        s.dma(stp[64 * gl:64 * gl + 64, d * 8:(d + 1) * 8],
                              bass.AP(D['s5_log_step'], (l * 2 + d) * 16 + gl, [[0, 64], [2, 8]]), writes=[stp], g=0)
                w16 = [sbuf(st, f"w16_{l}_{i}", [128, 16]) for i in range(10)]
                lrs, lis, mag, sn, cs, nr, den, cre, cim, t16 = w16
                r_, ar_, ai_ = P['s5r'], P['s5ar'], P['s5ai']

                def V(fn, rd, wr, eng='dve'):
                    s.op(eng, fn, reads=rd, writes=wr)
                V(lambda e: e.activation(out=stp[:], in_=stp[:], func=AF.Exp), [stp], [stp], 'act')
                V(lambda e: e.tensor_tensor(out=lrs[:], in0=lre[:], in1=stp[:], op=ALU.mult), [lre, stp], [lrs])
                V(lambda e: e.tensor_tensor(out=lis[:], in0=lim[:], in1=stp[:], op=ALU.mult), [lim, stp], [lis])
                V(lambda e: e.activation(out=r_[:], in_=lrs[:], func=AF.Exp), [lrs], [r_], 'act')
                ki16 = sbuf(st, f"ki16_{l}", [128, 16], mybir.dt.int32)
                kf16 = sbuf(st, f"kf16_{l}", [128, 16])

                def sin16(out, phase):
                    V(lambda e: e.tensor_scalar(out=t16[:], in0=lis[:], scalar1=phase, scalar2=None, op0=ALU.add), [lis], [t16])
                    V(lambda e: e.tensor_scalar(out=ki16[:], in0=t16[:], scalar1=1.0 / TWO_PI, scalar2=None, op0=ALU.mult), [t16], [ki16])
                    V(lambda e: e.tensor_copy(out=kf16[:], in_=ki16[:]), [ki16], [kf16])
                    V(lambda e: e.scalar_tensor_tensor(out=t16[:], in0=kf16[:], scalar=-TWO_PI, in1=t16[:], op0=ALU.mult, op1=ALU.add), [kf16, t16], [t16])
                    V(lambda e: e.tensor_scalar(out=t16[:], in0=t16[:], scalar1=-3.14159, scalar2=3.14159, op0=ALU.max, op1=ALU.min), [t16], [t16])
                    V(lambda e: e.activation(out=out[:], in_=t16[:], func=AF.Sin), [t16], [out], 'act')
                sin16(sn, 0.0)
                sin16(cs, 0.5 * math.pi)
                negpi = LP['_negpi']
                Zt = LP['_Z']
                V(lambda e: e.tensor_tensor(out=ar_[:], in0=r_[:], in1=cs[:], op=ALU.mult), [r_, cs], [ar_])
                V(lambda e: e.tensor_tensor(out=ai_[:], in0=r_[:], in1=sn[:], op=ALU.mult), [r_, sn], [ai_])
                V(lambda e: e.tensor_scalar(out=nr[:], in0=ar_[:], scalar1=-1.0, scalar2=None, op0=ALU.add), [ar_], [nr])
                V(lambda e: e.tensor_tensor(out=den[:], in0=lre[:], in1=lre[:], op=ALU.mult), [lre], [den])
                V(lambda e: e.tensor_tensor(out=t16[:], in0=lim[:], in1=lim[:], op=ALU.mult), [lim], [t16])
                V(lambda e: e.tensor_tensor(out=den[:], in0=den[:], in1=t16[:], op=ALU.add), [den, t16], [den])
                V(lambda e: e.reciprocal(out=den[:], in_=den[:]), [den], [den])
                V(lambda e: e.tensor_tensor(out=cre[:], in0=nr[:], in1=lre[:], op=ALU.mult), [nr, lre], [cre])
                V(lambda e: e.tensor_tensor(out=t16[:], in0=ai_[:], in1=lim[:], op=ALU.mult), [ai_, lim], [t16])
                V(lambda e: e.tensor_tensor(out=cre[:], in0=cre[:], in1=t16[:], op=ALU.add), [cre, t16], [cre])
                V(lambda e: e.tensor_tensor(out=cre[:], in0=cre[:], in1=den[:], op=ALU.mult), [cre, den], [cre])
                V(lambda e: e.tensor_tensor(out=cim[:], in0=ai_[:], in1=lre[:], op=ALU.mult), [ai_, lre], [cim])
                V(lambda e: e.tensor_tensor(out=t16[:], in0=nr[:], in1=lim[:], op=ALU.mult), [nr, lim], [t16])
                V(lambda e: e.tensor_tensor(out=cim[:], in0=cim[:], in1=t16[:], op=ALU.subtract), [cim, t16], [cim])
                V(lambda e: e.tensor_tensor(out=cim[:], in0=cim[:], in1=den[:], op=ALU.mult), [cim, den], [cim])
                rt = sbuf(st, f"rt{l}", [128, 256])
                rki = sbuf(st, f"rki{l}", [128, 256], mybir.dt.int32)
                rkf = sbuf(st, f"rkf{l}", [128, 256])
                for j in range(16):
                    for (tab, ph) in [(P['cosT'], 0.5 * math.pi), (P['sinT'], 0.0)]:
                        V(lambda e, j=j, ph=ph: e.tensor_scalar(out=rt[:], in0=iota[:], scalar1=lis[:, j:j + 1], scalar2=ph, op0=ALU.mult, op1=ALU.add),
                          [iota, lis], [rt])
                        V(lambda e: e.tensor_scalar(out=rki[:], in0=rt[:], scalar1=1.0 / TWO_PI, scalar2=None, op0=ALU.mult), [rt], [rki])
                        V(lambda e: e.tensor_copy(out=rkf[:], in_=rki[:]), [rki], [rkf])
                        V(lambda e: e.scalar_tensor_tensor(out=rt[:], in0=rkf[:], scalar=-TWO_PI, in1=rt[:], op0=ALU.mult, op1=ALU.add), [rkf, rt], [rt])
                        V(lambda e: e.tensor_scalar(out=rt[:], in0=rt[:], scalar1=-3.14159, scalar2=3.14159, op0=ALU.max, op1=ALU.min), [rt], [rt])
                        V(lambda e, j=j, tab=tab: e.activation(out=tab[:, j, :], in_=rt[:], func=AF.Sin), [rt], [(tab, j)], 'act')
                bre = sbuf(st, f"bre{l}", [128, 16, 16])
                bim = sbuf(st, f"bim{l}", [128, 16, 16])
                bbr = sbuf(st, f"bbr{l}", [128, 16, 16])
                bbi = sbuf(st, f"bbi{l}", [128, 16, 16])
                btm = sbuf(st, f"btm{l}", [128, 16, 16])
                for d in range(2):
                    s.dma(bre[:, d * 8:(d + 1) * 8, :], bass.AP(D['s5_b_re'], (l * 2 + d) * 16384, [[16, 128], [2048, 8], [1, 16]]), writes=[bre], g=0)
                    s.dma(bim[:, d * 8:(d + 1) * 8, :], bass.AP(D['s5_b_im'], (l * 2 + d) * 16384, [[16, 128], [2048, 8], [1, 16]]), writes=[bim], g=0)

                def B3(t):
                    return t[:].unsqueeze(2).to_broadcast([128, 16, 16])
                V(lambda e: e.tensor_tensor(out=bbr[:], in0=bre[:], in1=B3(cre), op=ALU.mult), [bre, cre], [bbr])
                V(lambda e: e.tensor_tensor(out=btm[:], in0=bim[:], in1=B3(cim), op=ALU.mult), [bim, cim], [btm])
                V(lambda e: e.tensor_tensor(out=bbr[:], in0=bbr[:], in1=btm[:], op=ALU.subtract), [bbr, btm], [bbr])
                V(lambda e: e.tensor_tensor(out=bbi[:], in0=bim[:], in1=B3(cre), op=ALU.mult), [bim, cre], [bbi])
                V(lambda e: e.tensor_tensor(out=btm[:], in0=bre[:], in1=B3(cim), op=ALU.mult), [bre, cim], [btm])
                V(lambda e: e.tensor_tensor(out=bbi[:], in0=bbi[:], in1=btm[:], op=ALU.add), [bbi, btm], [bbi])
                for d in range(2):
                    for q in range(8):
                        j = d * 8 + q
                        ql = q % 4
                        for ri, bb in enumerate([bbr, bbi]):
                            zi = ri * 4 + ql
                            for gl in range(2):
                                V(lambda e, gl=gl, zi=zi, ql=ql, bb=bb, j=j: e.tensor_copy(
                                    out=Zt[64 * gl:64 * gl + 64, zi, 32 * ql + 16 * gl:32 * ql + 16 * gl + 16], in_=bb[64 * gl:64 * gl + 64, j, :]),
                                  [bb], [(Zt, zi)])
                            pb = bank()
                            s.op('pe', lambda e, pb=pb, zi=zi: e.transpose(out=pb[:, 0:128], in_=Zt[:, zi, :], identity=ident_f[:]),
                                 reads=[(Zt, zi), ident_f], writes=[pb])
                            s.op('act', lambda e, pb=pb, d=d, q=q, ri=ri: e.copy(out=P['WB'][:, d, q, ri, :], in_=pb[:, 0:128]),
                                 reads=[pb], writes=[(P['WB'], (d, q, ri))])
                s.op('pool', lambda e: e.memset(P['WC'][:], 0.0), writes=[P['WC']])
                for d in range(2):
                    for q in range(8):
                        ql = q % 4
                        for gl in range(2):
                            g = 2 * q + gl
                            for ri, nm in enumerate(['s5_c_re', 's5_c_im']):
                                s.dma(P['WC'][64 * gl:64 * gl + 64, d, q, ri, 32 * ql + 16 * gl:32 * ql + 16 * gl + 16],
                                      bass.AP(D[nm], ((l * 2 + d) * 16 + g) * 1024, [[1, 64], [64, 16]]), writes=[P['WC']], g=4)
                V(lambda e: e.tensor_scalar(out=P['WC'][:, :, :, 1, :], in0=P['WC'][:, :, :, 1, :], scalar1=-1.0, scalar2=None, op0=ALU.mult),
                  [P['WC']], [P['WC']])
                for nm in ['WB', 'WC', 'cosT', 'sinT']:
                    dt_ = BF16 if nm == 'WB' else F32
                    dd = nc.dram_tensor(f"s5scr_{nm}{l}", [128, 4096], dt_)
                    S5SCR[(nm, l)] = (dd, Buf(dd, f"s5scr_{nm}{l}"))
                    src = P[nm][:].rearrange("p a b c d -> p (a b c d)") if nm in ('WB', 'WC') else P[nm][:].rearrange("p a b -> p (a b)")
                    s.dma(dd.ap(), src, reads=[P[nm]], writes=[S5SCR[(nm, l)][1]], g=2)
                    P[nm] = None
            s.barrier()

        def tcols(t):
            return slice(t * 128, (t + 1) * 128)

        PIECES = [(0, 256), (256, 768), (768, 1280), (1280, 1792), (1792, 2304)]

        def mkstg(st, n=3):
            return dict(t=[sbuf(st, "stg", [128, 1024]) for _ in range(n)], i=[0])

        def load_cast(stg, dst_ap, dst_buf, src_ap, shape, key=None, eng='dve'):
            t = stg['t'][stg['i'][0] % len(stg['t'])]
            stg['i'][0] += 1
            n = int(np.prod(shape))
            tv = t[:, 0:n]
            if len(shape) == 2:
                tv = tv.rearrange("p (a b) -> p a b", a=shape[0])
            s.dma(tv, src_ap, writes=[t], g=3)
            s.op(eng, lambda e: e.tensor_copy(out=dst_ap, in_=tv), reads=[t], writes=[(dst_buf, key)])

        def load_w_bf16(stg, wbuf, name, l, c0, n, dst0=0, nk=8):
            src = D[name].ap()[l].rearrange("(k p) c -> p k c", p=128)
            kk = max(1, 1024 // n)
            for k0 in range(0, nk, kk):
                k1 = min(nk, k0 + kk)
                if k1 - k0 == 1:
                    load_cast(stg, wbuf[:, k0, dst0:dst0 + n], wbuf, src[:, k0, c0:c0 + n], [n])
                else:
                    load_cast(stg, wbuf[:, k0:k1, dst0:dst0 + n], wbuf, src[:, k0:k1, c0:c0 + n], [k1 - k0, n])

        def rstd_from_ss(ss, rstd, n):
            s.op('act', lambda e: e.activation(out=rstd[:], in_=ss[:], func=AF.Sqrt, bias=EPS, scale=1.0 / n), reads=[ss], writes=[rstd])
            s.op('dve', lambda e: e.reciprocal(out=rstd[:], in_=rstd[:]), reads=[rstd], writes=[rstd])

        def nmt_front(st_bufs, xt):
            junk, ss, rstd, xn = st_bufs
            s.op('act', lambda e: e.activation(out=junk[:], in_=xt[:], func=AF.Square, accum_out=ss[:]), reads=[xt], writes=[junk, ss])
            rstd_from_ss(ss, rstd, 1024)
            s.op('dve', lambda e: e.tensor_scalar(out=xn[:], in0=xt[:], scalar1=rstd[:, 0:1], scalar2=None, op0=ALU.mult),
                 reads=[xt, rstd], writes=[xn])

        def nmt_back(st_bufs, hT, t, s1, sh, col):
            junk, ss, rstd, xn = st_bufs
            pb = bank()
            pbb = pb.t[:].bitcast(BF16)
            for k in range(8):
                s.op('pe', lambda e, k=k: e.transpose(out=pbb[:, k * 128:(k + 1) * 128], in_=xn[:, k * 128:(k + 1) * 128], identity=ident_b[:]),
                     reads=[xn, ident_b], writes=[pb])
            for k in range(8):
                s.op('act', lambda e, k=k: e.activation(out=hT[:, k, tcols(t)], in_=pbb[:, k * 128:(k + 1) * 128], func=AF.Identity,
                                                       scale=s1[:, k, col:col + 1], bias=sh[:, k, col:col + 1]),
                     reads=[pb, s1, sh], writes=[(hT, t)])

        def norm_mod_transpose(st_bufs, xt, hT, t, s1, sh, col):
            nmt_front(st_bufs, xt)
            nmt_back(st_bufs, hT, t, s1, sh, col)

        for b in range(NB):
            s.dma(xs_d.ap()[b, 0:256, :], ctx_d.ap()[b], writes=[(XS, b)], g=0)
            s.dma(xs_d.ap()[b, 256:2304, :], x_d.ap()[b], writes=[(XS, b)], g=0)
            for l in LAYERS:
                P = LP[l]
                last = (l == LAYERS[-1])
                need_ctx = not last
                with contextlib.ExitStack() as shs:
                  hT = sbuf(shs, "hT", [128, 8, SEQ], BF16)
                  early = False
                  with contextlib.ExitStack() as sm:
                    mixT = sbuf(sm, "mixT", [128, 8, SEQ], BF16)
                    with contextlib.ExitStack() as st:
                        xt2 = [sbuf(st, f"xt{i}", [128, 1024]) for i in range(3)]
                        nbs = [(sbuf(st, "junk", [128, 1024], BF16), sbuf(st, "ssA", [128, 1]), sbuf(st, "rstdA", [128, 1]),
                                sbuf(st, "xnA", [128, 1024], BF16)) for _ in range(2)]
                        def frontA(t):
                            xt = xt2[t % 3]
                            s.dma(xt[:], xs_d.ap()[b, tcols(t), :], reads=[(XS, b)], writes=[xt], g=0)
                            nmt_front(nbs[t % 2], xt)
                        frontA(0)
                        for t in range(NT):
                            if t + 1 < NT:
                                frontA(t + 1)
                            nmt_back(nbs[t % 2], hT, t, P['s1a'], P['sha'], NB if t < 2 else b)
                        s.barrier()
                    CONST = dict(ident_b=ident_b, ident_f=ident_f, U_f=U_f, UT_f=UT_f, mf_f=mf_f, mb_f=mb_f, ones_f=ones_f,
                                 ropeC=ropeC, ropeS=ropeS, ones_b=ones_b)
                    ENV = dict(S5SCR=S5SCR, nc=nc, s=s, sbuf=sbuf, bank=bank, P=P, D=D, l=l, b=b, NB=NB, NCOL=NCOL, hT=hT, mixT=mixT,
                               load_w=load_w_bf16, mkstg=mkstg, load_cast=load_cast, rstd_from_ss=rstd_from_ss, tcols=tcols, PIECES=PIECES, C=CONST, DB=DB, stop=stop,
                               need_ctx=need_ctx, last=last, xs_d=xs_d, XS=XS, out_d=out_d, OUT=OUT, gsc_d=gsc_d, GSC=GSC)
                    if stop == 'A':
                        s.dma(DB['hT'].ap(), hT[:], reads=[hT], q='pool', g=5)
                        early = True
                    if not early:
                        with contextlib.ExitStack() as st:
                            ssd_stage(st, **ENV)
                            s.barrier()
                        early = stop is not None and stop.startswith('SSD')
                    if not early:
                        with contextlib.ExitStack() as st:
                            s5_stage(st, **ENV)
                            s.barrier()
                        early = stop is not None and stop.startswith('S5')
                        if 'mix2' in DB:
                            s.dma(DB['mix2'].ap(), mixT[:], reads=[mixT], q='pool', g=5)
                            s.barrier()
                    if not early:
                        with contextlib.ExitStack() as st:
                            attn_stage(st, **ENV)
                            s.barrier()
                        early = stop is not None and stop.startswith('ATT')
                    if stop is not None and stop[:2] in ('SS', 'S5', 'AT') and 'mixT' in DB:
                        s.dma(DB['mixT'].ap(), mixT[:], reads=[mixT], q='pool', g=5)
                    tiles = list(range(NT)) if need_ctx else list(range(2, NT))
                    if not early:
                      with contextlib.ExitStack() as st:
                        wo = sbuf(st, "wo", [128, 8, 1024], BF16)
                        stgE = mkstg(st)
                        load_w_bf16(stgE, wo, 'w_out', l, 0, 1024)
                        gbc = sbuf(st, "gbc", [128, 2, 1024])
                        s.dma(gbc[:, 0, :], bass.AP(gsc_d, ((l * 2 + 0) * NCOL + b) * 1024, [[0, 128], [1, 1024]]), reads=[GSC], writes=[gbc], g=0)
                        s.dma(gbc[:, 1, :], bass.AP(gsc_d, ((l * 2 + 0) * NCOL + NB) * 1024, [[0, 128], [1, 1024]]), reads=[GSC], writes=[gbc], g=0)
                        NXT = 5
                        xt2 = [sbuf(st, f"xtE{i}", [128, 1024]) for i in range(NXT)]
                        junk1 = [sbuf(st, "junkE1", [128, 1024], BF16) for _ in range(2)]
                        nbs = [(sbuf(st, "junkE", [128, 1024], BF16), sbuf(st, "ssE", [128, 1]), sbuf(st, "rstdE", [128, 1]),
                                sbuf(st, "xnE", [128, 1024], BF16)) for _ in range(2)]
                        ss2s = [sbuf(st, "ss2", [128, 2]) for _ in range(2)]
                        rs2s = [sbuf(st, "rs2", [128, 1]) for _ in range(2)]
                        ytmps = [sbuf(st, "ytmp", [128, 1024]) for _ in range(2)]
                        def ldE(i):
                            s.dma(xt2[i % NXT][:], xs_d.ap()[b, tcols(tiles[i]), :], reads=[(XS, b)], writes=[xt2[i % NXT]], g=0)
                        def s1E(ti):
                            t = tiles[ti]
                            if ti + 3 < len(tiles):
                                ldE(ti + 3)
                            xt = xt2[ti % NXT]
                            ss2, rs2, ytmp, jk = ss2s[ti % 2], rs2s[ti % 2], ytmps[ti % 2], junk1[ti % 2]
                            pbs = [bank(), bank()]
                            for hf in range(2):
                                for k in range(8):
                                    s.op('pe', lambda e, k=k, hf=hf: e.matmul(pbs[hf][:, :], lhsT=mixT[:, k, tcols(t)], rhs=wo[:, k, hf * 512:(hf + 1) * 512],
                                                                             start=(k == 0), stop=(k == 7)), reads=[mixT, wo], writes=[pbs[hf]])
                                s.op('act', lambda e, hf=hf: e.activation(out=jk[:, hf * 512:(hf + 1) * 512], in_=pbs[hf][:, :], func=AF.Square,
                                                                         accum_out=ss2[:, hf:hf + 1]), reads=[pbs[hf]], writes=[jk, ss2])
                            s.op('dve', lambda e: e.tensor_tensor(out=rs2[:], in0=ss2[:, 0:1], in1=ss2[:, 1:2], op=ALU.add), reads=[ss2], writes=[rs2])
                            rstd_from_ss(rs2, rs2, 1024)
                            gi = 1 if t < 2 else 0
                            for hf in range(2):
                                s.op('dve', lambda e, hf=hf, gi=gi: e.scalar_tensor_tensor(out=ytmp[:, hf * 512:(hf + 1) * 512], in0=pbs[hf][:, :], scalar=rs2[:, 0:1],
                                                                                        in1=gbc[:, gi, hf * 512:(hf + 1) * 512], op0=ALU.mult, op1=ALU.mult),
                                     reads=[pbs[hf], rs2, gbc], writes=[ytmp])
                            s.op('pool', lambda e, xt=xt: e.tensor_tensor(out=xt[:], in0=xt[:], in1=ytmp[:], op=ALU.add), reads=[xt, ytmp], writes=[xt])
                            s.dma(xs_d.ap()[b, tcols(t), :], xt[:], reads=[xt], writes=[(XS, b)], g=1)

                        def s2E(ti):
                            nmt_front(nbs[ti % 2], xt2[ti % NXT])

                        def s3E(ti):
                            t = tiles[ti]
                            nmt_back(nbs[ti % 2], hT, t, P['s1f'], P['shf'], NB if t < 2 else b)

                        nE = len(tiles)
                        for i0 in range(min(3, nE)):
                            ldE(i0)
                        s1E(0)
                        if nE > 1:
                            s1E(1)
                        s2E(0)
                        for ti in range(nE):
                            if ti + 2 < nE:
                                s1E(ti + 2)
                            if ti + 1 < nE:
                                s2E(ti + 1)
                            s3E(ti)
                        s.barrier()
                  if stop == 'E':
                    early = True
                  if not early:
                    with contextlib.ExitStack() as st:
                        ffn_stage(st, **ENV)
                        s.barrier()
                    if stop is not None and stop.startswith('F'):
                        early = True
                  if stop in ('E', 'L0'):
                    s.dma(DB['xs'].ap(), xs_d.ap()[b], reads=[(XS, b)], g=5)
                    if stop == 'E':
                        s.dma(DB['hT'].ap(), hT[:], reads=[hT], q='pool', g=5)
                    early = True
                  if early:
                    break
            if stop is not None:
                break
        s.barrier(final=True)
    return nc


def shard_inputs(inputs, NB, ncores):
    consts = host_consts()
    maps = []
    for i in range(ncores):
        m = {}
        m['x'] = np.ascontiguousarray(inputs['x'][i * NB:(i + 1) * NB], dtype=np.float32)
        m['ctx'] = np.ascontiguousarray(inputs['ctx'][i * NB:(i + 1) * NB], dtype=np.float32)
        m['cc'] = np.ascontiguousarray(np.concatenate([inputs['c'][i * NB:(i + 1) * NB], inputs['c_ctx'][None, :]], 0), dtype=np.float32)
        for n, sh in PARAMS:
            m[n] = np.ascontiguousarray(inputs[n], dtype=np.float32)
        m.update(consts)
        maps.append(m)
    return maps


def kernel(**inputs):
    inputs = {k: np.asarray(v) for k, v in inputs.items()}
    NB = 4
    nc = build(NB=NB, LAYERS=(0, 1))
    maps = shard_inputs(inputs, NB, 8)
    res = run_bass_kernel_spmd(nc, maps, core_ids=list(range(8)))
    return np.concatenate([r["out"] for r in res.results], axis=0).astype(np.float32)
```

```python
import contextlib
import math
import numpy as np
import concourse.bass as bass
import concourse.mybir as mybir
from concourse.bass_utils import run_bass_kernel_spmd

F32 = mybir.dt.float32
BF16 = mybir.dt.bfloat16
AF = mybir.ActivationFunctionType
ALU = mybir.AluOpType

ENGS = ('pe', 'act', 'dve', 'pool', 'sp')
SAME_ENGINE_SYNC = ('act', 'dve', 'pool')
NDSEM = 6
EPS = 1e-6
NT = 18
SEQ = 2304
TWO_PI = 2.0 * math.pi


class Buf:
    def __init__(self, t, name, psum=False):
        self.t = t
        self.name = name
        self.psum = psum
        self.w = {}
        self.r = {}

    def _keys(self, key):
        if key is None:
            return list(set(self.w) | set(self.r) | {None})
        return [key, None]

    def wdeps(self, key):
        d = set()
        for k in self._keys(key):
            d |= self.w.get(k, set())
        return d

    def rdeps(self, key):
        d = set()
        for k in self._keys(key):
            d |= set(self.r.get(k, {}).items())
        return d

    def add_reader(self, key, ev):
        rr = self.r.setdefault(key, {})
        if rr.get(ev[0], 0) < ev[1]:
            rr[ev[0]] = ev[1]

    def set_writer(self, key, ev):
        if key is None:
            self.w = {None: {ev}}
            self.r = {}
        else:
            self.w[key] = {ev}
            self.r[key] = {}

    def add_dma_writer(self, key, ev):
        cur = self.w.get(key, set())
        cur = {e for e in cur if e[0][1] == ev[0][1] and e[0][0].startswith('q') and not (e[0] == ev[0] and e[1] <= ev[1])}
        cur.add(ev)
        if key is None:
            self.w = {None: cur}
            self.r = {}
        else:
            self.w[key] = cur
            self.r[key] = {}

    def __getitem__(self, idx):
        return self.t[idx]


def _norm(lst):
    return [(x, None) if isinstance(x, Buf) else x for x in lst]


class Sched:
    NQ = {'sp': 14, 'pool': 3}

    def __init__(self, nc, stack):
        self.nc = nc
        self.stack = stack
        self.eng = {'pe': nc.tensor, 'act': nc.scalar, 'dve': nc.vector, 'pool': nc.gpsimd, 'sp': nc.sync}
        self.phase = 0
        self.ninst = 0
        self.sets = []
        for i in range(2):
            d = {}
            for e in ENGS:
                d[e] = stack.enter_context(nc.semaphore(f"s_{e}_{i}"))
            for q, n in self.NQ.items():
                for j in range(n):
                    d[f"q{q}{j}"] = stack.enter_context(nc.semaphore(f"s_q{q}{j}_{i}"))
            self.sets.append(d)
        self.bsems = [stack.enter_context(nc.semaphore(f"s_bar{i}")) for i in range(2)]
        self.nbar = 0
        self._new_sems()

    def _new_sems(self):
        self.sem = self.sets[self.phase % 2]
        self.cnt = {k: 0 for k in self.sem}
        self.seen = {e: {} for e in ENGS}
        self.rr = {q: 0 for q in self.NQ}

    def _wait(self, e, deps):
        need = {}
        for (k, v) in deps:
            if k[1] != self.phase:
                continue
            if need.get(k, 0) < v:
                need[k] = v
        for k, v in need.items():
            if self.seen[e].get(k, 0) >= v:
                continue
            self.eng[e].wait_ge(self.sem[k[0]], v)
            self.seen[e][k] = v
            self.ninst += 1

    def op(self, e, fn, reads=(), writes=()):
        reads = _norm(reads)
        writes = _norm(writes)
        writes = writes + [r for r in reads if r[0].psum]
        reads = [r for r in reads if not r[0].psum]
        deps = set()
        for b, k in reads:
            deps |= b.wdeps(k)
        for b, k in writes:
            deps |= b.wdeps(k)
            for ev in b.rdeps(k):
                if ev[0][0] != e:
                    deps.add(ev)
        if e not in SAME_ENGINE_SYNC:
            deps = {d for d in deps if d[0][0] != e}
        self._wait(e, deps)
        inst = fn(self.eng[e])
        self.cnt[e] += 1
        inst.then_inc(self.sem[e], 1)
        self.ninst += 1
        ev = ((e, self.phase), self.cnt[e])
        for b, k in reads:
            b.add_reader(k, ev)
        for b, k in writes:
            b.set_writer(k, ev)
        return inst

    def dma(self, out, in_, reads=(), writes=(), q='sp', g=0, **kw):
        reads = _norm(reads)
        writes = _norm(writes)
        deps = set()
        for b, k in reads:
            deps |= b.wdeps(k)
        for b, k in writes:
            same = b.w.get(k, set())
            deps |= {ev for ev in b.wdeps(k) if not (ev in same and ev[0][0].startswith('q'))} | b.rdeps(k)
        self._wait(q, deps)
        sk = f"q{q}{self.rr[q] % self.NQ[q]}"
        self.rr[q] += 1
        kk = (sk, self.phase)
        if self.cnt[sk] > 0 and self.seen[q].get(kk, 0) < self.cnt[sk]:
            self.eng[q].wait_ge(self.sem[sk], self.cnt[sk])
            self.seen[q][kk] = self.cnt[sk]
        inst = self.eng[q].dma_start(out=out, in_=in_, **kw)
        self.cnt[sk] += 16
        inst.then_inc(self.sem[sk], 16)
        self.ninst += 1
        ev = (kk, self.cnt[sk])
        for b, k in reads:
            b.add_reader(k, ev)
        for b, k in writes:
            b.add_dma_writer(k, ev)
        return inst

    def barrier(self, final=False):
        bsem = self.bsems[self.nbar % 2]
        other = self.bsems[(self.nbar + 1) % 2]
        self.nbar += 1
        for k in self.sem:
            if k.startswith('q') and self.cnt[k] > 0:
                self.eng['sp'].wait_ge(self.sem[k], self.cnt[k])
        for e in ENGS:
            if self.cnt[e] > 0:
                self.eng[e].wait_ge(self.sem[e], self.cnt[e])
            self.eng[e].sem_inc(bsem, 1)
        self.eng['sp'].wait_ge(bsem, len(ENGS))
        if not final:
            for k, sm in self.sets[(self.phase + 1) % 2].items():
                self.eng['sp'].sem_clear(sm)
            self.eng['sp'].sem_clear(other)
        self.eng['sp'].sem_inc(bsem, 1)
        for e in ENGS:
            self.eng[e].wait_ge(bsem, len(ENGS) + 1)
        if not final:
            self.phase += 1
            self._new_sems()


PARAMS = [
    ('mod_w', (2, 1024, 6144)), ('mod_b', (2, 6144)), ('mix_norm_pre', (2, 1024)), ('mix_norm_post', (2, 1024)),
    ('ffn_norm_pre', (2, 1024)), ('ffn_norm_post', (2, 1024)), ('w_in', (2, 1024, 2568)), ('w_out', (2, 1024, 1024)),
    ('ssd_conv_w', (2, 5, 512)), ('ssd_conv_b', (2, 512)), ('ssd_dt_bias', (2, 2, 4)), ('ssd_a_log', (2, 2, 4)),
    ('ssd_d', (2, 4)), ('ssd_norm_w', (2, 256)), ('diff_lam_q1', (2, 64)), ('diff_lam_k1', (2, 64)),
    ('diff_lam_q2', (2, 64)), ('diff_lam_k2', (2, 64)), ('diff_subln_w', (2, 128)),
    ('s5_lam_re', (2, 2, 16, 64)), ('s5_lam_im', (2, 2, 16, 64)), ('s5_log_step', (2, 2, 16)),
    ('s5_b_re', (2, 2, 16, 64, 16)), ('s5_b_im', (2, 2, 16, 64, 16)), ('s5_c_re', (2, 2, 16, 16, 64)),
    ('s5_c_im', (2, 2, 16, 16, 64)), ('s5_d', (2, 16, 16)), ('s5_glu_w', (2, 256, 256)), ('s5_glu_b', (2, 256)),
    ('ffn_w_gate', (2, 1024, 2816)), ('ffn_w_up', (2, 1024, 2816)), ('ffn_conv_w', (2, 3, 2816)),
    ('ffn_conv_b', (2, 2816)), ('ffn_w_down', (2, 2816, 1024)),
]


def host_consts():
    r = np.arange(128)
    U = (r[:, None] <= r[None, :]).astype(np.float32)
    c = {}
    c['c_ident'] = np.eye(128, dtype=np.float32)
    c['c_U'] = U
    c['c_UT'] = np.ascontiguousarray(U.T)
    c['c_mf'] = np.where(r[None, :] >= r[:, None], 0.0, -30000.0).astype(np.float32)
    c['c_mb'] = np.where(r[None, :] <= r[:, None], 0.0, -30000.0).astype(np.float32)
    c['c_ones'] = np.ones((128, 128), np.float32)
    t = np.arange(2048)
    rr = (t // 64).astype(np.float32)
    col = (t % 64).astype(np.float32)
    inv = (10000.0 ** (-np.arange(16, dtype=np.float32) / 16)).astype(np.float32)
    ar = rr[:, None] * inv[None, :]
    ac = col[:, None] * inv[None, :]
    c['c_cos'] = np.concatenate([np.cos(ar), np.cos(ac)], 1).astype(np.float32)
    c['c_sin'] = np.concatenate([np.sin(ar), np.sin(ac)], 1).astype(np.float32)
    c['c_iota'] = np.tile(np.arange(256, dtype=np.float32)[None, :], (128, 1))
    return c


import os
_CUT = int(os.environ.get('SSDCUT', '0'))


def ssd_stage(st, nc, s, sbuf, bank, P, D, l, hT, mixT, load_w, rstd_from_ss, tcols, PIECES, C, DB, stop, mkstg, **_):
    wzd = sbuf(st, "wzd", [128, 8, 264], BF16)
    wx = sbuf(st, "wx", [128, 8, 512], BF16)
    with contextlib.ExitStack() as s0:
        stg = mkstg(s0)
        load_w(stg, wzd, 'w_in', l, 0, 256, 0)
        load_w(stg, wzd, 'w_in', l, 768, 8, 256)
        load_w(stg, wx, 'w_in', l, 256, 512, 0)
        s.barrier()
    if stop == 'SSD0':
        return
    zs = sbuf(st, "zs", [128, NT, 256], BF16)
    dtr = sbuf(st, "dtr", [128, NT, 8])
    xbcT = sbuf(st, "xbcT", [128, 4, SEQ], BF16)
    xbc = sbuf(st, "xbc", [128, NT, 512], BF16)
    yacc = sbuf(st, "yacc", [128, NT, 256])
    for t in range(NT):
        pb = bank()
        for k in range(8):
            s.op('pe', lambda e, k=k: e.matmul(pb[:, 0:264], lhsT=hT[:, k, tcols(t)], rhs=wzd[:, k, :], start=(k == 0), stop=(k == 7)),
                 reads=[(hT, t), wzd], writes=[pb])
        s.op('act', lambda e: e.activation(out=zs[:, t, :], in_=pb[:, 0:256], func=AF.Silu), reads=[pb], writes=[(zs, t)])
        s.op('dve', lambda e: e.tensor_copy(out=dtr[:, t, :], in_=pb[:, 256:264]), reads=[pb], writes=[dtr])
    if stop == 'SSD1':
        return
    with contextlib.ExitStack() as sx:
        xraw = sbuf(sx, "xraw", [128, 2312])
        xacc = sbuf(sx, "xacc", [128, 2312])
        s.op('pool', lambda e: e.memset(xraw[:], 0.0), writes=[xraw])
        for c in range(4):
            for (c0, c1) in PIECES:
                pb = bank()
                n = c1 - c0
                for k in range(8):
                    s.op('pe', lambda e, k=k: e.matmul(pb[:, 0:n], lhsT=wx[:, k, c * 128:(c + 1) * 128], rhs=hT[:, k, c0:c1], start=(k == 0), stop=(k == 7)),
                         reads=[hT, wx], writes=[pb])
                off = 2 if c0 == 0 else 6
                s.op('act', lambda e: e.copy(out=xraw[:, off + c0:off + c1], in_=pb[:, 0:n]), reads=[pb], writes=[xraw])
            s.op('act', lambda e: e.activation(out=xacc[:, 2:2310], in_=xraw[:, 2:2310], func=AF.Identity, scale=P['scw'][:, c, 2:3], bias=P['scb'][:, c:c + 1]),
                 reads=[xraw, P['scw'], P['scb']], writes=[xacc])
            for j in (0, 1, 3, 4):
                s.op('dve', lambda e, j=j: e.scalar_tensor_tensor(out=xacc[:, 2:2310], in0=xraw[:, j:j + 2308], scalar=P['scw'][:, c, j:j + 1],
                                                                 in1=xacc[:, 2:2310], op0=ALU.mult, op1=ALU.add),
                     reads=[xraw, xacc, P['scw']], writes=[xacc])
            s.op('act', lambda e: e.activation(out=xbcT[:, c, 0:256], in_=xacc[:, 2:258], func=AF.Silu), reads=[xacc], writes=[xbcT])
            s.op('act', lambda e: e.activation(out=xbcT[:, c, 256:2304], in_=xacc[:, 262:2310], func=AF.Silu), reads=[xacc], writes=[xbcT])

        s.barrier()
    if stop == 'SSD2':
        return
    for t in range(NT):
        pb = bank()
        pbb = pb.t[:].bitcast(BF16)
        for c in range(4):
            s.op('pe', lambda e, c=c: e.transpose(out=pbb[:, c * 128:(c + 1) * 128], in_=xbcT[:, c, tcols(t)], identity=C['ident_b'][:]),
                 reads=[xbcT, C['ident_b']], writes=[pb])
        s.op('act', lambda e: e.copy(out=xbc[:, t, :], in_=pbb[:, 0:512]), reads=[pb], writes=[(xbc, t)])
    if stop == 'SSD3':
        return
    dt = sbuf(st, "dt", [128, NT, 8])
    dta = sbuf(st, "dta", [128, NT, 8])
    tA = sbuf(st, "tA", [128, NT, 8])
    tB = sbuf(st, "tB", [128, NT, 8])

    def B8(tb):
        return tb[:].unsqueeze(1).to_broadcast([128, NT, 8])
    s.op('dve', lambda e: e.tensor_tensor(out=dtr[:], in0=dtr[:], in1=B8(P['dtb']), op=ALU.add), reads=[dtr, P['dtb']], writes=[dtr])
    s.op('dve', lambda e: e.tensor_scalar(out=tB[:], in0=dtr[:], scalar1=-1.0, scalar2=None, op0=ALU.mult), reads=[dtr], writes=[tB])
    s.op('dve', lambda e: e.tensor_tensor(out=tA[:], in0=dtr[:], in1=tB[:], op=ALU.max), reads=[dtr, tB], writes=[tA])
    s.op('act', lambda e: e.activation(out=tA[:], in_=tA[:], func=AF.Exp, scale=-1.0), reads=[tA], writes=[tA])
    s.op('act', lambda e: e.activation(out=tA[:], in_=tA[:], func=AF.Ln, bias=1.0, scale=1.0), reads=[tA], writes=[tA])
    s.op('dve', lambda e: e.tensor_single_scalar(out=tB[:], in_=dtr[:], scalar=0.0, op=ALU.max), reads=[dtr], writes=[tB])
    s.op('dve', lambda e: e.tensor_tensor(out=dt[:], in0=tA[:], in1=tB[:], op=ALU.add), reads=[tA, tB], writes=[dt])
    s.op('dve', lambda e: e.tensor_tensor(out=dta[:], in0=dt[:], in1=B8(P['aneg']), op=ALU.mult), reads=[dt, P['aneg']], writes=[dta])
    if stop == 'SSD4':
        return
    for t in range(NT):
        s.op('dve', lambda e, t=t: e.tensor_tensor(out=yacc[:, t, :].rearrange("p (h d) -> p h d", h=4),
                                                  in0=xbc[:, t, 0:256].rearrange("p (h d) -> p h d", h=4),
                                                  in1=P['dsk'][:].unsqueeze(2).to_broadcast([128, 4, 64]), op=ALU.mult),
             reads=[(xbc, t), P['dsk']], writes=[(yacc, t)])
    if stop == 'SSD5':
        return
    def two(name, shape, dt=F32):
        return [sbuf(st, f"{name}{i}", shape, dt) for i in range(2)]
    sm8s, nacss, eacss, wdecs, etots = two("sm8", [128, 8]), two("nacs", [128, 4]), two("eacs", [128, 4]), two("wdec", [128, 4]), two("etot", [128, 4])
    A1 = sbuf(st, "A1", [128, 4, 128])
    A1hs, A1ls = two("A1h", [128, 4, 128], BF16), two("A1l", [128, 4, 128], BF16)
    Ub = {}
    for nm in ['U_f', 'UT_f', 'mf_f', 'mb_f']:
        Ub[nm] = sbuf(st, nm + "_b", [128, 128], BF16)
        s.op('act', lambda e, nm=nm: e.copy(out=Ub[nm][:], in_=C[nm][:]), reads=[C[nm]], writes=[Ub[nm]])
    decTs = two("decT", [128, 4, 128])
    scTs = two("scTs", [128, 4, 128], BF16)
    xdts, xdtws = two("xdt", [128, 4, 64], BF16), two("xdtw", [128, 4, 64], BF16)
    state = sbuf(st, "state", [128, 2, 64])
    stateb = sbuf(st, "stateb", [128, 2, 64], BF16)
    ones3 = sbuf(st, "ones3", [128, 4, 128])
    s.op('pool', lambda e: e.memset(ones3[:], 1.0), writes=[ones3])
    seq = []
    for d in range(2):
        order = list(range(NT)) if d == 0 else [1, 0] + list(range(NT - 1, 1, -1))
        for oi, t in enumerate(order):
            seq.append((d, t, oi == 0))

    def front(i):
        d, t, first = seq[i]
        k2 = i % 2
        sm8, nacs, eacs, wdec, etot = sm8s[k2], nacss[k2], eacss[k2], wdecs[k2], etots[k2]
        A1h, A1l, decT, scT, xdt, xdtw = A1hs[k2], A1ls[k2], decTs[k2], scTs[k2], xdts[k2], xdtws[k2]
        Um = C['U_f'] if d == 0 else C['UT_f']
        Umb = Ub['U_f'] if d == 0 else Ub['UT_f']
        Mnb = Ub['mf_f'] if d == 0 else Ub['mb_f']
        dta_c = dta[:, t, d * 4:(d + 1) * 4]
        dt_c = dt[:, t, d * 4:(d + 1) * 4]
        pbs = bank(0)
        s.op('pe', lambda e: e.matmul(pbs[:, 0:4], lhsT=Um[:], rhs=dta_c, start=True, stop=True), reads=[Um, dta], writes=[pbs])
        s.op('pe', lambda e: e.matmul(pbs[:, 4:8], lhsT=C['ones_f'][:], rhs=dta_c, start=True, stop=True), reads=[C['ones_f'], dta], writes=[pbs])
        s.op('act', lambda e: e.copy(out=sm8[:], in_=pbs[:, 0:8]), reads=[pbs], writes=[sm8])
        s.op('dve', lambda e: e.tensor_scalar(out=nacs[:], in0=sm8[:, 0:4], scalar1=-1.0, scalar2=None, op0=ALU.mult), reads=[sm8], writes=[nacs])
        s.op('act', lambda e: e.activation(out=eacs[:], in_=sm8[:, 0:4], func=AF.Exp), reads=[sm8], writes=[eacs])
        s.op('dve', lambda e: e.tensor_tensor(out=wdec[:], in0=sm8[:, 4:8], in1=sm8[:, 0:4], op=ALU.subtract), reads=[sm8], writes=[wdec])
        s.op('act', lambda e: e.activation(out=wdec[:], in_=wdec[:], func=AF.Exp), reads=[wdec], writes=[wdec])
        s.op('act', lambda e: e.activation(out=etot[:], in_=sm8[:, 4:8], func=AF.Exp), reads=[sm8], writes=[etot])
        s.op('dve', lambda e: e.tensor_tensor(out=A1[:], in0=ones3[:], in1=dta_c.unsqueeze(2).to_broadcast([128, 4, 128]), op=ALU.mult),
             reads=[ones3, dta], writes=[A1])
        s.op('act', lambda e: e.copy(out=A1h[:], in_=A1[:]), reads=[A1], writes=[A1h])
        s.op('dve', lambda e: e.tensor_tensor(out=A1l[:], in0=A1[:], in1=A1h[:], op=ALU.subtract), reads=[A1, A1h], writes=[A1l])
        pD = bank(1)
        for h in range(4):
            s.op('pe', lambda e, h=h: e.matmul(pD[:, h * 128:(h + 1) * 128], lhsT=A1h[:, h, :], rhs=Umb[:], start=True, stop=False),
                 reads=[A1h, Umb], writes=[pD])
            s.op('pe', lambda e, h=h: e.matmul(pD[:, h * 128:(h + 1) * 128], lhsT=A1l[:, h, :], rhs=Umb[:], start=False, stop=False),
                 reads=[A1l, Umb], writes=[pD])
            s.op('pe', lambda e, h=h: e.matmul(pD[:, h * 128:(h + 1) * 128], lhsT=C['ident_b'][:], rhs=Mnb[:], start=False, stop=True),
                 reads=[C['ident_b'], Mnb], writes=[pD])
        for h in range(4):
            s.op('act', lambda e, h=h: e.activation(out=decT[:, h, :], in_=pD[:, h * 128:(h + 1) * 128], func=AF.Exp, bias=nacs[:, h:h + 1], scale=1.0),
                 reads=[pD, nacs], writes=[(decT, h)])
        pGs = [bank(2), bank(3)]
        for g in range(2):
            s.op('pe', lambda e, g=g: e.matmul(pGs[g][:, 0:128], lhsT=xbcT[64 * g:64 * g + 64, 2, tcols(t)],
                                              rhs=xbcT[64 * g:64 * g + 64, 3, tcols(t)], start=True, stop=True), reads=[xbcT], writes=[pGs[g]])
        for h in range(4):
            g = h // 2
            s.op('dve', lambda e, h=h, g=g: e.tensor_tensor(out=scT[:, h, :], in0=pGs[g][:, 0:128], in1=decT[:, h, :], op=ALU.mult),
                 reads=[pGs[g], (decT, h)], writes=[(scT, h)])
        xs4 = xbc[:, t, 0:256].rearrange("p (h d) -> p h d", h=4)
        s.op('dve', lambda e: e.tensor_tensor(out=xdt[:], in0=xs4, in1=dt_c.unsqueeze(2).to_broadcast([128, 4, 64]), op=ALU.mult),
             reads=[(xbc, t), dt], writes=[xdt])
        s.op('dve', lambda e: e.tensor_tensor(out=xdtw[:], in0=xdt[:], in1=wdec[:].unsqueeze(2).to_broadcast([128, 4, 64]), op=ALU.mult),
             reads=[xdt, wdec], writes=[xdtw])
        pY = bank(4)
        for h in range(4):
            s.op('pe', lambda e, h=h: e.matmul(pY[:, h * 64:(h + 1) * 64], lhsT=scT[:, h, :], rhs=xdt[:, h, :], start=True, stop=True),
                 reads=[(scT, h), xdt], writes=[pY])
        s.op('dve', lambda e: e.tensor_tensor(out=yacc[:, t, :], in0=yacc[:, t, :], in1=pY[:, 0:256], op=ALU.add), reads=[pY, (yacc, t)], writes=[(yacc, t)])

    def back(i):
        d, t, first = seq[i]
        k2 = i % 2
        eacs, etot, xdtw = eacss[k2], etots[k2], xdtws[k2]
        if first:
            s.op('pool', lambda e: e.memset(state[:], 0.0), writes=[state])
            s.op('pool', lambda e: e.memset(stateb[:], 0.0), writes=[stateb])
        pOs = [bank(5), bank(6)]
        for h in range(4):
            g, hh = h // 2, h % 2
            s.op('pe', lambda e, h=h, g=g, hh=hh: e.matmul(pOs[g][:, hh * 64:(hh + 1) * 64], lhsT=xbcT[64 * g:64 * g + 64, 3, tcols(t)],
                                                          rhs=stateb[64 * g:64 * g + 64, hh, :], start=True, stop=True),
                 reads=[xbcT, stateb], writes=[pOs[g]])
        pS = bank(7)
        s.op('pe', lambda e: e.matmul(pS[:, 0:256], lhsT=xbc[:, t, 256:384], rhs=xdtw[:].rearrange("p h d -> p (h d)"), start=True, stop=True),
             reads=[(xbc, t), xdtw], writes=[pS])
        for g in range(2):
            for hh in range(2):
                h = 2 * g + hh
                s.op('dve', lambda e, g=g, hh=hh, h=h: e.scalar_tensor_tensor(
                    out=state[64 * g:64 * g + 64, hh, :], in0=state[64 * g:64 * g + 64, hh, :], scalar=etot[64 * g:64 * g + 64, h:h + 1],
                    in1=pS[64 * g:64 * g + 64, h * 64:(h + 1) * 64], op0=ALU.mult, op1=ALU.add), reads=[state, etot, pS], writes=[state])
        s.op('act', lambda e: e.copy(out=stateb[:], in_=state[:]), reads=[state], writes=[stateb])
        for h in range(4):
            g, hh = h // 2, h % 2
            s.op('dve', lambda e, h=h, g=g, hh=hh: e.scalar_tensor_tensor(out=yacc[:, t, h * 64:(h + 1) * 64], in0=pOs[g][:, hh * 64:(hh + 1) * 64],
                                                             scalar=eacs[:, h:h + 1], in1=yacc[:, t, h * 64:(h + 1) * 64], op0=ALU.mult, op1=ALU.add),
                 reads=[pOs[g], eacs, (yacc, t)], writes=[(yacc, t)])

    front(0)
    for i in range(len(seq)):
        if i + 1 < len(seq):
            front(i + 1)
        back(i)
    if stop == 'SSD6':
        return
    gt = sbuf(st, "gt", [128, 256])
    gj = sbuf(st, "gj", [128, 256])
    gb = sbuf(st, "gb", [128, 256], BF16)
    ssg = sbuf(st, "ssg", [128, 1])
    for t in range(NT):
        s.op('dve', lambda e: e.tensor_tensor(out=gt[:], in0=yacc[:, t, :], in1=zs[:, t, :], op=ALU.mult), reads=[(yacc, t), (zs, t)], writes=[gt])
        s.op('act', lambda e: e.activation(out=gj[:], in_=gt[:], func=AF.Square, accum_out=ssg[:]), reads=[gt], writes=[gj, ssg])
        rstd_from_ss(ssg, ssg, 256)
        s.op('dve', lambda e: e.scalar_tensor_tensor(out=gb[:], in0=gt[:], scalar=ssg[:, 0:1], in1=P['snw'][:], op0=ALU.mult, op1=ALU.mult),
             reads=[gt, ssg, P['snw']], writes=[gb])
        pb = bank()
        pbb = pb.t[:].bitcast(BF16)
        for c in range(2):
            s.op('pe', lambda e, c=c: e.transpose(out=pbb[:, c * 128:(c + 1) * 128], in_=gb[:, c * 128:(c + 1) * 128], identity=C['ident_b'][:]),
                 reads=[gb, C['ident_b']], writes=[pb])
        s.op('act', lambda e: e.copy(out=mixT[:, 0:2, tcols(t)], in_=pbb[:, 0:256].rearrange("p (c n) -> p c n", c=2)), reads=[pb], writes=[(mixT, ('ssd', t))])
import os


def s5_stage(st, nc, s, sbuf, bank, P, D, l, hT, mixT, load_w, PIECES, DB, stop, S5SCR, mkstg, **_):
    P = dict(P)
    P['WB'] = sbuf(st, "WBs", [128, 2, 8, 2, 128], BF16)
    P['WC'] = sbuf(st, "WCs", [128, 2, 8, 2, 128])
    P['cosT'] = sbuf(st, "cosTs", [128, 16, 256])
    P['sinT'] = sbuf(st, "sinTs", [128, 16, 256])
    for nm in ['WB', 'WC', 'cosT', 'sinT']:
        dd, db = S5SCR[(nm, l)]
        dst = P[nm][:].rearrange("p a b c d -> p (a b c d)") if nm in ('WB', 'WC') else P[nm][:].rearrange("p a b -> p (a b)")
        s.dma(dst, dd.ap(), reads=[db], writes=[P[nm]], g=2)
    wu = sbuf(st, "wu", [128, 8, 256], BF16)
    with contextlib.ExitStack() as s0:
        load_w(mkstg(s0), wu, 'w_in', l, 2312, 256, 0)
        s.barrier()
    uT = sbuf(st, "uT", [128, 2, SEQ], BF16)
    yT = sbuf(st, "yT", [128, 2, SEQ])
    for c in range(2):
        for (c0, c1) in PIECES:
            pb = bank()
            n = c1 - c0
            for k in range(8):
                s.op('pe', lambda e, k=k: e.matmul(pb[:, 0:n], lhsT=wu[:, k, c * 128:(c + 1) * 128], rhs=hT[:, k, c0:c1], start=(k == 0), stop=(k == 7)),
                     reads=[hT, wu], writes=[pb])
            s.op('act', lambda e: e.copy(out=uT[:, c, c0:c1], in_=pb[:, 0:n]), reads=[pb], writes=[uT])
            s.op('dve', lambda e: e.tensor_scalar(out=yT[:, c, c0:c1], in0=pb[:, 0:n], scalar1=P['s5d'][:, c:c + 1], scalar2=None, op0=ALU.mult),
                 reads=[pb, P['s5d']], writes=[(yT, c)])
    T = 256
    NCH = SEQ // T
    bs = [sbuf(st, f"s5b{i}", [128, 2, T]) for i in range(2)]
    gin = [sbuf(st, f"s5g{i}", [128, 2, T]) for i in range(2)]
    tt = [sbuf(st, f"s5t{i}", [128, 4, T]) for i in range(2)]
    gs = [sbuf(st, f"s5s{i}", [128, 2, T]) for i in range(2)]
    hh_ = [sbuf(st, f"s5h{i}", [128, 2, T]) for i in range(2)]
    t2 = tt
    ini = [sbuf(st, f"s5c{q}", [128, 2]) for q in range(8)]
    ctmp = [sbuf(st, f"s5ct{i}", [128, 2]) for i in range(2)]
    nsin1 = sbuf(st, "s5nsin1", [128, 16])
    s.op('dve', lambda e: e.tensor_scalar(out=nsin1[:], in0=P['sinT'][:, :, 1], scalar1=-1.0, scalar2=None, op0=ALU.mult), reads=[P['sinT']], writes=[nsin1])
    items = []
    for d in range(2):
        order = list(range(NCH)) if d == 0 else [0] + list(range(NCH - 1, 0, -1))
        for kt in range(2):
            for ci, ch in enumerate(order):
                for ql in range(4):
                    items.append((d, kt, ci, ch, ql, len(order)))
    pYs = {}

    def front(i):
        d, kt, ci, ch, ql, nord = items[i]
        q = kt * 4 + ql
        j = d * 8 + q
        i2 = i % 2
        B_, G_, T_ = bs[i2], gin[i2], tt[i2]
        ucols = uT[:, kt, ch * T:(ch + 1) * T] if d == 0 else uT[:, kt, ch * T:(ch + 1) * T][:, ::-1]
        cosj = P['cosT'][:, j, :]
        sinj = P['sinT'][:, j, :]
        pB = bank(2 + (i % 6))
        for ri in range(2):
            s.op('pe', lambda e, ri=ri: e.matmul(pB[:, ri * T:(ri + 1) * T], lhsT=P['WB'][:, d, q, ri, :], rhs=ucols, start=True, stop=True),
                 reads=[uT, P['WB']], writes=[pB])
        s.op('act', lambda e: e.copy(out=B_[:].rearrange("p a t -> p (a t)"), in_=pB[:, 0:2 * T]), reads=[pB], writes=[B_])
        TTp = lambda o, a, b_, op, rd, wr: s.op('pool', lambda e: e.tensor_tensor(out=o, in0=a, in1=b_, op=op), reads=rd, writes=wr)
        TTp(T_[:, 0, :], B_[:, 0, :], cosj, ALU.mult, [B_, P['cosT']], [(T_, 0)])
        TTp(T_[:, 1, :], B_[:, 1, :], sinj, ALU.mult, [B_, P['sinT']], [(T_, 1)])
        TTp(T_[:, 2, :], B_[:, 1, :], cosj, ALU.mult, [B_, P['cosT']], [(T_, 2)])
        TTp(T_[:, 3, :], B_[:, 0, :], sinj, ALU.mult, [B_, P['sinT']], [(T_, 3)])
        TTp(G_[:, 0, :], T_[:, 0, :], T_[:, 1, :], ALU.add, [(T_, 0), (T_, 1)], [(G_, 0)])
        TTp(G_[:, 1, :], T_[:, 2, :], T_[:, 3, :], ALU.subtract, [(T_, 2), (T_, 3)], [(G_, 1)])

    def back(i):
        d, kt, ci, ch, ql, nord = items[i]
        q = kt * 4 + ql
        j = d * 8 + q
        i2 = i % 2
        G_, S_, H_, U_ = gin[i2], gs[i2], hh_[i2], t2[i2]
        cosj = P['cosT'][:, j, :]
        sinj = P['sinT'][:, j, :]
        if ql == 0:
            pYs['cur'] = bank(len(pYs.setdefault('n', [])) % 2)
            pYs['n'].append(0)
        pY = pYs['cur']
        rb = P['s5r'][:, j:j + 1].to_broadcast([128, T])
        for ri in range(2):
            init = ini[q][:, ri:ri + 1] if ci > 0 else 0.0
            s.op('dve', lambda e, ri=ri, init=init: e.tensor_tensor_scan(out=S_[:, ri, :], data0=rb, data1=G_[:, ri, :], initial=init, op0=ALU.mult, op1=ALU.add),
                 reads=[(G_, ri), P['s5r'], ini[q]], writes=[(S_, ri)])
        TTd = lambda o, a, b_, op, rd, wr: s.op('dve', lambda e: e.tensor_tensor(out=o, in0=a, in1=b_, op=op), reads=rd, writes=wr)
        TTd(U_[:, 0, :], S_[:, 0, :], cosj, ALU.mult, [(S_, 0), P['cosT']], [(U_, 0)])
        TTd(U_[:, 1, :], S_[:, 1, :], sinj, ALU.mult, [(S_, 1), P['sinT']], [(U_, 1)])
        TTd(U_[:, 2, :], S_[:, 0, :], sinj, ALU.mult, [(S_, 0), P['sinT']], [(U_, 2)])
        TTd(U_[:, 3, :], S_[:, 1, :], cosj, ALU.mult, [(S_, 1), P['cosT']], [(U_, 3)])
        TTd(H_[:, 0, :], U_[:, 0, :], U_[:, 1, :], ALU.subtract, [(U_, 0), (U_, 1)], [(H_, 0)])
        TTd(H_[:, 1, :], U_[:, 2, :], U_[:, 3, :], ALU.add, [(U_, 2), (U_, 3)], [(H_, 1)])
        if ci < nord - 1:
            c1 = P['cosT'][:, j, 1:2]
            s1_ = P['sinT'][:, j, 1:2]
            ns1 = nsin1[:, j:j + 1]
            ct = ctmp[i2]
            hre_l = H_[:, 0, T - 1:T]
            him_l = H_[:, 1, T - 1:T]
            s.op('act', lambda e: e.activation(out=ct[:, 0:1], in_=him_l, func=AF.Copy, scale=ns1), reads=[(H_, 1), nsin1], writes=[ct])
            s.op('act', lambda e: e.activation(out=ct[:, 1:2], in_=hre_l, func=AF.Copy, scale=s1_), reads=[(H_, 0), P['sinT']], writes=[ct])
            s.op('act', lambda e: e.activation(out=ini[q][:, 0:1], in_=hre_l, func=AF.Identity, scale=c1, bias=ct[:, 0:1]),
                 reads=[(H_, 0), ct, P['cosT']], writes=[ini[q]])
            s.op('act', lambda e: e.activation(out=ini[q][:, 1:2], in_=him_l, func=AF.Identity, scale=c1, bias=ct[:, 1:2]),
                 reads=[(H_, 1), ct, P['cosT']], writes=[ini[q]])
        for ri in range(2):
            s.op('pe', lambda e, ri=ri: e.matmul(pY[:, 0:T], lhsT=P['WC'][:, d, q, ri, :], rhs=H_[:, ri, :],
                                                start=(ql == 0 and ri == 0), stop=(ql == 3 and ri == 1)),
                 reads=[(H_, ri), P['WC']], writes=[pY])
        if ql == 3:
            ycols = yT[:, kt, ch * T:(ch + 1) * T] if d == 0 else yT[:, kt, ch * T:(ch + 1) * T][:, ::-1]
            s.op('dve', lambda e: e.tensor_tensor(out=ycols, in0=ycols, in1=pY[:, 0:T], op=ALU.add), reads=[pY, (yT, kt)], writes=[(yT, kt)])

    front(0)
    for i in range(len(items)):
        if i + 1 < len(items):
            front(i + 1)
        back(i)
    if 's5_cos' in DB:
        s.dma(DB['s5_cos'].ap(), P['cosT'][:].rearrange("p a b -> p (a b)"), reads=[P['cosT']], g=5)
        s.dma(DB['s5_sin'].ap(), P['sinT'][:].rearrange("p a b -> p (a b)"), reads=[P['sinT']], g=5)
        s.dma(DB['s5_wc'].ap(), P['WC'][:].rearrange("p a b c d -> p (a b c d)"), reads=[P['WC']], g=5)
    if 's5_y' in DB:
        s.dma(DB['s5_y'].ap(), yT[:].rearrange("p a b -> p (a b)"), reads=[yT], g=5)
        s.dma(DB['s5_u'].ap(), uT[:].rearrange("p a b -> p (a b)"), reads=[uT], q='pool', g=5)
    ygb = uT
    fi = 0
    for c in range(2):
        for (g0, g1) in [(0, 1024), (1024, 2048), (2048, 2304)]:
            tb = tt[fi % 2]
            fi += 1
            n = g1 - g0
            tmp = tb[:].rearrange("p a t -> p (a t)")[:, 0:n]
            yv = yT[:, c, g0:g1]
            s.op('pool', lambda e: e.tensor_tensor(out=tmp, in0=yv, in1=yv, op=ALU.mult), reads=[(yT, c)], writes=[tb])
            s.op('pool', lambda e: e.tensor_scalar(out=tmp, in0=tmp, scalar1=0.044715, scalar2=1.0, op0=ALU.mult, op1=ALU.add), reads=[tb], writes=[tb])
            s.op('dve', lambda e: e.tensor_tensor(out=tmp, in0=tmp, in1=yv, op=ALU.mult), reads=[tb, (yT, c)], writes=[tb])
            s.op('act', lambda e: e.activation(out=tmp, in_=tmp, func=AF.Sigmoid, scale=1.5957691216057308), reads=[tb], writes=[tb])
            s.op('dve', lambda e: e.tensor_tensor(out=yv, in0=yv, in1=tmp, op=ALU.mult), reads=[tb, (yT, c)], writes=[(yT, c)])
            s.op('act', lambda e: e.copy(out=ygb[:, c, g0:g1], in_=yv), reads=[(yT, c)], writes=[ygb])
    sg = sbuf(st, "s5sg", [128, 512])
    for c in range(2):
        for (c0, c1) in PIECES:
            pb = bank()
            n = c1 - c0
            for k in range(2):
                s.op('pe', lambda e, k=k: e.matmul(pb[:, 0:n], lhsT=P['gluw'][:, k, c * 128:(c + 1) * 128], rhs=ygb[:, k, c0:c1], start=(k == 0), stop=(k == 1)),
                     reads=[ygb, P['gluw']], writes=[pb])
            s.op('act', lambda e: e.activation(out=sg[:, 0:n], in_=pb[:, 0:n], func=AF.Sigmoid, bias=P['glub'][:, c:c + 1], scale=1.0),
                 reads=[pb, P['glub']], writes=[sg])
            s.op('dve', lambda e: e.tensor_tensor(out=mixT[:, 6 + c, c0:c1], in0=yT[:, c, c0:c1], in1=sg[:, 0:n], op=ALU.mult),
                 reads=[sg, (yT, c)], writes=[(mixT, ('s5', c, c0))])


def attn_stage(st, nc, s, sbuf, bank, P, D, l, hT, mixT, load_w, rstd_from_ss, tcols, need_ctx, C, DB, stop, mkstg, **_):
    _AC = int(os.environ.get('ACUT', '0'))
    QT = sbuf(st, "QT", [128, 4, SEQ], BF16)
    KT = sbuf(st, "KT", [128, 4, SEQ], BF16)
    Vt = sbuf(st, "Vt", [128, NT, 4, 129], BF16)
    s.op('pool', lambda e: e.memset(Vt[:], 1.0), writes=[Vt])
    with contextlib.ExitStack() as sq:
        wqk = sbuf(sq, "wqk", [128, 8, 1024], BF16)
        wv = sbuf(sq, "wv", [128, 8, 512], BF16)
        with contextlib.ExitStack() as s0:
            stg = mkstg(s0)
            load_w(stg, wqk, 'w_in', l, 776, 1024, 0)
            load_w(stg, wv, 'w_in', l, 1800, 512, 0)
            s.barrier()
        qkf = sbuf(sq, "qkf", [128, 1024])
        qkr = sbuf(sq, "qkr", [128, 1024], BF16)
        rt = sbuf(sq, "ropet", [128, 4, 16, 2, 16])
        for t in range(NT):
            pq, pk, pv = bank(), bank(), bank()
            for (pb, w, c0) in [(pq, wqk, 0), (pk, wqk, 512), (pv, wv, 0)]:
                for k in range(8):
                    s.op('pe', lambda e, k=k, pb=pb, w=w, c0=c0: e.matmul(pb[:, :], lhsT=hT[:, k, tcols(t)], rhs=w[:, k, c0:c0 + 512], start=(k == 0), stop=(k == 7)),
                         reads=[hT, w], writes=[pb])
            s.op('act', lambda e: e.copy(out=Vt[:, t, :, 0:128], in_=pv[:, :].rearrange("p (h e) -> p h e", h=4)), reads=[pv], writes=[(Vt, t)])
            if t < 2:
                s.op('act', lambda e: e.copy(out=qkr[:, 0:512], in_=pq[:, :]), reads=[pq], writes=[qkr])
                s.op('act', lambda e: e.copy(out=qkr[:, 512:1024], in_=pk[:, :]), reads=[pk], writes=[qkr])
            else:
                s.op('act', lambda e: e.copy(out=qkf[:, 0:512], in_=pq[:, :]), reads=[pq], writes=[qkf])
                s.op('act', lambda e: e.copy(out=qkf[:, 512:1024], in_=pk[:, :]), reads=[pk], writes=[qkf])
                tl = t - 2
                xv = qkf[:].rearrange("p (m a b j) -> p m a b j", m=16, a=2, b=2)
                ov = qkr[:].rearrange("p (m a b j) -> p m a b j", m=16, a=2, b=2)
                cosb = C['ropeC'][:, tl, :].rearrange("p (a j) -> p a j", a=2).unsqueeze(1).to_broadcast([128, 16, 2, 16])
                sinb = C['ropeS'][:, tl, :].rearrange("p (a j) -> p a j", a=2).unsqueeze(1).to_broadcast([128, 16, 2, 16])
                PO = lambda o, a, b_, op, rd, wr: s.op('pool', lambda e: e.tensor_tensor(out=o, in0=a, in1=b_, op=op), reads=rd, writes=wr)
                PO(rt[:, 0], xv[:, :, :, 0, :], cosb, ALU.mult, [qkf, C['ropeC']], [(rt, 0)])
                PO(rt[:, 1], xv[:, :, :, 1, :], sinb, ALU.mult, [qkf, C['ropeS']], [(rt, 1)])
                PO(rt[:, 2], xv[:, :, :, 1, :], cosb, ALU.mult, [qkf, C['ropeC']], [(rt, 2)])
                PO(rt[:, 3], xv[:, :, :, 0, :], sinb, ALU.mult, [qkf, C['ropeS']], [(rt, 3)])
                PO(ov[:, :, :, 0, :], rt[:, 0], rt[:, 1], ALU.subtract, [(rt, 0), (rt, 1)], [qkr])
                PO(ov[:, :, :, 1, :], rt[:, 2], rt[:, 3], ALU.add, [(rt, 2), (rt, 3)], [qkr])
            pb = bank()
            pbb = pb.t[:].bitcast(BF16)
            for m in range(8):
                s.op('pe', lambda e, m=m: e.transpose(out=pbb[:, m * 128:(m + 1) * 128], in_=qkr[:, m * 128:(m + 1) * 128], identity=C['ident_b'][:]),
                     reads=[qkr, C['ident_b']], writes=[pb])
            s.op('act', lambda e: e.copy(out=QT[:, :, tcols(t)], in_=pbb[:, 0:512].rearrange("p (h n) -> p h n", h=4)), reads=[pb], writes=[(QT, t)])
            s.op('dve', lambda e: e.tensor_copy(out=KT[:, :, tcols(t)], in_=pbb[:, 512:1024].rearrange("p (h n) -> p h n", h=4)), reads=[pb], writes=[(KT, t)])

        s.barrier()
    if _AC == 2:
        return
    ETs = [sbuf(st, f"ET{i}", [128, NT, 512], BF16) for i in range(2)]
    oc = sbuf(st, "oc", [128, 2, 4, 129])
    rr = sbuf(st, "att_rr", [128, 2, 4])
    o_ = sbuf(st, "att_o", [128, 128])
    oj = sbuf(st, "att_oj", [128, 128])
    ob = sbuf(st, "att_ob", [128, 128], BF16)
    sso = sbuf(st, "att_ss", [128, 1])
    groups = [(256 + 512 * i, 512, NT) for i in range(4)]
    if need_ctx:
        groups = [(0, 256, 2)] + groups
    for h in range(4):
        if _AC == 3 and h == 1:
            return
        for (q0, nq, nk) in groups:
            nsub = nq // 128
            for kt in range(nk):
                for c in range(2):
                    pS = bank()
                    s.op('pe', lambda e, kt=kt, c=c: e.matmul(pS[:, 0:nq], lhsT=KT[64 * c:64 * c + 64, h, tcols(kt)], rhs=QT[64 * c:64 * c + 64, h, q0:q0 + nq],
                                                             start=True, stop=True), reads=[KT, QT], writes=[pS])
                    s.op('act', lambda e, kt=kt, c=c: e.activation(out=ETs[c][:, kt, 0:nq], in_=pS[:, 0:nq], func=AF.Exp, scale=0.125), reads=[pS], writes=[(ETs[c], kt)])
            for c in range(2):
                ET = ETs[c]
                for qs in range(nsub):
                    pO = bank()
                    for kt in range(nk):
                        s.op('pe', lambda e, kt=kt: e.matmul(pO[:, 0:129], lhsT=ET[:, kt, qs * 128:(qs + 1) * 128], rhs=Vt[:, kt, h, :], start=(kt == 0), stop=(kt == nk - 1)),
                             reads=[(ET, kt), Vt], writes=[pO])
                    s.op('dve', lambda e, c=c, qs=qs: e.tensor_copy(out=oc[:, c, qs, :], in_=pO[:, 0:129]), reads=[pO], writes=[(oc, (c, qs))])
            s.op('dve', lambda e: e.reciprocal(out=rr[:, :, 0:nsub], in_=oc[:, :, 0:nsub, 128]), reads=[oc], writes=[rr])
            s.op('dve', lambda e: e.tensor_scalar(out=rr[:, 1, :], in0=rr[:, 1, :], scalar1=P['nlam'][:, 0:1], scalar2=None, op0=ALU.mult), reads=[rr, P['nlam']], writes=[rr])
            for qs in range(nsub):
                s.op('dve', lambda e: e.tensor_scalar(out=o_[:], in0=oc[:, 0, qs, 0:128], scalar1=rr[:, 0, qs:qs + 1], scalar2=None, op0=ALU.mult), reads=[oc, rr], writes=[o_])
                s.op('dve', lambda e: e.scalar_tensor_tensor(out=o_[:], in0=oc[:, 1, qs, 0:128], scalar=rr[:, 1, qs:qs + 1], in1=o_[:], op0=ALU.mult, op1=ALU.add),
                     reads=[oc, rr, o_], writes=[o_])
                s.op('act', lambda e: e.activation(out=oj[:], in_=o_[:], func=AF.Square, accum_out=sso[:]), reads=[o_], writes=[oj, sso])
                s.op('act', lambda e: e.activation(out=sso[:], in_=sso[:], func=AF.Ln, bias=EPS, scale=1.0 / 128), reads=[sso], writes=[sso])
                s.op('act', lambda e: e.activation(out=sso[:], in_=sso[:], func=AF.Exp, scale=-0.5), reads=[sso], writes=[sso])
                s.op('dve', lambda e: e.scalar_tensor_tensor(out=ob[:], in0=o_[:], scalar=sso[:, 0:1], in1=P['subw'][:], op0=ALU.mult, op1=ALU.mult),
                     reads=[o_, sso, P['subw']], writes=[ob])
                pb = bank()
                pbb = pb.t[:].bitcast(BF16)
                s.op('pe', lambda e: e.transpose(out=pbb[:, 0:128], in_=ob[:], identity=C['ident_b'][:]), reads=[ob, C['ident_b']], writes=[pb])
                qc = q0 + qs * 128
                s.op('act', lambda e: e.copy(out=mixT[:, 2 + h, qc:qc + 128], in_=pbb[:, 0:128]), reads=[pb], writes=[(mixT, ('att', h, qc))])


NWB = int(os.environ.get('NWB', '2'))


def ffn_stage(st, nc, s, sbuf, bank, P, D, l, b, NB, NCOL, hT, need_ctx, last, xs_d, XS, out_d, OUT, gsc_d, GSC, load_w, rstd_from_ss, tcols, DB, stop, mkstg, load_cast, **_):
    stg = mkstg(st, 4)
    wd = sbuf(st, "wd", [128, 22, 1024], BF16)
    wdsrc = D['ffn_w_down'].ap()[l].rearrange("(f p) c -> p f c", p=128)
    for f in range(22):
        load_cast(stg, wd[:, f, :], wd, wdsrc[:, f, :], [1024], key=f)
    gbc = sbuf(st, "gbcF", [128, 2, 1024])
    s.dma(gbc[:, 0, :], bass.AP(gsc_d, ((l * 2 + 1) * NCOL + b) * 1024, [[0, 128], [1, 1024]]), reads=[GSC], writes=[gbc], g=0)
    s.dma(gbc[:, 1, :], bass.AP(gsc_d, ((l * 2 + 1) * NCOL + NB) * 1024, [[0, 128], [1, 1024]]), reads=[GSC], writes=[gbc], g=0)
    actT = sbuf(st, "actT", [128, 22, 1152], BF16)
    wg = [sbuf(st, f"wg{i}", [128, 8, 128], BF16) for i in range(NWB)]
    wu = [sbuf(st, f"wu{i}", [128, 8, 128], BF16) for i in range(NWB)]
    graw = sbuf(st, "graw", [128, 1160])
    gc = sbuf(st, "gc", [128, 1160])
    ub = sbuf(st, "ub", [128, 1152], BF16)
    xt2 = [sbuf(st, f"xtF{i}", [128, 1024]) for i in range(2)]
    junk = sbuf(st, "junkF", [128, 1024], BF16)
    ss2 = sbuf(st, "ssF", [128, 2])
    rs2 = sbuf(st, "rsF", [128, 1])
    ytmp = sbuf(st, "ytmpF", [128, 1024])
    s.op('pool', lambda e: e.memset(graw[:], 0.0), writes=[graw])
    if stop == 'F0':
        return
    halves = []
    if need_ctx:
        halves.append(dict(gp=[(0, 256, 1), (256, 768, 259), (768, 1153, 771)], up=[(0, 256, 0), (256, 768, 256), (768, 1152, 768)],
                           outs=[(0, 256, 1), (256, 1152, 259)], tiles=list(range(0, 9)), base=0, pads=[0, 257, 258]))
    else:
        halves.append(dict(gp=[(256, 768, 259), (768, 1153, 771)], up=[(256, 768, 256), (768, 1152, 768)],
                           outs=[(256, 1152, 259)], tiles=list(range(2, 9)), base=0, pads=[0, 257, 258]))
    halves.append(dict(gp=[(1151, 1663, 0), (1663, 2175, 512), (2175, 2304, 1024)], up=[(1152, 1664, 0), (1664, 2176, 512), (2176, 2304, 1024)],
                       outs=[(0, 1152, 1)], tiles=list(range(9, 18)), base=1152, pads=[1153]))
    for hv in halves:
        for pcol in hv['pads']:
            s.op('pool', lambda e, pcol=pcol: e.memset(graw[:, pcol:pcol + 1], 0.0), writes=[graw])
        for f in range(int(os.environ.get('FSTART', '0')), int(os.environ.get('FCUT', '22'))):
            g_, u_ = wg[f % NWB], wu[f % NWB]
            load_cast(stg, g_[:], g_, D['ffn_w_gate'].ap()[l].rearrange("(k p) c -> p k c", p=128)[:, :, f * 128:(f + 1) * 128], [8, 128], eng='pool')
            load_cast(stg, u_[:], u_, D['ffn_w_up'].ap()[l].rearrange("(k p) c -> p k c", p=128)[:, :, f * 128:(f + 1) * 128], [8, 128], eng='pool')
            _FS = int(os.environ.get('FSTEP', '0')); _lastf = f == int(os.environ.get('FCUT', '22')) - 1
            if _lastf and _FS == 1:
                return
            for (c0, c1, dst) in hv['gp']:
                pb = bank()
                n = c1 - c0
                for k in range(8):
                    s.op('pe', lambda e, k=k: e.matmul(pb[:, 0:n], lhsT=g_[:, k, :], rhs=hT[:, k, c0:c1], start=(k == 0), stop=(k == 7)), reads=[hT, g_], writes=[pb])
                s.op('act', lambda e: e.copy(out=graw[:, dst:dst + n], in_=pb[:, 0:n]), reads=[pb], writes=[graw])
            if _lastf and _FS == 2:
                return
            for (c0, c1, dst) in hv['up']:
                pb = bank()
                n = c1 - c0
                for k in range(8):
                    s.op('pe', lambda e, k=k: e.matmul(pb[:, 0:n], lhsT=u_[:, k, :], rhs=hT[:, k, c0:c1], start=(k == 0), stop=(k == 7)), reads=[hT, u_], writes=[pb])
                s.op('dve', lambda e: e.tensor_copy(out=ub[:, dst:dst + n], in_=pb[:, 0:n]), reads=[pb], writes=[ub])
            if _lastf and _FS == 3:
                return
            s.op('act', lambda e: e.activation(out=gc[:, 1:1157], in_=graw[:, 1:1157], func=AF.Identity, scale=P['fcw'][:, f, 1:2], bias=P['fcb'][:, f:f + 1]),
                 reads=[graw, P['fcw'], P['fcb']], writes=[gc])
            for j in (0, 2):
                s.op('pool', lambda e, j=j: e.scalar_tensor_tensor(out=gc[:, 1:1157], in0=graw[:, j:j + 1156], scalar=P['fcw'][:, f, j:j + 1], in1=gc[:, 1:1157],
                                                                  op0=ALU.mult, op1=ALU.add), reads=[graw, gc, P['fcw']], writes=[gc]) if False else \
                    s.op('dve', lambda e, j=j: e.scalar_tensor_tensor(out=gc[:, 1:1157], in0=graw[:, j:j + 1156], scalar=P['fcw'][:, f, j:j + 1], in1=gc[:, 1:1157],
                                                                     op0=ALU.mult, op1=ALU.add), reads=[graw, gc, P['fcw']], writes=[gc])
            s.op('act', lambda e: e.activation(out=gc[:, 1:1157], in_=gc[:, 1:1157], func=AF.Silu), reads=[gc], writes=[gc])
            for (a0, a1, src) in hv['outs']:
                n = a1 - a0
                s.op('dve', lambda e: e.tensor_tensor(out=actT[:, f, a0:a1], in0=gc[:, src:src + n], in1=ub[:, a0:a1], op=ALU.mult),
                     reads=[gc, ub], writes=[(actT, f)])
            if stop == 'F1':
                return
        if stop == 'F2':
            return
        htl = hv['tiles']

        def ldF(i):
            s.dma(xt2[i % 2][:], xs_d.ap()[b, tcols(htl[i]), :], reads=[(XS, b)], writes=[xt2[i % 2]], g=0)
        ldF(0)
        for ti, t in enumerate(htl):
            if ti + 1 < len(htl):
                ldF(ti + 1)
            xt = xt2[ti % 2]
            tc = t * 128 - hv['base']
            pbs = [bank(), bank()]
            for hf in range(2):
                for f in range(22):
                    s.op('pe', lambda e, f=f, hf=hf: e.matmul(pbs[hf][:, :], lhsT=actT[:, f, tc:tc + 128], rhs=wd[:, f, hf * 512:(hf + 1) * 512],
                                                             start=(f == 0), stop=(f == 21)), reads=[actT, wd], writes=[pbs[hf]])
                s.op('act', lambda e, hf=hf: e.activation(out=junk[:, hf * 512:(hf + 1) * 512], in_=pbs[hf][:, :], func=AF.Square, accum_out=ss2[:, hf:hf + 1]),
                     reads=[pbs[hf]], writes=[junk, ss2])
            s.op('dve', lambda e: e.tensor_tensor(out=rs2[:], in0=ss2[:, 0:1], in1=ss2[:, 1:2], op=ALU.add), reads=[ss2], writes=[rs2])
            rstd_from_ss(rs2, rs2, 1024)
            gi = 1 if t < 2 else 0
            for hf in range(2):
                s.op('dve', lambda e, hf=hf: e.scalar_tensor_tensor(out=ytmp[:, hf * 512:(hf + 1) * 512], in0=pbs[hf][:, :], scalar=rs2[:, 0:1],
                                                                   in1=gbc[:, gi, hf * 512:(hf + 1) * 512], op0=ALU.mult, op1=ALU.mult),
                     reads=[pbs[hf], rs2, gbc], writes=[ytmp])
            s.op('pool', lambda e: e.tensor_tensor(out=xt[:], in0=xt[:], in1=ytmp[:], op=ALU.add), reads=[xt, ytmp], writes=[xt])
            if last:
                if t >= 2:
                    s.dma(out_d.ap()[b, (t - 2) * 128:(t - 1) * 128, :], xt[:], reads=[xt], writes=[OUT], g=1)
            else:
                s.dma(xs_d.ap()[b, tcols(t), :], xt[:], reads=[xt], writes=[(XS, b)], g=1)


def build(NB=4, LAYERS=(0, 1), dbg=(), stop=None):
    nc = bass.Bass("TRN2", target_bir_lowering=False)
    D = {}

    def din(name, shape):
        D[name] = nc.dram_tensor(name, list(shape), F32, kind="ExternalInput")
        return D[name]

    x_d = din("x", [NB, 2048, 1024])
    ctx_d = din("ctx", [NB, 256, 1024])
    cc_d = din("cc", [NB + 1, 1024])
    for n, sh in PARAMS:
        din(n, sh)
    for n, a in host_consts().items():
        din(n, a.shape)
    out_d = nc.dram_tensor("out", [NB, 2048, 1024], F32, kind="ExternalOutput")
    xs_d = nc.dram_tensor("xs", [NB, SEQ, 1024], F32)
    gsc_d = nc.dram_tensor("gsc", [2, 2, NB + 1, 1024], F32)
    DB = {}
    for n, sh in dbg:
        DB[n] = nc.dram_tensor("dbg_" + n, list(sh), F32, kind="ExternalOutput")

    NCOL = NB + 1
    with contextlib.ExitStack() as top:
        s = Sched(nc, top)
        top.enter_context(nc.allow_non_contiguous_dma(reason="small param layout loads"))
        top.enter_context(nc.allow_low_precision(reason="bf16 matmul operands"))
        XS = Buf(xs_d, "xs")
        GSC = Buf(gsc_d, "gsc")
        OUT = Buf(out_d, "out")
        DIN = Buf(None, "din")

        uid = [0]

        def sbuf(st, name, shape, dt=F32):
            uid[0] += 1
            name = f"{name}_u{uid[0]}"
            return Buf(st.enter_context(nc.sbuf_tensor(name, list(shape), dt)), name)

        banks = [Buf(top.enter_context(nc.psum_tensor(f"pb{i}", [128, 512], F32)), f"pb{i}", psum=True) for i in range(8)]
        bank_i = [0]

        def bank(idx=None):
            if idx is not None:
                return banks[idx]
            b = banks[bank_i[0] % 8]
            bank_i[0] += 1
            return b

        def dmp(name, ap, buf, key=None):
            if name in DB:
                s.dma(DB[name].ap() if not isinstance(name, tuple) else None, ap, reads=[(buf, key)], g=5)

        ident_f = sbuf(top, "ident_f", [128, 128])
        ident_b = sbuf(top, "ident_b", [128, 128], BF16)
        U_f = sbuf(top, "U_f", [128, 128])
        UT_f = sbuf(top, "UT_f", [128, 128])
        mf_f = sbuf(top, "mf_f", [128, 128])
        mb_f = sbuf(top, "mb_f", [128, 128])
        ones_f = sbuf(top, "ones_f", [128, 128])
        ropeC = sbuf(top, "ropeC", [128, 16, 32])
        ropeS = sbuf(top, "ropeS", [128, 16, 32])
        for bufc, nm in [(ident_f, 'c_ident'), (U_f, 'c_U'), (UT_f, 'c_UT'), (mf_f, 'c_mf'), (mb_f, 'c_mb'), (ones_f, 'c_ones')]:
            s.dma(bufc[:], D[nm].ap(), writes=[bufc], g=0)
        s.dma(ropeC[:], D['c_cos'].ap().rearrange("(t p) j -> p t j", p=128), writes=[ropeC], g=0)
        s.dma(ropeS[:], D['c_sin'].ap().rearrange("(t p) j -> p t j", p=128), writes=[ropeS], g=0)
        s.op('act', lambda e: e.copy(out=ident_b[:], in_=ident_f[:]), reads=[ident_f], writes=[ident_b])
        ones_b = sbuf(top, "ones_b", [128, 128], BF16)
        s.op('act', lambda e: e.copy(out=ones_b[:], in_=ones_f[:]), reads=[ones_f], writes=[ones_b])

        LP = {}
        for l in LAYERS:
            P = {}
            P['s1a'] = sbuf(top, f"s1a{l}", [128, 8, NCOL])
            P['sha'] = sbuf(top, f"sha{l}", [128, 8, NCOL])
            P['s1f'] = sbuf(top, f"s1f{l}", [128, 8, NCOL])
            P['shf'] = sbuf(top, f"shf{l}", [128, 8, NCOL])
            P['dtb'] = sbuf(top, f"dtb{l}", [128, 8])
            P['aneg'] = sbuf(top, f"aneg{l}", [128, 8])
            P['dsk'] = sbuf(top, f"dsk{l}", [128, 4])
            P['snw'] = sbuf(top, f"snw{l}", [128, 256])
            P['scw'] = sbuf(top, f"scw{l}", [128, 4, 5])
            P['scb'] = sbuf(top, f"scb{l}", [128, 4])
            P['subw'] = sbuf(top, f"subw{l}", [128, 128])
            P['nlam'] = sbuf(top, f"nlam{l}", [128, 1])
            P['subwc'] = sbuf(top, f"subwc{l}", [128, 1])
            P['fcw'] = sbuf(top, f"fcw{l}", [128, 22, 3])
            P['fcb'] = sbuf(top, f"fcb{l}", [128, 22])
            P['s5d'] = sbuf(top, f"s5d{l}", [128, 2])
            P['glub'] = sbuf(top, f"glub{l}", [128, 2])
            P['gluw'] = sbuf(top, f"gluw{l}", [128, 2, 256], BF16)
            P['s5r'] = sbuf(top, f"s5r{l}", [128, 16])
            P['s5ar'] = sbuf(top, f"s5ar{l}", [128, 16])
            P['s5ai'] = sbuf(top, f"s5ai{l}", [128, 16])
            LP[l] = P

        S5SCR = {}
        LP['_negpi'] = sbuf(top, "negpi", [128, 1])
        s.op('pool', lambda e: e.memset(LP['_negpi'][:], -math.pi), writes=[LP['_negpi']])
        LP['_Z'] = sbuf(top, "Zt", [128, 8, 128])
        s.op('pool', lambda e: e.memset(LP['_Z'][:], 0.0), writes=[LP['_Z']])
        with contextlib.ExitStack() as st:
            scT = sbuf(st, "scT", [128, 8, NCOL])
            for col in range(NCOL):
                s.dma(scT[:, :, col], bass.AP(cc_d, col * 1024, [[1, 128], [128, 8]]), writes=[scT], g=0)
            s.op('act', lambda e: e.activation(out=scT[:], in_=scT[:], func=AF.Silu), reads=[scT], writes=[scT])
            wm = [sbuf(st, f"wm{i}", [128, 8, 512]) for i in range(2)]
            modT = sbuf(st, "modT", [128, 48, NCOL])
            modb = sbuf(st, "modb", [128, 48])
            nrm = sbuf(st, "nrm", [128, 4, 8])
            tmpg = sbuf(st, "tmpg", [128, 2, 8, NCOL])
            iota = sbuf(st, "iota", [128, 256])
            s.dma(iota[:], D['c_iota'].ap(), writes=[iota], g=0)
            S5D = {}
            for l in LAYERS:
                P = LP[l]
                P['WB'] = sbuf(st, f"WB{l}", [128, 2, 8, 2, 128], BF16)
                P['WC'] = sbuf(st, f"WC{l}", [128, 2, 8, 2, 128])
                P['cosT'] = sbuf(st, f"cosT{l}", [128, 16, 256])
                P['sinT'] = sbuf(st, f"sinT{l}", [128, 16, 256])
                s.dma(modb[:], bass.AP(D['mod_b'], l * 6144, [[1, 128], [128, 48]]), writes=[modb], g=0)
                for i, nm in enumerate(['mix_norm_pre', 'mix_norm_post', 'ffn_norm_pre', 'ffn_norm_post']):
                    s.dma(nrm[:, i, :], bass.AP(D[nm], l * 1024, [[1, 128], [128, 8]]), writes=[nrm], g=0)
                for cch in range(12):
                    w = wm[cch % 2]
                    s.dma(w[:], D['mod_w'].ap()[l].rearrange("(k p) c -> p k c", p=128)[:, :, cch * 512:(cch + 1) * 512],
                          writes=[w], g=1)
                    for jj in range(4):
                        j = cch * 4 + jj
                        pb = bank()
                        for k in range(8):
                            s.op('pe', lambda e, k=k, jj=jj, w=w, pb=pb: e.matmul(pb[:, 0:NCOL], lhsT=w[:, k, jj * 128:(jj + 1) * 128],
                                                                                 rhs=scT[:, k, :], start=(k == 0), stop=(k == 7)),
                                 reads=[w, scT], writes=[pb])
                        s.op('act', lambda e, j=j, pb=pb: e.activation(out=modT[:, j, :], in_=pb[:, 0:NCOL], func=AF.Identity,
                                                                      bias=modb[:, j:j + 1], scale=1.0),
                             reads=[pb, modb], writes=[modT])
                for (s1, sh, o_sh, o_sc, o_g, ipre, ipost, wi) in [(P['s1a'], P['sha'], 0, 8, 16, 0, 1, 0),
                                                                    (P['s1f'], P['shf'], 24, 32, 40, 2, 3, 1)]:
                    s.op('dve', lambda e, s1=s1, o_sc=o_sc: e.tensor_scalar(out=s1[:], in0=modT[:, o_sc:o_sc + 8, :], scalar1=1.0,
                                                                           scalar2=None, op0=ALU.add),
                         reads=[modT], writes=[s1])
                    s.op('dve', lambda e, s1=s1, ipre=ipre: e.tensor_tensor(out=s1[:], in0=s1[:],
                                                                           in1=nrm[:, ipre, :].unsqueeze(2).to_broadcast([128, 8, NCOL]),
                                                                           op=ALU.mult),
                         reads=[s1, nrm], writes=[s1])
                    s.op('dve', lambda e, sh=sh, o_sh=o_sh: e.tensor_copy(out=sh[:], in_=modT[:, o_sh:o_sh + 8, :]),
                         reads=[modT], writes=[sh])
                    s.op('dve', lambda e, wi=wi, o_g=o_g, ipost=ipost: e.tensor_tensor(
                        out=tmpg[:, wi], in0=modT[:, o_g:o_g + 8, :],
                        in1=nrm[:, ipost, :].unsqueeze(2).to_broadcast([128, 8, NCOL]), op=ALU.mult),
                        reads=[modT, nrm], writes=[tmpg])
                    for col in range(NCOL):
                        s.dma(bass.AP(gsc_d, ((l * 2 + wi) * NCOL + col) * 1024, [[1, 128], [128, 8]]), tmpg[:, wi, :, col],
                              reads=[tmpg], writes=[GSC], g=2)

                def bc(name, off, n):
                    return bass.AP(D[name], off, [[0, 128], [1, n]])
                s.dma(P['dtb'][:], bc('ssd_dt_bias', l * 8, 8), writes=[P['dtb']], g=0)
                s.dma(P['aneg'][:], bc('ssd_a_log', l * 8, 8), writes=[P['aneg']], g=0)
                s.op('act', lambda e, P=P: e.activation(out=P['aneg'][:], in_=P['aneg'][:], func=AF.Exp), reads=[P['aneg']], writes=[P['aneg']])
                s.op('dve', lambda e, P=P: e.tensor_scalar(out=P['aneg'][:], in0=P['aneg'][:], scalar1=-1.0, scalar2=None, op0=ALU.mult),
                     reads=[P['aneg']], writes=[P['aneg']])
                s.dma(P['dsk'][:], bc('ssd_d', l * 4, 4), writes=[P['dsk']], g=0)
                s.dma(P['snw'][:], bc('ssd_norm_w', l * 256, 256), writes=[P['snw']], g=0)
                for j in range(5):
                    s.dma(P['scw'][:, :, j], bass.AP(D['ssd_conv_w'], l * 2560 + j * 512, [[1, 128], [128, 4]]), writes=[P['scw']], g=0)
                s.dma(P['scb'][:], bass.AP(D['ssd_conv_b'], l * 512, [[1, 128], [128, 4]]), writes=[P['scb']], g=0)
                for j in range(3):
                    s.dma(P['fcw'][:, :, j], bass.AP(D['ffn_conv_w'], (l * 3 + j) * 2816, [[1, 128], [128, 22]]), writes=[P['fcw']], g=0)
                s.dma(P['fcb'][:], bass.AP(D['ffn_conv_b'], l * 2816, [[1, 128], [128, 22]]), writes=[P['fcb']], g=0)
                s.dma(P['s5d'][:], bass.AP(D['s5_d'], l * 256, [[1, 128], [128, 2]]), writes=[P['s5d']], g=0)
                s.dma(P['glub'][:], bass.AP(D['s5_glu_b'], l * 256, [[1, 128], [128, 2]]), writes=[P['glub']], g=0)
                gst = sbuf(st, f"gst{l}", [128, 2, 256])
                s.dma(gst[:], D['s5_glu_w'].ap()[l].rearrange("(k p) c -> p k c", p=128), writes=[gst], g=3)
                s.op('pool', lambda e, P=P, gst=gst: e.tensor_copy(out=P['gluw'][:], in_=gst[:]), reads=[gst], writes=[P['gluw']])
                lam_init = 0.8 - 0.6 * math.exp(-0.3 * l)
                s.dma(P['subw'][:], bc('diff_subln_w', l * 128, 128), writes=[P['subw']], g=0)
                s.dma(P['subwc'][:], bass.AP(D['diff_subln_w'], l * 128, [[1, 128], [1, 1]]), writes=[P['subwc']], g=0)
                s.op('dve', lambda e, P=P, li=lam_init: e.tensor_scalar(out=P['subwc'][:], in0=P['subwc'][:], scalar1=1.0 - li, scalar2=None,
                                                                      op0=ALU.mult), reads=[P['subwc']], writes=[P['subwc']])
                s.op('dve', lambda e, P=P, li=lam_init: e.tensor_scalar(out=P['subw'][:], in0=P['subw'][:], scalar1=1.0 - li, scalar2=None,
                                                                      op0=ALU.mult), reads=[P['subw']], writes=[P['subw']])
                lt = sbuf(st, f"lt{l}", [128, 4, 64])
                lsum = sbuf(st, f"lsum{l}", [128, 4])
                for i, nm in enumerate(['diff_lam_q1', 'diff_lam_k1', 'diff_lam_q2', 'diff_lam_k2']):
                    s.dma(lt[:, i, :], bc(nm, l * 64, 64), writes=[lt], g=0)
                s.op('dve', lambda e, lt=lt: e.tensor_tensor(out=lt[:, 0, :], in0=lt[:, 0, :], in1=lt[:, 1, :], op=ALU.mult), reads=[lt], writes=[lt])
                s.op('dve', lambda e, lt=lt: e.tensor_tensor(out=lt[:, 2, :], in0=lt[:, 2, :], in1=lt[:, 3, :], op=ALU.mult), reads=[lt], writes=[lt])
                s.op('dve', lambda e, lt=lt, lsum=lsum: e.reduce_sum(out=lsum[:, 0:1], in_=lt[:, 0, :], axis=mybir.AxisListType.X), reads=[lt], writes=[lsum])
                s.op('dve', lambda e, lt=lt, lsum=lsum: e.reduce_sum(out=lsum[:, 1:2], in_=lt[:, 2, :], axis=mybir.AxisListType.X), reads=[lt], writes=[lsum])
                s.op('act', lambda e, lsum=lsum: e.activation(out=lsum[:, 2:4], in_=lsum[:, 0:2], func=AF.Exp), reads=[lsum], writes=[lsum])
                s.op('dve', lambda e, lsum=lsum, P=P, li=lam_init: e.scalar_tensor_tensor(out=P['nlam'][:], in0=lsum[:, 3:4], scalar=-li,
                                                                                        in1=lsum[:, 2:3], op0=ALU.add, op1=ALU.subtract),
                     reads=[lsum], writes=[P['nlam']])

                lre = sbuf(st, f"lre{l}", [128, 16])
                lim = sbuf(st, f"lim{l}", [128, 16])
                stp = sbuf(st, f"stp{l}", [128, 16])
                for d in range(2):
                    s.dma(lre[:, d * 8:(d + 1) * 8], bass.AP(D['s5_lam_re'], (l * 2 + d) * 1024, [[1, 128], [128, 8]]), writes=[lre], g=0)
                    s.dma(lim[:, d * 8:(d + 1) * 8], bass.AP(D['s5_lam_im'], (l * 2 + d) * 1024, [[1, 128], [128, 8]]), writes=[lim], g=0)
                    for gl in range(2):
                        s.dma(stp[64 * gl:64 * gl + 64, d * 8:(d + 1) * 8],
                              bass.AP(D['s5_log_step'], (l * 2 + d) * 16 + gl, [[0, 64], [2, 8]]), writes=[stp], g=0)
                w16 = [sbuf(st, f"w16_{l}_{i}", [128, 16]) for i in range(10)]
                lrs, lis, mag, sn, cs, nr, den, cre, cim, t16 = w16
                r_, ar_, ai_ = P['s5r'], P['s5ar'], P['s5ai']

                def V(fn, rd, wr, eng='dve'):
                    s.op(eng, fn, reads=rd, writes=wr)
                V(lambda e: e.activation(out=stp[:], in_=stp[:], func=AF.Exp), [stp], [stp], 'act')
                V(lambda e: e.tensor_tensor(out=lrs[:], in0=lre[:], in1=stp[:], op=ALU.mult), [lre, stp], [lrs])
                V(lambda e: e.tensor_tensor(out=lis[:], in0=lim[:], in1=stp[:], op=ALU.mult), [lim, stp], [lis])
                V(lambda e: e.activation(out=r_[:], in_=lrs[:], func=AF.Exp), [lrs], [r_], 'act')
                ki16 = sbuf(st, f"ki16_{l}", [128, 16], mybir.dt.int32)
                kf16 = sbuf(st, f"kf16_{l}", [128, 16])

                def sin16(out, phase):
                    V(lambda e: e.tensor_scalar(out=t16[:], in0=lis[:], scalar1=phase, scalar2=None, op0=ALU.add), [lis], [t16])
                    V(lambda e: e.tensor_scalar(out=ki16[:], in0=t16[:], scalar1=1.0 / TWO_PI, scalar2=None, op0=ALU.mult), [t16], [ki16])
                    V(lambda e: e.tensor_copy(out=kf16[:], in_=ki16[:]), [ki16], [kf16])
                    V(lambda e: e.scalar_tensor_tensor(out=t16[:], in0=kf16[:], scalar=-TWO_PI, in1=t16[:], op0=ALU.mult, op1=ALU.add), [kf16, t16], [t16])
                    V(lambda e: e.tensor_scalar(out=t16[:], in0=t16[:], scalar1=-3.14159, scalar2=3.14159, op0=ALU.max, op1=ALU.min), [t16], [t16])
                    V(lambda e: e.activation(out=out[:], in_=t16[:], func=AF.Sin), [t16], [out], 'act')
                sin16(sn, 0.0)
                sin16(cs, 0.5 * math.pi)
                negpi = LP['_negpi']
                Zt = LP['_Z']
                V(lambda e: e.tensor_tensor(out=ar_[:], in0=r_[:], in1=cs[:], op=ALU.mult), [r_, cs], [ar_])
                V(lambda e: e.tensor_tensor(out=ai_[:], in0=r_[:], in1=sn[:], op=ALU.mult), [r_, sn], [ai_])
                V(lambda e: e.tensor_scalar(out=nr[:], in0=ar_[:], scalar1=-1.0, scalar2=None, op0=ALU.add), [ar_], [nr])
                V(lambda e: e.tensor_tensor(out=den[:], in0=lre[:], in1=lre[:], op=ALU.mult), [lre], [den])
                V(lambda e: e.tensor_tensor(out=t16[:], in0=lim[:], in1=lim[:], op=ALU.mult), [lim], [t16])
                V(lambda e: e.tensor_tensor(out=den[:], in0=den[:], in1=t16[:], op=ALU.add), [den, t16], [den])
                V(lambda e: e.reciprocal(out=den[:], in_=den[:]), [den], [den])
                V(lambda e: e.tensor_tensor(out=cre[:], in0=nr[:], in1=lre[:], op=ALU.mult), [nr, lre], [cre])
                V(lambda e: e.tensor_tensor(out=t16[:], in0=ai_[:], in1=lim[:], op=ALU.mult), [ai_, lim], [t16])
                V(lambda e: e.tensor_tensor(out=cre[:], in0=cre[:], in1=t16[:], op=ALU.add), [cre, t16], [cre])
                V(lambda e: e.tensor_tensor(out=cre[:], in0=cre[:], in1=den[:], op=ALU.mult), [cre, den], [cre])
                V(lambda e: e.tensor_tensor(out=cim[:], in0=ai_[:], in1=lre[:], op=ALU.mult), [ai_, lre], [cim])
                V(lambda e: e.tensor_tensor(out=t16[:], in0=nr[:], in1=lim[:], op=ALU.mult), [nr, lim], [t16])
                V(lambda e: e.tensor_tensor(out=cim[:], in0=cim[:], in1=t16[:], op=ALU.subtract), [cim, t16], [cim])
                V(lambda e: e.tensor_tensor(out=cim[:], in0=cim[:], in1=den[:], op=ALU.mult), [cim, den], [cim])
                rt = sbuf(st, f"rt{l}", [128, 256])
                rki = sbuf(st, f"rki{l}", [128, 256], mybir.dt.int32)
                rkf = sbuf(st, f"rkf{l}", [128, 256])
                for j in range(16):
                    for (tab, ph) in [(P['cosT'], 0.5 * math.pi), (P['sinT'], 0.0)]:
                        V(lambda e, j=j, ph=ph: e.tensor_scalar(out=rt[:], in0=iota[:], scalar1=lis[:, j:j + 1], scalar2=ph, op0=ALU.mult, op1=ALU.add),
                          [iota, lis], [rt])
                        V(lambda e: e.tensor_scalar(out=rki[:], in0=rt[:], scalar1=1.0 / TWO_PI, scalar2=None, op0=ALU.mult), [rt], [rki])
                        V(lambda e: e.tensor_copy(out=rkf[:], in_=rki[:]), [rki], [rkf])
                        V(lambda e: e.scalar_tensor_tensor(out=rt[:], in0=rkf[:], scalar=-TWO_PI, in1=rt[:], op0=ALU.mult, op1=ALU.add), [rkf, rt], [rt])
                        V(lambda e: e.tensor_scalar(out=rt[:], in0=rt[:], scalar1=-3.14159, scalar2=3.14159, op0=ALU.max, op1=ALU.min), [rt], [rt])
                        V(lambda e, j=j, tab=tab: e.activation(out=tab[:, j, :], in_=rt[:], func=AF.Sin), [rt], [(tab, j)], 'act')
                bre = sbuf(st, f"bre{l}", [128, 16, 16])
                bim = sbuf(st, f"bim{l}", [128, 16, 16])
                bbr = sbuf(st, f"bbr{l}", [128, 16, 16])
                bbi = sbuf(st, f"bbi{l}", [128, 16, 16])
                btm = sbuf(st, f"btm{l}", [128, 16, 16])
                for d in range(2):
                    s.dma(bre[:, d * 8:(d + 1) * 8, :], bass.AP(D['s5_b_re'], (l * 2 + d) * 16384, [[16, 128], [2048, 8], [1, 16]]), writes=[bre], g=0)
                    s.dma(bim[:, d * 8:(d + 1) * 8, :], bass.AP(D['s5_b_im'], (l * 2 + d) * 16384, [[16, 128], [2048, 8], [1, 16]]), writes=[bim], g=0)

                def B3(t):
                    return t[:].unsqueeze(2).to_broadcast([128, 16, 16])
                V(lambda e: e.tensor_tensor(out=bbr[:], in0=bre[:], in1=B3(cre), op=ALU.mult), [bre, cre], [bbr])
                V(lambda e: e.tensor_tensor(out=btm[:], in0=bim[:], in1=B3(cim), op=ALU.mult), [bim, cim], [btm])
                V(lambda e: e.tensor_tensor(out=bbr[:], in0=bbr[:], in1=btm[:], op=ALU.subtract), [bbr, btm], [bbr])
                V(lambda e: e.tensor_tensor(out=bbi[:], in0=bim[:], in1=B3(cre), op=ALU.mult), [bim, cre], [bbi])
                V(lambda e: e.tensor_tensor(out=btm[:], in0=bre[:], in1=B3(cim), op=ALU.mult), [bre, cim], [btm])
                V(lambda e: e.tensor_tensor(out=bbi[:], in0=bbi[:], in1=btm[:], op=ALU.add), [bbi, btm], [bbi])
                for d in range(2):
                    for q in range(8):
                        j = d * 8 + q
                        ql = q % 4
                        for ri, bb in enumerate([bbr, bbi]):
                            zi = ri * 4 + ql
                            for gl in range(2):
                                V(lambda e, gl=gl, zi=zi, ql=ql, bb=bb, j=j: e.tensor_copy(
                                    out=Zt[64 * gl:64 * gl + 64, zi, 32 * ql + 16 * gl:32 * ql + 16 * gl + 16], in_=bb[64 * gl:64 * gl + 64, j, :]),
                                  [bb], [(Zt, zi)])
                            pb = bank()
                            s.op('pe', lambda e, pb=pb, zi=zi: e.transpose(out=pb[:, 0:128], in_=Zt[:, zi, :], identity=ident_f[:]),
                                 reads=[(Zt, zi), ident_f], writes=[pb])
                            s.op('act', lambda e, pb=pb, d=d, q=q, ri=ri: e.copy(out=P['WB'][:, d, q, ri, :], in_=pb[:, 0:128]),
                                 reads=[pb], writes=[(P['WB'], (d, q, ri))])
                s.op('pool', lambda e: e.memset(P['WC'][:], 0.0), writes=[P['WC']])
                for d in range(2):
                    for q in range(8):
                        ql = q % 4
                        for gl in range(2):
                            g = 2 * q + gl
                            for ri, nm in enumerate(['s5_c_re', 's5_c_im']):
                                s.dma(P['WC'][64 * gl:64 * gl + 64, d, q, ri, 32 * ql + 16 * gl:32 * ql + 16 * gl + 16],
                                      bass.AP(D[nm], ((l * 2 + d) * 16 + g) * 1024, [[1, 64], [64, 16]]), writes=[P['WC']], g=4)
                V(lambda e: e.tensor_scalar(out=P['WC'][:, :, :, 1, :], in0=P['WC'][:, :, :, 1, :], scalar1=-1.0, scalar2=None, op0=ALU.mult),
                  [P['WC']], [P['WC']])
                for nm in ['WB', 'WC', 'cosT', 'sinT']:
                    dt_ = BF16 if nm == 'WB' else F32
                    dd = nc.dram_tensor(f"s5scr_{nm}{l}", [128, 4096], dt_)
                    S5SCR[(nm, l)] = (dd, Buf(dd, f"s5scr_{nm}{l}"))
                    src = P[nm][:].rearrange("p a b c d -> p (a b c d)") if nm in ('WB', 'WC') else P[nm][:].rearrange("p a b -> p (a b)")
                    s.dma(dd.ap(), src, reads=[P[nm]], writes=[S5SCR[(nm, l)][1]], g=2)
                    P[nm] = None
            s.barrier()

        def tcols(t):
            return slice(t * 128, (t + 1) * 128)

        PIECES = [(0, 256), (256, 768), (768, 1280), (1280, 1792), (1792, 2304)]

        def mkstg(st, n=3):
            return dict(t=[sbuf(st, "stg", [128, 1024]) for _ in range(n)], i=[0])

        def load_cast(stg, dst_ap, dst_buf, src_ap, shape, key=None, eng='dve'):
            t = stg['t'][stg['i'][0] % len(stg['t'])]
            stg['i'][0] += 1
            n = int(np.prod(shape))
            tv = t[:, 0:n]
            if len(shape) == 2:
                tv = tv.rearrange("p (a b) -> p a b", a=shape[0])
            s.dma(tv, src_ap, writes=[t], g=3)
            s.op(eng, lambda e: e.tensor_copy(out=dst_ap, in_=tv), reads=[t], writes=[(dst_buf, key)])

        def load_w_bf16(stg, wbuf, name, l, c0, n, dst0=0, nk=8):
            src = D[name].ap()[l].rearrange("(k p) c -> p k c", p=128)
            kk = max(1, 1024 // n)
            for k0 in range(0, nk, kk):
                k1 = min(nk, k0 + kk)
                if k1 - k0 == 1:
                    load_cast(stg, wbuf[:, k0, dst0:dst0 + n], wbuf, src[:, k0, c0:c0 + n], [n])
                else:
                    load_cast(stg, wbuf[:, k0:k1, dst0:dst0 + n], wbuf, src[:, k0:k1, c0:c0 + n], [k1 - k0, n])

        def rstd_from_ss(ss, rstd, n):
            s.op('act', lambda e: e.activation(out=rstd[:], in_=ss[:], func=AF.Sqrt, bias=EPS, scale=1.0 / n), reads=[ss], writes=[rstd])
            s.op('dve', lambda e: e.reciprocal(out=rstd[:], in_=rstd[:]), reads=[rstd], writes=[rstd])

        def nmt_front(st_bufs, xt):
            junk, ss, rstd, xn = st_bufs
            s.op('act', lambda e: e.activation(out=junk[:], in_=xt[:], func=AF.Square, accum_out=ss[:]), reads=[xt], writes=[junk, ss])
            rstd_from_ss(ss, rstd, 1024)
            s.op('dve', lambda e: e.tensor_scalar(out=xn[:], in0=xt[:], scalar1=rstd[:, 0:1], scalar2=None, op0=ALU.mult),
                 reads=[xt, rstd], writes=[xn])

        def nmt_back(st_bufs, hT, t, s1, sh, col):
            junk, ss, rstd, xn = st_bufs
            pb = bank()
            pbb = pb.t[:].bitcast(BF16)
            for k in range(8):
                s.op('pe', lambda e, k=k: e.transpose(out=pbb[:, k * 128:(k + 1) * 128], in_=xn[:, k * 128:(k + 1) * 128], identity=ident_b[:]),
                     reads=[xn, ident_b], writes=[pb])
            for k in range(8):
                s.op('act', lambda e, k=k: e.activation(out=hT[:, k, tcols(t)], in_=pbb[:, k * 128:(k + 1) * 128], func=AF.Identity,
                                                       scale=s1[:, k, col:col + 1], bias=sh[:, k, col:col + 1]),
                     reads=[pb, s1, sh], writes=[(hT, t)])

        def norm_mod_transpose(st_bufs, xt, hT, t, s1, sh, col):
            nmt_front(st_bufs, xt)
            nmt_back(st_bufs, hT, t, s1, sh, col)

        for b in range(NB):
            s.dma(xs_d.ap()[b, 0:256, :], ctx_d.ap()[b], writes=[(XS, b)], g=0)
            s.dma(xs_d.ap()[b, 256:2304, :], x_d.ap()[b], writes=[(XS, b)], g=0)
            for l in LAYERS:
                P = LP[l]
                last = (l == LAYERS[-1])
                need_ctx = not last
                with contextlib.ExitStack() as shs:
                  hT = sbuf(shs, "hT", [128, 8, SEQ], BF16)
                  early = False
                  with contextlib.ExitStack() as sm:
                    mixT = sbuf(sm, "mixT", [128, 8, SEQ], BF16)
                    with contextlib.ExitStack() as st:
                        xt2 = [sbuf(st, f"xt{i}", [128, 1024]) for i in range(3)]
                        nbs = [(sbuf(st, "junk", [128, 1024], BF16), sbuf(st, "ssA", [128, 1]), sbuf(st, "rstdA", [128, 1]),
                                sbuf(st, "xnA", [128, 1024], BF16)) for _ in range(2)]
                        def frontA(t):
                            xt = xt2[t % 3]
                            s.dma(xt[:], xs_d.ap()[b, tcols(t), :], reads=[(XS, b)], writes=[xt], g=0)
                            nmt_front(nbs[t % 2], xt)
                        frontA(0)
                        for t in range(NT):
                            if t + 1 < NT:
                                frontA(t + 1)
                            nmt_back(nbs[t % 2], hT, t, P['s1a'], P['sha'], NB if t < 2 else b)
                        s.barrier()
                    CONST = dict(ident_b=ident_b, ident_f=ident_f, U_f=U_f, UT_f=UT_f, mf_f=mf_f, mb_f=mb_f, ones_f=ones_f,
                                 ropeC=ropeC, ropeS=ropeS, ones_b=ones_b)
                    ENV = dict(S5SCR=S5SCR, nc=nc, s=s, sbuf=sbuf, bank=bank, P=P, D=D, l=l, b=b, NB=NB, NCOL=NCOL, hT=hT, mixT=mixT,
                               load_w=load_w_bf16, mkstg=mkstg, load_cast=load_cast, rstd_from_ss=rstd_from_ss, tcols=tcols, PIECES=PIECES, C=CONST, DB=DB, stop=stop,
                               need_ctx=need_ctx, last=last, xs_d=xs_d, XS=XS, out_d=out_d, OUT=OUT, gsc_d=gsc_d, GSC=GSC)
                    if stop == 'A':
                        s.dma(DB['hT'].ap(), hT[:], reads=[hT], q='pool', g=5)
                        early = True
                    if not early:
                        with contextlib.ExitStack() as st:
                            ssd_stage(st, **ENV)
                            s.barrier()
                        early = stop is not None and stop.startswith('SSD')
                    if not early:
                        with contextlib.ExitStack() as st:
                            s5_stage(st, **ENV)
                            s.barrier()
                        early = stop is not None and stop.startswith('S5')
                        if 'mix2' in DB:
                            s.dma(DB['mix2'].ap(), mixT[:], reads=[mixT], q='pool', g=5)
                            s.barrier()
                    if not early:
                        with contextlib.ExitStack() as st:
                            attn_stage(st, **ENV)
                            s.barrier()
                        early = stop is not None and stop.startswith('ATT')
                    if stop is not None and stop[:2] in ('SS', 'S5', 'AT') and 'mixT' in DB:
                        s.dma(DB['mixT'].ap(), mixT[:], reads=[mixT], q='pool', g=5)
                    tiles = list(range(NT)) if need_ctx else list(range(2, NT))
                    if not early:
                      with contextlib.ExitStack() as st:
                        wo = sbuf(st, "wo", [128, 8, 1024], BF16)
                        stgE = mkstg(st)
                        load_w_bf16(stgE, wo, 'w_out', l, 0, 1024)
                        gbc = sbuf(st, "gbc", [128, 2, 1024])
                        s.dma(gbc[:, 0, :], bass.AP(gsc_d, ((l * 2 + 0) * NCOL + b) * 1024, [[0, 128], [1, 1024]]), reads=[GSC], writes=[gbc], g=0)
                        s.dma(gbc[:, 1, :], bass.AP(gsc_d, ((l * 2 + 0) * NCOL + NB) * 1024, [[0, 128], [1, 1024]]), reads=[GSC], writes=[gbc], g=0)
                        NXT = 5
                        xt2 = [sbuf(st, f"xtE{i}", [128, 1024]) for i in range(NXT)]
                        junk1 = [sbuf(st, "junkE1", [128, 1024], BF16) for _ in range(2)]
                        nbs = [(sbuf(st, "junkE", [128, 1024], BF16), sbuf(st, "ssE", [128, 1]), sbuf(st, "rstdE", [128, 1]),
                                sbuf(st, "xnE", [128, 1024], BF16)) for _ in range(2)]
                        ss2s = [sbuf(st, "ss2", [128, 2]) for _ in range(2)]
                        rs2s = [sbuf(st, "rs2", [128, 1]) for _ in range(2)]
                        ytmps = [sbuf(st, "ytmp", [128, 1024]) for _ in range(2)]
                        def ldE(i):
                            s.dma(xt2[i % NXT][:], xs_d.ap()[b, tcols(tiles[i]), :], reads=[(XS, b)], writes=[xt2[i % NXT]], g=0)
                        def s1E(ti):
                            t = tiles[ti]
                            if ti + 3 < len(tiles):
                                ldE(ti + 3)
                            xt = xt2[ti % NXT]
                            ss2, rs2, ytmp, jk = ss2s[ti % 2], rs2s[ti % 2], ytmps[ti % 2], junk1[ti % 2]
                            pbs = [bank(), bank()]
                            for hf in range(2):
                                for k in range(8):
                                    s.op('pe', lambda e, k=k, hf=hf: e.matmul(pbs[hf][:, :], lhsT=mixT[:, k, tcols(t)], rhs=wo[:, k, hf * 512:(hf + 1) * 512],
                                                                             start=(k == 0), stop=(k == 7)), reads=[mixT, wo], writes=[pbs[hf]])
                                s.op('act', lambda e, hf=hf: e.activation(out=jk[:, hf * 512:(hf + 1) * 512], in_=pbs[hf][:, :], func=AF.Square,
                                                                         accum_out=ss2[:, hf:hf + 1]), reads=[pbs[hf]], writes=[jk, ss2])
                            s.op('dve', lambda e: e.tensor_tensor(out=rs2[:], in0=ss2[:, 0:1], in1=ss2[:, 1:2], op=ALU.add), reads=[ss2], writes=[rs2])
                            rstd_from_ss(rs2, rs2, 1024)
                            gi = 1 if t < 2 else 0
                            for hf in range(2):
                                s.op('dve', lambda e, hf=hf, gi=gi: e.scalar_tensor_tensor(out=ytmp[:, hf * 512:(hf + 1) * 512], in0=pbs[hf][:, :], scalar=rs2[:, 0:1],
                                                                                        in1=gbc[:, gi, hf * 512:(hf + 1) * 512], op0=ALU.mult, op1=ALU.mult),
                                     reads=[pbs[hf], rs2, gbc], writes=[ytmp])
                            s.op('pool', lambda e, xt=xt: e.tensor_tensor(out=xt[:], in0=xt[:], in1=ytmp[:], op=ALU.add), reads=[xt, ytmp], writes=[xt])
                            s.dma(xs_d.ap()[b, tcols(t), :], xt[:], reads=[xt], writes=[(XS, b)], g=1)

                        def s2E(ti):
                            nmt_front(nbs[ti % 2], xt2[ti % NXT])

                        def s3E(ti):
                            t = tiles[ti]
                            nmt_back(nbs[ti % 2], hT, t, P['s1f'], P['shf'], NB if t < 2 else b)

                        nE = len(tiles)
                        for i0 in range(min(3, nE)):
                            ldE(i0)
                        s1E(0)
                        if nE > 1:
                            s1E(1)
                        s2E(0)
                        for ti in range(nE):
                            if ti + 2 < nE:
                                s1E(ti + 2)
                            if ti + 1 < nE:
                                s2E(ti + 1)
                            s3E(ti)
                        s.barrier()
                  if stop == 'E':
                    early = True
                  if not early:
                    with contextlib.ExitStack() as st:
                        ffn_stage(st, **ENV)
                        s.barrier()
                    if stop is not None and stop.startswith('F'):
                        early = True
                  if stop in ('E', 'L0'):
                    s.dma(DB['xs'].ap(), xs_d.ap()[b], reads=[(XS, b)], g=5)
                    if stop == 'E':
                        s.dma(DB['hT'].ap(), hT[:], reads=[hT], q='pool', g=5)
                    early = True
                  if early:
                    break
            if stop is not None:
                break
        s.barrier(final=True)
    return nc


def shard_inputs(inputs, NB, ncores):
    consts = host_consts()
    maps = []
    for i in range(ncores):
        m = {}
        m['x'] = np.ascontiguousarray(inputs['x'][i * NB:(i + 1) * NB], dtype=np.float32)
        m['ctx'] = np.ascontiguousarray(inputs['ctx'][i * NB:(i + 1) * NB], dtype=np.float32)
        m['cc'] = np.ascontiguousarray(np.concatenate([inputs['c'][i * NB:(i + 1) * NB], inputs['c_ctx'][None, :]], 0), dtype=np.float32)
        for n, sh in PARAMS:
            m[n] = np.ascontiguousarray(inputs[n], dtype=np.float32)
        m.update(consts)
        maps.append(m)
    return maps


def kernel(**inputs):
    inputs = {k: np.asarray(v) for k, v in inputs.items()}
    NB = 4
    nc = build(NB=NB, LAYERS=(0, 1))
    maps = shard_inputs(inputs, NB, 8)
    res = run_bass_kernel_spmd(nc, maps, core_ids=list(range(8)))
    return np.concatenate([r["out"] for r in res.results], axis=0).astype(np.float32)
```

```python
import contextlib
import math
import numpy as np
import concourse.bass as bass
import concourse.mybir as mybir
from concourse.bass_utils import run_bass_kernel_spmd

F32 = mybir.dt.float32
BF16 = mybir.dt.bfloat16
AF = mybir.ActivationFunctionType
ALU = mybir.AluOpType

ENGS = ('pe', 'act', 'dve', 'pool', 'sp')
SAME_ENGINE_SYNC = ('act', 'dve', 'pool')
NDSEM = 6
EPS = 1e-6
NT = 18
SEQ = 2304
TWO_PI = 2.0 * math.pi


class Buf:
    def __init__(self, t, name, psum=False):
        self.t = t
        self.name = name
        self.psum = psum
        self.w = {}
        self.r = {}

    def _keys(self, key):
        if key is None:
            return list(set(self.w) | set(self.r) | {None})
        return [key, None]

    def wdeps(self, key):
        d = set()
        for k in self._keys(key):
            d |= self.w.get(k, set())
        return d

    def rdeps(self, key):
        d = set()
        for k in self._keys(key):
            d |= set(self.r.get(k, {}).items())
        return d

    def add_reader(self, key, ev):
        rr = self.r.setdefault(key, {})
        if rr.get(ev[0], 0) < ev[1]:
            rr[ev[0]] = ev[1]

    def set_writer(self, key, ev):
        if key is None:
            self.w = {None: {ev}}
            self.r = {}
        else:
            self.w[key] = {ev}
            self.r[key] = {}

    def add_dma_writer(self, key, ev):
        cur = self.w.get(key, set())
        cur = {e for e in cur if e[0][1] == ev[0][1] and e[0][0].startswith('q') and not (e[0] == ev[0] and e[1] <= ev[1])}
        cur.add(ev)
        if key is None:
            self.w = {None: cur}
            self.r = {}
        else:
            self.w[key] = cur
            self.r[key] = {}

    def __getitem__(self, idx):
        return self.t[idx]


def _norm(lst):
    return [(x, None) if isinstance(x, Buf) else x for x in lst]


class Sched:
    NQ = {'sp': 14, 'pool': 3}

    def __init__(self, nc, stack):
        self.nc = nc
        self.stack = stack
        self.eng = {'pe': nc.tensor, 'act': nc.scalar, 'dve': nc.vector, 'pool': nc.gpsimd, 'sp': nc.sync}
        self.phase = 0
        self.ninst = 0
        self.sets = []
        for i in range(2):
            d = {}
            for e in ENGS:
                d[e] = stack.enter_context(nc.semaphore(f"s_{e}_{i}"))
            for q, n in self.NQ.items():
                for j in range(n):
                    d[f"q{q}{j}"] = stack.enter_context(nc.semaphore(f"s_q{q}{j}_{i}"))
            self.sets.append(d)
        self.bsems = [stack.enter_context(nc.semaphore(f"s_bar{i}")) for i in range(2)]
        self.nbar = 0
        self._new_sems()

    def _new_sems(self):
        self.sem = self.sets[self.phase % 2]
        self.cnt = {k: 0 for k in self.sem}
        self.seen = {e: {} for e in ENGS}
        self.rr = {q: 0 for q in self.NQ}

    def _wait(self, e, deps):
        need = {}
        for (k, v) in deps:
            if k[1] != self.phase:
                continue
            if need.get(k, 0) < v:
                need[k] = v
        for k, v in need.items():
            if self.seen[e].get(k, 0) >= v:
                continue
            self.eng[e].wait_ge(self.sem[k[0]], v)
            self.seen[e][k] = v
            self.ninst += 1

    def op(self, e, fn, reads=(), writes=()):
        reads = _norm(reads)
        writes = _norm(writes)
        writes = writes + [r for r in reads if r[0].psum]
        reads = [r for r in reads if not r[0].psum]
        deps = set()
        for b, k in reads:
            deps |= b.wdeps(k)
        for b, k in writes:
            deps |= b.wdeps(k)
            for ev in b.rdeps(k):
                if ev[0][0] != e:
                    deps.add(ev)
        if e not in SAME_ENGINE_SYNC:
            deps = {d for d in deps if d[0][0] != e}
        self._wait(e, deps)
        inst = fn(self.eng[e])
        self.cnt[e] += 1
        inst.then_inc(self.sem[e], 1)
        self.ninst += 1
        ev = ((e, self.phase), self.cnt[e])
        for b, k in reads:
            b.add_reader(k, ev)
        for b, k in writes:
            b.set_writer(k, ev)
        return inst

    def dma(self, out, in_, reads=(), writes=(), q='sp', g=0, **kw):
        reads = _norm(reads)
        writes = _norm(writes)
        deps = set()
        for b, k in reads:
            deps |= b.wdeps(k)
        for b, k in writes:
            same = b.w.get(k, set())
            deps |= {ev for ev in b.wdeps(k) if not (ev in same and ev[0][0].startswith('q'))} | b.rdeps(k)
        self._wait(q, deps)
        sk = f"q{q}{self.rr[q] % self.NQ[q]}"
        self.rr[q] += 1
        kk = (sk, self.phase)
        if self.cnt[sk] > 0 and self.seen[q].get(kk, 0) < self.cnt[sk]:
            self.eng[q].wait_ge(self.sem[sk], self.cnt[sk])
            self.seen[q][kk] = self.cnt[sk]
        inst = self.eng[q].dma_start(out=out, in_=in_, **kw)
        self.cnt[sk] += 16
        inst.then_inc(self.sem[sk], 16)
        self.ninst += 1
        ev = (kk, self.cnt[sk])
        for b, k in reads:
            b.add_reader(k, ev)
        for b, k in writes:
            b.add_dma_writer(k, ev)
        return inst

    def barrier(self, final=False):
        bsem = self.bsems[self.nbar % 2]
        other = self.bsems[(self.nbar + 1) % 2]
        self.nbar += 1
        for k in self.sem:
            if k.startswith('q') and self.cnt[k] > 0:
                self.eng['sp'].wait_ge(self.sem[k], self.cnt[k])
        for e in ENGS:
            if self.cnt[e] > 0:
                self.eng[e].wait_ge(self.sem[e], self.cnt[e])
            self.eng[e].sem_inc(bsem, 1)
        self.eng['sp'].wait_ge(bsem, len(ENGS))
        if not final:
            for k, sm in self.sets[(self.phase + 1) % 2].items():
                self.eng['sp'].sem_clear(sm)
            self.eng['sp'].sem_clear(other)
        self.eng['sp'].sem_inc(bsem, 1)
        for e in ENGS:
            self.eng[e].wait_ge(bsem, len(ENGS) + 1)
        if not final:
            self.phase += 1
            self._new_sems()


PARAMS = [
    ('mod_w', (2, 1024, 6144)), ('mod_b', (2, 6144)), ('mix_norm_pre', (2, 1024)), ('mix_norm_post', (2, 1024)),
    ('ffn_norm_pre', (2, 1024)), ('ffn_norm_post', (2, 1024)), ('w_in', (2, 1024, 2568)), ('w_out', (2, 1024, 1024)),
    ('ssd_conv_w', (2, 5, 512)), ('ssd_conv_b', (2, 512)), ('ssd_dt_bias', (2, 2, 4)), ('ssd_a_log', (2, 2, 4)),
    ('ssd_d', (2, 4)), ('ssd_norm_w', (2, 256)), ('diff_lam_q1', (2, 64)), ('diff_lam_k1', (2, 64)),
    ('diff_lam_q2', (2, 64)), ('diff_lam_k2', (2, 64)), ('diff_subln_w', (2, 128)),
    ('s5_lam_re', (2, 2, 16, 64)), ('s5_lam_im', (2, 2, 16, 64)), ('s5_log_step', (2, 2, 16)),
    ('s5_b_re', (2, 2, 16, 64, 16)), ('s5_b_im', (2, 2, 16, 64, 16)), ('s5_c_re', (2, 2, 16, 16, 64)),
    ('s5_c_im', (2, 2, 16, 16, 64)), ('s5_d', (2, 16, 16)), ('s5_glu_w', (2, 256, 256)), ('s5_glu_b', (2, 256)),
    ('ffn_w_gate', (2, 1024, 2816)), ('ffn_w_up', (2, 1024, 2816)), ('ffn_conv_w', (2, 3, 2816)),
    ('ffn_conv_b', (2, 2816)), ('ffn_w_down', (2, 2816, 1024)),
]


def host_consts():
    r = np.arange(128)
    U = (r[:, None] <= r[None, :]).astype(np.float32)
    c = {}
    c['c_ident'] = np.eye(128, dtype=np.float32)
    c['c_U'] = U
    c['c_UT'] = np.ascontiguousarray(U.T)
    c['c_mf'] = np.where(r[None, :] >= r[:, None], 0.0, -30000.0).astype(np.float32)
    c['c_mb'] = np.where(r[None, :] <= r[:, None], 0.0, -30000.0).astype(np.float32)
    c['c_ones'] = np.ones((128, 128), np.float32)
    t = np.arange(2048)
    rr = (t // 64).astype(np.float32)
    col = (t % 64).astype(np.float32)
    inv = (10000.0 ** (-np.arange(16, dtype=np.float32) / 16)).astype(np.float32)
    ar = rr[:, None] * inv[None, :]
    ac = col[:, None] * inv[None, :]
    c['c_cos'] = np.concatenate([np.cos(ar), np.cos(ac)], 1).astype(np.float32)
    c['c_sin'] = np.concatenate([np.sin(ar), np.sin(ac)], 1).astype(np.float32)
    c['c_iota'] = np.tile(np.arange(256, dtype=np.float32)[None, :], (128, 1))
    return c


import os
_CUT = int(os.environ.get('SSDCUT', '0'))


def ssd_stage(st, nc, s, sbuf, bank, P, D, l, hT, mixT, load_w, rstd_from_ss, tcols, PIECES, C, DB, stop, mkstg, **_):
    wzd = sbuf(st, "wzd", [128, 8, 264], BF16)
    wx = sbuf(st, "wx", [128, 8, 512], BF16)
    with contextlib.ExitStack() as s0:
        stg = mkstg(s0)
        load_w(stg, wzd, 'w_in', l, 0, 256, 0)
        load_w(stg, wzd, 'w_in', l, 768, 8, 256)
        load_w(stg, wx, 'w_in', l, 256, 512, 0)
        s.barrier()
    if stop == 'SSD0':
        return
    zs = sbuf(st, "zs", [128, NT, 256], BF16)
    dtr = sbuf(st, "dtr", [128, NT, 8])
    xbcT = sbuf(st, "xbcT", [128, 4, SEQ], BF16)
    xbc = sbuf(st, "xbc", [128, NT, 512], BF16)
    yacc = sbuf(st, "yacc", [128, NT, 256])
    for t in range(NT):
        pb = bank()
        for k in range(8):
            s.op('pe', lambda e, k=k: e.matmul(pb[:, 0:264], lhsT=hT[:, k, tcols(t)], rhs=wzd[:, k, :], start=(k == 0), stop=(k == 7)),
                 reads=[(hT, t), wzd], writes=[pb])
        s.op('act', lambda e: e.activation(out=zs[:, t, :], in_=pb[:, 0:256], func=AF.Silu), reads=[pb], writes=[(zs, t)])
        s.op('dve', lambda e: e.tensor_copy(out=dtr[:, t, :], in_=pb[:, 256:264]), reads=[pb], writes=[dtr])
    if stop == 'SSD1':
        return
    with contextlib.ExitStack() as sx:
        xraw = sbuf(sx, "xraw", [128, 2312])
        xacc = sbuf(sx, "xacc", [128, 2312])
        s.op('pool', lambda e: e.memset(xraw[:], 0.0), writes=[xraw])
        for c in range(4):
            for (c0, c1) in PIECES:
                pb = bank()
                n = c1 - c0
                for k in range(8):
                    s.op('pe', lambda e, k=k: e.matmul(pb[:, 0:n], lhsT=wx[:, k, c * 128:(c + 1) * 128], rhs=hT[:, k, c0:c1], start=(k == 0), stop=(k == 7)),
                         reads=[hT, wx], writes=[pb])
                off = 2 if c0 == 0 else 6
                s.op('act', lambda e: e.copy(out=xraw[:, off + c0:off + c1], in_=pb[:, 0:n]), reads=[pb], writes=[xraw])
            s.op('act', lambda e: e.activation(out=xacc[:, 2:2310], in_=xraw[:, 2:2310], func=AF.Identity, scale=P['scw'][:, c, 2:3], bias=P['scb'][:, c:c + 1]),
                 reads=[xraw, P['scw'], P['scb']], writes=[xacc])
            for j in (0, 1, 3, 4):
                s.op('dve', lambda e, j=j: e.scalar_tensor_tensor(out=xacc[:, 2:2310], in0=xraw[:, j:j + 2308], scalar=P['scw'][:, c, j:j + 1],
                                                                 in1=xacc[:, 2:2310], op0=ALU.mult, op1=ALU.add),
                     reads=[xraw, xacc, P['scw']], writes=[xacc])
            s.op('act', lambda e: e.activation(out=xbcT[:, c, 0:256], in_=xacc[:, 2:258], func=AF.Silu), reads=[xacc], writes=[xbcT])
            s.op('act', lambda e: e.activation(out=xbcT[:, c, 256:2304], in_=xacc[:, 262:2310], func=AF.Silu), reads=[xacc], writes=[xbcT])

        s.barrier()
    if stop == 'SSD2':
        return
    for t in range(NT):
        pb = bank()
        pbb = pb.t[:].bitcast(BF16)
        for c in range(4):
            s.op('pe', lambda e, c=c: e.transpose(out=pbb[:, c * 128:(c + 1) * 128], in_=xbcT[:, c, tcols(t)], identity=C['ident_b'][:]),
                 reads=[xbcT, C['ident_b']], writes=[pb])
        s.op('act', lambda e: e.copy(out=xbc[:, t, :], in_=pbb[:, 0:512]), reads=[pb], writes=[(xbc, t)])
    if stop == 'SSD3':
        return
    dt = sbuf(st, "dt", [128, NT, 8])
    dta = sbuf(st, "dta", [128, NT, 8])
    tA = sbuf(st, "tA", [128, NT, 8])
    tB = sbuf(st, "tB", [128, NT, 8])

    def B8(tb):
        return tb[:].unsqueeze(1).to_broadcast([128, NT, 8])
    s.op('dve', lambda e: e.tensor_tensor(out=dtr[:], in0=dtr[:], in1=B8(P['dtb']), op=ALU.add), reads=[dtr, P['dtb']], writes=[dtr])
    s.op('dve', lambda e: e.tensor_scalar(out=tB[:], in0=dtr[:], scalar1=-1.0, scalar2=None, op0=ALU.mult), reads=[dtr], writes=[tB])
    s.op('dve', lambda e: e.tensor_tensor(out=tA[:], in0=dtr[:], in1=tB[:], op=ALU.max), reads=[dtr, tB], writes=[tA])
    s.op('act', lambda e: e.activation(out=tA[:], in_=tA[:], func=AF.Exp, scale=-1.0), reads=[tA], writes=[tA])
    s.op('act', lambda e: e.activation(out=tA[:], in_=tA[:], func=AF.Ln, bias=1.0, scale=1.0), reads=[tA], writes=[tA])
    s.op('dve', lambda e: e.tensor_single_scalar(out=tB[:], in_=dtr[:], scalar=0.0, op=ALU.max), reads=[dtr], writes=[tB])
    s.op('dve', lambda e: e.tensor_tensor(out=dt[:], in0=tA[:], in1=tB[:], op=ALU.add), reads=[tA, tB], writes=[dt])
    s.op('dve', lambda e: e.tensor_tensor(out=dta[:], in0=dt[:], in1=B8(P['aneg']), op=ALU.mult), reads=[dt, P['aneg']], writes=[dta])
    if stop == 'SSD4':
        return
    for t in range(NT):
        s.op('dve', lambda e, t=t: e.tensor_tensor(out=yacc[:, t, :].rearrange("p (h d) -> p h d", h=4),
                                                  in0=xbc[:, t, 0:256].rearrange("p (h d) -> p h d", h=4),
                                                  in1=P['dsk'][:].unsqueeze(2).to_broadcast([128, 4, 64]), op=ALU.mult),
             reads=[(xbc, t), P['dsk']], writes=[(yacc, t)])
    if stop == 'SSD5':
        return
    def two(name, shape, dt=F32):
        return [sbuf(st, f"{name}{i}", shape, dt) for i in range(2)]
    sm8s, nacss, eacss, wdecs, etots = two("sm8", [128, 8]), two("nacs", [128, 4]), two("eacs", [128, 4]), two("wdec", [128, 4]), two("etot", [128, 4])
    A1 = sbuf(st, "A1", [128, 4, 128])
    A1hs, A1ls = two("A1h", [128, 4, 128], BF16), two("A1l", [128, 4, 128], BF16)
    Ub = {}
    for nm in ['U_f', 'UT_f', 'mf_f', 'mb_f']:
        Ub[nm] = sbuf(st, nm + "_b", [128, 128], BF16)
        s.op('act', lambda e, nm=nm: e.copy(out=Ub[nm][:], in_=C[nm][:]), reads=[C[nm]], writes=[Ub[nm]])
    decTs = two("decT", [128, 4, 128])
    scTs = two("scTs", [128, 4, 128], BF16)
    xdts, xdtws = two("xdt", [128, 4, 64], BF16), two("xdtw", [128, 4, 64], BF16)
    state = sbuf(st, "state", [128, 2, 64])
    stateb = sbuf(st, "stateb", [128, 2, 64], BF16)
    ones3 = sbuf(st, "ones3", [128, 4, 128])
    s.op('pool', lambda e: e.memset(ones3[:], 1.0), writes=[ones3])
    seq = []
    for d in range(2):
        order = list(range(NT)) if d == 0 else [1, 0] + list(range(NT - 1, 1, -1))
        for oi, t in enumerate(order):
            seq.append((d, t, oi == 0))

    def front(i):
        d, t, first = seq[i]
        k2 = i % 2
        sm8, nacs, eacs, wdec, etot = sm8s[k2], nacss[k2], eacss[k2], wdecs[k2], etots[k2]
        A1h, A1l, decT, scT, xdt, xdtw = A1hs[k2], A1ls[k2], decTs[k2], scTs[k2], xdts[k2], xdtws[k2]
        Um = C['U_f'] if d == 0 else C['UT_f']
        Umb = Ub['U_f'] if d == 0 else Ub['UT_f']
        Mnb = Ub['mf_f'] if d == 0 else Ub['mb_f']
        dta_c = dta[:, t, d * 4:(d + 1) * 4]
        dt_c = dt[:, t, d * 4:(d + 1) * 4]
        pbs = bank(0)
        s.op('pe', lambda e: e.matmul(pbs[:, 0:4], lhsT=Um[:], rhs=dta_c, start=True, stop=True), reads=[Um, dta], writes=[pbs])
        s.op('pe', lambda e: e.matmul(pbs[:, 4:8], lhsT=C['ones_f'][:], rhs=dta_c, start=True, stop=True), reads=[C['ones_f'], dta], writes=[pbs])
        s.op('act', lambda e: e.copy(out=sm8[:], in_=pbs[:, 0:8]), reads=[pbs], writes=[sm8])
        s.op('dve', lambda e: e.tensor_scalar(out=nacs[:], in0=sm8[:, 0:4], scalar1=-1.0, scalar2=None, op0=ALU.mult), reads=[sm8], writes=[nacs])
        s.op('act', lambda e: e.activation(out=eacs[:], in_=sm8[:, 0:4], func=AF.Exp), reads=[sm8], writes=[eacs])
        s.op('dve', lambda e: e.tensor_tensor(out=wdec[:], in0=sm8[:, 4:8], in1=sm8[:, 0:4], op=ALU.subtract), reads=[sm8], writes=[wdec])
        s.op('act', lambda e: e.activation(out=wdec[:], in_=wdec[:], func=AF.Exp), reads=[wdec], writes=[wdec])
        s.op('act', lambda e: e.activation(out=etot[:], in_=sm8[:, 4:8], func=AF.Exp), reads=[sm8], writes=[etot])
        s.op('dve', lambda e: e.tensor_tensor(out=A1[:], in0=ones3[:], in1=dta_c.unsqueeze(2).to_broadcast([128, 4, 128]), op=ALU.mult),
             reads=[ones3, dta], writes=[A1])
        s.op('act', lambda e: e.copy(out=A1h[:], in_=A1[:]), reads=[A1], writes=[A1h])
        s.op('dve', lambda e: e.tensor_tensor(out=A1l[:], in0=A1[:], in1=A1h[:], op=ALU.subtract), reads=[A1, A1h], writes=[A1l])
        pD = bank(1)
        for h in range(4):
            s.op('pe', lambda e, h=h: e.matmul(pD[:, h * 128:(h + 1) * 128], lhsT=A1h[:, h, :], rhs=Umb[:], start=True, stop=False),
                 reads=[A1h, Umb], writes=[pD])
            s.op('pe', lambda e, h=h: e.matmul(pD[:, h * 128:(h + 1) * 128], lhsT=A1l[:, h, :], rhs=Umb[:], start=False, stop=False),
                 reads=[A1l, Umb], writes=[pD])
            s.op('pe', lambda e, h=h: e.matmul(pD[:, h * 128:(h + 1) * 128], lhsT=C['ident_b'][:], rhs=Mnb[:], start=False, stop=True),
                 reads=[C['ident_b'], Mnb], writes=[pD])
        for h in range(4):
            s.op('act', lambda e, h=h: e.activation(out=decT[:, h, :], in_=pD[:, h * 128:(h + 1) * 128], func=AF.Exp, bias=nacs[:, h:h + 1], scale=1.0),
                 reads=[pD, nacs], writes=[(decT, h)])
        pGs = [bank(2), bank(3)]
        for g in range(2):
            s.op('pe', lambda e, g=g: e.matmul(pGs[g][:, 0:128], lhsT=xbcT[64 * g:64 * g + 64, 2, tcols(t)],
                                              rhs=xbcT[64 * g:64 * g + 64, 3, tcols(t)], start=True, stop=True), reads=[xbcT], writes=[pGs[g]])
        for h in range(4):
            g = h // 2
            s.op('dve', lambda e, h=h, g=g: e.tensor_tensor(out=scT[:, h, :], in0=pGs[g][:, 0:128], in1=decT[:, h, :], op=ALU.mult),
                 reads=[pGs[g], (decT, h)], writes=[(scT, h)])
        xs4 = xbc[:, t, 0:256].rearrange("p (h d) -> p h d", h=4)
        s.op('dve', lambda e: e.tensor_tensor(out=xdt[:], in0=xs4, in1=dt_c.unsqueeze(2).to_broadcast([128, 4, 64]), op=ALU.mult),
             reads=[(xbc, t), dt], writes=[xdt])
        s.op('dve', lambda e: e.tensor_tensor(out=xdtw[:], in0=xdt[:], in1=wdec[:].unsqueeze(2).to_broadcast([128, 4, 64]), op=ALU.mult),
             reads=[xdt, wdec], writes=[xdtw])
        pY = bank(4)
        for h in range(4):
            s.op('pe', lambda e, h=h: e.matmul(pY[:, h * 64:(h + 1) * 64], lhsT=scT[:, h, :], rhs=xdt[:, h, :], start=True, stop=True),
                 reads=[(scT, h), xdt], writes=[pY])
        s.op('dve', lambda e: e.tensor_tensor(out=yacc[:, t, :], in0=yacc[:, t, :], in1=pY[:, 0:256], op=ALU.add), reads=[pY, (yacc, t)], writes=[(yacc, t)])

    def back(i):
        d, t, first = seq[i]
        k2 = i % 2
        eacs, etot, xdtw = eacss[k2], etots[k2], xdtws[k2]
        if first:
            s.op('pool', lambda e: e.memset(state[:], 0.0), writes=[state])
            s.op('pool', lambda e: e.memset(stateb[:], 0.0), writes=[stateb])
        pOs = [bank(5), bank(6)]
        for h in range(4):
            g, hh = h // 2, h % 2
            s.op('pe', lambda e, h=h, g=g, hh=hh: e.matmul(pOs[g][:, hh * 64:(hh + 1) * 64], lhsT=xbcT[64 * g:64 * g + 64, 3, tcols(t)],
                                                          rhs=stateb[64 * g:64 * g + 64, hh, :], start=True, stop=True),
                 reads=[xbcT, stateb], writes=[pOs[g]])
        pS = bank(7)
        s.op('pe', lambda e: e.matmul(pS[:, 0:256], lhsT=xbc[:, t, 256:384], rhs=xdtw[:].rearrange("p h d -> p (h d)"), start=True, stop=True),
             reads=[(xbc, t), xdtw], writes=[pS])
        for g in range(2):
            for hh in range(2):
                h = 2 * g + hh
                s.op('dve', lambda e, g=g, hh=hh, h=h: e.scalar_tensor_tensor(
                    out=state[64 * g:64 * g + 64, hh, :], in0=state[64 * g:64 * g + 64, hh, :], scalar=etot[64 * g:64 * g + 64, h:h + 1],
                    in1=pS[64 * g:64 * g + 64, h * 64:(h + 1) * 64], op0=ALU.mult, op1=ALU.add), reads=[state, etot, pS], writes=[state])
        s.op('act', lambda e: e.copy(out=stateb[:], in_=state[:]), reads=[state], writes=[stateb])
        for h in range(4):
            g, hh = h // 2, h % 2
            s.op('dve', lambda e, h=h, g=g, hh=hh: e.scalar_tensor_tensor(out=yacc[:, t, h * 64:(h + 1) * 64], in0=pOs[g][:, hh * 64:(hh + 1) * 64],
                                                             scalar=eacs[:, h:h + 1], in1=yacc[:, t, h * 64:(h + 1) * 64], op0=ALU.mult, op1=ALU.add),
                 reads=[pOs[g], eacs, (yacc, t)], writes=[(yacc, t)])

    front(0)
    for i in range(len(seq)):
        if i + 1 < len(seq):
            front(i + 1)
        back(i)
    if stop == 'SSD6':
        return
    gt = sbuf(st, "gt", [128, 256])
    gj = sbuf(st, "gj", [128, 256])
    gb = sbuf(st, "gb", [128, 256], BF16)
    ssg = sbuf(st, "ssg", [128, 1])
    for t in range(NT):
        s.op('dve', lambda e: e.tensor_tensor(out=gt[:], in0=yacc[:, t, :], in1=zs[:, t, :], op=ALU.mult), reads=[(yacc, t), (zs, t)], writes=[gt])
        s.op('act', lambda e: e.activation(out=gj[:], in_=gt[:], func=AF.Square, accum_out=ssg[:]), reads=[gt], writes=[gj, ssg])
        rstd_from_ss(ssg, ssg, 256)
        s.op('dve', lambda e: e.scalar_tensor_tensor(out=gb[:], in0=gt[:], scalar=ssg[:, 0:1], in1=P['snw'][:], op0=ALU.mult, op1=ALU.mult),
             reads=[gt, ssg, P['snw']], writes=[gb])
        pb = bank()
        pbb = pb.t[:].bitcast(BF16)
        for c in range(2):
            s.op('pe', lambda e, c=c: e.transpose(out=pbb[:, c * 128:(c + 1) * 128], in_=gb[:, c * 128:(c + 1) * 128], identity=C['ident_b'][:]),
                 reads=[gb, C['ident_b']], writes=[pb])
        s.op('act', lambda e: e.copy(out=mixT[:, 0:2, tcols(t)], in_=pbb[:, 0:256].rearrange("p (c n) -> p c n", c=2)), reads=[pb], writes=[(mixT, ('ssd', t))])
import os


def s5_stage(st, nc, s, sbuf, bank, P, D, l, hT, mixT, load_w, PIECES, DB, stop, S5SCR, mkstg, **_):
    P = dict(P)
    P['WB'] = sbuf(st, "WBs", [128, 2, 8, 2, 128], BF16)
    P['WC'] = sbuf(st, "WCs", [128, 2, 8, 2, 128])
    P['cosT'] = sbuf(st, "cosTs", [128, 16, 256])
    P['sinT'] = sbuf(st, "sinTs", [128, 16, 256])
    for nm in ['WB', 'WC', 'cosT', 'sinT']:
        dd, db = S5SCR[(nm, l)]
        dst = P[nm][:].rearrange("p a b c d -> p (a b c d)") if nm in ('WB', 'WC') else P[nm][:].rearrange("p a b -> p (a b)")
        s.dma(dst, dd.ap(), reads=[db], writes=[P[nm]], g=2)
    wu = sbuf(st, "wu", [128, 8, 256], BF16)
    with contextlib.ExitStack() as s0:
        load_w(mkstg(s0), wu, 'w_in', l, 2312, 256, 0)
        s.barrier()
    uT = sbuf(st, "uT", [128, 2, SEQ], BF16)
    yT = sbuf(st, "yT", [128, 2, SEQ])
    for c in range(2):
        for (c0, c1) in PIECES:
            pb = bank()
            n = c1 - c0
            for k in range(8):
                s.op('pe', lambda e, k=k: e.matmul(pb[:, 0:n], lhsT=wu[:, k, c * 128:(c + 1) * 128], rhs=hT[:, k, c0:c1], start=(k == 0), stop=(k == 7)),
                     reads=[hT, wu], writes=[pb])
            s.op('act', lambda e: e.copy(out=uT[:, c, c0:c1], in_=pb[:, 0:n]), reads=[pb], writes=[uT])
            s.op('dve', lambda e: e.tensor_scalar(out=yT[:, c, c0:c1], in0=pb[:, 0:n], scalar1=P['s5d'][:, c:c + 1], scalar2=None, op0=ALU.mult),
                 reads=[pb, P['s5d']], writes=[(yT, c)])
    T = 256
    NCH = SEQ // T
    bs = [sbuf(st, f"s5b{i}", [128, 2, T]) for i in range(2)]
    gin = [sbuf(st, f"s5g{i}", [128, 2, T]) for i in range(2)]
    tt = [sbuf(st, f"s5t{i}", [128, 4, T]) for i in range(2)]
    gs = [sbuf(st, f"s5s{i}", [128, 2, T]) for i in range(2)]
    hh_ = [sbuf(st, f"s5h{i}", [128, 2, T]) for i in range(2)]
    t2 = tt
    ini = [sbuf(st, f"s5c{q}", [128, 2]) for q in range(8)]
    ctmp = [sbuf(st, f"s5ct{i}", [128, 2]) for i in range(2)]
    nsin1 = sbuf(st, "s5nsin1", [128, 16])
    s.op('dve', lambda e: e.tensor_scalar(out=nsin1[:], in0=P['sinT'][:, :, 1], scalar1=-1.0, scalar2=None, op0=ALU.mult), reads=[P['sinT']], writes=[nsin1])
    items = []
    for d in range(2):
        order = list(range(NCH)) if d == 0 else [0] + list(range(NCH - 1, 0, -1))
        for kt in range(2):
            for ci, ch in enumerate(order):
                for ql in range(4):
                    items.append((d, kt, ci, ch, ql, len(order)))
    pYs = {}

    def front(i):
        d, kt, ci, ch, ql, nord = items[i]
        q = kt * 4 + ql
        j = d * 8 + q
        i2 = i % 2
        B_, G_, T_ = bs[i2], gin[i2], tt[i2]
        ucols = uT[:, kt, ch * T:(ch + 1) * T] if d == 0 else uT[:, kt, ch * T:(ch + 1) * T][:, ::-1]
        cosj = P['cosT'][:, j, :]
        sinj = P['sinT'][:, j, :]
        pB = bank(2 + (i % 6))
        for ri in range(2):
            s.op('pe', lambda e, ri=ri: e.matmul(pB[:, ri * T:(ri + 1) * T], lhsT=P['WB'][:, d, q, ri, :], rhs=ucols, start=True, stop=True),
                 reads=[uT, P['WB']], writes=[pB])
        s.op('act', lambda e: e.copy(out=B_[:].rearrange("p a t -> p (a t)"), in_=pB[:, 0:2 * T]), reads=[pB], writes=[B_])
        TTp = lambda o, a, b_, op, rd, wr: s.op('pool', lambda e: e.tensor_tensor(out=o, in0=a, in1=b_, op=op), reads=rd, writes=wr)
        TTp(T_[:, 0, :], B_[:, 0, :], cosj, ALU.mult, [B_, P['cosT']], [(T_, 0)])
        TTp(T_[:, 1, :], B_[:, 1, :], sinj, ALU.mult, [B_, P['sinT']], [(T_, 1)])
        TTp(T_[:, 2, :], B_[:, 1, :], cosj, ALU.mult, [B_, P['cosT']], [(T_, 2)])
        TTp(T_[:, 3, :], B_[:, 0, :], sinj, ALU.mult, [B_, P['sinT']], [(T_, 3)])
        TTp(G_[:, 0, :], T_[:, 0, :], T_[:, 1, :], ALU.add, [(T_, 0), (T_, 1)], [(G_, 0)])
        TTp(G_[:, 1, :], T_[:, 2, :], T_[:, 3, :], ALU.subtract, [(T_, 2), (T_, 3)], [(G_, 1)])

    def back(i):
        d, kt, ci, ch, ql, nord = items[i]
        q = kt * 4 + ql
        j = d * 8 + q
        i2 = i % 2
        G_, S_, H_, U_ = gin[i2], gs[i2], hh_[i2], t2[i2]
        cosj = P['cosT'][:, j, :]
        sinj = P['sinT'][:, j, :]
        if ql == 0:
            pYs['cur'] = bank(len(pYs.setdefault('n', [])) % 2)
            pYs['n'].append(0)
        pY = pYs['cur']
        rb = P['s5r'][:, j:j + 1].to_broadcast([128, T])
        for ri in range(2):
            init = ini[q][:, ri:ri + 1] if ci > 0 else 0.0
            s.op('dve', lambda e, ri=ri, init=init: e.tensor_tensor_scan(out=S_[:, ri, :], data0=rb, data1=G_[:, ri, :], initial=init, op0=ALU.mult, op1=ALU.add),
                 reads=[(G_, ri), P['s5r'], ini[q]], writes=[(S_, ri)])
        TTd = lambda o, a, b_, op, rd, wr: s.op('dve', lambda e: e.tensor_tensor(out=o, in0=a, in1=b_, op=op), reads=rd, writes=wr)
        TTd(U_[:, 0, :], S_[:, 0, :], cosj, ALU.mult, [(S_, 0), P['cosT']], [(U_, 0)])
        TTd(U_[:, 1, :], S_[:, 1, :], sinj, ALU.mult, [(S_, 1), P['sinT']], [(U_, 1)])
        TTd(U_[:, 2, :], S_[:, 0, :], sinj, ALU.mult, [(S_, 0), P['sinT']], [(U_, 2)])
        TTd(U_[:, 3, :], S_[:, 1, :], cosj, ALU.mult, [(S_, 1), P['cosT']], [(U_, 3)])
        TTd(H_[:, 0, :], U_[:, 0, :], U_[:, 1, :], ALU.subtract, [(U_, 0), (U_, 1)], [(H_, 0)])
        TTd(H_[:, 1, :], U_[:, 2, :], U_[:, 3, :], ALU.add, [(U_, 2), (U_, 3)], [(H_, 1)])
        if ci < nord - 1:
            c1 = P['cosT'][:, j, 1:2]
            s1_ = P['sinT'][:, j, 1:2]
            ns1 = nsin1[:, j:j + 1]
            ct = ctmp[i2]
            hre_l = H_[:, 0, T - 1:T]
            him_l = H_[:, 1, T - 1:T]
            s.op('act', lambda e: e.activation(out=ct[:, 0:1], in_=him_l, func=AF.Copy, scale=ns1), reads=[(H_, 1), nsin1], writes=[ct])
            s.op('act', lambda e: e.activation(out=ct[:, 1:2], in_=hre_l, func=AF.Copy, scale=s1_), reads=[(H_, 0), P['sinT']], writes=[ct])
            s.op('act', lambda e: e.activation(out=ini[q][:, 0:1], in_=hre_l, func=AF.Identity, scale=c1, bias=ct[:, 0:1]),
                 reads=[(H_, 0), ct, P['cosT']], writes=[ini[q]])
            s.op('act', lambda e: e.activation(out=ini[q][:, 1:2], in_=him_l, func=AF.Identity, scale=c1, bias=ct[:, 1:2]),
                 reads=[(H_, 1), ct, P['cosT']], writes=[ini[q]])
        for ri in range(2):
            s.op('pe', lambda e, ri=ri: e.matmul(pY[:, 0:T], lhsT=P['WC'][:, d, q, ri, :], rhs=H_[:, ri, :],
                                                start=(ql == 0 and ri == 0), stop=(ql == 3 and ri == 1)),
                 reads=[(H_, ri), P['WC']], writes=[pY])
        if ql == 3:
            ycols = yT[:, kt, ch * T:(ch + 1) * T] if d == 0 else yT[:, kt, ch * T:(ch + 1) * T][:, ::-1]
            s.op('dve', lambda e: e.tensor_tensor(out=ycols, in0=ycols, in1=pY[:, 0:T], op=ALU.add), reads=[pY, (yT, kt)], writes=[(yT, kt)])

    front(0)
    for i in range(len(items)):
        if i + 1 < len(items):
            front(i + 1)
        back(i)
    if 's5_cos' in DB:
        s.dma(DB['s5_cos'].ap(), P['cosT'][:].rearrange("p a b -> p (a b)"), reads=[P['cosT']], g=5)
        s.dma(DB['s5_sin'].ap(), P['sinT'][:].rearrange("p a b -> p (a b)"), reads=[P['sinT']], g=5)
        s.dma(DB['s5_wc'].ap(), P['WC'][:].rearrange("p a b c d -> p (a b c d)"), reads=[P['WC']], g=5)
    if 's5_y' in DB:
        s.dma(DB['s5_y'].ap(), yT[:].rearrange("p a b -> p (a b)"), reads=[yT], g=5)
        s.dma(DB['s5_u'].ap(), uT[:].rearrange("p a b -> p (a b)"), reads=[uT], q='pool', g=5)
    ygb = uT
    fi = 0
    for c in range(2):
        for (g0, g1) in [(0, 1024), (1024, 2048), (2048, 2304)]:
            tb = tt[fi % 2]
            fi += 1
            n = g1 - g0
            tmp = tb[:].rearrange("p a t -> p (a t)")[:, 0:n]
            yv = yT[:, c, g0:g1]
            s.op('pool', lambda e: e.tensor_tensor(out=tmp, in0=yv, in1=yv, op=ALU.mult), reads=[(yT, c)], writes=[tb])
            s.op('pool', lambda e: e.tensor_scalar(out=tmp, in0=tmp, scalar1=0.044715, scalar2=1.0, op0=ALU.mult, op1=ALU.add), reads=[tb], writes=[tb])
            s.op('dve', lambda e: e.tensor_tensor(out=tmp, in0=tmp, in1=yv, op=ALU.mult), reads=[tb, (yT, c)], writes=[tb])
            s.op('act', lambda e: e.activation(out=tmp, in_=tmp, func=AF.Sigmoid, scale=1.5957691216057308), reads=[tb], writes=[tb])
            s.op('dve', lambda e: e.tensor_tensor(out=yv, in0=yv, in1=tmp, op=ALU.mult), reads=[tb, (yT, c)], writes=[(yT, c)])
            s.op('act', lambda e: e.copy(out=ygb[:, c, g0:g1], in_=yv), reads=[(yT, c)], writes=[ygb])
    sg = sbuf(st, "s5sg", [128, 512])
    for c in range(2):
        for (c0, c1) in PIECES:
            pb = bank()
            n = c1 - c0
            for k in range(2):
                s.op('pe', lambda e, k=k: e.matmul(pb[:, 0:n], lhsT=P['gluw'][:, k, c * 128:(c + 1) * 128], rhs=ygb[:, k, c0:c1], start=(k == 0), stop=(k == 1)),
                     reads=[ygb, P['gluw']], writes=[pb])
            s.op('act', lambda e: e.activation(out=sg[:, 0:n], in_=pb[:, 0:n], func=AF.Sigmoid, bias=P['glub'][:, c:c + 1], scale=1.0),
                 reads=[pb, P['glub']], writes=[sg])
            s.op('dve', lambda e: e.tensor_tensor(out=mixT[:, 6 + c, c0:c1], in0=yT[:, c, c0:c1], in1=sg[:, 0:n], op=ALU.mult),
                 reads=[sg, (yT, c)], writes=[(mixT, ('s5', c, c0))])


def attn_stage(st, nc, s, sbuf, bank, P, D, l, hT, mixT, load_w, rstd_from_ss, tcols, need_ctx, C, DB, stop, mkstg, **_):
    _AC = int(os.environ.get('ACUT', '0'))
    QT = sbuf(st, "QT", [128, 4, SEQ], BF16)
    KT = sbuf(st, "KT", [128, 4, SEQ], BF16)
    Vt = sbuf(st, "Vt", [128, NT, 4, 129], BF16)
    s.op('pool', lambda e: e.memset(Vt[:], 1.0), writes=[Vt])
    with contextlib.ExitStack() as sq:
        wqk = sbuf(sq, "wqk", [128, 8, 1024], BF16)
        wv = sbuf(sq, "wv", [128, 8, 512], BF16)
        with contextlib.ExitStack() as s0:
            stg = mkstg(s0)
            load_w(stg, wqk, 'w_in', l, 776, 1024, 0)
            load_w(stg, wv, 'w_in', l, 1800, 512, 0)
            s.barrier()
        qkf = sbuf(sq, "qkf", [128, 1024])
        qkr = sbuf(sq, "qkr", [128, 1024], BF16)
        rt = sbuf(sq, "ropet", [128, 4, 16, 2, 16])
        for t in range(NT):
            pq, pk, pv = bank(), bank(), bank()
            for (pb, w, c0) in [(pq, wqk, 0), (pk, wqk, 512), (pv, wv, 0)]:
                for k in range(8):
                    s.op('pe', lambda e, k=k, pb=pb, w=w, c0=c0: e.matmul(pb[:, :], lhsT=hT[:, k, tcols(t)], rhs=w[:, k, c0:c0 + 512], start=(k == 0), stop=(k == 7)),
                         reads=[hT, w], writes=[pb])
            s.op('act', lambda e: e.copy(out=Vt[:, t, :, 0:128], in_=pv[:, :].rearrange("p (h e) -> p h e", h=4)), reads=[pv], writes=[(Vt, t)])
            if t < 2:
                s.op('act', lambda e: e.copy(out=qkr[:, 0:512], in_=pq[:, :]), reads=[pq], writes=[qkr])
                s.op('act', lambda e: e.copy(out=qkr[:, 512:1024], in_=pk[:, :]), reads=[pk], writes=[qkr])
            else:
                s.op('act', lambda e: e.copy(out=qkf[:, 0:512], in_=pq[:, :]), reads=[pq], writes=[qkf])
                s.op('act', lambda e: e.copy(out=qkf[:, 512:1024], in_=pk[:, :]), reads=[pk], writes=[qkf])
                tl = t - 2
                xv = qkf[:].rearrange("p (m a b j) -> p m a b j", m=16, a=2, b=2)
                ov = qkr[:].rearrange("p (m a b j) -> p m a b j", m=16, a=2, b=2)
                cosb = C['ropeC'][:, tl, :].rearrange("p (a j) -> p a j", a=2).unsqueeze(1).to_broadcast([128, 16, 2, 16])
                sinb = C['ropeS'][:, tl, :].rearrange("p (a j) -> p a j", a=2).unsqueeze(1).to_broadcast([128, 16, 2, 16])
                PO = lambda o, a, b_, op, rd, wr: s.op('pool', lambda e: e.tensor_tensor(out=o, in0=a, in1=b_, op=op), reads=rd, writes=wr)
                PO(rt[:, 0], xv[:, :, :, 0, :], cosb, ALU.mult, [qkf, C['ropeC']], [(rt, 0)])
                PO(rt[:, 1], xv[:, :, :, 1, :], sinb, ALU.mult, [qkf, C['ropeS']], [(rt, 1)])
                PO(rt[:, 2], xv[:, :, :, 1, :], cosb, ALU.mult, [qkf, C['ropeC']], [(rt, 2)])
                PO(rt[:, 3], xv[:, :, :, 0, :], sinb, ALU.mult, [qkf, C['ropeS']], [(rt, 3)])
                PO(ov[:, :, :, 0, :], rt[:, 0], rt[:, 1], ALU.subtract, [(rt, 0), (rt, 1)], [qkr])
                PO(ov[:, :, :, 1, :], rt[:, 2], rt[:, 3], ALU.add, [(rt, 2), (rt, 3)], [qkr])
            pb = bank()
            pbb = pb.t[:].bitcast(BF16)
            for m in range(8):
                s.op('pe', lambda e, m=m: e.transpose(out=pbb[:, m * 128:(m + 1) * 128], in_=qkr[:, m * 128:(m + 1) * 128], identity=C['ident_b'][:]),
                     reads=[qkr, C['ident_b']], writes=[pb])
            s.op('act', lambda e: e.copy(out=QT[:, :, tcols(t)], in_=pbb[:, 0:512].rearrange("p (h n) -> p h n", h=4)), reads=[pb], writes=[(QT, t)])
            s.op('dve', lambda e: e.tensor_copy(out=KT[:, :, tcols(t)], in_=pbb[:, 512:1024].rearrange("p (h n) -> p h n", h=4)), reads=[pb], writes=[(KT, t)])

        s.barrier()
    if _AC == 2:
        return
    ETs = [sbuf(st, f"ET{i}", [128, NT, 512], BF16) for i in range(2)]
    oc = sbuf(st, "oc", [128, 2, 4, 129])
    rr = sbuf(st, "att_rr", [128, 2, 4])
    o_ = sbuf(st, "att_o", [128, 128])
    oj = sbuf(st, "att_oj", [128, 128])
    ob = sbuf(st, "att_ob", [128, 128], BF16)
    sso = sbuf(st, "att_ss", [128, 1])
    groups = [(256 + 512 * i, 512, NT) for i in range(4)]
    if need_ctx:
        groups = [(0, 256, 2)] + groups
    for h in range(4):
        if _AC == 3 and h == 1:
            return
        for (q0, nq, nk) in groups:
            nsub = nq // 128
            for kt in range(nk):
                for c in range(2):
                    pS = bank()
                    s.op('pe', lambda e, kt=kt, c=c: e.matmul(pS[:, 0:nq], lhsT=KT[64 * c:64 * c + 64, h, tcols(kt)], rhs=QT[64 * c:64 * c + 64, h, q0:q0 + nq],
                                                             start=True, stop=True), reads=[KT, QT], writes=[pS])
                    s.op('act', lambda e, kt=kt, c=c: e.activation(out=ETs[c][:, kt, 0:nq], in_=pS[:, 0:nq], func=AF.Exp, scale=0.125), reads=[pS], writes=[(ETs[c], kt)])
            for c in range(2):
                ET = ETs[c]
                for qs in range(nsub):
                    pO = bank()
                    for kt in range(nk):
                        s.op('pe', lambda e, kt=kt: e.matmul(pO[:, 0:129], lhsT=ET[:, kt, qs * 128:(qs + 1) * 128], rhs=Vt[:, kt, h, :], start=(kt == 0), stop=(kt == nk - 1)),
                             reads=[(ET, kt), Vt], writes=[pO])
                    s.op('dve', lambda e, c=c, qs=qs: e.tensor_copy(out=oc[:, c, qs, :], in_=pO[:, 0:129]), reads=[pO], writes=[(oc, (c, qs))])
            s.op('dve', lambda e: e.reciprocal(out=rr[:, :, 0:nsub], in_=oc[:, :, 0:nsub, 128]), reads=[oc], writes=[rr])
            s.op('dve', lambda e: e.tensor_scalar(out=rr[:, 1, :], in0=rr[:, 1, :], scalar1=P['nlam'][:, 0:1], scalar2=None, op0=ALU.mult), reads=[rr, P['nlam']], writes=[rr])
            for qs in range(nsub):
                s.op('dve', lambda e: e.tensor_scalar(out=o_[:], in0=oc[:, 0, qs, 0:128], scalar1=rr[:, 0, qs:qs + 1], scalar2=None, op0=ALU.mult), reads=[oc, rr], writes=[o_])
                s.op('dve', lambda e: e.scalar_tensor_tensor(out=o_[:], in0=oc[:, 1, qs, 0:128], scalar=rr[:, 1, qs:qs + 1], in1=o_[:], op0=ALU.mult, op1=ALU.add),
                     reads=[oc, rr, o_], writes=[o_])
                s.op('act', lambda e: e.activation(out=oj[:], in_=o_[:], func=AF.Square, accum_out=sso[:]), reads=[o_], writes=[oj, sso])
                s.op('act', lambda e: e.activation(out=sso[:], in_=sso[:], func=AF.Ln, bias=EPS, scale=1.0 / 128), reads=[sso], writes=[sso])
                s.op('act', lambda e: e.activation(out=sso[:], in_=sso[:], func=AF.Exp, scale=-0.5), reads=[sso], writes=[sso])
                s.op('dve', lambda e: e.scalar_tensor_tensor(out=ob[:], in0=o_[:], scalar=sso[:, 0:1], in1=P['subw'][:], op0=ALU.mult, op1=ALU.mult),
                     reads=[o_, sso, P['subw']], writes=[ob])
                pb = bank()
                pbb = pb.t[:].bitcast(BF16)
                s.op('pe', lambda e: e.transpose(out=pbb[:, 0:128], in_=ob[:], identity=C['ident_b'][:]), reads=[ob, C['ident_b']], writes=[pb])
                qc = q0 + qs * 128
                s.op('act', lambda e: e.copy(out=mixT[:, 2 + h, qc:qc + 128], in_=pbb[:, 0:128]), reads=[pb], writes=[(mixT, ('att', h, qc))])


NWB = int(os.environ.get('NWB', '2'))


def ffn_stage(st, nc, s, sbuf, bank, P, D, l, b, NB, NCOL, hT, need_ctx, last, xs_d, XS, out_d, OUT, gsc_d, GSC, load_w, rstd_from_ss, tcols, DB, stop, mkstg, load_cast, **_):
    stg = mkstg(st, 4)
    wd = sbuf(st, "wd", [128, 22, 1024], BF16)
    wdsrc = D['ffn_w_down'].ap()[l].rearrange("(f p) c -> p f c", p=128)
    for f in range(22):
        load_cast(stg, wd[:, f, :], wd, wdsrc[:, f, :], [1024], key=f)
    gbc = sbuf(st, "gbcF", [128, 2, 1024])
    s.dma(gbc[:, 0, :], bass.AP(gsc_d, ((l * 2 + 1) * NCOL + b) * 1024, [[0, 128], [1, 1024]]), reads=[GSC], writes=[gbc], g=0)
    s.dma(gbc[:, 1, :], bass.AP(gsc_d, ((l * 2 + 1) * NCOL + NB) * 1024, [[0, 128], [1, 1024]]), reads=[GSC], writes=[gbc], g=0)
    actT = sbuf(st, "actT", [128, 22, 1152], BF16)
    wg = [sbuf(st, f"wg{i}", [128, 8, 128], BF16) for i in range(NWB)]
    wu = [sbuf(st, f"wu{i}", [128, 8, 128], BF16) for i in range(NWB)]
    graw = sbuf(st, "graw", [128, 1160])
    gc = sbuf(st, "gc", [128, 1160])
    ub = sbuf(st, "ub", [128, 1152], BF16)
    xt2 = [sbuf(st, f"xtF{i}", [128, 1024]) for i in range(2)]
    junk = sbuf(st, "junkF", [128, 1024], BF16)
    ss2 = sbuf(st, "ssF", [128, 2])
    rs2 = sbuf(st, "rsF", [128, 1])
    ytmp = sbuf(st, "ytmpF", [128, 1024])
    s.op('pool', lambda e: e.memset(graw[:], 0.0), writes=[graw])
    if stop == 'F0':
        return
    halves = []
    if need_ctx:
        halves.append(dict(gp=[(0, 256, 1), (256, 768, 259), (768, 1153, 771)], up=[(0, 256, 0), (256, 768, 256), (768, 1152, 768)],
                           outs=[(0, 256, 1), (256, 1152, 259)], tiles=list(range(0, 9)), base=0, pads=[0, 257, 258]))
    else:
        halves.append(dict(gp=[(256, 768, 259), (768, 1153, 771)], up=[(256, 768, 256), (768, 1152, 768)],
                           outs=[(256, 1152, 259)], tiles=list(range(2, 9)), base=0, pads=[0, 257, 258]))
    halves.append(dict(gp=[(1151, 1663, 0), (1663, 2175, 512), (2175, 2304, 1024)], up=[(1152, 1664, 0), (1664, 2176, 512), (2176, 2304, 1024)],
                       outs=[(0, 1152, 1)], tiles=list(range(9, 18)), base=1152, pads=[1153]))
    for hv in halves:
        for pcol in hv['pads']:
            s.op('pool', lambda e, pcol=pcol: e.memset(graw[:, pcol:pcol + 1], 0.0), writes=[graw])
        for f in range(int(os.environ.get('FSTART', '0')), int(os.environ.get('FCUT', '22'))):
            g_, u_ = wg[f % NWB], wu[f % NWB]
            load_cast(stg, g_[:], g_, D['ffn_w_gate'].ap()[l].rearrange("(k p) c -> p k c", p=128)[:, :, f * 128:(f + 1) * 128], [8, 128], eng='pool')
            load_cast(stg, u_[:], u_, D['ffn_w_up'].ap()[l].rearrange("(k p) c -> p k c", p=128)[:, :, f * 128:(f + 1) * 128], [8, 128], eng='pool')
            _FS = int(os.environ.get('FSTEP', '0')); _lastf = f == int(os.environ.get('FCUT', '22')) - 1
            if _lastf and _FS == 1:
                return
            for (c0, c1, dst) in hv['gp']:
                pb = bank()
                n = c1 - c0
                for k in range(8):
                    s.op('pe', lambda e, k=k: e.matmul(pb[:, 0:n], lhsT=g_[:, k, :], rhs=hT[:, k, c0:c1], start=(k == 0), stop=(k == 7)), reads=[hT, g_], writes=[pb])
                s.op('act', lambda e: e.copy(out=graw[:, dst:dst + n], in_=pb[:, 0:n]), reads=[pb], writes=[graw])
            if _lastf and _FS == 2:
                return
            for (c0, c1, dst) in hv['up']:
                pb = bank()
                n = c1 - c0
                for k in range(8):
                    s.op('pe', lambda e, k=k: e.matmul(pb[:, 0:n], lhsT=u_[:, k, :], rhs=hT[:, k, c0:c1], start=(k == 0), stop=(k == 7)), reads=[hT, u_], writes=[pb])
                s.op('dve', lambda e: e.tensor_copy(out=ub[:, dst:dst + n], in_=pb[:, 0:n]), reads=[pb], writes=[ub])
            if _lastf and _FS == 3:
                return
            s.op('act', lambda e: e.activation(out=gc[:, 1:1157], in_=graw[:, 1:1157], func=AF.Identity, scale=P['fcw'][:, f, 1:2], bias=P['fcb'][:, f:f + 1]),
                 reads=[graw, P['fcw'], P['fcb']], writes=[gc])
            for j in (0, 2):
                s.op('pool', lambda e, j=j: e.scalar_tensor_tensor(out=gc[:, 1:1157], in0=graw[:, j:j + 1156], scalar=P['fcw'][:, f, j:j + 1], in1=gc[:, 1:1157],
                                                                  op0=ALU.mult, op1=ALU.add), reads=[graw, gc, P['fcw']], writes=[gc]) if False else \
                    s.op('dve', lambda e, j=j: e.scalar_tensor_tensor(out=gc[:, 1:1157], in0=graw[:, j:j + 1156], scalar=P['fcw'][:, f, j:j + 1], in1=gc[:, 1:1157],
                                                                     op0=ALU.mult, op1=ALU.add), reads=[graw, gc, P['fcw']], writes=[gc])
            s.op('act', lambda e: e.activation(out=gc[:, 1:1157], in_=gc[:, 1:1157], func=AF.Silu), reads=[gc], writes=[gc])
            for (a0, a1, src) in hv['outs']:
                n = a1 - a0
                s.op('dve', lambda e: e.tensor_tensor(out=actT[:, f, a0:a1], in0=gc[:, src:src + n], in1=ub[:, a0:a1], op=ALU.mult),
                     reads=[gc, ub], writes=[(actT, f)])
            if stop == 'F1':
                return
        if stop == 'F2':
            return
        htl = hv['tiles']

        def ldF(i):
            s.dma(xt2[i % 2][:], xs_d.ap()[b, tcols(htl[i]), :], reads=[(XS, b)], writes=[xt2[i % 2]], g=0)
        ldF(0)
        for ti, t in enumerate(htl):
            if ti + 1 < len(htl):
                ldF(ti + 1)
            xt = xt2[ti % 2]
            tc = t * 128 - hv['base']
            pbs = [bank(), bank()]
            for hf in range(2):
                for f in range(22):
                    s.op('pe', lambda e, f=f, hf=hf: e.matmul(pbs[hf][:, :], lhsT=actT[:, f, tc:tc + 128], rhs=wd[:, f, hf * 512:(hf + 1) * 512],
                                                             start=(f == 0), stop=(f == 21)), reads=[actT, wd], writes=[pbs[hf]])
                s.op('act', lambda e, hf=hf: e.activation(out=junk[:, hf * 512:(hf + 1) * 512], in_=pbs[hf][:, :], func=AF.Square, accum_out=ss2[:, hf:hf + 1]),
                     reads=[pbs[hf]], writes=[junk, ss2])
            s.op('dve', lambda e: e.tensor_tensor(out=rs2[:], in0=ss2[:, 0:1], in1=ss2[:, 1:2], op=ALU.add), reads=[ss2], writes=[rs2])
            rstd_from_ss(rs2, rs2, 1024)
            gi = 1 if t < 2 else 0
            for hf in range(2):
                s.op('dve', lambda e, hf=hf: e.scalar_tensor_tensor(out=ytmp[:, hf * 512:(hf + 1) * 512], in0=pbs[hf][:, :], scalar=rs2[:, 0:1],
                                                                   in1=gbc[:, gi, hf * 512:(hf + 1) * 512], op0=ALU.mult, op1=ALU.mult),
                     reads=[pbs[hf], rs2, gbc], writes=[ytmp])
            s.op('pool', lambda e: e.tensor_tensor(out=xt[:], in0=xt[:], in1=ytmp[:], op=ALU.add), reads=[xt, ytmp], writes=[xt])
            if last:
                if t >= 2:
                    s.dma(out_d.ap()[b, (t - 2) * 128:(t - 1) * 128, :], xt[:], reads=[xt], writes=[OUT], g=1)
            else:
                s.dma(xs_d.ap()[b, tcols(t), :], xt[:], reads=[xt], writes=[(XS, b)], g=1)


def build(NB=4, LAYERS=(0, 1), dbg=(), stop=None):
    nc = bass.Bass("TRN2", target_bir_lowering=False)
    D = {}

    def din(name, shape):
        D[name] = nc.dram_tensor(name, list(shape), F32, kind="ExternalInput")
        return D[name]

    x_d = din("x", [NB, 2048, 1024])
    ctx_d = din("ctx", [NB, 256, 1024])
    cc_d = din("cc", [NB + 1, 1024])
    for n, sh in PARAMS:
        din(n, sh)
    for n, a in host_consts().items():
        din(n, a.shape)
    out_d = nc.dram_tensor("out", [NB, 2048, 1024], F32, kind="ExternalOutput")
    xs_d = nc.dram_tensor("xs", [NB, SEQ, 1024], F32)
    gsc_d = nc.dram_tensor("gsc", [2, 2, NB + 1, 1024], F32)
    DB = {}
    for n, sh in dbg:
        DB[n] = nc.dram_tensor("dbg_" + n, list(sh), F32, kind="ExternalOutput")

    NCOL = NB + 1
    with contextlib.ExitStack() as top:
        s = Sched(nc, top)
        top.enter_context(nc.allow_non_contiguous_dma(reason="small param layout loads"))
        top.enter_context(nc.allow_low_precision(reason="bf16 matmul operands"))
        XS = Buf(xs_d, "xs")
        GSC = Buf(gsc_d, "gsc")
        OUT = Buf(out_d, "out")
        DIN = Buf(None, "din")

        uid = [0]

        def sbuf(st, name, shape, dt=F32):
            uid[0] += 1
            name = f"{name}_u{uid[0]}"
            return Buf(st.enter_context(nc.sbuf_tensor(name, list(shape), dt)), name)

        banks = [Buf(top.enter_context(nc.psum_tensor(f"pb{i}", [128, 512], F32)), f"pb{i}", psum=True) for i in range(8)]
        bank_i = [0]

        def bank(idx=None):
            if idx is not None:
                return banks[idx]
            b = banks[bank_i[0] % 8]
            bank_i[0] += 1
            return b

        def dmp(name, ap, buf, key=None):
            if name in DB:
                s.dma(DB[name].ap() if not isinstance(name, tuple) else None, ap, reads=[(buf, key)], g=5)

        ident_f = sbuf(top, "ident_f", [128, 128])
        ident_b = sbuf(top, "ident_b", [128, 128], BF16)
        U_f = sbuf(top, "U_f", [128, 128])
        UT_f = sbuf(top, "UT_f", [128, 128])
        mf_f = sbuf(top, "mf_f", [128, 128])
        mb_f = sbuf(top, "mb_f", [128, 128])
        ones_f = sbuf(top, "ones_f", [128, 128])
        ropeC = sbuf(top, "ropeC", [128, 16, 32])
        ropeS = sbuf(top, "ropeS", [128, 16, 32])
        for bufc, nm in [(ident_f, 'c_ident'), (U_f, 'c_U'), (UT_f, 'c_UT'), (mf_f, 'c_mf'), (mb_f, 'c_mb'), (ones_f, 'c_ones')]:
            s.dma(bufc[:], D[nm].ap(), writes=[bufc], g=0)
        s.dma(ropeC[:], D['c_cos'].ap().rearrange("(t p) j -> p t j", p=128), writes=[ropeC], g=0)
        s.dma(ropeS[:], D['c_sin'].ap().rearrange("(t p) j -> p t j", p=128), writes=[ropeS], g=0)
        s.op('act', lambda e: e.copy(out=ident_b[:], in_=ident_f[:]), reads=[ident_f], writes=[ident_b])
        ones_b = sbuf(top, "ones_b", [128, 128], BF16)
        s.op('act', lambda e: e.copy(out=ones_b[:], in_=ones_f[:]), reads=[ones_f], writes=[ones_b])

        LP = {}
        for l in LAYERS:
            P = {}
            P['s1a'] = sbuf(top, f"s1a{l}", [128, 8, NCOL])
            P['sha'] = sbuf(top, f"sha{l}", [128, 8, NCOL])
            P['s1f'] = sbuf(top, f"s1f{l}", [128, 8, NCOL])
            P['shf'] = sbuf(top, f"shf{l}", [128, 8, NCOL])
            P['dtb'] = sbuf(top, f"dtb{l}", [128, 8])
            P['aneg'] = sbuf(top, f"aneg{l}", [128, 8])
            P['dsk'] = sbuf(top, f"dsk{l}", [128, 4])
            P['snw'] = sbuf(top, f"snw{l}", [128, 256])
            P['scw'] = sbuf(top, f"scw{l}", [128, 4, 5])
            P['scb'] = sbuf(top, f"scb{l}", [128, 4])
            P['subw'] = sbuf(top, f"subw{l}", [128, 128])
            P['nlam'] = sbuf(top, f"nlam{l}", [128, 1])
            P['subwc'] = sbuf(top, f"subwc{l}", [128, 1])
            P['fcw'] = sbuf(top, f"fcw{l}", [128, 22, 3])
            P['fcb'] = sbuf(top, f"fcb{l}", [128, 22])
            P['s5d'] = sbuf(top, f"s5d{l}", [128, 2])
            P['glub'] = sbuf(top, f"glub{l}", [128, 2])
            P['gluw'] = sbuf(top, f"gluw{l}", [128, 2, 256], BF16)
            P['s5r'] = sbuf(top, f"s5r{l}", [128, 16])
            P['s5ar'] = sbuf(top, f"s5ar{l}", [128, 16])
            P['s5ai'] = sbuf(top, f"s5ai{l}", [128, 16])
            LP[l] = P

        S5SCR = {}
        LP['_negpi'] = sbuf(top, "negpi", [128, 1])
        s.op('pool', lambda e: e.memset(LP['_negpi'][:], -math.pi), writes=[LP['_negpi']])
        LP['_Z'] = sbuf(top, "Zt", [128, 8, 128])
        s.op('pool', lambda e: e.memset(LP['_Z'][:], 0.0), writes=[LP['_Z']])
        with contextlib.ExitStack() as st:
            scT = sbuf(st, "scT", [128, 8, NCOL])
            for col in range(NCOL):
                s.dma(scT[:, :, col], bass.AP(cc_d, col * 1024, [[1, 128], [128, 8]]), writes=[scT], g=0)
            s.op('act', lambda e: e.activation(out=scT[:], in_=scT[:], func=AF.Silu), reads=[scT], writes=[scT])
            wm = [sbuf(st, f"wm{i}", [128, 8, 512]) for i in range(2)]
            modT = sbuf(st, "modT", [128, 48, NCOL])
            modb = sbuf(st, "modb", [128, 48])
            nrm = sbuf(st, "nrm", [128, 4, 8])
            tmpg = sbuf(st, "tmpg", [128, 2, 8, NCOL])
            iota = sbuf(st, "iota", [128, 256])
            s.dma(iota[:], D['c_iota'].ap(), writes=[iota], g=0)
            S5D = {}
            for l in LAYERS:
                P = LP[l]
                P['WB'] = sbuf(st, f"WB{l}", [128, 2, 8, 2, 128], BF16)
                P['WC'] = sbuf(st, f"WC{l}", [128, 2, 8, 2, 128])
                P['cosT'] = sbuf(st, f"cosT{l}", [128, 16, 256])
                P['sinT'] = sbuf(st, f"sinT{l}", [128, 16, 256])
                s.dma(modb[:], bass.AP(D['mod_b'], l * 6144, [[1, 128], [128, 48]]), writes=[modb], g=0)
                for i, nm in enumerate(['mix_norm_pre', 'mix_norm_post', 'ffn_norm_pre', 'ffn_norm_post']):
                    s.dma(nrm[:, i, :], bass.AP(D[nm], l * 1024, [[1, 128], [128, 8]]), writes=[nrm], g=0)
                for cch in range(12):
                    w = wm[cch % 2]
                    s.dma(w[:], D['mod_w'].ap()[l].rearrange("(k p) c -> p k c", p=128)[:, :, cch * 512:(cch + 1) * 512],
                          writes=[w], g=1)
                    for jj in range(4):
                        j = cch * 4 + jj
                        pb = bank()
                        for k in range(8):
                            s.op('pe', lambda e, k=k, jj=jj, w=w, pb=pb: e.matmul(pb[:, 0:NCOL], lhsT=w[:, k, jj * 128:(jj + 1) * 128],
                                                                                 rhs=scT[:, k, :], start=(k == 0), stop=(k == 7)),
                                 reads=[w, scT], writes=[pb])
                        s.op('act', lambda e, j=j, pb=pb: e.activation(out=modT[:, j, :], in_=pb[:, 0:NCOL], func=AF.Identity,
                                                                      bias=modb[:, j:j + 1], scale=1.0),
                             reads=[pb, modb], writes=[modT])
                for (s1, sh, o_sh, o_sc, o_g, ipre, ipost, wi) in [(P['s1a'], P['sha'], 0, 8, 16, 0, 1, 0),
                                                                    (P['s1f'], P['shf'], 24, 32, 40, 2, 3, 1)]:
                    s.op('dve', lambda e, s1=s1, o_sc=o_sc: e.tensor_scalar(out=s1[:], in0=modT[:, o_sc:o_sc + 8, :], scalar1=1.0,
                                                                           scalar2=None, op0=ALU.add),
                         reads=[modT], writes=[s1])
                    s.op('dve', lambda e, s1=s1, ipre=ipre: e.tensor_tensor(out=s1[:], in0=s1[:],
                                                                           in1=nrm[:, ipre, :].unsqueeze(2).to_broadcast([128, 8, NCOL]),
                                                                           op=ALU.mult),
                         reads=[s1, nrm], writes=[s1])
                    s.op('dve', lambda e, sh=sh, o_sh=o_sh: e.tensor_copy(out=sh[:], in_=modT[:, o_sh:o_sh + 8, :]),
                         reads=[modT], writes=[sh])
                    s.op('dve', lambda e, wi=wi, o_g=o_g, ipost=ipost: e.tensor_tensor(
                        out=tmpg[:, wi], in0=modT[:, o_g:o_g + 8, :],
                        in1=nrm[:, ipost, :].unsqueeze(2).to_broadcast([128, 8, NCOL]), op=ALU.mult),
                        reads=[modT, nrm], writes=[tmpg])
                    for col in range(NCOL):
                        s.dma(bass.AP(gsc_d, ((l * 2 + wi) * NCOL + col) * 1024, [[1, 128], [128, 8]]), tmpg[:, wi, :, col],
                              reads=[tmpg], writes=[GSC], g=2)

                def bc(name, off, n):
                    return bass.AP(D[name], off, [[0, 128], [1, n]])
                s.dma(P['dtb'][:], bc('ssd_dt_bias', l * 8, 8), writes=[P['dtb']], g=0)
                s.dma(P['aneg'][:], bc('ssd_a_log', l * 8, 8), writes=[P['aneg']], g=0)
                s.op('act', lambda e, P=P: e.activation(out=P['aneg'][:], in_=P['aneg'][:], func=AF.Exp), reads=[P['aneg']], writes=[P['aneg']])
                s.op('dve', lambda e, P=P: e.tensor_scalar(out=P['aneg'][:], in0=P['aneg'][:], scalar1=-1.0, scalar2=None, op0=ALU.mult),
                     reads=[P['aneg']], writes=[P['aneg']])
                s.dma(P['dsk'][:], bc('ssd_d', l * 4, 4), writes=[P['dsk']], g=0)
                s.dma(P['snw'][:], bc('ssd_norm_w', l * 256, 256), writes=[P['snw']], g=0)
                for j in range(5):
                    s.dma(P['scw'][:, :, j], bass.AP(D['ssd_conv_w'], l * 2560 + j * 512, [[1, 128], [128, 4]]), writes=[P['scw']], g=0)
                s.dma(P['scb'][:], bass.AP(D['ssd_conv_b'], l * 512, [[1, 128], [128, 4]]), writes=[P['scb']], g=0)
                for j in range(3):
                    s.dma(P['fcw'][:, :, j], bass.AP(D['ffn_conv_w'], (l * 3 + j) * 2816, [[1, 128], [128, 22]]), writes=[P['fcw']], g=0)
                s.dma(P['fcb'][:], bass.AP(D['ffn_conv_b'], l * 2816, [[1, 128], [128, 22]]), writes=[P['fcb']], g=0)
                s.dma(P['s5d'][:], bass.AP(D['s5_d'], l * 256, [[1, 128], [128, 2]]), writes=[P['s5d']], g=0)
                s.dma(P['glub'][:], bass.AP(D['s5_glu_b'], l * 256, [[1, 128], [128, 2]]), writes=[P['glub']], g=0)
                gst = sbuf(st, f"gst{l}", [128, 2, 256])
                s.dma(gst[:], D['s5_glu_w'].ap()[l].rearrange("(k p) c -> p k c", p=128), writes=[gst], g=3)
                s.op('pool', lambda e, P=P, gst=gst: e.tensor_copy(out=P['gluw'][:], in_=gst[:]), reads=[gst], writes=[P['gluw']])
                lam_init = 0.8 - 0.6 * math.exp(-0.3 * l)
                s.dma(P['subw'][:], bc('diff_subln_w', l * 128, 128), writes=[P['subw']], g=0)
                s.dma(P['subwc'][:], bass.AP(D['diff_subln_w'], l * 128, [[1, 128], [1, 1]]), writes=[P['subwc']], g=0)
                s.op('dve', lambda e, P=P, li=lam_init: e.tensor_scalar(out=P['subwc'][:], in0=P['subwc'][:], scalar1=1.0 - li, scalar2=None,
                                                                      op0=ALU.mult), reads=[P['subwc']], writes=[P['subwc']])
                s.op('dve', lambda e, P=P, li=lam_init: e.tensor_scalar(out=P['subw'][:], in0=P['subw'][:], scalar1=1.0 - li, scalar2=None,
                                                                      op0=ALU.mult), reads=[P['subw']], writes=[P['subw']])
                lt = sbuf(st, f"lt{l}", [128, 4, 64])
                lsum = sbuf(st, f"lsum{l}", [128, 4])
                for i, nm in enumerate(['diff_lam_q1', 'diff_lam_k1', 'diff_lam_q2', 'diff_lam_k2']):
                    s.dma(lt[:, i, :], bc(nm, l * 64, 64), writes=[lt], g=0)
                s.op('dve', lambda e, lt=lt: e.tensor_tensor(out=lt[:, 0, :], in0=lt[:, 0, :], in1=lt[:, 1, :], op=ALU.mult), reads=[lt], writes=[lt])
                s.op('dve', lambda e, lt=lt: e.tensor_tensor(out=lt[:, 2, :], in0=lt[:, 2, :], in1=lt[:, 3, :], op=ALU.mult), reads=[lt], writes=[lt])
                s.op('dve', lambda e, lt=lt, lsum=lsum: e.reduce_sum(out=lsum[:, 0:1], in_=lt[:, 0, :], axis=mybir.AxisListType.X), reads=[lt], writes=[lsum])
                s.op('dve', lambda e, lt=lt, lsum=lsum: e.reduce_sum(out=lsum[:, 1:2], in_=lt[:, 2, :], axis=mybir.AxisListType.X), reads=[lt], writes=[lsum])
                s.op('act', lambda e, lsum=lsum: e.activation(out=lsum[:, 2:4], in_=lsum[:, 0:2], func=AF.Exp), reads=[lsum], writes=[lsum])
                s.op('dve', lambda e, lsum=lsum, P=P, li=lam_init: e.scalar_tensor_tensor(out=P['nlam'][:], in0=lsum[:, 3:4], scalar=-li,
                                                                                        in1=lsum[:, 2:3], op0=ALU.add, op1=ALU.subtract),
                     reads=[lsum], writes=[P['nlam']])

                lre = sbuf(st, f"lre{l}", [128, 16])
                lim = sbuf(st, f"lim{l}", [128, 16])
                stp = sbuf(st, f"stp{l}", [128, 16])
                for d in range(2):
                    s.dma(lre[:, d * 8:(d + 1) * 8], bass.AP(D['s5_lam_re'], (l * 2 + d) * 1024, [[1, 128], [128, 8]]), writes=[lre], g=0)
                    s.dma(lim[:, d * 8:(d + 1) * 8], bass.AP(D['s5_lam_im'], (l * 2 + d) * 1024, [[1, 128], [128, 8]]), writes=[lim], g=0)
                    for gl in range(2):
                        s.dma(stp[64 * gl:64 * gl + 64, d * 8:(d + 1) * 8],
                              bass.AP(D['s5_log_step'], (l * 2 + d) * 16 + gl, [[0, 64], [2, 8]]), writes=[stp], g=0)
                w16 = [sbuf(st, f"w16_{l}_{i}", [128, 16]) for i in range(10)]
                lrs, lis, mag, sn, cs, nr, den, cre, cim, t16 = w16
                r_, ar_, ai_ = P['s5r'], P['s5ar'], P['s5ai']

                def V(fn, rd, wr, eng='dve'):
                    s.op(eng, fn, reads=rd, writes=wr)
                V(lambda e: e.activation(out=stp[:], in_=stp[:], func=AF.Exp), [stp], [stp], 'act')
                V(lambda e: e.tensor_tensor(out=lrs[:], in0=lre[:], in1=stp[:], op=ALU.mult), [lre, stp], [lrs])
                V(lambda e: e.tensor_tensor(out=lis[:], in0=lim[:], in1=stp[:], op=ALU.mult), [lim, stp], [lis])
                V(lambda e: e.activation(out=r_[:], in_=lrs[:], func=AF.Exp), [lrs], [r_], 'act')
                ki16 = sbuf(st, f"ki16_{l}", [128, 16], mybir.dt.int32)
                kf16 = sbuf(st, f"kf16_{l}", [128, 16])

                def sin16(out, phase):
                    V(lambda e: e.tensor_scalar(out=t16[:], in0=lis[:], scalar1=phase, scalar2=None, op0=ALU.add), [lis], [t16])
                    V(lambda e: e.tensor_scalar(out=ki16[:], in0=t16[:], scalar1=1.0 / TWO_PI, scalar2=None, op0=ALU.mult), [t16], [ki16])
                    V(lambda e: e.tensor_copy(out=kf16[:], in_=ki16[:]), [ki16], [kf16])
                    V(lambda e: e.scalar_tensor_tensor(out=t16[:], in0=kf16[:], scalar=-TWO_PI, in1=t16[:], op0=ALU.mult, op1=ALU.add), [kf16, t16], [t16])
                    V(lambda e: e.tensor_scalar(out=t16[:], in0=t16[:], scalar1=-3.14159, scalar2=3.14159, op0=ALU.max, op1=ALU.min), [t16], [t16])
                    V(lambda e: e.activation(out=out[:], in_=t16[:], func=AF.Sin), [t16], [out], 'act')
                sin16(sn, 0.0)
                sin16(cs, 0.5 * math.pi)
                negpi = LP['_negpi']
                Zt = LP['_Z']
                V(lambda e: e.tensor_tensor(out=ar_[:], in0=r_[:], in1=cs[:], op=ALU.mult), [r_, cs], [ar_])
                V(lambda e: e.tensor_tensor(out=ai_[:], in0=r_[:], in1=sn[:], op=ALU.mult), [r_, sn], [ai_])
                V(lambda e: e.tensor_scalar(out=nr[:], in0=ar_[:], scalar1=-1.0, scalar2=None, op0=ALU.add), [ar_], [nr])
                V(lambda e: e.tensor_tensor(out=den[:], in0=lre[:], in1=lre[:], op=ALU.mult), [lre], [den])
                V(lambda e: e.tensor_tensor(out=t16[:], in0=lim[:], in1=lim[:], op=ALU.mult), [lim], [t16])
                V(lambda e: e.tensor_tensor(out=den[:], in0=den[:], in1=t16[:], op=ALU.add), [den, t16], [den])
                V(lambda e: e.reciprocal(out=den[:], in_=den[:]), [den], [den])
                V(lambda e: e.tensor_tensor(out=cre[:], in0=nr[:], in1=lre[:], op=ALU.mult), [nr, lre], [cre])
                V(lambda e: e.tensor_tensor(out=t16[:], in0=ai_[:], in1=lim[:], op=ALU.mult), [ai_, lim], [t16])
                V(lambda e: e.tensor_tensor(out=cre[:], in0=cre[:], in1=t16[:], op=ALU.add), [cre, t16], [cre])
                V(lambda e: e.tensor_tensor(out=cre[:], in0=cre[:], in1=den[:], op=ALU.mult), [cre, den], [cre])
                V(lambda e: e.tensor_tensor(out=cim[:], in0=ai_[:], in1=lre[:], op=ALU.mult), [ai_, lre], [cim])
                V(lambda e: e.tensor_tensor(out=t16[:], in0=nr[:], in1=lim[:], op=ALU.mult), [nr, lim], [t16])
                V(lambda e: e.tensor_tensor(out=cim[:], in0=cim[:], in1=t16[:], op=ALU.subtract), [cim, t16], [cim])
                V(lambda e: e.tensor_tensor(out=cim[:], in0=cim[:], in1=den[:], op=ALU.mult), [cim, den], [cim])
                rt = sbuf(st, f"rt{l}", [128, 256])
                rki = sbuf(st, f"rki{l}", [128, 256], mybir.dt.int32)
                rkf = sbuf(st, f"rkf{l}", [128, 256])
                for j in range(16):
                    for (tab, ph) in [(P['cosT'], 0.5 * math.pi), (P['sinT'], 0.0)]:
                        V(lambda e, j=j, ph=ph: e.tensor_scalar(out=rt[:], in0=iota[:], scalar1=lis[:, j:j + 1], scalar2=ph, op0=ALU.mult, op1=ALU.add),
                          [iota, lis], [rt])
                        V(lambda e: e.tensor_scalar(out=rki[:], in0=rt[:], scalar1=1.0 / TWO_PI, scalar2=None, op0=ALU.mult), [rt], [rki])
                        V(lambda e: e.tensor_copy(out=rkf[:], in_=rki[:]), [rki], [rkf])
                        V(lambda e: e.scalar_tensor_tensor(out=rt[:], in0=rkf[:], scalar=-TWO_PI, in1=rt[:], op0=ALU.mult, op1=ALU.add), [rkf, rt], [rt])
                        V(lambda e: e.tensor_scalar(out=rt[:], in0=rt[:], scalar1=-3.14159, scalar2=3.14159, op0=ALU.max, op1=ALU.min), [rt], [rt])
                        V(lambda e, j=j, tab=tab: e.activation(out=tab[:, j, :], in_=rt[:], func=AF.Sin), [rt], [(tab, j)], 'act')
                bre = sbuf(st, f"bre{l}", [128, 16, 16])
                bim = sbuf(st, f"bim{l}", [128, 16, 16])
                bbr = sbuf(st, f"bbr{l}", [128, 16, 16])
                bbi = sbuf(st, f"bbi{l}", [128, 16, 16])
                btm = sbuf(st, f"btm{l}", [128, 16, 16])
                for d in range(2):
                    s.dma(bre[:, d * 8:(d + 1) * 8, :], bass.AP(D['s5_b_re'], (l * 2 + d) * 16384, [[16, 128], [2048, 8], [1, 16]]), writes=[bre], g=0)
                    s.dma(bim[:, d * 8:(d + 1) * 8, :], bass.AP(D['s5_b_im'], (l * 2 + d) * 16384, [[16, 128], [2048, 8], [1, 16]]), writes=[bim], g=0)

                def B3(t):
                    return t[:].unsqueeze(2).to_broadcast([128, 16, 16])
                V(lambda e: e.tensor_tensor(out=bbr[:], in0=bre[:], in1=B3(cre), op=ALU.mult), [bre, cre], [bbr])
                V(lambda e: e.tensor_tensor(out=btm[:], in0=bim[:], in1=B3(cim), op=ALU.mult), [bim, cim], [btm])
                V(lambda e: e.tensor_tensor(out=bbr[:], in0=bbr[:], in1=btm[:], op=ALU.subtract), [bbr, btm], [bbr])
                V(lambda e: e.tensor_tensor(out=bbi[:], in0=bim[:], in1=B3(cre), op=ALU.mult), [bim, cre], [bbi])
                V(lambda e: e.tensor_tensor(out=btm[:], in0=bre[:], in1=B3(cim), op=ALU.mult), [bre, cim], [btm])
                V(lambda e: e.tensor_tensor(out=bbi[:], in0=bbi[:], in1=btm[:], op=ALU.add), [bbi, btm], [bbi])
                for d in range(2):
                    for q in range(8):
                        j = d * 8 + q
                        ql = q % 4
                        for ri, bb in enumerate([bbr, bbi]):
                            zi = ri * 4 + ql
                            for gl in range(2):
                                V(lambda e, gl=gl, zi=zi, ql=ql, bb=bb, j=j: e.tensor_copy(
                                    out=Zt[64 * gl:64 * gl + 64, zi, 32 * ql + 16 * gl:32 * ql + 16 * gl + 16], in_=bb[64 * gl:64 * gl + 64, j, :]),
                                  [bb], [(Zt, zi)])
                            pb = bank()
                            s.op('pe', lambda e, pb=pb, zi=zi: e.transpose(out=pb[:, 0:128], in_=Zt[:, zi, :], identity=ident_f[:]),
                                 reads=[(Zt, zi), ident_f], writes=[pb])
                            s.op('act', lambda e, pb=pb, d=d, q=q, ri=ri: e.copy(out=P['WB'][:, d, q, ri, :], in_=pb[:, 0:128]),
                                 reads=[pb], writes=[(P['WB'], (d, q, ri))])
                s.op('pool', lambda e: e.memset(P['WC'][:], 0.0), writes=[P['WC']])
                for d in range(2):
                    for q in range(8):
                        ql = q % 4
                        for gl in range(2):
                            g = 2 * q + gl
                            for ri, nm in enumerate(['s5_c_re', 's5_c_im']):
                                s.dma(P['WC'][64 * gl:64 * gl + 64, d, q, ri, 32 * ql + 16 * gl:32 * ql + 16 * gl + 16],
                                      bass.AP(D[nm], ((l * 2 + d) * 16 + g) * 1024, [[1, 64], [64, 16]]), writes=[P['WC']], g=4)
                V(lambda e: e.tensor_scalar(out=P['WC'][:, :, :, 1, :], in0=P['WC'][:, :, :, 1, :], scalar1=-1.0, scalar2=None, op0=ALU.mult),
                  [P['WC']], [P['WC']])
                for nm in ['WB', 'WC', 'cosT', 'sinT']:
                    dt_ = BF16 if nm == 'WB' else F32
                    dd = nc.dram_tensor(f"s5scr_{nm}{l}", [128, 4096], dt_)
                    S5SCR[(nm, l)] = (dd, Buf(dd, f"s5scr_{nm}{l}"))
                    src = P[nm][:].rearrange("p a b c d -> p (a b c d)") if nm in ('WB', 'WC') else P[nm][:].rearrange("p a b -> p (a b)")
                    s.dma(dd.ap(), src, reads=[P[nm]], writes=[S5SCR[(nm, l)][1]], g=2)
                    P[nm] = None
            s.barrier()

        def tcols(t):
            return slice(t * 128, (t + 1) * 128)

        PIECES = [(0, 256), (256, 768), (768, 1280), (1280, 1792), (1792, 2304)]

        def mkstg(st, n=3):
            return dict(t=[sbuf(st, "stg", [128, 1024]) for _ in range(n)], i=[0])

        def load_cast(stg, dst_ap, dst_buf, src_ap, shape, key=None, eng='dve'):
            t = stg['t'][stg['i'][0] % len(stg['t'])]
            stg['i'][0] += 1
            n = int(np.prod(shape))
            tv = t[:, 0:n]
            if len(shape) == 2:
                tv = tv.rearrange("p (a b) -> p a b", a=shape[0])
            s.dma(tv, src_ap, writes=[t], g=3)
            s.op(eng, lambda e: e.tensor_copy(out=dst_ap, in_=tv), reads=[t], writes=[(dst_buf, key)])

        def load_w_bf16(stg, wbuf, name, l, c0, n, dst0=0, nk=8):
            src = D[name].ap()[l].rearrange("(k p) c -> p k c", p=128)
            kk = max(1, 1024 // n)
            for k0 in range(0, nk, kk):
                k1 = min(nk, k0 + kk)
                if k1 - k0 == 1:
                    load_cast(stg, wbuf[:, k0, dst0:dst0 + n], wbuf, src[:, k0, c0:c0 + n], [n])
                else:
                    load_cast(stg, wbuf[:, k0:k1, dst0:dst0 + n], wbuf, src[:, k0:k1, c0:c0 + n], [k1 - k0, n])

        def rstd_from_ss(ss, rstd, n):
            s.op('act', lambda e: e.activation(out=rstd[:], in_=ss[:], func=AF.Sqrt, bias=EPS, scale=1.0 / n), reads=[ss], writes=[rstd])
            s.op('dve', lambda e: e.reciprocal(out=rstd[:], in_=rstd[:]), reads=[rstd], writes=[rstd])

        def nmt_front(st_bufs, xt):
            junk, ss, rstd, xn = st_bufs
            s.op('act', lambda e: e.activation(out=junk[:], in_=xt[:], func=AF.Square, accum_out=ss[:]), reads=[xt], writes=[junk, ss])
            rstd_from_ss(ss, rstd, 1024)
            s.op('dve', lambda e: e.tensor_scalar(out=xn[:], in0=xt[:], scalar1=rstd[:, 0:1], scalar2=None, op0=ALU.mult),
                 reads=[xt, rstd], writes=[xn])

        def nmt_back(st_bufs, hT, t, s1, sh, col):
            junk, ss, rstd, xn = st_bufs
            pbs_ = [bank(), bank()]
            pbb = [pb.t[:].bitcast(BF16) for pb in pbs_]
            for k in range(8):
                hb, kk = k // 4, k % 4
                s.op('pe', lambda e, k=k, hb=hb, kk=kk: e.transpose(out=pbb[hb][:, kk * 128:(kk + 1) * 128], in_=xn[:, k * 128:(k + 1) * 128], identity=ident_b[:]),
                     reads=[xn, ident_b], writes=[pbs_[hb]])
            for k in range(8):
                hb, kk = k // 4, k % 4
                if hb == 0:
                    s.op('act', lambda e, k=k, kk=kk: e.activation(out=hT[:, k, tcols(t)], in_=pbb[0][:, kk * 128:(kk + 1) * 128], func=AF.Identity,
                                                                  scale=s1[:, k, col:col + 1], bias=sh[:, k, col:col + 1]),
                         reads=[pbs_[0], s1, sh], writes=[(hT, (t, 0))])
                else:
                    s.op('dve', lambda e, k=k, kk=kk: e.tensor_scalar(out=hT[:, k, tcols(t)], in0=pbb[1][:, kk * 128:(kk + 1) * 128],
                                                                     scalar1=s1[:, k, col:col + 1], scalar2=sh[:, k, col:col + 1], op0=ALU.mult, op1=ALU.add),
                         reads=[pbs_[1], s1, sh], writes=[(hT, (t, 1))])

        def norm_mod_transpose(st_bufs, xt, hT, t, s1, sh, col):
            nmt_front(st_bufs, xt)
            nmt_back(st_bufs, hT, t, s1, sh, col)

        for b in range(NB):
            s.dma(xs_d.ap()[b, 0:256, :], ctx_d.ap()[b], writes=[(XS, b)], g=0)
            s.dma(xs_d.ap()[b, 256:2304, :], x_d.ap()[b], writes=[(XS, b)], g=0)
            for l in LAYERS:
                P = LP[l]
                last = (l == LAYERS[-1])
                need_ctx = not last
                with contextlib.ExitStack() as shs:
                  hT = sbuf(shs, "hT", [128, 8, SEQ], BF16)
                  early = False
                  with contextlib.ExitStack() as sm:
                    mixT = sbuf(sm, "mixT", [128, 8, SEQ], BF16)
                    with contextlib.ExitStack() as st:
                        xt2 = [sbuf(st, f"xt{i}", [128, 1024]) for i in range(3)]
                        nbs = [(sbuf(st, "junk", [128, 1024], BF16), sbuf(st, "ssA", [128, 1]), sbuf(st, "rstdA", [128, 1]),
                                sbuf(st, "xnA", [128, 1024], BF16)) for _ in range(2)]
                        def frontA(t):
                            xt = xt2[t % 3]
                            s.dma(xt[:], xs_d.ap()[b, tcols(t), :], reads=[(XS, b)], writes=[xt], g=0)
                            nmt_front(nbs[t % 2], xt)
                        frontA(0)
                        for t in range(NT):
                            if t + 1 < NT:
                                frontA(t + 1)
                            nmt_back(nbs[t % 2], hT, t, P['s1a'], P['sha'], NB if t < 2 else b)
                        s.barrier()
                    CONST = dict(ident_b=ident_b, ident_f=ident_f, U_f=U_f, UT_f=UT_f, mf_f=mf_f, mb_f=mb_f, ones_f=ones_f,
                                 ropeC=ropeC, ropeS=ropeS, ones_b=ones_b)
                    ENV = dict(S5SCR=S5SCR, nc=nc, s=s, sbuf=sbuf, bank=bank, P=P, D=D, l=l, b=b, NB=NB, NCOL=NCOL, hT=hT, mixT=mixT,
                               load_w=load_w_bf16, mkstg=mkstg, load_cast=load_cast, rstd_from_ss=rstd_from_ss, tcols=tcols, PIECES=PIECES, C=CONST, DB=DB, stop=stop,
                               need_ctx=need_ctx, last=last, xs_d=xs_d, XS=XS, out_d=out_d, OUT=OUT, gsc_d=gsc_d, GSC=GSC)
                    if stop == 'A':
                        s.dma(DB['hT'].ap(), hT[:], reads=[hT], q='pool', g=5)
                        early = True
                    if not early:
                        with contextlib.ExitStack() as st:
                            ssd_stage(st, **ENV)
                            s.barrier()
                        early = stop is not None and stop.startswith('SSD')
                    if not early:
                        with contextlib.ExitStack() as st:
                            s5_stage(st, **ENV)
                            s.barrier()
                        early = stop is not None and stop.startswith('S5')
                        if 'mix2' in DB:
                            s.dma(DB['mix2'].ap(), mixT[:], reads=[mixT], q='pool', g=5)
                            s.barrier()
                    if not early:
                        with contextlib.ExitStack() as st:
                            attn_stage(st, **ENV)
                            s.barrier()
                        early = stop is not None and stop.startswith('ATT')
                    if stop is not None and stop[:2] in ('SS', 'S5', 'AT') and 'mixT' in DB:
                        s.dma(DB['mixT'].ap(), mixT[:], reads=[mixT], q='pool', g=5)
                    tiles = list(range(NT)) if need_ctx else list(range(2, NT))
                    if not early:
                      with contextlib.ExitStack() as st:
                        wo = sbuf(st, "wo", [128, 8, 1024], BF16)
                        stgE = mkstg(st)
                        load_w_bf16(stgE, wo, 'w_out', l, 0, 1024)
                        gbc = sbuf(st, "gbc", [128, 2, 1024])
                        s.dma(gbc[:, 0, :], bass.AP(gsc_d, ((l * 2 + 0) * NCOL + b) * 1024, [[0, 128], [1, 1024]]), reads=[GSC], writes=[gbc], g=0)
                        s.dma(gbc[:, 1, :], bass.AP(gsc_d, ((l * 2 + 0) * NCOL + NB) * 1024, [[0, 128], [1, 1024]]), reads=[GSC], writes=[gbc], g=0)
                        NXT = 5
                        xt2 = [sbuf(st, f"xtE{i}", [128, 1024]) for i in range(NXT)]
                        junk1 = [sbuf(st, "junkE1", [128, 1024], BF16) for _ in range(2)]
                        nbs = [(sbuf(st, "junkE", [128, 1024], BF16), sbuf(st, "ssE", [128, 1]), sbuf(st, "rstdE", [128, 1]),
                                sbuf(st, "xnE", [128, 1024], BF16)) for _ in range(2)]
                        ss2s = [sbuf(st, "ss2", [128, 2]) for _ in range(2)]
                        rs2s = [sbuf(st, "rs2", [128, 1]) for _ in range(2)]
                        ytmps = [sbuf(st, "ytmp", [128, 1024]) for _ in range(2)]
                        def ldE(i):
                            s.dma(xt2[i % NXT][:], xs_d.ap()[b, tcols(tiles[i]), :], reads=[(XS, b)], writes=[xt2[i % NXT]], g=0)
                        def s1E(ti):
                            t = tiles[ti]
                            if ti + 3 < len(tiles):
                                ldE(ti + 3)
                            xt = xt2[ti % NXT]
                            ss2, rs2, ytmp, jk = ss2s[ti % 2], rs2s[ti % 2], ytmps[ti % 2], junk1[ti % 2]
                            pbs = [bank(), bank()]
                            for hf in range(2):
                                for k in range(8):
                                    s.op('pe', lambda e, k=k, hf=hf: e.matmul(pbs[hf][:, :], lhsT=mixT[:, k, tcols(t)], rhs=wo[:, k, hf * 512:(hf + 1) * 512],
                                                                             start=(k == 0), stop=(k == 7)), reads=[mixT, wo], writes=[pbs[hf]])
                                s.op('act', lambda e, hf=hf: e.activation(out=jk[:, hf * 512:(hf + 1) * 512], in_=pbs[hf][:, :], func=AF.Square,
                                                                         accum_out=ss2[:, hf:hf + 1]), reads=[pbs[hf]], writes=[jk, ss2])
                            s.op('dve', lambda e: e.tensor_tensor(out=rs2[:], in0=ss2[:, 0:1], in1=ss2[:, 1:2], op=ALU.add), reads=[ss2], writes=[rs2])
                            rstd_from_ss(rs2, rs2, 1024)
                            gi = 1 if t < 2 else 0
                            for hf in range(2):
                                s.op('dve', lambda e, hf=hf, gi=gi: e.scalar_tensor_tensor(out=ytmp[:, hf * 512:(hf + 1) * 512], in0=pbs[hf][:, :], scalar=rs2[:, 0:1],
                                                                                        in1=gbc[:, gi, hf * 512:(hf + 1) * 512], op0=ALU.mult, op1=ALU.mult),
                                     reads=[pbs[hf], rs2, gbc], writes=[ytmp])
                            s.op('pool', lambda e, xt=xt: e.tensor_tensor(out=xt[:], in0=xt[:], in1=ytmp[:], op=ALU.add), reads=[xt, ytmp], writes=[xt])
                            s.dma(xs_d.ap()[b, tcols(t), :], xt[:], reads=[xt], writes=[(XS, b)], g=1)

                        def s2E(ti):
                            nmt_front(nbs[ti % 2], xt2[ti % NXT])

                        def s3E(ti):
                            t = tiles[ti]
                            nmt_back(nbs[ti % 2], hT, t, P['s1f'], P['shf'], NB if t < 2 else b)

                        nE = len(tiles)
                        for i0 in range(min(3, nE)):
                            ldE(i0)
                        s1E(0)
                        if nE > 1:
                            s1E(1)
                        s2E(0)
                        for ti in range(nE):
                            if ti + 2 < nE:
                                s1E(ti + 2)
                            if ti + 1 < nE:
                                s2E(ti + 1)
                            s3E(ti)
                        s.barrier()
                  if stop == 'E':
                    early = True
                  if not early:
                    with contextlib.ExitStack() as st:
                        ffn_stage(st, **ENV)
                        s.barrier()
                    if stop is not None and stop.startswith('F'):
                        early = True
                  if stop in ('E', 'L0'):
                    s.dma(DB['xs'].ap(), xs_d.ap()[b], reads=[(XS, b)], g=5)
                    if stop == 'E':
                        s.dma(DB['hT'].ap(), hT[:], reads=[hT], q='pool', g=5)
                    early = True
                  if early:
                    break
            if stop is not None:
                break
        s.barrier(final=True)
    return nc


def shard_inputs(inputs, NB, ncores):
    consts = host_consts()
    maps = []
    for i in range(ncores):
        m = {}
        m['x'] = np.ascontiguousarray(inputs['x'][i * NB:(i + 1) * NB], dtype=np.float32)
        m['ctx'] = np.ascontiguousarray(inputs['ctx'][i * NB:(i + 1) * NB], dtype=np.float32)
        m['cc'] = np.ascontiguousarray(np.concatenate([inputs['c'][i * NB:(i + 1) * NB], inputs['c_ctx'][None, :]], 0), dtype=np.float32)
        for n, sh in PARAMS:
            m[n] = np.ascontiguousarray(inputs[n], dtype=np.float32)
        m.update(consts)
        maps.append(m)
    return maps


def kernel(**inputs):
    inputs = {k: np.asarray(v) for k, v in inputs.items()}
    NB = 4
    nc = build(NB=NB, LAYERS=(0, 1))
    maps = shard_inputs(inputs, NB, 8)
    res = run_bass_kernel_spmd(nc, maps, core_ids=list(range(8)))
    return np.concatenate([r["out"] for r in res.results], axis=0).astype(np.float32)
```
